# Optimizing a Trainium2 kernel written in Bass

```python
import jax, jax.numpy as jnp
from jax import lax
import numpy as np

D_MODEL = 1024
BATCH = 8
SEQ = 4096
DEPTH = 2

MLA_HEADS = 8
QK_NOPE_DIM = 64
QK_ROPE_DIM = 32
V_HEAD_DIM = 64
Q_LORA_RANK = 256
KV_LORA_RANK = 128
QK_HEAD_DIM = QK_NOPE_DIM + QK_ROPE_DIM
D_ATTN = MLA_HEADS * V_HEAD_DIM
ROPE_THETA = 10000.0
Q_BLOCK = 128
SSD_HEADS = 8
SSD_HEAD_DIM = 64
SSD_GROUPS = 2
SSD_STATE = 128
CONV_WIDTH = 4
CHUNK = 128
D_SSD = SSD_HEADS * SSD_HEAD_DIM
D_CONV = D_SSD + 2 * SSD_GROUPS * SSD_STATE
D_MIX = D_ATTN + D_SSD
D_IN = Q_LORA_RANK + KV_LORA_RANK + QK_ROPE_DIM + D_SSD + D_CONV + SSD_HEADS
D_FF = ((-(-8 * D_MODEL // 3) + 255) // 256) * 256
N_MOD = 6
EPS = 1e-6

kernel_name = "hymba_mla_ssd_adaln_block"


def rmsnorm(x, w):
    xf = x.astype(jnp.float32)
    y = xf * lax.rsqrt(jnp.mean(xf * xf, axis=-1, keepdims=True) + EPS)
    return (y * w.astype(jnp.float32)).astype(x.dtype)


def rope(x, cos, sin):
    x1, x2 = jnp.split(x, 2, axis=-1)
    return jnp.concatenate([x1 * cos - x2 * sin, x2 * cos + x1 * sin], axis=-1)


def causal_attention(q, k, v):
    b, h, s, d = q.shape
    nb = s // Q_BLOCK
    scale = d ** -0.5
    qb = q.reshape(b, h, nb, Q_BLOCK, d).transpose(2, 0, 1, 3, 4)
    kpos = jnp.arange(s)

    def one_block(args):
        qi, i = args
        sc = jnp.einsum('bhqd,bhkd->bhqk', qi, k).astype(jnp.float32) * scale
        qpos = i * Q_BLOCK + jnp.arange(Q_BLOCK)
        mask = kpos[None, :] <= qpos[:, None]
        sc = jnp.where(mask, sc, jnp.finfo(jnp.float32).min)
        p = jax.nn.softmax(sc, axis=-1)
        return jnp.einsum('bhqk,bhkd->bhqd', p.astype(v.dtype), v)

    out = lax.map(one_block, (qb, jnp.arange(nb)))
    return out.transpose(1, 0, 3, 2, 4).reshape(b, s, h * v.shape[-1])


def ssd_chunked(x, dt, a, bm, cm):
    b, l, h, p = x.shape
    rep = h // bm.shape[2]
    nc = l // CHUNK
    xf = x.astype(jnp.float32)
    xdt = xf * dt[..., None]
    adt = dt * a
    bh = jnp.repeat(bm.astype(jnp.float32), rep, axis=2)
    ch = jnp.repeat(cm.astype(jnp.float32), rep, axis=2)
    n = bh.shape[-1]
    xc = xdt.reshape(b, nc, CHUNK, h, p)
    bc = bh.reshape(b, nc, CHUNK, h, n)
    cc = ch.reshape(b, nc, CHUNK, h, n)
    acs = jnp.cumsum(adt.reshape(b, nc, CHUNK, h), axis=2)
    seg = acs[:, :, :, None, :] - acs[:, :, None, :, :]
    tri = jnp.tril(jnp.ones((CHUNK, CHUNK), dtype=bool))[None, None, :, :, None]
    lmat = jnp.exp(jnp.where(tri, seg, -jnp.inf))
    scores = jnp.einsum('bclhn,bcshn->bclsh', cc, bc) * lmat
    y_diag = jnp.einsum('bclsh,bcshp->bclhp', scores, xc)
    decay_states = jnp.exp(acs[:, :, -1:, :] - acs)
    states = jnp.einsum('bcshn,bcsh,bcshp->bchpn', bc, decay_states, xc)
    chunk_decay = jnp.exp(acs[:, :, -1, :])

    def step(carry, inp):
        s_c, d_c = inp
        new = d_c[:, :, None, None] * carry + s_c
        return new, carry

    init = jnp.zeros((b, h, p, n), jnp.float32)
    _, prev = lax.scan(step, init, (states.transpose(1, 0, 2, 3, 4), chunk_decay.transpose(1, 0, 2)))
    prev = prev.transpose(1, 0, 2, 3, 4)
    y_off = jnp.einsum('bclhn,bchpn,bclh->bclhp', cc, prev, jnp.exp(acs))
    return (y_diag + y_off).reshape(b, l, h, p)


def hybrid_mixer(h, cos, sin, w_in, q_a_norm_w, w_q_up, kv_a_norm_w, w_kv_up,
                 q_nope_norm_w, q_pe_norm_w, k_nope_norm_w, k_pe_norm_w,
                 conv_w, conv_b, dt_bias, a_log, d_skip, ssd_norm_w, w_out):
    b, s, _ = h.shape
    proj = h @ w_in
    cuts = np.cumsum([Q_LORA_RANK, KV_LORA_RANK, QK_ROPE_DIM, D_SSD, D_CONV])
    q_a, kv_a, k_pe, z, xbc, dt_raw = jnp.split(proj, [int(i) for i in cuts], axis=-1)

    q = (rmsnorm(q_a, q_a_norm_w) @ w_q_up).reshape(b, s, MLA_HEADS, QK_HEAD_DIM)
    q_nope, q_pe = jnp.split(q, [QK_NOPE_DIM], axis=-1)
    kv = (rmsnorm(kv_a, kv_a_norm_w) @ w_kv_up).reshape(b, s, MLA_HEADS, QK_NOPE_DIM + V_HEAD_DIM)
    k_nope, v = jnp.split(kv, [QK_NOPE_DIM], axis=-1)
    q_nope = rmsnorm(q_nope, q_nope_norm_w)
    q_pe = rope(rmsnorm(q_pe, q_pe_norm_w), cos, sin)
    k_nope = rmsnorm(k_nope, k_nope_norm_w)
    k_pe = rope(rmsnorm(k_pe[:, :, None, :], k_pe_norm_w), cos, sin)
    q_full = jnp.concatenate([q_nope, q_pe], axis=-1)
    k_full = jnp.concatenate([k_nope, jnp.broadcast_to(k_pe, (b, s, MLA_HEADS, QK_ROPE_DIM))], axis=-1)
    attn_out = causal_attention(q_full.transpose(0, 2, 1, 3), k_full.transpose(0, 2, 1, 3),
                                v.transpose(0, 2, 1, 3))

    xbc = lax.conv_general_dilated(xbc, conv_w[:, None, :], window_strides=(1,),
                                   padding=[(CONV_WIDTH - 1, 0)],
                                   dimension_numbers=('NWC', 'WIO', 'NWC'),
                                   feature_group_count=D_CONV)
    xbc = jax.nn.silu(xbc + conv_b)
    xs, bm, cm = jnp.split(xbc, [D_SSD, D_SSD + SSD_GROUPS * SSD_STATE], axis=-1)
    xs = xs.reshape(b, s, SSD_HEADS, SSD_HEAD_DIM)
    bm = bm.reshape(b, s, SSD_GROUPS, SSD_STATE)
    cm = cm.reshape(b, s, SSD_GROUPS, SSD_STATE)
    dt = jax.nn.softplus(dt_raw.astype(jnp.float32) + dt_bias.astype(jnp.float32))
    a = -jnp.exp(a_log.astype(jnp.float32))
    y = ssd_chunked(xs, dt, a, bm, cm) + d_skip.astype(jnp.float32)[:, None] * xs.astype(jnp.float32)
    y = y.astype(h.dtype).reshape(b, s, D_SSD)
    yg = (y * jax.nn.silu(z)).reshape(b, s, SSD_GROUPS, D_SSD // SSD_GROUPS)
    yg = rmsnorm(yg, jnp.ones((D_SSD // SSD_GROUPS,), h.dtype)).reshape(b, s, D_SSD) * ssd_norm_w

    return jnp.concatenate([attn_out, yg], axis=-1) @ w_out


def swiglu(h, w_gate_up, w_down):
    g, u = jnp.split(h @ w_gate_up, 2, axis=-1)
    return (jax.nn.silu(g) * u) @ w_down


def setup_inputs(seed: int = 0) -> dict:
    key = jax.random.key(seed)
    ks = jax.random.split(key, 32)
    f32 = jnp.float32

    def nrm(k, shape, scale):
        return jax.random.normal(k, shape, f32) * scale

    def gain(k, shape):
        return 1.0 + 0.05 * jax.random.normal(k, shape, f32)

    L = DEPTH
    dt0 = jnp.exp(jax.random.uniform(ks[18], (L, SSD_HEADS), f32, np.log(1e-3), np.log(1e-1)))
    dt_bias = dt0 + jnp.log(-jnp.expm1(-dt0))
    pos_off = jax.random.randint(ks[2], (BATCH, 1), 0, 1024, dtype=jnp.int32)
    return {
        "x": nrm(ks[0], (BATCH, SEQ, D_MODEL), 1.0),
        "c": nrm(ks[1], (BATCH, D_MODEL), 1.0),
        "positions": pos_off + jnp.arange(SEQ, dtype=jnp.int32)[None, :],
        "norm1_w": gain(ks[3], (L, D_MODEL)),
        "norm2_w": gain(ks[4], (L, D_MODEL)),
        "w_ada": nrm(ks[5], (L, D_MODEL, N_MOD * D_MODEL), 0.5 * D_MODEL ** -0.5),
        "b_ada": nrm(ks[6], (L, N_MOD * D_MODEL), 0.01),
        "w_in": nrm(ks[7], (L, D_MODEL, D_IN), D_MODEL ** -0.5),
        "q_a_norm_w": gain(ks[8], (L, Q_LORA_RANK)),
        "w_q_up": nrm(ks[9], (L, Q_LORA_RANK, MLA_HEADS * QK_HEAD_DIM), Q_LORA_RANK ** -0.5),
        "kv_a_norm_w": gain(ks[10], (L, KV_LORA_RANK)),
        "w_kv_up": nrm(ks[11], (L, KV_LORA_RANK, MLA_HEADS * (QK_NOPE_DIM + V_HEAD_DIM)), KV_LORA_RANK ** -0.5),
        "q_nope_norm_w": gain(ks[12], (L, QK_NOPE_DIM)),
        "q_pe_norm_w": gain(ks[13], (L, QK_ROPE_DIM)),
        "k_nope_norm_w": gain(ks[14], (L, QK_NOPE_DIM)),
        "k_pe_norm_w": gain(ks[15], (L, QK_ROPE_DIM)),
        "conv_w": nrm(ks[16], (L, CONV_WIDTH, D_CONV), CONV_WIDTH ** -0.5),
        "conv_b": nrm(ks[17], (L, D_CONV), 0.01),
        "dt_bias": dt_bias,
        "a_log": jnp.log(jax.random.uniform(ks[19], (L, SSD_HEADS), f32, 1.0, 16.0)),
        "d_skip": 1.0 + 0.1 * jax.random.normal(ks[20], (L, SSD_HEADS), f32),
        "ssd_norm_w": gain(ks[21], (L, D_SSD)),
        "w_out": nrm(ks[22], (L, D_MIX, D_MODEL), D_MIX ** -0.5),
        "w_gate_up": nrm(ks[23], (L, D_MODEL, 2 * D_FF), D_MODEL ** -0.5),
        "w_down": nrm(ks[24], (L, D_FF, D_MODEL), D_FF ** -0.5),
    }


def reference(x, c, positions, norm1_w, norm2_w, w_ada, b_ada, w_in, q_a_norm_w, w_q_up,
              kv_a_norm_w, w_kv_up, q_nope_norm_w, q_pe_norm_w, k_nope_norm_w, k_pe_norm_w,
              conv_w, conv_b, dt_bias, a_log, d_skip, ssd_norm_w, w_out, w_gate_up, w_down):
    inv_freq = 1.0 / (ROPE_THETA ** (jnp.arange(0, QK_ROPE_DIM, 2, dtype=jnp.float32) / QK_ROPE_DIM))
    ang = positions.astype(jnp.float32)[..., None] * inv_freq
    cos = jnp.cos(ang)[:, :, None, :].astype(x.dtype)
    sin = jnp.sin(ang)[:, :, None, :].astype(x.dtype)
    c_act = jax.nn.silu(c)
    for l in range(DEPTH):
        mod = (c_act @ w_ada[l] + b_ada[l])[:, None, :]
        sh1, sc1, g1, sh2, sc2, g2 = jnp.split(mod, N_MOD, axis=-1)
        h = rmsnorm(x, norm1_w[l]) * (1.0 + sc1) + sh1
        x = x + g1 * hybrid_mixer(h, cos, sin, w_in[l], q_a_norm_w[l], w_q_up[l], kv_a_norm_w[l],
                                  w_kv_up[l], q_nope_norm_w[l], q_pe_norm_w[l], k_nope_norm_w[l],
                                  k_pe_norm_w[l], conv_w[l], conv_b[l], dt_bias[l], a_log[l],
                                  d_skip[l], ssd_norm_w[l], w_out[l])
        h = rmsnorm(x, norm2_w[l]) * (1.0 + sc2) + sh2
        x = x + g2 * swiglu(h, w_gate_up[l], w_down[l])
    return x
```

```python
import contextlib
import math
from functools import partial

import numpy as np
import concourse.bass as bass
import concourse.mybir as mybir
from concourse.bass_utils import run_bass_kernel_spmd

F32 = mybir.dt.float32
BF16 = mybir.dt.bfloat16
I32 = mybir.dt.int32
ALU = mybir.AluOpType
AF = mybir.ActivationFunctionType
AX = mybir.AxisListType

D = 1024
DEPTH = 2
NH = 8
D_IN = 1960
D_FF = 2816
EPS = 1e-6
ENGS = ("pe", "act", "dve", "pool", "sp")
N_DMA_SEMS = 16


class _Op:
    __slots__ = ("eng", "fn", "deps", "dma", "signal", "count", "sem")


class Prog:
    def __init__(self, nc, stack):
        self.nc = nc
        self.esem = {e: stack.enter_context(nc.semaphore("s_" + e)) for e in ENGS}
        self.dsem = [stack.enter_context(nc.semaphore("d%d" % i)) for i in range(N_DMA_SEMS)]
        self.ecnt = {e: 0 for e in ENGS}
        self.dcnt = [0] * N_DMA_SEMS
        self.nd = 0
        self.nops = 0
        self._reset()

    def _reset(self):
        self.ops = []
        self.last_w = {}
        self.readers = {}
        self.eng_ops = {e: [] for e in ENGS}

    def op(self, eng, fn, reads=(), writes=(), dma=False):
        o = _Op()
        o.eng, o.fn, o.dma, o.signal, o.count, o.sem = eng, fn, dma, False, 0, None
        need = {}
        for r in reads:
            w = self.last_w.get(r)
            if w is not None:
                self._dep(need, o, w, "raw")
        for r in writes:
            w = self.last_w.get(r)
            if w is not None:
                self._dep(need, o, w, "waw")
            for rd in self.readers.get(r, ()):
                self._dep(need, o, rd, "war")
        o.deps = list(need.values())
        for r in reads:
            self.readers.setdefault(r, []).append(o)
        for r in writes:
            self.last_w[r] = o
            self.readers[r] = []
        self.ops.append(o)
        self.eng_ops[eng].append(o)
        return o

    @staticmethod
    def _dep(need, o, d, kind):
        if d is o:
            return
        if d.eng == o.eng and not d.dma and not o.dma and o.eng == "pe":
            return
        need[id(d)] = d

    def dma(self, out, in_, reads=(), writes=(), eng="sp", **kw):
        q = {"sp": self.nc.sync, "act": self.nc.scalar, "pool": self.nc.gpsimd}[eng]
        return self.op(eng, partial(q.dma_start, out=out, in_=in_, **kw), reads, writes, dma=True)

    def flush(self):
        nc = self.nc
        self.nops += len(self.ops)
        for o in self.ops:
            for d in o.deps:
                d.signal = True
        for e in ENGS:
            comp = [o for o in self.eng_ops[e] if not o.dma]
            if comp:
                comp[-1].signal = True
        dlast = [None] * N_DMA_SEMS
        for o in self.ops:
            if o.dma:
                o.signal = True
                k = self.nd % N_DMA_SEMS
                self.nd += 1
                self.dcnt[k] += 16
                o.sem, o.count = self.dsem[k], self.dcnt[k]
                if dlast[k] is not None:
                    o.deps.append(dlast[k])
                dlast[k] = o
            elif o.signal:
                self.ecnt[o.eng] += 1
                o.sem, o.count = self.esem[o.eng], self.ecnt[o.eng]
        dcnt = list(self.dcnt)
        ecnt = dict(self.ecnt)
        eng_ops = self.eng_ops
        dsem, esem = self.dsem, self.esem

        def run(engname, e):
            waited = {}
            for o in eng_ops[engname]:
                for d in o.deps:
                    key = id(d.sem)
                    if waited.get(key, 0) >= d.count:
                        continue
                    e.wait_ge(d.sem, d.count)
                    waited[key] = d.count
                ins = o.fn()
                if o.signal:
                    ins.then_inc(o.sem, 16 if o.dma else 1)
            for en in ENGS:
                if ecnt[en] and waited.get(id(esem[en]), 0) < ecnt[en]:
                    e.wait_ge(esem[en], ecnt[en])
            for k in range(N_DMA_SEMS):
                if dcnt[k] and waited.get(id(dsem[k]), 0) < dcnt[k]:
                    e.wait_ge(dsem[k], dcnt[k])

        with nc.Block() as blk:
            @blk.tensor
            def _(e):
                run("pe", e)

            @blk.scalar
            def _(e):
                run("act", e)

            @blk.vector
            def _(e):
                run("dve", e)

            @blk.gpsimd
            def _(e):
                run("pool", e)

            @blk.sync
            def _(e):
                run("sp", e)
        self._reset()


DBG_STOP = [None]


def build(S, nlayers=DEPTH, debug=False, upto=None):
    NT = S // 128
    NB = S // 512
    nc = bass.Bass("TRN2", target_bir_lowering=False)
    okind = "ExternalOutput" if debug else "Internal"

    def din(name, shape, dt=F32):
        return nc.dram_tensor(name, list(shape), dt, kind="ExternalInput").ap()

    def dscr(name, shape, dt):
        return nc.dram_tensor(name, list(shape), dt, kind=okind).ap()

    x_in = din("x", [S, D])
    c_in = din("c", [1, D])
    pos_in = din("positions", [1, S], I32)
    Ld = DEPTH
    norm1_w = din("norm1_w", [Ld, D]); norm2_w = din("norm2_w", [Ld, D])
    w_ada = din("w_ada", [Ld, D, 6 * D]); b_ada = din("b_ada", [Ld, 6 * D])
    w_in = din("w_in", [Ld, D, D_IN])
    q_a_norm_w = din("q_a_norm_w", [Ld, 256]); w_q_up = din("w_q_up", [Ld, 256, 768])
    kv_a_norm_w = din("kv_a_norm_w", [Ld, 128]); w_kv_up = din("w_kv_up", [Ld, 128, 1024])
    q_nope_norm_w = din("q_nope_norm_w", [Ld, 64]); q_pe_norm_w = din("q_pe_norm_w", [Ld, 32])
    k_nope_norm_w = din("k_nope_norm_w", [Ld, 64]); k_pe_norm_w = din("k_pe_norm_w", [Ld, 32])
    conv_w = din("conv_w", [Ld, 4, 1024]); conv_b = din("conv_b", [Ld, 1024])
    dt_bias = din("dt_bias", [Ld, 8]); a_log = din("a_log", [Ld, 8]); d_skip = din("d_skip", [Ld, 8])
    ssd_norm_w = din("ssd_norm_w", [Ld, 512])
    w_out = din("w_out", [Ld, D, D]); w_gate_up = din("w_gate_up", [Ld, D, 2 * D_FF])
    w_down = din("w_down", [Ld, D_FF, D])
    out = nc.dram_tensor("out", [S, D], F32, kind="ExternalOutput").ap()

    modrow_d_L = [dscr("modrow_d%d" % i, [6, D], F32) for i in range(DEPTH)]
    winb_d_L = [dscr("winb_d%d" % i, [D, D_IN], BF16) for i in range(DEPTH)]
    wqb_d_L = [dscr("wqb_d%d" % i, [256, 768], BF16) for i in range(DEPTH)]
    wkvb_d_L = [dscr("wkvb_d%d" % i, [128, 1024], BF16) for i in range(DEPTH)]
    woutb_d_L = [dscr("woutb_d%d" % i, [D, D], BF16) for i in range(DEPTH)]
    wgub_d_L = [dscr("wgub_d%d" % i, [D, 2 * D_FF], BF16) for i in range(DEPTH)]
    wdb_d_L = [dscr("wdb_d%d" % i, [D_FF, D], BF16) for i in range(DEPTH)]
    QT_d = dscr("QT_d", [NH, 96, S], BF16)
    KT_d = dscr("KT_d", [NH, 96, S], BF16)
    V_d = dscr("V_d", [S, NH * 128], BF16)
    z_d = dscr("z_d", [S, 512], BF16)
    xs_d = dscr("xs_d", [S, 512], BF16)
    B_d = dscr("B_d", [S, 256], BF16)
    BT_d = dscr("BT_d", [2, 128, S], BF16)
    CT_d = dscr("CT_d", [2, 128, S], BF16)
    mixT_d = dscr("mixT_d", [D, S], BF16)
    cs_d = dscr("cs_d", [2, 128, NT * 16], F32)
    cwcb_d_L = [dscr("cwcb_d%d" % i, [128, 40], F32) for i in range(DEPTH)]
    xa_d = dscr("xa_d", [S, D], F32)
    xb_d = dscr("xb_d", [S, D], F32)

    with contextlib.ExitStack() as top:
        P = Prog(nc, top)

        gcnt = [0]

        def mk_sb(st):
            def sb(shape, dt=F32, name=None):
                gcnt[0] += 1
                return st.enter_context(nc.sbuf_tensor(name or ("t%d" % gcnt[0]), list(shape), dt))
            return sb

        def E(eng, obj, fname, *a, r=(), w=(), **kw):
            return P.op(eng, partial(getattr(obj, fname), *a, **kw), r, w)

        def dve(fname, *a, r=(), w=(), **kw):
            return E("dve", nc.vector, fname, *a, r=r, w=w, **kw)

        def pool(fname, *a, r=(), w=(), **kw):
            return E("pool", nc.gpsimd, fname, *a, r=r, w=w, **kw)

        def act(fname, *a, r=(), w=(), **kw):
            return E("act", nc.scalar, fname, *a, r=r, w=w, **kw)

        def pe(fname, *a, r=(), w=(), **kw):
            return E("pe", nc.tensor, fname, *a, r=r, w=w, **kw)

        psb = mk_sb(top)
        psum = top.enter_context(nc.psum_tensor("psum", [128, 8, 512], F32))
        ident = psb([128, 128], BF16, "ident")
        ones_f = psb([128, 128], F32, "ones_f")
        ltri = psb([128, 128], F32, "ltri")
        ustr = psb([128, 128], F32, "ustr")
        ltri_bf = psb([128, 128], BF16, "ltri_bf")
        cst = psb([128, 4], F32, "cst")
        dt_all = psb([128, NT, 8], F32, "dt_all")

        bank_rr = [0]

        def nbank():
            b = bank_rr[0] % 4
            bank_rr[0] += 1
            return b

        pair_rr = [0]

        def npair():
            b = 4 + 2 * (pair_rr[0] % 2)
            pair_rr[0] += 1
            return b

        def PS(b, n=512):
            return psum[:, b, 0:n]

        def PS2(b, n=1024):
            return psum[:, b:b + 2, :].rearrange("p a b -> p (a b)")[:, 0:n]

        def PS16(b):
            return psum[:, b, :].bitcast(BF16)

        def pk(b):
            return ("ps", b)

        def cols_from_row(dst, row, n, rkey, wkey):
            b = nbank()
            for i in range(n):
                pe("matmul", psum[:, b, i:i + 1], row[0:1, i * 128:(i + 1) * 128], ones_f[0:1, 0:1], start=True, stop=True,
                   r=[rkey, "ones_f"], w=[pk(b)])
            dve("tensor_copy", dst, psum[:, b, 0:n], r=[pk(b)], w=[wkey])

        def phase_const():
            with contextlib.ExitStack() as st:
                sb = mk_sb(st)
                pool("memset", ones_f[:], 1.0, w=["ones_f"])
                pool("memset", cst[:, 0:1], EPS, w=["cst"])
                pool("memset", cst[:, 1:2], 1.0, w=["cst"])
                pool("affine_select", ltri[:], ones_f[:], pattern=[[1, 128]], compare_op=ALU.is_ge,
                     fill=0.0, base=0, channel_multiplier=-1, r=["ones_f"], w=["ltri"])
                pool("affine_select", ustr[:], ones_f[:], pattern=[[-1, 128]], compare_op=ALU.is_gt,
                     fill=0.0, base=0, channel_multiplier=1, r=["ones_f"], w=["ustr"])
                pool("affine_select", ident[:], ones_f[:], pattern=[[1, 128]], compare_op=ALU.is_equal,
                     fill=0.0, base=0, channel_multiplier=-1, r=["ones_f"], w=["ident"])
                dve("tensor_copy", ltri_bf[:], ltri[:], r=["ltri"], w=["ltri_bf"])
                posf = sb([128, NT], F32)
                invf = sb([128, 16], F32)
                ang = sb([128, NT, 16], F32)
                kq = sb([128, NT, 16], F32)
                ki = sb([128, NT, 16], I32)
                m1 = sb([128, NT, 16], F32)
                rc = sb([128, NT, 16], F32)
                cosT = sb([128, NT, 16], F32)
                sinT = sb([128, NT, 16], F32)
                prow_i = sb([1, S], I32)
                prow_f = sb([1, S], F32)
                P.dma(prow_i[:], pos_in, writes=["prow_i"])
                dve("tensor_copy", prow_f[:], prow_i[:], r=["prow_i"], w=["prow_f"])
                cols_from_row(posf[:], prow_f, NT, "prow_f", "posf")
                inv = (1.0 / (np.float32(10000.0) ** (np.arange(0, 32, 2, dtype=np.float32) / np.float32(32)))).astype(np.float32)
                for j in range(16):
                    pool("memset", invf[:, j:j + 1], float(inv[j]), w=["invf"])
                dve("tensor_tensor", ang[:], posf[:].unsqueeze(2).to_broadcast([128, NT, 16]),
                    invf[:].unsqueeze(1).to_broadcast([128, NT, 16]), ALU.mult, r=["posf", "invf"], w=["ang"])
                TWO_PI = 2.0 * math.pi
                C1 = 6.28125
                C2 = TWO_PI - C1
                PI_LO = 3.1415925

                def reduce_to_pi(src, skey, dst, dkey):
                    dve("tensor_scalar", kq[:], src[:], 1.0 / TWO_PI, None, ALU.mult, r=[skey], w=["kq"])
                    dve("tensor_copy", ki[:], kq[:], r=["kq"], w=["ki"])
                    dve("tensor_copy", kq[:], ki[:], r=["ki"], w=["kq"])
                    dve("scalar_tensor_tensor", dst[:], kq[:], -C1, src[:], ALU.mult, ALU.add, r=["kq", skey], w=[dkey])
                    dve("scalar_tensor_tensor", dst[:], kq[:], -C2, dst[:], ALU.mult, ALU.add, r=["kq", dkey], w=[dkey])
                    dve("tensor_scalar", m1[:], dst[:], math.pi, None, ALU.is_gt, r=[dkey], w=["m1"])
                    dve("scalar_tensor_tensor", dst[:], m1[:], -TWO_PI, dst[:], ALU.mult, ALU.add, r=["m1", dkey], w=[dkey])
                    dve("tensor_scalar", m1[:], dst[:], -math.pi, None, ALU.is_lt, r=[dkey], w=["m1"])
                    dve("scalar_tensor_tensor", dst[:], m1[:], TWO_PI, dst[:], ALU.mult, ALU.add, r=["m1", dkey], w=[dkey])
                    dve("tensor_scalar", dst[:], dst[:], PI_LO, -PI_LO, ALU.min, ALU.max, r=[dkey], w=[dkey])

                reduce_to_pi(ang, "ang", rc, "rc")
                act("activation", sinT[:], rc[:], AF.Sin, r=["rc"], w=["sinT"])
                dve("tensor_scalar", ang[:], rc[:], math.pi / 2, None, ALU.add, r=["rc"], w=["ang"])
                reduce_to_pi(ang, "ang", rc, "rc")
                act("activation", cosT[:], rc[:], AF.Sin, r=["rc"], w=["cosT"])
                P.dma(cs_d[0], cosT[:].rearrange("p t j -> p (t j)"), reads=["cosT"])
                P.dma(cs_d[1], sinT[:].rearrange("p t j -> p (t j)"), reads=["sinT"])
                P.flush()

        def mod_gen(l, T):
            cT, wst, mrow, brow, nrow, orow, crow = T
            P.dma(crow[:], c_in, writes=["crow"])
            act("activation", crow[:], crow[:], AF.Silu, r=["crow"], w=["crow"])
            cols_from_row(cT[:], crow, 8, "crow", "cT")
            P.dma(brow[:], b_ada[l:l + 1, :], writes=["brow"])
            P.dma(nrow[:, 0:D], norm1_w[l:l + 1, :], writes=["nrow"])
            P.dma(nrow[:, D:2 * D], norm2_w[l:l + 1, :], writes=["nrow"])
            for n in range(12):
                ws = wst[n % 2]
                P.dma(ws[:], w_ada[l, :, n * 512:(n + 1) * 512].rearrange("(k p) c -> p k c", p=128),
                      writes=[("wst", n % 2)])
                b = nbank()
                for k in range(8):
                    pe("matmul", psum[0:1, b, :], cT[:, k:k + 1], ws[:, k, :], start=(k == 0), stop=(k == 7),
                       r=["cT", ("wst", n % 2)], w=[pk(b)])
                dve("tensor_tensor", mrow[:, n * 512:(n + 1) * 512], psum[0:1, b, :], brow[:, n * 512:(n + 1) * 512],
                    ALU.add, r=[pk(b), "brow"], w=["mrow"])
                yield
            for half in range(2):
                o = half * 3 * D
                dve("scalar_tensor_tensor", orow[:, o:o + D], mrow[:, o + D:o + 2 * D], 1.0, nrow[:, half * D:(half + 1) * D],
                    ALU.add, ALU.mult, r=["mrow", "nrow"], w=["orow"])
                dve("tensor_copy", orow[:, o + D:o + 2 * D], mrow[:, o:o + D], r=["mrow"], w=["orow"])
                dve("tensor_copy", orow[:, o + 2 * D:o + 3 * D], mrow[:, o + 2 * D:o + 3 * D], r=["mrow"], w=["orow"])
            P.dma(modrow_d_L[l].rearrange("(o a) b -> o (a b)", o=1), orow[:], reads=["orow"], writes=[("modrow", l)], eng="act")

        def phase_start():
            with contextlib.ExitStack() as st:
                sb = mk_sb(st)
                T = (sb([128, 8], F32), [sb([128, 8, 512], F32) for _ in range(2)], sb([1, 6 * D], F32), sb([1, 6 * D], F32),
                     sb([1, 2 * D], F32), sb([1, 6 * D], F32), sb([1, D], F32))
                cwrow = sb([1, 4 * 1024], F32); cbrow = sb([1, 1024], F32); cwcb = sb([128, 40], F32)
                for _ in mod_gen(0, T):
                    pass
                for ll in range(nlayers):
                    P.dma(cwrow[:], conv_w[ll:ll + 1].rearrange("o k c -> o (k c)"), writes=["cwrow"])
                    P.dma(cbrow[:], conv_b[ll:ll + 1, :], writes=["cbrow"])
                    cols_from_row(cwcb[:, 0:32], cwrow, 32, "cwrow", "cwcb")
                    cols_from_row(cwcb[:, 32:40], cbrow, 8, "cbrow", "cwcb")
                    P.dma(cwcb_d_L[ll], cwcb[:], reads=["cwcb"])
                wg = wprep_gen(0, sb, 2048, "act", which="a")
                for l in range(1, nlayers):
                    for _ in mod_gen(l, T):
                        for _k in range(5):
                            next(wg, None)
                for _ in wg:
                    pass
                P.flush()


        def wprep_gen(l, sb, CW, store_eng, which="all"):
            sin_ = [sb([128, CW], F32) for _ in range(3)]
            sout = [sb([128, CW], BF16) for _ in range(3)]
            g1b = sb([128, D], F32); g2b = sb([128, D], F32)
            P.dma(g1b[:], modrow_d_L[l][2:3, :].partition_broadcast(128), reads=[("modrow", l)], writes=["g1b"])
            P.dma(g2b[:], modrow_d_L[l][5:6, :].partition_broadcast(128), reads=[("modrow", l)], writes=["g2b"])
            jobs = []

            def add(src, dst, R, C, gate=None):
                for r0 in range(0, R, 128):
                    for c0 in range(0, C, CW):
                        c1 = min(C, c0 + CW)
                        jobs.append((src[r0:r0 + 128, c0:c1], dst[r0:r0 + 128, c0:c1], c1 - c0, c0, gate))
            wi = w_in[l]
            if which in ("all", "a"):
                add(wi[:, 0:416], winb_d_L[l][:, 0:416], D, 416)
                add(wi[:, 1952:1960], winb_d_L[l][:, 416:424], D, 8)
                add(wi[:, 416:1952], winb_d_L[l][:, 424:1960], D, 1536)
                add(w_q_up[l], wqb_d_L[l], 256, 768)
                add(w_kv_up[l], wkvb_d_L[l], 128, 1024)
            if which in ("all", "b"):
                add(w_out[l], woutb_d_L[l], D, D, gate=(g1b, "g1b"))
                add(w_gate_up[l], wgub_d_L[l], D, 2 * D_FF)
                add(w_down[l], wdb_d_L[l], D_FF, D, gate=(g2b, "g2b"))
            def load(i):
                if i < len(jobs):
                    P.dma(sin_[i % 3][:, 0:jobs[i][2]], jobs[i][0], writes=[("sin", i % 3)])
            load(0)
            load(1)
            for i, (src, dst, cw, c0, gate) in enumerate(jobs):
                k = i % 3
                load(i + 2)
                if gate is not None:
                    gt, gk = gate
                    eng = pool if store_eng == "sp" else (dve if i % 2 == 0 else pool)
                    eng("tensor_tensor", sout[k][:, 0:cw], sin_[k][:, 0:cw], gt[:, c0:c0 + cw], ALU.mult,
                        r=[("sin", k), gk], w=[("sout", k)])
                elif store_eng == "sp":
                    pool("tensor_copy", sout[k][:, 0:cw], sin_[k][:, 0:cw], r=[("sin", k)], w=[("sout", k)])
                elif k == 1:
                    act("copy", sout[k][:, 0:cw], sin_[k][:, 0:cw], r=[("sin", k)], w=[("sout", k)])
                elif k == 0:
                    dve("tensor_copy", sout[k][:, 0:cw], sin_[k][:, 0:cw], r=[("sin", k)], w=[("sout", k)])
                else:
                    pool("tensor_copy", sout[k][:, 0:cw], sin_[k][:, 0:cw], r=[("sin", k)], w=[("sout", k)])
                P.dma(dst, sout[k][:, 0:cw], reads=[("sout", k)], eng=store_eng)
                yield

        def phase_wprep(l):
            with contextlib.ExitStack() as st:
                sb = mk_sb(st)
                cwrow = sb([1, 4 * 1024], F32); cbrow = sb([1, 1024], F32); cwcb = sb([128, 40], F32)
                for ll in range(nlayers):
                    P.dma(cwrow[:], conv_w[ll:ll + 1].rearrange("o k c -> o (k c)"), writes=["cwrow"])
                    P.dma(cbrow[:], conv_b[ll:ll + 1, :], writes=["cbrow"])
                    cols_from_row(cwcb[:, 0:32], cwrow, 32, "cwrow", "cwcb")
                    cols_from_row(cwcb[:, 32:40], cbrow, 8, "cbrow", "cwcb")
                    P.dma(cwcb_d_L[ll], cwcb[:], reads=["cwcb"])
                for _ in wprep_gen(l, sb, 2048, "act"):
                    pass
                P.flush()

        def phase_A(l, xsrc):
            with contextlib.ExitStack() as st:
                sb = mk_sb(st)
                win = sb([128, 8, D_IN], BF16)
                wq = sb([128, 2, 768], BF16)
                wkv = sb([128, 1024], BF16)
                weff = sb([128, D], F32); shb = sb([128, D], F32)
                qaw = sb([128, 256], F32); kvaw = sb([128, 128], F32); kpw = sb([128, 32], F32)
                qnw = sb([128, 64], F32); qpw = sb([128, 32], F32); knw = sb([128, 64], F32)
                cw = sb([128, 4, 8], F32); cb = sb([128, 8], F32); dtb = sb([128, 8], F32)
                invn = sb([128, 27], F32)
                cosT = sb([128, NT, 16], F32)
                sinT = sb([128, NT, 16], F32)
                P.dma(cosT[:].rearrange("p t j -> p (t j)"), cs_d[0], writes=["cosT"])
                P.dma(sinT[:].rearrange("p t j -> p (t j)"), cs_d[1], writes=["sinT"])
                P.dma(win[:], winb_d_L[l].rearrange("(k p) c -> p k c", p=128), writes=["win"])
                P.dma(wq[:], wqb_d_L[l].rearrange("(k p) c -> p k c", p=128), writes=["wq"])
                P.dma(wkv[:], wkvb_d_L[l], writes=["wkv"])
                P.dma(weff[:], modrow_d_L[l][0:1, :].partition_broadcast(128), writes=["weff"])
                P.dma(shb[:], modrow_d_L[l][1:2, :].partition_broadcast(128), writes=["shb"])
                P.dma(qaw[:], q_a_norm_w[l:l + 1, :].partition_broadcast(128), writes=["qaw"])
                P.dma(kvaw[:], kv_a_norm_w[l:l + 1, :].partition_broadcast(128), writes=["kvaw"])
                P.dma(kpw[:], k_pe_norm_w[l:l + 1, :].partition_broadcast(128), writes=["kpw"])
                P.dma(qnw[:], q_nope_norm_w[l:l + 1, :].partition_broadcast(128), writes=["qnw"])
                P.dma(qpw[:], q_pe_norm_w[l:l + 1, :].partition_broadcast(128), writes=["qpw"])
                P.dma(knw[:], k_nope_norm_w[l:l + 1, :].partition_broadcast(128), writes=["knw"])
                P.dma(dtb[:], dt_bias[l:l + 1, :].partition_broadcast(128), writes=["dtb"])
                P.dma(cw[:].rearrange("p k m -> p (k m)"), cwcb_d_L[l][:, 0:32], writes=["cw"])
                P.dma(cb[:], cwcb_d_L[l][:, 32:40], writes=["cb"])
                scale = 96.0 ** -0.5
                dve("tensor_scalar", qnw[:], qnw[:], scale, None, ALU.mult, r=["qnw"], w=["qnw"])
                dve("tensor_scalar", qpw[:], qpw[:], scale, None, ALU.mult, r=["qpw"], w=["qpw"])
                pool("memset", invn[:, 0:1], 1.0 / 256, w=["invn"])
                pool("memset", invn[:, 1:2], 1.0 / 128, w=["invn"])
                pool("memset", invn[:, 2:3], 1.0 / 32, w=["invn"])
                pool("memset", invn[:, 3:11], 1.0 / 64, w=["invn"])
                pool("memset", invn[:, 11:19], 1.0 / 32, w=["invn"])
                pool("memset", invn[:, 19:27], 1.0 / 64, w=["invn"])

                xt = [sb([128, D], F32) for _ in range(4)]
                junk = sb([128, D], BF16)
                ssx = sb([128, 4], F32); rsx = sb([128, 4], F32)
                htmp_2 = [sb([128, D], F32) for _ in range(2)]
                h = [sb([128, D], BF16) for _ in range(2)]
                hT = sb([128, 8, 512], BF16)
                xbcT = [sb([128, 516], BF16) for _ in range(8)]
                dg = sb([128, 8, 4, 128], BF16)
                xsT = sb([128, 8, 512], BF16)
                raw1 = [sb([128, 424], F32) for _ in range(4)]
                zt = [sb([128, 512], BF16) for _ in range(2)]
                dtr = sb([128, 4, 8], F32)
                ss1 = sb([128, 4, 3], F32); rs1 = sb([128, 4, 3], F32)
                latn_2 = [sb([128, 384], BF16) for _ in range(2)]
                latT_2 = [sb([128, 3, 128], BF16) for _ in range(2)]
                qsb = [sb([128, 768], F32) for _ in range(4)]
                kvsb = [sb([128, 8, 64], F32) for _ in range(4)]
                sq_2 = [sb([128, 768], F32) for _ in range(2)]
                ss2 = sb([128, 4, 24], F32); rs2 = sb([128, 4, 24], F32)
                kpn = [sb([128, 32], F32) for _ in range(4)]
                Qf_2 = [sb([128, 8, 96], BF16) for _ in range(2)]; Kf_2 = [sb([128, 8, 96], BF16) for _ in range(2)]
                ra_2 = [sb([128, 8, 32], F32) for _ in range(2)]; rb_2 = [sb([128, 8, 32], F32) for _ in range(2)]
                rq_2 = [sb([128, 8, 32], F32) for _ in range(2)]
                ka_2 = [sb([128, 32], F32) for _ in range(2)]; kb_2 = [sb([128, 32], F32) for _ in range(2)]
                kr_2 = [sb([128, 32], F32) for _ in range(2)]
                vaug = [sb([128, 8, 128], BF16) for _ in range(2)]
                QTb = sb([96, 8, 512], BF16); KTb = sb([96, 8, 512], BF16)
                tok_o = [sb([128, 768], BF16) for _ in range(2)]

                for i in range(2):
                    pool("memset", vaug[i][:], 1.0, w=[("vaug", i)])
                for m in range(8):
                    pool("memset", xbcT[m][:, 512:515], 0.0, w=[("xbcT", m)])
                for m in range(8):
                    for k in range(4):
                        dve("tensor_scalar", dg[:, m, k, :], ident[:], cw[:, k, m:m + 1], None, ALU.mult, r=["ident", "cw"], w=["dg"])

                def n_stats(blk, j):
                    t = blk * 4 + j
                    P.dma(xt[j][:], xsrc[t * 128:(t + 1) * 128, :], writes=[("xt", j)])
                    act("activation", junk[:], xt[j][:], AF.Square, accum_out=ssx[:, j:j + 1],
                        r=[("xt", j)], w=["junk", "ssx"])

                def n_rstd():
                    act("activation", rsx[:], ssx[:], AF.Ln, scale=1.0 / D, bias=cst[:, 0:1], r=["ssx", "cst"], w=["rsx"])
                    act("activation", rsx[:], rsx[:], AF.Exp, scale=-0.5, r=["rsx"], w=["rsx"])

                def n_hT(j, bank=None):
                    hh = h[j % 2]; htmp = htmp_2[j % 2]
                    dve("scalar_tensor_tensor", htmp[:], xt[j][:], rsx[:, j:j + 1], weff[:], ALU.mult, ALU.mult,
                        r=[("xt", j), "rsx", "weff"], w=[("htmp", j % 2)])
                    pool("tensor_tensor", hh[:], htmp[:], shb[:], ALU.add, r=[("htmp", j % 2), "shb"], w=[("h", j % 2)])
                    b = nbank() if bank is None else bank
                    for k in range(8):
                        pe("transpose", PS16(b)[:, k * 128:(k + 1) * 128], hh[:, k * 128:(k + 1) * 128], ident[:],
                           r=[("h", j % 2), "ident"], w=[pk(b)])
                    act("copy", hT[:, :, j * 128:(j + 1) * 128], PS16(b).rearrange("p (k t) -> p k t", k=8),
                        r=[pk(b)], w=["hT"])

                for j in range(4):
                    n_stats(0, j)
                n_rstd()
                for j in range(4):
                    n_hT(j)

                def P_gen(blk):
                    pb = blk % 2
                    for m in range(9):
                        if m == 5:
                            yield
                        if m < 8:
                            b = nbank()
                            c0 = 936 + m * 128
                            for k in range(8):
                                pe("matmul", PS(b), win[:, k, c0:c0 + 128], hT[:, k, :], start=(k == 0), stop=(k == 7),
                                   r=["win", "hT"], w=[pk(b)])
                            cur = xbcT[m]
                            act("copy", cur[:, 0:3], cur[:, 512:515], r=[("xbcT", m)], w=[("xbcT", m)])
                            act("copy", cur[:, 3:515], PS(b), r=[pk(b), ("xbcT", m)], w=[("xbcT", m)])
                        if m >= 1:
                            mm = m - 1
                            b2 = nbank()
                            for kk in range(4):
                                pe("matmul", PS(b2), dg[:, mm, kk, :], xbcT[mm][:, kk:kk + 512], start=(kk == 0), stop=(kk == 3),
                                   r=["dg", ("xbcT", mm)], w=[pk(b2)])
                            act("activation", xsT[:, mm, :], PS(b2), AF.Silu, bias=cb[:, mm:mm + 1], r=[pk(b2), "cb"], w=[("xsT", mm)])
                    yield
                    for g in range(2):
                        P.dma(BT_d[g, :, blk * 512:(blk + 1) * 512], xsT[:, 4 + g, :], reads=[("xsT", 4 + g)], eng="act")
                        P.dma(CT_d[g, :, blk * 512:(blk + 1) * 512], xsT[:, 6 + g, :], reads=[("xsT", 6 + g)], eng="act")
                    for j in range(4):
                        t = blk * 4 + j
                        b = nbank()
                        for m in range(6):
                            pe("transpose", PS16(b)[:, m * 128:(m + 1) * 128], xsT[:, m, j * 128:(j + 1) * 128], ident[:],
                               r=[("xsT", m), "ident"], w=[pk(b)])
                        to = tok_o[j % 2]
                        dve("tensor_copy", to[:], PS16(b)[:, 0:768], r=[pk(b)], w=[("tok_o", j % 2)])
                        P.dma(xs_d[t * 128:(t + 1) * 128, :], to[:, 0:512], reads=[("tok_o", j % 2)], eng="act")
                        P.dma(B_d[t * 128:(t + 1) * 128, :], to[:, 512:768], reads=[("tok_o", j % 2)], eng="act")
                    yield
                    for j in range(4):
                        t = blk * 4 + j
                        b1 = nbank()
                        for k in range(8):
                            pe("matmul", PS(b1, 424), hT[:, k, j * 128:(j + 1) * 128], win[:, k, 0:424], start=(k == 0), stop=(k == 7),
                               r=["hT", "win"], w=[pk(b1)])
                        b2 = nbank()
                        for k in range(8):
                            pe("matmul", PS(b2), hT[:, k, j * 128:(j + 1) * 128], win[:, k, 424:936], start=(k == 0), stop=(k == 7),
                               r=["hT", "win"], w=[pk(b2)])
                        dve("tensor_copy", raw1[j][:], PS(b1, 424), r=[pk(b1)], w=[("raw1", j)])
                        act("copy", zt[j % 2][:], PS(b2), r=[pk(b2)], w=[("zt", j % 2)])
                        P.dma(z_d[t * 128:(t + 1) * 128, :], zt[j % 2][:], reads=[("zt", j % 2)], eng="act")
                        act("activation", junk[:, 0:256], raw1[j][:, 0:256], AF.Square, accum_out=ss1[:, j, 0:1],
                            r=[("raw1", j)], w=["junk", "ss1"])
                        act("activation", junk[:, 0:128], raw1[j][:, 256:384], AF.Square, accum_out=ss1[:, j, 1:2],
                            r=[("raw1", j)], w=["junk", "ss1"])
                        act("activation", junk[:, 0:32], raw1[j][:, 384:416], AF.Square, accum_out=ss1[:, j, 2:3],
                            r=[("raw1", j)], w=["junk", "ss1"])
                        dve("tensor_tensor", dtr[:, j, :], raw1[j][:, 416:424], dtb[:], ALU.add, r=[("raw1", j), "dtb"], w=["dtr"])
                    yield
                    dve("tensor_tensor", ss1[:], ss1[:], invn[:, 0:3].unsqueeze(1).to_broadcast([128, 4, 3]), ALU.mult,
                        r=["ss1", "invn"], w=["ss1"])
                    act("activation", dtr[:], dtr[:], AF.Exp, r=["dtr"], w=["dtr"])
                    act("activation", dt_all[:, blk * 4:(blk + 1) * 4, :], dtr[:], AF.Ln, bias=cst[:, 1:2], r=["dtr", "cst"], w=["dt_all"])
                    act("activation", rs1[:], ss1[:], AF.Ln, bias=cst[:, 0:1], r=["ss1", "cst"], w=["rs1"])
                    act("activation", rs1[:], rs1[:], AF.Exp, scale=-0.5, r=["rs1"], w=["rs1"])

                for _ in P_gen(0):
                    pass
                for blk in range(NB):
                    pb = blk % 2
                    nxt = blk + 1 < NB
                    def La(j):
                        latn = latn_2[j % 2]; latT = latT_2[j % 2]; sq = sq_2[j % 2]
                        dve("scalar_tensor_tensor", latn[:, 0:256], raw1[j][:, 0:256], rs1[:, j, 0:1], qaw[:], ALU.mult, ALU.mult,
                            r=[("raw1", j), "rs1", "qaw"], w=[("latn", j % 2)])
                        dve("scalar_tensor_tensor", latn[:, 256:384], raw1[j][:, 256:384], rs1[:, j, 1:2], kvaw[:], ALU.mult, ALU.mult,
                            r=[("raw1", j), "rs1", "kvaw"], w=[("latn", j % 2)])
                        dve("scalar_tensor_tensor", kpn[j][:], raw1[j][:, 384:416], rs1[:, j, 2:3], kpw[:], ALU.mult, ALU.mult,
                             r=[("raw1", j), "rs1", "kpw"], w=[("kpn", j)])
                        b = 2 if j % 2 == 0 else 6
                        for k in range(3):
                            pe("transpose", PS16(b)[:, k * 128:(k + 1) * 128], latn[:, k * 128:(k + 1) * 128], ident[:],
                               r=[("latn", j % 2), "ident"], w=[pk(b)])
                        act("copy", latT[:], PS16(b)[:, 0:384].rearrange("p (k t) -> p k t", k=3), r=[pk(b)], w=[("latT", j % 2)])
                        bq = 0 if j % 2 == 0 else 4
                        for (n0, n1, bb) in ((0, 512, bq), (512, 768, bq + 1)):
                            for k in range(2):
                                pe("matmul", PS(bb, n1 - n0), latT[:, k, :], wq[:, k, n0:n1], start=(k == 0), stop=(k == 1),
                                   r=[("latT", j % 2), "wq"], w=[pk(bb)])
                        bk = 2 if j % 2 == 0 else 6
                        for hf in range(2):
                            pe("matmul", PS(bk + hf), latT[:, 2, :], wkv[:, hf * 512:(hf + 1) * 512], start=True, stop=True,
                               r=[("latT", j % 2), "wkv"], w=[pk(bk + hf)])
                        return bq, bk

                    def Lb(j, bq, bk):
                        sq = sq_2[j % 2]
                        act("copy", qsb[j][:], PS2(bq, 768), r=[pk(bq), pk(bq + 1)], w=[("qsb", j)])
                        kvv = PS2(bk).rearrange("p (h e) -> p h e", h=8)
                        dve("tensor_copy", kvsb[j][:], kvv[:, :, 0:64], r=[pk(bk), pk(bk + 1)], w=[("kvsb", j)])
                        va = vaug[j % 2]
                        vap = va[:].rearrange("p (c two) e -> p c two e", two=2)
                        for hf in range(2):
                            kvp = PS(bk + hf).rearrange("p (c two e) -> p c two e", two=2, e=128)
                            dve("tensor_copy", vap[:, 2 * hf:2 * hf + 2, 0, 0:64], kvp[:, :, 0, 64:128], r=[pk(bk + hf)], w=[("vaug", j % 2)])
                            dve("tensor_copy", vap[:, 2 * hf:2 * hf + 2, 1, 64:128], kvp[:, :, 1, 64:128], r=[pk(bk + hf)], w=[("vaug", j % 2)])
                        t = blk * 4 + j
                        if True:
                            P.dma(V_d[t * 128:(t + 1) * 128, :], va[:].rearrange("p h e -> p (h e)"), reads=[("vaug", j % 2)], eng="act")
                        act("activation", sq[:], qsb[j][:], AF.Square, r=[("qsb", j)], w=[("sq", j % 2)])
                        sqv = sq[:].rearrange("p (h e) -> p h e", h=8)
                        dve("tensor_reduce", ss2[:, j, 0:8], sqv[:, :, 0:64], AX.X, ALU.add, r=[("sq", j % 2)], w=["ss2"])
                        dve("tensor_reduce", ss2[:, j, 8:16], sqv[:, :, 64:96], AX.X, ALU.add, r=[("sq", j % 2)], w=["ss2"])
                        act("activation", sq[:, 0:512], kvsb[j][:].rearrange("p h e -> p (h e)"), AF.Square, r=[("kvsb", j)], w=[("sq", j % 2)])
                        dve("tensor_reduce", ss2[:, j, 16:24], sq[:, 0:512].rearrange("p (h e) -> p h e", h=8), AX.X, ALU.add,
                            r=[("sq", j % 2)], w=["ss2"])

                    if nxt:
                        for j in range(4):
                            n_stats(blk + 1, j)
                    pend = {}
                    for j in range(5):
                        if j < 4:
                            pend[j] = La(j)
                        if j == 1 and nxt:
                            n_rstd()
                        if j >= 1:
                            Lb(j - 1, *pend[j - 1])
                            if nxt:
                                n_hT(j - 1, bank=(0 if (j - 1) % 2 == 0 else 4))
                    pg = P_gen(blk + 1) if nxt else None
                    dve("tensor_tensor", ss2[:], ss2[:], invn[:, 3:27].unsqueeze(1).to_broadcast([128, 4, 24]), ALU.mult,
                        r=["ss2", "invn"], w=["ss2"])
                    act("activation", rs2[:], ss2[:], AF.Ln, bias=cst[:, 0:1], r=["ss2", "cst"], w=["rs2"])
                    act("activation", rs2[:], rs2[:], AF.Exp, scale=-0.5, r=["rs2"], w=["rs2"])
                    for j in range(4):
                        t = blk * 4 + j
                        sq = sq_2[j % 2]; Qf = Qf_2[j % 2]; Kf = Kf_2[j % 2]; ra = ra_2[j % 2]; rb = rb_2[j % 2]; rq = rq_2[j % 2]
                        ka = ka_2[j % 2]; kb = kb_2[j % 2]; kr = kr_2[j % 2]
                        qv = qsb[j][:].rearrange("p (h e) -> p h e", h=8)
                        cosb = cosT[:, t, :]; sinb = sinT[:, t, :]
                        dve("tensor_tensor", sq[:, 0:512].rearrange("p (h e) -> p h e", h=8), qv[:, :, 0:64],
                            rs2[:, j, 0:8].unsqueeze(2).to_broadcast([128, 8, 64]), ALU.mult, r=[("qsb", j), "rs2"], w=[("sq", j % 2)])
                        dve("tensor_tensor", Qf[:, :, 0:64], sq[:, 0:512].rearrange("p (h e) -> p h e", h=8),
                            qnw[:].unsqueeze(1).to_broadcast([128, 8, 64]), ALU.mult, r=[("sq", j % 2), "qnw"], w=[("Qf", j % 2)])
                        pool("tensor_tensor", ra[:], qv[:, :, 64:96], rs2[:, j, 8:16].unsqueeze(2).to_broadcast([128, 8, 32]), ALU.mult,
                             r=[("qsb", j), "rs2"], w=[("ra", j % 2)])
                        pool("tensor_tensor", ra[:], ra[:], qpw[:].unsqueeze(1).to_broadcast([128, 8, 32]), ALU.mult, r=[("ra", j % 2), "qpw"], w=[("ra", j % 2)])
                        cb8 = cosb.unsqueeze(1).to_broadcast([128, 8, 16]); sb8 = sinb.unsqueeze(1).to_broadcast([128, 8, 16])
                        pool("tensor_tensor", rb[:, :, 0:16], ra[:, :, 0:16], cb8, ALU.mult, r=[("ra", j % 2), "cosT"], w=[("rb", j % 2)])
                        pool("tensor_tensor", rb[:, :, 16:32], ra[:, :, 16:32], cb8, ALU.mult, r=[("ra", j % 2), "cosT"], w=[("rb", j % 2)])
                        pool("tensor_tensor", rq[:, :, 0:16], ra[:, :, 16:32], sb8, ALU.mult, r=[("ra", j % 2), "sinT"], w=[("rq", j % 2)])
                        pool("tensor_tensor", rq[:, :, 16:32], ra[:, :, 0:16], sb8, ALU.mult, r=[("ra", j % 2), "sinT"], w=[("rq", j % 2)])
                        pool("tensor_tensor", Qf[:, :, 64:80], rb[:, :, 0:16], rq[:, :, 0:16], ALU.subtract, r=[("rb", j % 2), ("rq", j % 2)], w=[("Qf", j % 2)])
                        pool("tensor_tensor", Qf[:, :, 80:96], rb[:, :, 16:32], rq[:, :, 16:32], ALU.add, r=[("rb", j % 2), ("rq", j % 2)], w=[("Qf", j % 2)])
                        dve("tensor_tensor", sq[:, 0:512].rearrange("p (h e) -> p h e", h=8), kvsb[j][:],
                            rs2[:, j, 16:24].unsqueeze(2).to_broadcast([128, 8, 64]), ALU.mult, r=[("kvsb", j), "rs2"], w=[("sq", j % 2)])
                        dve("tensor_tensor", Kf[:, :, 0:64], sq[:, 0:512].rearrange("p (h e) -> p h e", h=8),
                            knw[:].unsqueeze(1).to_broadcast([128, 8, 64]), ALU.mult, r=[("sq", j % 2), "knw"], w=[("Kf", j % 2)])
                        dve("tensor_tensor", ka[:, 0:16], kpn[j][:, 0:16], cosb, ALU.mult, r=[("kpn", j), "cosT"], w=[("ka", j % 2)])
                        dve("tensor_tensor", ka[:, 16:32], kpn[j][:, 16:32], cosb, ALU.mult, r=[("kpn", j), "cosT"], w=[("ka", j % 2)])
                        dve("tensor_tensor", kb[:, 0:16], kpn[j][:, 16:32], sinb, ALU.mult, r=[("kpn", j), "sinT"], w=[("kb", j % 2)])
                        dve("tensor_tensor", kb[:, 16:32], kpn[j][:, 0:16], sinb, ALU.mult, r=[("kpn", j), "sinT"], w=[("kb", j % 2)])
                        dve("tensor_tensor", kr[:, 0:16], ka[:, 0:16], kb[:, 0:16], ALU.subtract, r=[("ka", j % 2), ("kb", j % 2)], w=[("kr", j % 2)])
                        dve("tensor_tensor", kr[:, 16:32], ka[:, 16:32], kb[:, 16:32], ALU.add, r=[("ka", j % 2), ("kb", j % 2)], w=[("kr", j % 2)])
                        dve("tensor_copy", Kf[:, :, 64:96], kr[:].unsqueeze(1).to_broadcast([128, 8, 32]), r=[("kr", j % 2)], w=[("Kf", j % 2)])
                        bqt = nbank()
                        for hh_ in range(8):
                            pe("transpose", PS16(bqt)[0:96, hh_ * 128:(hh_ + 1) * 128], Qf[:, hh_, :], ident[:],
                               r=[("Qf", j % 2), "ident"], w=[pk(bqt)])
                        act("copy", QTb[:, :, j * 128:(j + 1) * 128], PS16(bqt)[0:96, :].rearrange("p (h t) -> p h t", h=8),
                            r=[pk(bqt)], w=["QTb"])
                        bkt = nbank()
                        for hh_ in range(8):
                            pe("transpose", PS16(bkt)[0:96, hh_ * 128:(hh_ + 1) * 128], Kf[:, hh_, :], ident[:],
                               r=[("Kf", j % 2), "ident"], w=[pk(bkt)])
                        act("copy", KTb[:, :, j * 128:(j + 1) * 128], PS16(bkt)[0:96, :].rearrange("p (h t) -> p h t", h=8),
                            r=[pk(bkt)], w=["KTb"])
                        if pg is not None:
                            next(pg, None)
                            if j == 3:
                                for _ in pg:
                                    pass
                    P.dma(QT_d[:, :, blk * 512:(blk + 1) * 512].rearrange("h d s -> d h s"), QTb[:], reads=["QTb"], eng="act")
                    P.dma(KT_d[:, :, blk * 512:(blk + 1) * 512].rearrange("h d s -> d h s"), KTb[:], reads=["KTb"], eng="act")
                P.flush()

        def phase_B(l):
            with contextlib.ExitStack() as st:
                sb = mk_sb(st)
                KT = sb([96, 8, S], BF16)
                Vs = sb([128, NT, 8 * 128], BF16)
                QTs = [sb([96, 8, 512], BF16) for _ in range(2)]
                onesb = sb([128, 512], BF16)
                amask = [sb([128, 512], BF16) for _ in range(4)]
                pT = [sb([128, 2, 512], BF16) for _ in range(4)]
                rden = [sb([128, 512], F32) for _ in range(2)]
                aT = [sb([128, 4, 512], BF16) for _ in range(2)]
                pool("memset", onesb[:], 1.0, w=["onesb"])
                for j in range(4):
                    pool("affine_select", amask[j][:], onesb[:], pattern=[[1, 512]], compare_op=ALU.is_ge, fill=0.0,
                         base=-128 * j, channel_multiplier=-1, r=["onesb"], w=[("amask", j)])
                P.dma(KT[:], KT_d.rearrange("h d s -> d h s"), writes=["KT"])
                for t0 in range(0, NT, 8):
                    t1 = min(NT, t0 + 8)
                    P.dma(Vs[:, t0:t1, :], V_d[t0 * 128:t1 * 128, :].rearrange("(t p) f -> p t f", p=128), writes=[("Vs", t0)])

                def load_q(blk):
                    P.dma(QTs[blk % 2][:], QT_d[:, :, blk * 512:(blk + 1) * 512].rearrange("h d s -> d h s"),
                          writes=[("QTs", blk % 2)])

                its = [(blk, hd, p) for blk in range(NB) for hd in range(NH) for p in range(2 * blk + 2)]

                def emit_s(i):
                    blk, hd, p = its[i]
                    if hd == 0 and p == 0:
                        if blk == 0:
                            load_q(0)
                        if blk + 1 < NB:
                            load_q(blk + 1)
                    qb = QTs[blk % 2]
                    sbp = 2 * (i % 3)
                    pp = pT[i % 4]; pkey = ("pT", i % 4)
                    for a_ in range(2):
                        kt = 2 * p + a_
                        pe("matmul", PS(sbp + a_), KT[:, hd, kt * 128:(kt + 1) * 128], qb[:, hd, :], start=True, stop=True,
                           r=["KT", ("QTs", blk % 2)], w=[pk(sbp + a_)])
                    act("activation", pp[:], psum[:, sbp:sbp + 2, :], AF.Exp, r=[pk(sbp), pk(sbp + 1)], w=[pkey])
                    for a_ in range(2):
                        jd = 2 * p + a_ - 4 * blk
                        if jd >= 0:
                            w_ = 128 * (jd + 1)
                            eng = dve if (jd % 2 == 0 or l + 1 < nlayers) else pool
                            eng("tensor_tensor", pp[:, a_, 0:w_], pp[:, a_, 0:w_], amask[jd][:, 0:w_], ALU.mult,
                                r=[pkey, ("amask", jd)], w=[pkey])

                def emit_pv(i):
                    blk, hd, p = its[i]
                    nk = 4 * blk + 4
                    pp = pT[i % 4]; pkey = ("pT", i % 4)
                    bo = 6 + (hd % 2)
                    for a_ in range(2):
                        kt = 2 * p + a_
                        pe("matmul", PS(bo), Vs[:, kt, hd * 128:(hd + 1) * 128], pp[:, a_, :], start=(kt == 0), stop=(kt == nk - 1),
                           r=[("Vs", (kt // 8) * 8), pkey], w=[pk(bo)])
                    if p == 2 * blk + 1:
                        at = aT[blk % 2]
                        rd = rden[hd % 2]
                        c = hd // 2
                        if hd % 2 == 0:
                            dve("reciprocal", rd[64:128, :], psum[64:128, bo, :], r=[pk(bo)], w=[("rden", 0)])
                            dve("tensor_tensor", at[0:64, c, :], psum[0:64, bo, :], rd[64:128, :], ALU.mult,
                                r=[pk(bo), ("rden", 0)], w=[("aT", blk % 2)])
                        else:
                            dve("reciprocal", rd[0:64, :], psum[0:64, bo, :], r=[pk(bo)], w=[("rden", 1)])
                            dve("tensor_tensor", at[64:128, c, :], psum[64:128, bo, :], rd[0:64, :], ALU.mult,
                                r=[pk(bo), ("rden", 1)], w=[("aT", blk % 2)])
                        if hd == NH - 1:
                            P.dma(mixT_d[0:512, blk * 512:(blk + 1) * 512].rearrange("(c p) s -> p c s", p=128), at[:],
                                  reads=[("aT", blk % 2)])

                n = len(its)
                gen = wprep_gen(l + 1, sb, 1024, "sp") if l + 1 < nlayers else None
                for i in range(n + 2):
                    if i < n:
                        emit_s(i)
                    if i >= 2:
                        emit_pv(i - 2)
                    if gen is not None and i % 3 == 2:
                        next(gen, None)
                if gen is not None:
                    for _ in gen:
                        pass
                P.flush()

        def phase_C(l):
            with contextlib.ExitStack() as st:
                sb = mk_sb(st)
                ab = sb([128, 8], F32); dsk = sb([128, 8], F32); nwb = sb([128, 512], F32)
                P.dma(ab[:], a_log[l:l + 1, :].partition_broadcast(128), writes=["ab"])
                P.dma(dsk[:], d_skip[l:l + 1, :].partition_broadcast(128), writes=["dsk"])
                P.dma(nwb[:], ssd_norm_w[l:l + 1, :].partition_broadcast(128), writes=["nwb"])
                act("activation", ab[:], ab[:], AF.Exp, r=["ab"], w=["ab"])
                dve("tensor_scalar", ab[:], ab[:], -1.0, None, ALU.mult, r=["ab"], w=["ab"])
                BTs = sb([128, 2, S], BF16); CTs = sb([128, 2, S], BF16)
                P.dma(BTs[:], BT_d.rearrange("g n s -> n g s"), writes=["BTs"])
                P.dma(CTs[:], CT_d.rearrange("g n s -> n g s"), writes=["CTs"])
                xs4 = [sb([128, 4, 512], BF16) for _ in range(2)]
                B4 = [sb([128, 4, 256], BF16) for _ in range(2)]
                z4 = [sb([128, 4, 512], BF16) for _ in range(2)]
                sz4 = [sb([128, 4, 512], F32) for _ in range(2)]
                adt_2 = [sb([128, 8], F32) for _ in range(3)]
                rhs_all_2 = [sb([128, 8, 128], F32) for _ in range(3)]
                ET_2 = [sb([128, 8, 128], BF16) for _ in range(3)]
                small_2 = [sb([128, 16], F32) for _ in range(3)]
                dec_2 = [sb([128, 8], F32) for _ in range(3)]
                sm_2 = [sb([128, 2, 128], BF16) for _ in range(3)]
                MT_2 = [sb([128, 8, 128], BF16) for _ in range(3)]
                xdt_2 = [sb([128, 8, 64], BF16) for _ in range(3)]; xdd_2 = [sb([128, 8, 64], BF16) for _ in range(3)]
                state = sb([128, 8, 64], F32); state_bf = sb([128, 8, 64], BF16)
                yo_2 = [sb([128, 8, 64], F32) for _ in range(3)]; y_2 = [sb([128, 512], F32) for _ in range(3)]
                xsd_2 = [sb([128, 512], F32) for _ in range(3)]
                yg = sb([128, 512], F32); junk = sb([128, 256], F32)
                ssg = sb([128, 4, 2], F32); rsg = sb([128, 4, 2], F32)
                yg4 = sb([128, 4, 512], F32)
                yn_2 = [sb([128, 512], BF16) for _ in range(2)]
                ygT = sb([128, 4, 512], BF16)
                dve("memset", state[:], 0.0, w=["state"])
                dve("memset", state_bf[:], 0.0, w=["state_bf"])
                def load_blk(blk):
                    pb = blk % 2
                    rows = slice(blk * 512, (blk + 1) * 512)
                    P.dma(xs4[pb][:], xs_d[rows, :].rearrange("(j p) f -> p j f", p=128), writes=[("xs4", pb)])
                    P.dma(B4[pb][:], B_d[rows, :].rearrange("(j p) f -> p j f", p=128), writes=[("B4", pb)])
                    P.dma(z4[pb][:], z_d[rows, :].rearrange("(j p) f -> p j f", p=128), writes=[("z4", pb)])
                    act("activation", sz4[pb][:], z4[pb][:], AF.Silu, r=[("z4", pb)], w=[("sz4", pb)])

                def s1(blk, j):
                    pb = blk % 2
                    jb = (blk * 4 + j) % 3
                    t = blk * 4 + j
                    cs = slice(t * 128, (t + 1) * 128)
                    dtt = dt_all[:, t, :]
                    adt = adt_2[jb]; rhs_all = rhs_all_2[jb]; ET = ET_2[jb]; small = small_2[jb]; dec = dec_2[jb]
                    sm = sm_2[jb]; MT = MT_2[jb]; xdt = xdt_2[jb]; xdd = xdd_2[jb]
                    yo = yo_2[jb]; y = y_2[jb]; xsd = xsd_2[jb]
                    dve("tensor_tensor", adt[:], dtt, ab[:], ALU.mult, r=["dt_all", "ab"], w=[("adt", jb)])
                    pool("tensor_tensor", rhs_all[:], ltri[:].unsqueeze(1).to_broadcast([128, 8, 128]),
                        adt[:].unsqueeze(2).to_broadcast([128, 8, 128]), ALU.mult, r=["ltri", ("adt", jb)], w=[("rhs_all", jb)])
                    bs = npair()
                    for hf in range(2):
                        pe("matmul", PS(bs + hf), ustr[:], rhs_all[:, hf * 4:(hf + 1) * 4, :].rearrange("p h l -> p (h l)"),
                           start=True, stop=True, r=["ustr", ("rhs_all", jb)], w=[pk(bs + hf)])
                    bsm = nbank()
                    pe("matmul", PS(bsm, 8), ltri[:], adt[:], start=True, stop=True, r=["ltri", ("adt", jb)], w=[pk(bsm)])
                    pe("matmul", psum[:, bsm, 8:16], ones_f[:], adt[:], start=True, stop=True, r=["ones_f", ("adt", jb)], w=[pk(bsm)])
                    segv = PS2(bs).rearrange("p (h l) -> p h l", h=8)
                    act("activation", ET[:], segv, AF.Exp, r=[pk(bs), pk(bs + 1)], w=[("ET", jb)])
                    act("activation", dec[:], segv[:, :, 127], AF.Exp, r=[pk(bs), pk(bs + 1)], w=[("dec", jb)])
                    act("activation", small[:], PS(bsm, 16), AF.Exp, r=[pk(bsm)], w=[("small", jb)])
                    bsc = nbank()
                    for g in range(2):
                        pe("matmul", psum[:, bsc, g * 128:(g + 1) * 128], BTs[:, g, cs], CTs[:, g, cs], start=True, stop=True,
                           r=["BTs", "CTs"], w=[pk(bsc)])
                    dve("tensor_tensor", sm[:], PS(bsc, 256).rearrange("p (g l) -> p g l", g=2),
                        ltri_bf[:].unsqueeze(1).to_broadcast([128, 2, 128]), ALU.mult, r=[pk(bsc), "ltri_bf"], w=[("sm", jb)])
                    for g in range(2):
                        dve("tensor_tensor", MT[:, g * 4:(g + 1) * 4, :], ET[:, g * 4:(g + 1) * 4, :],
                            sm[:, g:g + 1, :].to_broadcast([128, 4, 128]), ALU.mult, r=[("ET", jb), ("sm", jb)], w=[("MT", jb)])
                    xsv = xs4[pb][:, j, :].rearrange("p (h e) -> p h e", h=8)
                    pool("tensor_tensor", xdt[:], xsv, dtt.unsqueeze(2).to_broadcast([128, 8, 64]), ALU.mult,
                         r=[("xs4", pb), "dt_all"], w=[("xdt", jb)])
                    pool("tensor_tensor", xdd[:], xdt[:], dec[:].unsqueeze(2).to_broadcast([128, 8, 64]), ALU.mult,
                         r=[("xdt", jb), ("dec", jb)], w=[("xdd", jb)])

                def s2(blk, j):
                    pb = blk % 2
                    jb = (blk * 4 + j) % 3
                    t = blk * 4 + j
                    cs = slice(t * 128, (t + 1) * 128)
                    adt = adt_2[jb]; rhs_all = rhs_all_2[jb]; ET = ET_2[jb]; small = small_2[jb]; dec = dec_2[jb]
                    sm = sm_2[jb]; MT = MT_2[jb]; xdt = xdt_2[jb]; xdd = xdd_2[jb]
                    yo = yo_2[jb]; y = y_2[jb]; xsd = xsd_2[jb]
                    xsv = xs4[pb][:, j, :].rearrange("p (h e) -> p h e", h=8)
                    byd = nbank()
                    for hd in range(8):
                        pe("matmul", psum[:, byd, hd * 64:(hd + 1) * 64], MT[:, hd, :], xdt[:, hd, :], start=True, stop=True,
                           r=[("MT", jb), ("xdt", jb)], w=[pk(byd)])
                    byo = nbank()
                    for hd in range(8):
                        pe("matmul", psum[:, byo, hd * 64:(hd + 1) * 64], CTs[:, hd // 4, cs], state_bf[:, hd, :], start=True, stop=True,
                           r=["CTs", "state_bf"], w=[pk(byo)])
                    bst = nbank()
                    for hd in range(8):
                        pe("matmul", psum[:, bst, hd * 64:(hd + 1) * 64], B4[pb][:, j, (hd // 4) * 128:(hd // 4 + 1) * 128],
                           xdd[:, hd, :], start=True, stop=True, r=[("B4", pb), ("xdd", jb)], w=[pk(bst)])
                    dve("tensor_tensor", yo[:], PS(byo).rearrange("p (h e) -> p h e", h=8),
                        small[:, 0:8].unsqueeze(2).to_broadcast([128, 8, 64]), ALU.mult, r=[pk(byo), ("small", jb)], w=[("yo", jb)])
                    dve("tensor_tensor", y[:], yo[:].rearrange("p h e -> p (h e)"), PS(byd), ALU.add, r=[("yo", jb), pk(byd)], w=[("y", jb)])
                    pool("tensor_tensor", xsd[:].rearrange("p (h e) -> p h e", h=8), xsv,
                         dsk[:].unsqueeze(2).to_broadcast([128, 8, 64]), ALU.mult, r=[("xs4", pb), "dsk"], w=[("xsd", jb)])
                    pool("tensor_tensor", y[:], y[:], xsd[:], ALU.add, r=[("y", jb), ("xsd", jb)], w=[("y", jb)])
                    dve("tensor_tensor", state[:], state[:], small[:, 8:16].unsqueeze(2).to_broadcast([128, 8, 64]), ALU.mult,
                        r=["state", ("small", jb)], w=["state"])
                    dve("tensor_tensor", state[:], state[:], PS(bst).rearrange("p (h e) -> p h e", h=8), ALU.add,
                        r=["state", pk(bst)], w=["state"])
                    act("copy", state_bf[:], state[:], r=["state"], w=["state_bf"])
                    dve("tensor_tensor", yg4[:, j, :], y[:], sz4[pb][:, j, :], ALU.mult, r=[("y", jb), ("sz4", pb)], w=["yg4"])
                    for g in range(2):
                        act("activation", junk[:], yg4[:, j, g * 256:(g + 1) * 256], AF.Square, accum_out=ssg[:, j, g:g + 1],
                            r=["yg4"], w=["junk", "ssg"])

                def end_blk(blk):
                    act("activation", rsg[:], ssg[:], AF.Ln, scale=1.0 / 256, bias=cst[:, 0:1], r=["ssg", "cst"], w=["rsg"])
                    act("activation", rsg[:], rsg[:], AF.Exp, scale=-0.5, r=["rsg"], w=["rsg"])
                    for j in range(4):
                        yn = yn_2[j % 2]
                        for g in range(2):
                            dve("scalar_tensor_tensor", yn[:, g * 256:(g + 1) * 256], yg4[:, j, g * 256:(g + 1) * 256], rsg[:, j, g:g + 1],
                                nwb[:, g * 256:(g + 1) * 256], ALU.mult, ALU.mult, r=["yg4", "rsg", "nwb"], w=[("yn", j % 2)])
                        b = nbank()
                        for cch in range(4):
                            pe("transpose", PS16(b)[:, cch * 128:(cch + 1) * 128], yn[:, cch * 128:(cch + 1) * 128], ident[:],
                               r=[("yn", j % 2), "ident"], w=[pk(b)])
                        act("copy", ygT[:, :, j * 128:(j + 1) * 128], PS16(b)[:, 0:512].rearrange("p (c t) -> p c t", c=4),
                            r=[pk(b)], w=["ygT"])
                    P.dma(mixT_d[512:1024, blk * 512:(blk + 1) * 512].rearrange("(c p) s -> p c s", p=128), ygT[:],
                          reads=["ygT"], eng="act")

                wgc = wprep_gen(0, sb, 1024, "sp", which="b") if l == 0 else None
                for i in range(NT + 2):
                    if wgc is not None:
                        for _k in range(3):
                            next(wgc, None)
                    if i < NT:
                        if i % 4 == 0:
                            load_blk(i // 4)
                        s1(i // 4, i % 4)
                    if i >= 2:
                        s2((i - 2) // 4, (i - 2) % 4)
                        if (i - 2) % 4 == 3:
                            end_blk((i - 2) // 4)
                if wgc is not None:
                    for _ in wgc:
                        pass
                P.flush()

        def phase_D(l, xsrc, xdst):
            with contextlib.ExitStack() as st:
                sb = mk_sb(st)
                wo = sb([128, 8, D], BF16)
                P.dma(wo[:], woutb_d_L[l].rearrange("(k p) c -> p k c", p=128), writes=["wo"])
                mT = [sb([128, 8, 512], BF16) for _ in range(2)]
                xt = [sb([128, D], F32) for _ in range(4)]
                for blk in range(NB):
                    pb = blk % 2
                    P.dma(mT[pb][:], mixT_d[:, blk * 512:(blk + 1) * 512].rearrange("(k p) s -> p k s", p=128), writes=[("mT", pb)])
                    for j in range(4):
                        t = blk * 4 + j
                        i2 = t % 4
                        P.dma(xt[i2][:], xsrc[t * 128:(t + 1) * 128, :], writes=[("xt", i2)])
                        bp = npair()
                        for hf in range(2):
                            for k in range(8):
                                pe("matmul", PS(bp + hf), mT[pb][:, k, j * 128:(j + 1) * 128], wo[:, k, hf * 512:(hf + 1) * 512],
                                   start=(k == 0), stop=(k == 7), r=[("mT", pb), "wo"], w=[pk(bp + hf)])
                        dve("tensor_tensor", xt[i2][:], PS2(bp), xt[i2][:], ALU.add, r=[pk(bp), pk(bp + 1), ("xt", i2)], w=[("xt", i2)])
                        P.dma(xdst[t * 128:(t + 1) * 128, :], xt[i2][:], reads=[("xt", i2)], eng="act")
                P.flush()

        def phase_E(l, xsrc, xdst):
            with contextlib.ExitStack() as st:
                sb = mk_sb(st)
                NF = D_FF // 128
                wgu = sb([128, 8, 2 * D_FF], BF16)
                wd = sb([128, NF, D], BF16)
                weff = sb([128, D], F32); shb = sb([128, D], F32)
                for k in range(8):
                    P.dma(wgu[:, k, :], wgub_d_L[l][k * 128:(k + 1) * 128, :], writes=[("wgu", k)])
                P.dma(wd[:], wdb_d_L[l].rearrange("(k p) c -> p k c", p=128), writes=["wd"])
                P.dma(weff[:], modrow_d_L[l][3:4, :].partition_broadcast(128), writes=["weff"])
                P.dma(shb[:], modrow_d_L[l][4:5, :].partition_broadcast(128), writes=["shb"])
                xt = [sb([128, D], F32) for _ in range(4)]
                ssx = sb([128, 4], F32); rsx = sb([128, 4], F32)
                htmp = [sb([128, D], F32)] * 2
                h = [sb([128, D], BF16) for _ in range(4)]
                hT = sb([128, 8, 512], BF16)
                sg = [sb([128, 512], F32) for _ in range(2)]
                aT = sb([128, NF, 512], BF16)

                def n_stats(blk, j):
                    t = blk * 4 + j
                    P.dma(xt[j][:], xsrc[t * 128:(t + 1) * 128, :], writes=[("xt", j)])
                    act("activation", h[3][:], xt[j][:], AF.Square, accum_out=ssx[:, j:j + 1],
                        r=[("xt", j)], w=[("h", 3), "ssx"])

                def n_rstd():
                    act("activation", rsx[:], ssx[:], AF.Ln, scale=1.0 / D, bias=cst[:, 0:1], r=["ssx", "cst"], w=["rsx"])
                    act("activation", rsx[:], rsx[:], AF.Exp, scale=-0.5, r=["rsx"], w=["rsx"])

                def n_h(j):
                    ht = htmp[j % 2]
                    dve("scalar_tensor_tensor", ht[:], xt[j][:], rsx[:, j:j + 1], weff[:], ALU.mult, ALU.mult,
                        r=[("xt", j), "rsx", "weff"], w=["htmp"])
                    pool("tensor_tensor", h[j][:], ht[:], shb[:], ALU.add, r=["htmp", "shb"], w=[("h", j)])

                def n_T():
                    for j in range(4):
                        b = nbank()
                        for k in range(8):
                            pe("transpose", PS16(b)[:, k * 128:(k + 1) * 128], h[j][:, k * 128:(k + 1) * 128], ident[:],
                               r=[("h", j), "ident"], w=[pk(b)])
                        act("copy", hT[:, :, j * 128:(j + 1) * 128], PS16(b).rearrange("p (k t) -> p k t", k=8),
                            r=[pk(b)], w=["hT"])

                for j in range(4):
                    n_stats(0, j)
                n_rstd()
                for j in range(4):
                    n_h(j)
                n_T()
                for blk in range(NB):
                    nxt = blk + 1 < NB
                    for f in range(NF):
                        bg = nbank()
                        for k in range(8):
                            pe("matmul", PS(bg), wgu[:, k, f * 128:(f + 1) * 128], hT[:, k, :], start=(k == 0), stop=(k == 7),
                               r=[("wgu", k), "hT"], w=[pk(bg)])
                        bu = nbank()
                        for k in range(8):
                            pe("matmul", PS(bu), wgu[:, k, D_FF + f * 128:D_FF + (f + 1) * 128], hT[:, k, :], start=(k == 0), stop=(k == 7),
                               r=[("wgu", k), "hT"], w=[pk(bu)])
                        act("activation", sg[f % 2][:], PS(bg), AF.Silu, r=[pk(bg)], w=[("sg", f % 2)])
                        dve("tensor_tensor", aT[:, f, :], PS(bu), sg[f % 2][:], ALU.mult, r=[pk(bu), ("sg", f % 2)], w=["aT"])
                        if nxt:
                            if f in (1, 3, 5, 7):
                                n_stats(blk + 1, (f - 1) // 2)
                            elif f == 9:
                                n_rstd()
                            elif f in (11, 13, 15, 17):
                                n_h((f - 11) // 2)
                    for j in range(4):
                        t = blk * 4 + j
                        P.dma(xt[j][:], xsrc[t * 128:(t + 1) * 128, :], writes=[("xt", j)])
                        bp = npair()
                        for hf in range(2):
                            for f in range(NF):
                                pe("matmul", PS(bp + hf), aT[:, f, j * 128:(j + 1) * 128], wd[:, f, hf * 512:(hf + 1) * 512],
                                   start=(f == 0), stop=(f == NF - 1), r=["aT", "wd"], w=[pk(bp + hf)])
                        if j == 0 and nxt:
                            n_T()
                        dve("tensor_tensor", xt[j][:], PS2(bp), xt[j][:], ALU.add, r=[pk(bp), pk(bp + 1), ("xt", j)], w=[("xt", j)])
                        P.dma(xdst[t * 128:(t + 1) * 128, :], xt[j][:], reads=[("xt", j)], eng="act")
                P.flush()

        stages = []
        stages.append(("const", phase_const))
        stages.append(("wprep0", phase_start))
        for l in range(nlayers):
            xsrc = x_in if l == 0 else xb_d
            xfin = out if l == nlayers - 1 else xb_d
            stages.append(("A%d" % l, partial(phase_A, l, xsrc)))
            stages.append(("B%d" % l, partial(phase_B, l)))
            stages.append(("C%d" % l, partial(phase_C, l)))
            stages.append(("D%d" % l, partial(phase_D, l, xsrc, xa_d)))
            stages.append(("E%d" % l, partial(phase_E, l, xa_d, xfin)))
        for name, fn in stages:
            fn()
            if upto is not None and name == upto:
                break
    return nc


_INPUT_NAMES = ["norm1_w", "norm2_w", "w_ada", "b_ada", "w_in", "q_a_norm_w", "w_q_up", "kv_a_norm_w", "w_kv_up",
                "q_nope_norm_w", "q_pe_norm_w", "k_nope_norm_w", "k_pe_norm_w", "conv_w", "conv_b", "dt_bias",
                "a_log", "d_skip", "ssd_norm_w", "w_out", "w_gate_up", "w_down"]


def kernel(x, c, positions, **w):
    x = np.asarray(x); c = np.asarray(c); positions = np.asarray(positions)
    Bn, S, _ = x.shape
    nc = build(S)
    shared = {k: np.ascontiguousarray(np.asarray(w[k], dtype=np.float32)) for k in _INPUT_NAMES}
    in_maps = []
    for b in range(Bn):
        m = dict(shared)
        m["x"] = np.ascontiguousarray(x[b], dtype=np.float32)
        m["c"] = np.ascontiguousarray(c[b:b + 1], dtype=np.float32)
        m["positions"] = np.ascontiguousarray(positions[b:b + 1], dtype=np.int32)
        in_maps.append(m)
    res = run_bass_kernel_spmd(nc, in_maps, core_ids=list(range(Bn)))
    return np.stack([np.asarray(r["out"], dtype=np.float32) for r in res.results], axis=0)
```

```python
import contextlib
import math
from functools import partial

import numpy as np
import concourse.bass as bass
import concourse.mybir as mybir
from concourse.bass_utils import run_bass_kernel_spmd

F32 = mybir.dt.float32
BF16 = mybir.dt.bfloat16
I32 = mybir.dt.int32
ALU = mybir.AluOpType
AF = mybir.ActivationFunctionType
AX = mybir.AxisListType

D = 1024
DEPTH = 2
NH = 8
D_IN = 1960
D_FF = 2816
EPS = 1e-6
ENGS = ("pe", "act", "dve", "pool", "sp")
N_DMA_SEMS = 16


class _Op:
    __slots__ = ("eng", "fn", "deps", "dma", "signal", "count", "sem")


class Prog:
    def __init__(self, nc, stack):
        self.nc = nc
        self.esem = {e: stack.enter_context(nc.semaphore("s_" + e)) for e in ENGS}
        self.dsem = [stack.enter_context(nc.semaphore("d%d" % i)) for i in range(N_DMA_SEMS)]
        self.ecnt = {e: 0 for e in ENGS}
        self.dcnt = [0] * N_DMA_SEMS
        self.nd = 0
        self.nops = 0
        self._reset()

    def _reset(self):
        self.ops = []
        self.last_w = {}
        self.readers = {}
        self.eng_ops = {e: [] for e in ENGS}

    def op(self, eng, fn, reads=(), writes=(), dma=False):
        o = _Op()
        o.eng, o.fn, o.dma, o.signal, o.count, o.sem = eng, fn, dma, False, 0, None
        need = {}
        for r in reads:
            w = self.last_w.get(r)
            if w is not None:
                self._dep(need, o, w, "raw")
        for r in writes:
            w = self.last_w.get(r)
            if w is not None:
                self._dep(need, o, w, "waw")
            for rd in self.readers.get(r, ()):
                self._dep(need, o, rd, "war")
        o.deps = list(need.values())
        for r in reads:
            self.readers.setdefault(r, []).append(o)
        for r in writes:
            self.last_w[r] = o
            self.readers[r] = []
        self.ops.append(o)
        self.eng_ops[eng].append(o)
        return o

    @staticmethod
    def _dep(need, o, d, kind):
        if d is o:
            return
        if d.eng == o.eng and not d.dma and not o.dma and o.eng == "pe":
            return
        need[id(d)] = d

    def dma(self, out, in_, reads=(), writes=(), eng="sp", **kw):
        q = {"sp": self.nc.sync, "act": self.nc.scalar, "pool": self.nc.gpsimd}[eng]
        return self.op(eng, partial(q.dma_start, out=out, in_=in_, **kw), reads, writes, dma=True)

    def flush(self):
        nc = self.nc
        self.nops += len(self.ops)
        for o in self.ops:
            for d in o.deps:
                d.signal = True
        for e in ENGS:
            comp = [o for o in self.eng_ops[e] if not o.dma]
            if comp:
                comp[-1].signal = True
        dlast = [None] * N_DMA_SEMS
        for o in self.ops:
            if o.dma:
                o.signal = True
                k = self.nd % N_DMA_SEMS
                self.nd += 1
                self.dcnt[k] += 16
                o.sem, o.count = self.dsem[k], self.dcnt[k]
                if dlast[k] is not None:
                    o.deps.append(dlast[k])
                dlast[k] = o
            elif o.signal:
                self.ecnt[o.eng] += 1
                o.sem, o.count = self.esem[o.eng], self.ecnt[o.eng]
        dcnt = list(self.dcnt)
        ecnt = dict(self.ecnt)
        eng_ops = self.eng_ops
        dsem, esem = self.dsem, self.esem

        def run(engname, e):
            waited = {}
            for o in eng_ops[engname]:
                for d in o.deps:
                    key = id(d.sem)
                    if waited.get(key, 0) >= d.count:
                        continue
                    e.wait_ge(d.sem, d.count)
                    waited[key] = d.count
                ins = o.fn()
                if o.signal:
                    ins.then_inc(o.sem, 16 if o.dma else 1)
            for en in ENGS:
                if ecnt[en] and waited.get(id(esem[en]), 0) < ecnt[en]:
                    e.wait_ge(esem[en], ecnt[en])
            for k in range(N_DMA_SEMS):
                if dcnt[k] and waited.get(id(dsem[k]), 0) < dcnt[k]:
                    e.wait_ge(dsem[k], dcnt[k])

        with nc.Block() as blk:
            @blk.tensor
            def _(e):
                run("pe", e)

            @blk.scalar
            def _(e):
                run("act", e)

            @blk.vector
            def _(e):
                run("dve", e)

            @blk.gpsimd
            def _(e):
                run("pool", e)

            @blk.sync
            def _(e):
                run("sp", e)
        self._reset()


DBG_STOP = [None]


def build(S, nlayers=DEPTH, debug=False, upto=None):
    NT = S // 128
    NB = S // 512
    nc = bass.Bass("TRN2", target_bir_lowering=False)
    okind = "ExternalOutput" if debug else "Internal"

    def din(name, shape, dt=F32):
        return nc.dram_tensor(name, list(shape), dt, kind="ExternalInput").ap()

    def dscr(name, shape, dt):
        return nc.dram_tensor(name, list(shape), dt, kind=okind).ap()

    x_in = din("x", [S, D])
    c_in = din("c", [1, D])
    pos_in = din("positions", [1, S], I32)
    Ld = DEPTH
    norm1_w = din("norm1_w", [Ld, D]); norm2_w = din("norm2_w", [Ld, D])
    w_ada = din("w_ada", [Ld, D, 6 * D]); b_ada = din("b_ada", [Ld, 6 * D])
    w_in = din("w_in", [Ld, D, D_IN])
    q_a_norm_w = din("q_a_norm_w", [Ld, 256]); w_q_up = din("w_q_up", [Ld, 256, 768])
    kv_a_norm_w = din("kv_a_norm_w", [Ld, 128]); w_kv_up = din("w_kv_up", [Ld, 128, 1024])
    q_nope_norm_w = din("q_nope_norm_w", [Ld, 64]); q_pe_norm_w = din("q_pe_norm_w", [Ld, 32])
    k_nope_norm_w = din("k_nope_norm_w", [Ld, 64]); k_pe_norm_w = din("k_pe_norm_w", [Ld, 32])
    conv_w = din("conv_w", [Ld, 4, 1024]); conv_b = din("conv_b", [Ld, 1024])
    dt_bias = din("dt_bias", [Ld, 8]); a_log = din("a_log", [Ld, 8]); d_skip = din("d_skip", [Ld, 8])
    ssd_norm_w = din("ssd_norm_w", [Ld, 512])
    w_out = din("w_out", [Ld, D, D]); w_gate_up = din("w_gate_up", [Ld, D, 2 * D_FF])
    w_down = din("w_down", [Ld, D_FF, D])
    out = nc.dram_tensor("out", [S, D], F32, kind="ExternalOutput").ap()

    modrow_d_L = [dscr("modrow_d%d" % i, [6, D], F32) for i in range(DEPTH)]
    winb_d_L = [dscr("winb_d%d" % i, [D, D_IN], BF16) for i in range(DEPTH)]
    wqb_d_L = [dscr("wqb_d%d" % i, [256, 768], BF16) for i in range(DEPTH)]
    wkvb_d_L = [dscr("wkvb_d%d" % i, [128, 1024], BF16) for i in range(DEPTH)]
    woutb_d_L = [dscr("woutb_d%d" % i, [D, D], BF16) for i in range(DEPTH)]
    wgub_d_L = [dscr("wgub_d%d" % i, [D, 2 * D_FF], BF16) for i in range(DEPTH)]
    wdb_d_L = [dscr("wdb_d%d" % i, [D_FF, D], BF16) for i in range(DEPTH)]
    QT_d = dscr("QT_d", [NH, 96, S], BF16)
    KT_d = dscr("KT_d", [NH, 96, S], BF16)
    V_d = dscr("V_d", [S, NH * 128], BF16)
    z_d = dscr("z_d", [S, 512], BF16)
    xs_d = dscr("xs_d", [S, 512], BF16)
    B_d = dscr("B_d", [S, 256], BF16)
    BT_d = dscr("BT_d", [2, 128, S], BF16)
    CT_d = dscr("CT_d", [2, 128, S], BF16)
    mixT_d = dscr("mixT_d", [D, S], BF16)
    cs_d = dscr("cs_d", [2, 128, NT * 16], F32)
    cwcb_d_L = [dscr("cwcb_d%d" % i, [128, 40], F32) for i in range(DEPTH)]
    xa_d = dscr("xa_d", [S, D], F32)
    xb_d = dscr("xb_d", [S, D], F32)

    with contextlib.ExitStack() as top:
        P = Prog(nc, top)

        gcnt = [0]

        def mk_sb(st):
            def sb(shape, dt=F32, name=None):
                gcnt[0] += 1
                return st.enter_context(nc.sbuf_tensor(name or ("t%d" % gcnt[0]), list(shape), dt))
            return sb

        def E(eng, obj, fname, *a, r=(), w=(), **kw):
            return P.op(eng, partial(getattr(obj, fname), *a, **kw), r, w)

        def dve(fname, *a, r=(), w=(), **kw):
            return E("dve", nc.vector, fname, *a, r=r, w=w, **kw)

        def pool(fname, *a, r=(), w=(), **kw):
            return E("pool", nc.gpsimd, fname, *a, r=r, w=w, **kw)

        def act(fname, *a, r=(), w=(), **kw):
            return E("act", nc.scalar, fname, *a, r=r, w=w, **kw)

        def pe(fname, *a, r=(), w=(), **kw):
            return E("pe", nc.tensor, fname, *a, r=r, w=w, **kw)

        psb = mk_sb(top)
        psum = top.enter_context(nc.psum_tensor("psum", [128, 8, 512], F32))
        ident = psb([128, 128], BF16, "ident")
        ones_f = psb([128, 128], F32, "ones_f")
        ltri = psb([128, 128], F32, "ltri")
        ustr = psb([128, 128], F32, "ustr")
        ltri_bf = psb([128, 128], BF16, "ltri_bf")
        cst = psb([128, 4], F32, "cst")
        dt_all = psb([128, NT, 8], F32, "dt_all")

        bank_rr = [0]

        def nbank():
            b = bank_rr[0] % 4
            bank_rr[0] += 1
            return b

        pair_rr = [0]

        def npair():
            b = 4 + 2 * (pair_rr[0] % 2)
            pair_rr[0] += 1
            return b

        def PS(b, n=512):
            return psum[:, b, 0:n]

        def PS2(b, n=1024):
            return psum[:, b:b + 2, :].rearrange("p a b -> p (a b)")[:, 0:n]

        def PS16(b):
            return psum[:, b, :].bitcast(BF16)

        def pk(b):
            return ("ps", b)

        def cols_from_row(dst, row, n, rkey, wkey):
            b = nbank()
            for i in range(n):
                pe("matmul", psum[:, b, i:i + 1], row[0:1, i * 128:(i + 1) * 128], ones_f[0:1, 0:1], start=True, stop=True,
                   r=[rkey, "ones_f"], w=[pk(b)])
            dve("tensor_copy", dst, psum[:, b, 0:n], r=[pk(b)], w=[wkey])

        def phase_const():
            with contextlib.ExitStack() as st:
                sb = mk_sb(st)
                pool("memset", ones_f[:], 1.0, w=["ones_f"])
                pool("memset", cst[:, 0:1], EPS, w=["cst"])
                pool("memset", cst[:, 1:2], 1.0, w=["cst"])
                pool("affine_select", ltri[:], ones_f[:], pattern=[[1, 128]], compare_op=ALU.is_ge,
                     fill=0.0, base=0, channel_multiplier=-1, r=["ones_f"], w=["ltri"])
                pool("affine_select", ustr[:], ones_f[:], pattern=[[-1, 128]], compare_op=ALU.is_gt,
                     fill=0.0, base=0, channel_multiplier=1, r=["ones_f"], w=["ustr"])
                pool("affine_select", ident[:], ones_f[:], pattern=[[1, 128]], compare_op=ALU.is_equal,
                     fill=0.0, base=0, channel_multiplier=-1, r=["ones_f"], w=["ident"])
                dve("tensor_copy", ltri_bf[:], ltri[:], r=["ltri"], w=["ltri_bf"])
                posf = sb([128, NT], F32)
                invf = sb([128, 16], F32)
                ang = sb([128, NT, 16], F32)
                kq = sb([128, NT, 16], F32)
                ki = sb([128, NT, 16], I32)
                m1 = sb([128, NT, 16], F32)
                rc = sb([128, NT, 16], F32)
                cosT = sb([128, NT, 16], F32)
                sinT = sb([128, NT, 16], F32)
                prow_i = sb([1, S], I32)
                prow_f = sb([1, S], F32)
                P.dma(prow_i[:], pos_in, writes=["prow_i"])
                dve("tensor_copy", prow_f[:], prow_i[:], r=["prow_i"], w=["prow_f"])
                cols_from_row(posf[:], prow_f, NT, "prow_f", "posf")
                inv = (1.0 / (np.float32(10000.0) ** (np.arange(0, 32, 2, dtype=np.float32) / np.float32(32)))).astype(np.float32)
                for j in range(16):
                    pool("memset", invf[:, j:j + 1], float(inv[j]), w=["invf"])
                dve("tensor_tensor", ang[:], posf[:].unsqueeze(2).to_broadcast([128, NT, 16]),
                    invf[:].unsqueeze(1).to_broadcast([128, NT, 16]), ALU.mult, r=["posf", "invf"], w=["ang"])
                TWO_PI = 2.0 * math.pi
                C1 = 6.28125
                C2 = TWO_PI - C1
                PI_LO = 3.1415925

                def reduce_to_pi(src, skey, dst, dkey):
                    dve("tensor_scalar", kq[:], src[:], 1.0 / TWO_PI, None, ALU.mult, r=[skey], w=["kq"])
                    dve("tensor_copy", ki[:], kq[:], r=["kq"], w=["ki"])
                    dve("tensor_copy", kq[:], ki[:], r=["ki"], w=["kq"])
                    dve("scalar_tensor_tensor", dst[:], kq[:], -C1, src[:], ALU.mult, ALU.add, r=["kq", skey], w=[dkey])
                    dve("scalar_tensor_tensor", dst[:], kq[:], -C2, dst[:], ALU.mult, ALU.add, r=["kq", dkey], w=[dkey])
                    dve("tensor_scalar", m1[:], dst[:], math.pi, None, ALU.is_gt, r=[dkey], w=["m1"])
                    dve("scalar_tensor_tensor", dst[:], m1[:], -TWO_PI, dst[:], ALU.mult, ALU.add, r=["m1", dkey], w=[dkey])
                    dve("tensor_scalar", m1[:], dst[:], -math.pi, None, ALU.is_lt, r=[dkey], w=["m1"])
                    dve("scalar_tensor_tensor", dst[:], m1[:], TWO_PI, dst[:], ALU.mult, ALU.add, r=["m1", dkey], w=[dkey])
                    dve("tensor_scalar", dst[:], dst[:], PI_LO, -PI_LO, ALU.min, ALU.max, r=[dkey], w=[dkey])

                reduce_to_pi(ang, "ang", rc, "rc")
                act("activation", sinT[:], rc[:], AF.Sin, r=["rc"], w=["sinT"])
                dve("tensor_scalar", ang[:], rc[:], math.pi / 2, None, ALU.add, r=["rc"], w=["ang"])
                reduce_to_pi(ang, "ang", rc, "rc")
                act("activation", cosT[:], rc[:], AF.Sin, r=["rc"], w=["cosT"])
                P.dma(cs_d[0], cosT[:].rearrange("p t j -> p (t j)"), reads=["cosT"])
                P.dma(cs_d[1], sinT[:].rearrange("p t j -> p (t j)"), reads=["sinT"])
                P.flush()

        def mod_gen(l, T):
            cT, wst, mrow, brow, nrow, orow, crow = T
            P.dma(crow[:], c_in, writes=["crow"])
            act("activation", crow[:], crow[:], AF.Silu, r=["crow"], w=["crow"])
            cols_from_row(cT[:], crow, 8, "crow", "cT")
            P.dma(brow[:], b_ada[l:l + 1, :], writes=["brow"])
            P.dma(nrow[:, 0:D], norm1_w[l:l + 1, :], writes=["nrow"])
            P.dma(nrow[:, D:2 * D], norm2_w[l:l + 1, :], writes=["nrow"])
            for n in range(12):
                ws = wst[n % 2]
                P.dma(ws[:], w_ada[l, :, n * 512:(n + 1) * 512].rearrange("(k p) c -> p k c", p=128),
                      writes=[("wst", n % 2)])
                b = nbank()
                for k in range(8):
                    pe("matmul", psum[0:1, b, :], cT[:, k:k + 1], ws[:, k, :], start=(k == 0), stop=(k == 7),
                       r=["cT", ("wst", n % 2)], w=[pk(b)])
                dve("tensor_tensor", mrow[:, n * 512:(n + 1) * 512], psum[0:1, b, :], brow[:, n * 512:(n + 1) * 512],
                    ALU.add, r=[pk(b), "brow"], w=["mrow"])
                yield
            for half in range(2):
                o = half * 3 * D
                dve("scalar_tensor_tensor", orow[:, o:o + D], mrow[:, o + D:o + 2 * D], 1.0, nrow[:, half * D:(half + 1) * D],
                    ALU.add, ALU.mult, r=["mrow", "nrow"], w=["orow"])
                dve("tensor_copy", orow[:, o + D:o + 2 * D], mrow[:, o:o + D], r=["mrow"], w=["orow"])
                dve("tensor_copy", orow[:, o + 2 * D:o + 3 * D], mrow[:, o + 2 * D:o + 3 * D], r=["mrow"], w=["orow"])
            P.dma(modrow_d_L[l].rearrange("(o a) b -> o (a b)", o=1), orow[:], reads=["orow"], writes=[("modrow", l)], eng="act")

        def phase_start():
            with contextlib.ExitStack() as st:
                sb = mk_sb(st)
                T = (sb([128, 8], F32), [sb([128, 8, 512], F32) for _ in range(2)], sb([1, 6 * D], F32), sb([1, 6 * D], F32),
                     sb([1, 2 * D], F32), sb([1, 6 * D], F32), sb([1, D], F32))
                cwrow = sb([1, 4 * 1024], F32); cbrow = sb([1, 1024], F32); cwcb = sb([128, 40], F32)
                for _ in mod_gen(0, T):
                    pass
                for ll in range(nlayers):
                    P.dma(cwrow[:], conv_w[ll:ll + 1].rearrange("o k c -> o (k c)"), writes=["cwrow"])
                    P.dma(cbrow[:], conv_b[ll:ll + 1, :], writes=["cbrow"])
                    cols_from_row(cwcb[:, 0:32], cwrow, 32, "cwrow", "cwcb")
                    cols_from_row(cwcb[:, 32:40], cbrow, 8, "cbrow", "cwcb")
                    P.dma(cwcb_d_L[ll], cwcb[:], reads=["cwcb"])
                wg = wprep_gen(0, sb, 2048, "act", which="a")
                for l in range(1, nlayers):
                    for _ in mod_gen(l, T):
                        for _k in range(5):
                            next(wg, None)
                for _ in wg:
                    pass
                P.flush()


        def wprep_gen(l, sb, CW, store_eng, which="all", plain_eng="pool"):
            sin_ = [sb([128, CW], F32) for _ in range(3)]
            sout = [sb([128, CW], BF16) for _ in range(3)]
            g1b = sb([128, D], F32); g2b = sb([128, D], F32)
            P.dma(g1b[:], modrow_d_L[l][2:3, :].partition_broadcast(128), reads=[("modrow", l)], writes=["g1b"])
            P.dma(g2b[:], modrow_d_L[l][5:6, :].partition_broadcast(128), reads=[("modrow", l)], writes=["g2b"])
            jobs = []

            def add(src, dst, R, C, gate=None):
                for r0 in range(0, R, 128):
                    for c0 in range(0, C, CW):
                        c1 = min(C, c0 + CW)
                        jobs.append((src[r0:r0 + 128, c0:c1], dst[r0:r0 + 128, c0:c1], c1 - c0, c0, gate))
            wi = w_in[l]
            if which in ("all", "a"):
                add(wi[:, 0:416], winb_d_L[l][:, 0:416], D, 416)
                add(wi[:, 1952:1960], winb_d_L[l][:, 416:424], D, 8)
                add(wi[:, 416:1952], winb_d_L[l][:, 424:1960], D, 1536)
                add(w_q_up[l], wqb_d_L[l], 256, 768)
                add(w_kv_up[l], wkvb_d_L[l], 128, 1024)
            if which in ("all", "b"):
                add(w_out[l], woutb_d_L[l], D, D, gate=(g1b, "g1b"))
                add(w_gate_up[l], wgub_d_L[l], D, 2 * D_FF)
                add(w_down[l], wdb_d_L[l], D_FF, D, gate=(g2b, "g2b"))
            def load(i):
                if i < len(jobs):
                    P.dma(sin_[i % 3][:, 0:jobs[i][2]], jobs[i][0], writes=[("sin", i % 3)])
            load(0)
            load(1)
            for i, (src, dst, cw, c0, gate) in enumerate(jobs):
                k = i % 3
                load(i + 2)
                if gate is not None:
                    gt, gk = gate
                    eng = pool if store_eng == "sp" else (dve if i % 2 == 0 else pool)
                    eng("tensor_tensor", sout[k][:, 0:cw], sin_[k][:, 0:cw], gt[:, c0:c0 + cw], ALU.mult,
                        r=[("sin", k), gk], w=[("sout", k)])
                elif store_eng == "sp" and plain_eng == "act":
                    act("copy", sout[k][:, 0:cw], sin_[k][:, 0:cw], r=[("sin", k)], w=[("sout", k)])
                elif store_eng == "sp":
                    pool("tensor_copy", sout[k][:, 0:cw], sin_[k][:, 0:cw], r=[("sin", k)], w=[("sout", k)])
                elif k == 1:
                    act("copy", sout[k][:, 0:cw], sin_[k][:, 0:cw], r=[("sin", k)], w=[("sout", k)])
                elif k == 0:
                    dve("tensor_copy", sout[k][:, 0:cw], sin_[k][:, 0:cw], r=[("sin", k)], w=[("sout", k)])
                else:
                    pool("tensor_copy", sout[k][:, 0:cw], sin_[k][:, 0:cw], r=[("sin", k)], w=[("sout", k)])
                P.dma(dst, sout[k][:, 0:cw], reads=[("sout", k)], eng=store_eng)
                yield

        def phase_wprep(l):
            with contextlib.ExitStack() as st:
                sb = mk_sb(st)
                cwrow = sb([1, 4 * 1024], F32); cbrow = sb([1, 1024], F32); cwcb = sb([128, 40], F32)
                for ll in range(nlayers):
                    P.dma(cwrow[:], conv_w[ll:ll + 1].rearrange("o k c -> o (k c)"), writes=["cwrow"])
                    P.dma(cbrow[:], conv_b[ll:ll + 1, :], writes=["cbrow"])
                    cols_from_row(cwcb[:, 0:32], cwrow, 32, "cwrow", "cwcb")
                    cols_from_row(cwcb[:, 32:40], cbrow, 8, "cbrow", "cwcb")
                    P.dma(cwcb_d_L[ll], cwcb[:], reads=["cwcb"])
                for _ in wprep_gen(l, sb, 2048, "act"):
                    pass
                P.flush()

        def phase_A(l, xsrc):
            with contextlib.ExitStack() as st:
                sb = mk_sb(st)
                win = sb([128, 8, D_IN], BF16)
                wq = sb([128, 2, 768], BF16)
                wkv = sb([128, 1024], BF16)
                weff = sb([128, D], F32); shb = sb([128, D], F32)
                qaw = sb([128, 256], F32); kvaw = sb([128, 128], F32); kpw = sb([128, 32], F32)
                qnw = sb([128, 64], F32); qpw = sb([128, 32], F32); knw = sb([128, 64], F32)
                cw = sb([128, 4, 8], F32); cb = sb([128, 8], F32); dtb = sb([128, 8], F32)
                invn = sb([128, 27], F32)
                cosT = sb([128, NT, 16], F32)
                sinT = sb([128, NT, 16], F32)
                P.dma(cosT[:].rearrange("p t j -> p (t j)"), cs_d[0], writes=["cosT"])
                P.dma(sinT[:].rearrange("p t j -> p (t j)"), cs_d[1], writes=["sinT"])
                P.dma(win[:], winb_d_L[l].rearrange("(k p) c -> p k c", p=128), writes=["win"])
                P.dma(wq[:], wqb_d_L[l].rearrange("(k p) c -> p k c", p=128), writes=["wq"])
                P.dma(wkv[:], wkvb_d_L[l], writes=["wkv"])
                P.dma(weff[:], modrow_d_L[l][0:1, :].partition_broadcast(128), writes=["weff"])
                P.dma(shb[:], modrow_d_L[l][1:2, :].partition_broadcast(128), writes=["shb"])
                P.dma(qaw[:], q_a_norm_w[l:l + 1, :].partition_broadcast(128), writes=["qaw"])
                P.dma(kvaw[:], kv_a_norm_w[l:l + 1, :].partition_broadcast(128), writes=["kvaw"])
                P.dma(kpw[:], k_pe_norm_w[l:l + 1, :].partition_broadcast(128), writes=["kpw"])
                P.dma(qnw[:], q_nope_norm_w[l:l + 1, :].partition_broadcast(128), writes=["qnw"])
                P.dma(qpw[:], q_pe_norm_w[l:l + 1, :].partition_broadcast(128), writes=["qpw"])
                P.dma(knw[:], k_nope_norm_w[l:l + 1, :].partition_broadcast(128), writes=["knw"])
                P.dma(dtb[:], dt_bias[l:l + 1, :].partition_broadcast(128), writes=["dtb"])
                P.dma(cw[:].rearrange("p k m -> p (k m)"), cwcb_d_L[l][:, 0:32], writes=["cw"])
                P.dma(cb[:], cwcb_d_L[l][:, 32:40], writes=["cb"])
                scale = 96.0 ** -0.5
                dve("tensor_scalar", qnw[:], qnw[:], scale, None, ALU.mult, r=["qnw"], w=["qnw"])
                dve("tensor_scalar", qpw[:], qpw[:], scale, None, ALU.mult, r=["qpw"], w=["qpw"])
                pool("memset", invn[:, 0:1], 1.0 / 256, w=["invn"])
                pool("memset", invn[:, 1:2], 1.0 / 128, w=["invn"])
                pool("memset", invn[:, 2:3], 1.0 / 32, w=["invn"])
                pool("memset", invn[:, 3:11], 1.0 / 64, w=["invn"])
                pool("memset", invn[:, 11:19], 1.0 / 32, w=["invn"])
                pool("memset", invn[:, 19:27], 1.0 / 64, w=["invn"])

                xt = [sb([128, D], F32) for _ in range(4)]
                junk = sb([128, D], BF16)
                ssx = sb([128, 4], F32); rsx = sb([128, 4], F32)
                htmp_2 = [sb([128, D], F32) for _ in range(2)]
                h = [sb([128, D], BF16) for _ in range(2)]
                hT = sb([128, 8, 512], BF16)
                xbcT = [sb([128, 516], BF16) for _ in range(8)]
                dg = sb([128, 8, 4, 128], BF16)
                xsT = sb([128, 8, 512], BF16)
                raw1 = [sb([128, 424], F32) for _ in range(4)]
                zt = [sb([128, 512], BF16) for _ in range(2)]
                dtr = sb([128, 4, 8], F32)
                ss1 = sb([128, 4, 3], F32); rs1 = sb([128, 4, 3], F32)
                latn_2 = [sb([128, 384], BF16) for _ in range(2)]
                latT_2 = [sb([128, 3, 128], BF16) for _ in range(2)]
                qsb = [sb([128, 768], F32) for _ in range(4)]
                kvsb = [sb([128, 8, 64], F32) for _ in range(4)]
                sq_2 = [sb([128, 768], F32) for _ in range(2)]
                ss2 = sb([128, 4, 24], F32); rs2 = sb([128, 4, 24], F32)
                kpn = [sb([128, 32], F32) for _ in range(4)]
                Qf_2 = [sb([128, 8, 96], BF16) for _ in range(2)]; Kf_2 = [sb([128, 8, 96], BF16) for _ in range(2)]
                ra_2 = [sb([128, 8, 32], F32) for _ in range(2)]; rb_2 = [sb([128, 8, 32], F32) for _ in range(2)]
                rq_2 = [sb([128, 8, 32], F32) for _ in range(2)]
                ka_2 = [sb([128, 32], F32) for _ in range(2)]; kb_2 = [sb([128, 32], F32) for _ in range(2)]
                kr_2 = [sb([128, 32], F32) for _ in range(2)]
                vaug = [sb([128, 8, 128], BF16) for _ in range(2)]
                QTb = sb([96, 8, 512], BF16); KTb = sb([96, 8, 512], BF16)
                tok_o = [sb([128, 768], BF16) for _ in range(2)]

                for i in range(2):
                    pool("memset", vaug[i][:], 1.0, w=[("vaug", i)])
                for m in range(8):
                    pool("memset", xbcT[m][:, 512:515], 0.0, w=[("xbcT", m)])
                for m in range(8):
                    for k in range(4):
                        dve("tensor_scalar", dg[:, m, k, :], ident[:], cw[:, k, m:m + 1], None, ALU.mult, r=["ident", "cw"], w=["dg"])

                def n_stats(blk, j):
                    t = blk * 4 + j
                    P.dma(xt[j][:], xsrc[t * 128:(t + 1) * 128, :], writes=[("xt", j)])
                    act("activation", junk[:], xt[j][:], AF.Square, accum_out=ssx[:, j:j + 1],
                        r=[("xt", j)], w=["junk", "ssx"])

                def n_rstd():
                    act("activation", rsx[:], ssx[:], AF.Ln, scale=1.0 / D, bias=cst[:, 0:1], r=["ssx", "cst"], w=["rsx"])
                    act("activation", rsx[:], rsx[:], AF.Exp, scale=-0.5, r=["rsx"], w=["rsx"])

                def n_hT(j, bank=None):
                    hh = h[j % 2]; htmp = htmp_2[j % 2]
                    dve("scalar_tensor_tensor", htmp[:], xt[j][:], rsx[:, j:j + 1], weff[:], ALU.mult, ALU.mult,
                        r=[("xt", j), "rsx", "weff"], w=[("htmp", j % 2)])
                    pool("tensor_tensor", hh[:], htmp[:], shb[:], ALU.add, r=[("htmp", j % 2), "shb"], w=[("h", j % 2)])
                    b = nbank() if bank is None else bank
                    for k in range(8):
                        pe("transpose", PS16(b)[:, k * 128:(k + 1) * 128], hh[:, k * 128:(k + 1) * 128], ident[:],
                           r=[("h", j % 2), "ident"], w=[pk(b)])
                    act("copy", hT[:, :, j * 128:(j + 1) * 128], PS16(b).rearrange("p (k t) -> p k t", k=8),
                        r=[pk(b)], w=["hT"])

                for j in range(4):
                    n_stats(0, j)
                n_rstd()
                for j in range(4):
                    n_hT(j)

                def P_gen(blk):
                    pb = blk % 2
                    for m in range(9):
                        if m == 5:
                            yield
                        if m < 8:
                            b = nbank()
                            c0 = 936 + m * 128
                            for k in range(8):
                                pe("matmul", PS(b), win[:, k, c0:c0 + 128], hT[:, k, :], start=(k == 0), stop=(k == 7),
                                   r=["win", "hT"], w=[pk(b)])
                            cur = xbcT[m]
                            act("copy", cur[:, 0:3], cur[:, 512:515], r=[("xbcT", m)], w=[("xbcT", m)])
                            act("copy", cur[:, 3:515], PS(b), r=[pk(b), ("xbcT", m)], w=[("xbcT", m)])
                        if m >= 1:
                            mm = m - 1
                            b2 = nbank()
                            for kk in range(4):
                                pe("matmul", PS(b2), dg[:, mm, kk, :], xbcT[mm][:, kk:kk + 512], start=(kk == 0), stop=(kk == 3),
                                   r=["dg", ("xbcT", mm)], w=[pk(b2)])
                            act("activation", xsT[:, mm, :], PS(b2), AF.Silu, bias=cb[:, mm:mm + 1], r=[pk(b2), "cb"], w=[("xsT", mm)])
                    yield
                    for g in range(2):
                        P.dma(BT_d[g, :, blk * 512:(blk + 1) * 512], xsT[:, 4 + g, :], reads=[("xsT", 4 + g)], eng="act")
                        P.dma(CT_d[g, :, blk * 512:(blk + 1) * 512], xsT[:, 6 + g, :], reads=[("xsT", 6 + g)], eng="act")
                    for j in range(4):
                        t = blk * 4 + j
                        b = nbank()
                        for m in range(6):
                            pe("transpose", PS16(b)[:, m * 128:(m + 1) * 128], xsT[:, m, j * 128:(j + 1) * 128], ident[:],
                               r=[("xsT", m), "ident"], w=[pk(b)])
                        to = tok_o[j % 2]
                        dve("tensor_copy", to[:], PS16(b)[:, 0:768], r=[pk(b)], w=[("tok_o", j % 2)])
                        P.dma(xs_d[t * 128:(t + 1) * 128, :], to[:, 0:512], reads=[("tok_o", j % 2)], eng="act")
                        P.dma(B_d[t * 128:(t + 1) * 128, :], to[:, 512:768], reads=[("tok_o", j % 2)], eng="act")
                    yield
                    for j in range(4):
                        t = blk * 4 + j
                        b1 = nbank()
                        for k in range(8):
                            pe("matmul", PS(b1, 424), hT[:, k, j * 128:(j + 1) * 128], win[:, k, 0:424], start=(k == 0), stop=(k == 7),
                               r=["hT", "win"], w=[pk(b1)])
                        b2 = nbank()
                        for k in range(8):
                            pe("matmul", PS(b2), hT[:, k, j * 128:(j + 1) * 128], win[:, k, 424:936], start=(k == 0), stop=(k == 7),
                               r=["hT", "win"], w=[pk(b2)])
                        dve("tensor_copy", raw1[j][:], PS(b1, 424), r=[pk(b1)], w=[("raw1", j)])
                        act("copy", zt[j % 2][:], PS(b2), r=[pk(b2)], w=[("zt", j % 2)])
                        P.dma(z_d[t * 128:(t + 1) * 128, :], zt[j % 2][:], reads=[("zt", j % 2)], eng="act")
                        act("activation", junk[:, 0:256], raw1[j][:, 0:256], AF.Square, accum_out=ss1[:, j, 0:1],
                            r=[("raw1", j)], w=["junk", "ss1"])
                        act("activation", junk[:, 0:128], raw1[j][:, 256:384], AF.Square, accum_out=ss1[:, j, 1:2],
                            r=[("raw1", j)], w=["junk", "ss1"])
                        act("activation", junk[:, 0:32], raw1[j][:, 384:416], AF.Square, accum_out=ss1[:, j, 2:3],
                            r=[("raw1", j)], w=["junk", "ss1"])
                        dve("tensor_tensor", dtr[:, j, :], raw1[j][:, 416:424], dtb[:], ALU.add, r=[("raw1", j), "dtb"], w=["dtr"])
                    yield
                    dve("tensor_tensor", ss1[:], ss1[:], invn[:, 0:3].unsqueeze(1).to_broadcast([128, 4, 3]), ALU.mult,
                        r=["ss1", "invn"], w=["ss1"])
                    act("activation", dtr[:], dtr[:], AF.Exp, r=["dtr"], w=["dtr"])
                    act("activation", dt_all[:, blk * 4:(blk + 1) * 4, :], dtr[:], AF.Ln, bias=cst[:, 1:2], r=["dtr", "cst"], w=["dt_all"])
                    act("activation", rs1[:], ss1[:], AF.Ln, bias=cst[:, 0:1], r=["ss1", "cst"], w=["rs1"])
                    act("activation", rs1[:], rs1[:], AF.Exp, scale=-0.5, r=["rs1"], w=["rs1"])

                for _ in P_gen(0):
                    pass
                for blk in range(NB):
                    pb = blk % 2
                    nxt = blk + 1 < NB
                    def La(j):
                        latn = latn_2[j % 2]; latT = latT_2[j % 2]; sq = sq_2[j % 2]
                        dve("scalar_tensor_tensor", latn[:, 0:256], raw1[j][:, 0:256], rs1[:, j, 0:1], qaw[:], ALU.mult, ALU.mult,
                            r=[("raw1", j), "rs1", "qaw"], w=[("latn", j % 2)])
                        dve("scalar_tensor_tensor", latn[:, 256:384], raw1[j][:, 256:384], rs1[:, j, 1:2], kvaw[:], ALU.mult, ALU.mult,
                            r=[("raw1", j), "rs1", "kvaw"], w=[("latn", j % 2)])
                        dve("scalar_tensor_tensor", kpn[j][:], raw1[j][:, 384:416], rs1[:, j, 2:3], kpw[:], ALU.mult, ALU.mult,
                             r=[("raw1", j), "rs1", "kpw"], w=[("kpn", j)])
                        b = 2 if j % 2 == 0 else 6
                        for k in range(3):
                            pe("transpose", PS16(b)[:, k * 128:(k + 1) * 128], latn[:, k * 128:(k + 1) * 128], ident[:],
                               r=[("latn", j % 2), "ident"], w=[pk(b)])
                        act("copy", latT[:], PS16(b)[:, 0:384].rearrange("p (k t) -> p k t", k=3), r=[pk(b)], w=[("latT", j % 2)])
                        bq = 0 if j % 2 == 0 else 4
                        for (n0, n1, bb) in ((0, 512, bq), (512, 768, bq + 1)):
                            for k in range(2):
                                pe("matmul", PS(bb, n1 - n0), latT[:, k, :], wq[:, k, n0:n1], start=(k == 0), stop=(k == 1),
                                   r=[("latT", j % 2), "wq"], w=[pk(bb)])
                        bk = 2 if j % 2 == 0 else 6
                        for hf in range(2):
                            pe("matmul", PS(bk + hf), latT[:, 2, :], wkv[:, hf * 512:(hf + 1) * 512], start=True, stop=True,
                               r=[("latT", j % 2), "wkv"], w=[pk(bk + hf)])
                        return bq, bk

                    def Lb(j, bq, bk):
                        sq = sq_2[j % 2]
                        act("copy", qsb[j][:], PS2(bq, 768), r=[pk(bq), pk(bq + 1)], w=[("qsb", j)])
                        kvv = PS2(bk).rearrange("p (h e) -> p h e", h=8)
                        dve("tensor_copy", kvsb[j][:], kvv[:, :, 0:64], r=[pk(bk), pk(bk + 1)], w=[("kvsb", j)])
                        va = vaug[j % 2]
                        vap = va[:].rearrange("p (c two) e -> p c two e", two=2)
                        for hf in range(2):
                            kvp = PS(bk + hf).rearrange("p (c two e) -> p c two e", two=2, e=128)
                            dve("tensor_copy", vap[:, 2 * hf:2 * hf + 2, 0, 0:64], kvp[:, :, 0, 64:128], r=[pk(bk + hf)], w=[("vaug", j % 2)])
                            dve("tensor_copy", vap[:, 2 * hf:2 * hf + 2, 1, 64:128], kvp[:, :, 1, 64:128], r=[pk(bk + hf)], w=[("vaug", j % 2)])
                        t = blk * 4 + j
                        if True:
                            P.dma(V_d[t * 128:(t + 1) * 128, :], va[:].rearrange("p h e -> p (h e)"), reads=[("vaug", j % 2)], eng="act")
                        act("activation", sq[:], qsb[j][:], AF.Square, r=[("qsb", j)], w=[("sq", j % 2)])
                        sqv = sq[:].rearrange("p (h e) -> p h e", h=8)
                        dve("tensor_reduce", ss2[:, j, 0:8], sqv[:, :, 0:64], AX.X, ALU.add, r=[("sq", j % 2)], w=["ss2"])
                        dve("tensor_reduce", ss2[:, j, 8:16], sqv[:, :, 64:96], AX.X, ALU.add, r=[("sq", j % 2)], w=["ss2"])
                        act("activation", sq[:, 0:512], kvsb[j][:].rearrange("p h e -> p (h e)"), AF.Square, r=[("kvsb", j)], w=[("sq", j % 2)])
                        dve("tensor_reduce", ss2[:, j, 16:24], sq[:, 0:512].rearrange("p (h e) -> p h e", h=8), AX.X, ALU.add,
                            r=[("sq", j % 2)], w=["ss2"])

                    if nxt:
                        for j in range(4):
                            n_stats(blk + 1, j)
                    pend = {}
                    for j in range(5):
                        if j < 4:
                            pend[j] = La(j)
                        if j == 1 and nxt:
                            n_rstd()
                        if j >= 1:
                            Lb(j - 1, *pend[j - 1])
                            if nxt:
                                n_hT(j - 1, bank=(0 if (j - 1) % 2 == 0 else 4))
                    pg = P_gen(blk + 1) if nxt else None
                    dve("tensor_tensor", ss2[:], ss2[:], invn[:, 3:27].unsqueeze(1).to_broadcast([128, 4, 24]), ALU.mult,
                        r=["ss2", "invn"], w=["ss2"])
                    act("activation", rs2[:], ss2[:], AF.Ln, bias=cst[:, 0:1], r=["ss2", "cst"], w=["rs2"])
                    act("activation", rs2[:], rs2[:], AF.Exp, scale=-0.5, r=["rs2"], w=["rs2"])
                    for j in range(4):
                        t = blk * 4 + j
                        sq = sq_2[j % 2]; Qf = Qf_2[j % 2]; Kf = Kf_2[j % 2]; ra = ra_2[j % 2]; rb = rb_2[j % 2]; rq = rq_2[j % 2]
                        ka = ka_2[j % 2]; kb = kb_2[j % 2]; kr = kr_2[j % 2]
                        qv = qsb[j][:].rearrange("p (h e) -> p h e", h=8)
                        cosb = cosT[:, t, :]; sinb = sinT[:, t, :]
                        dve("tensor_tensor", sq[:, 0:512].rearrange("p (h e) -> p h e", h=8), qv[:, :, 0:64],
                            rs2[:, j, 0:8].unsqueeze(2).to_broadcast([128, 8, 64]), ALU.mult, r=[("qsb", j), "rs2"], w=[("sq", j % 2)])
                        dve("tensor_tensor", Qf[:, :, 0:64], sq[:, 0:512].rearrange("p (h e) -> p h e", h=8),
                            qnw[:].unsqueeze(1).to_broadcast([128, 8, 64]), ALU.mult, r=[("sq", j % 2), "qnw"], w=[("Qf", j % 2)])
                        pool("tensor_tensor", ra[:], qv[:, :, 64:96], rs2[:, j, 8:16].unsqueeze(2).to_broadcast([128, 8, 32]), ALU.mult,
                             r=[("qsb", j), "rs2"], w=[("ra", j % 2)])
                        pool("tensor_tensor", ra[:], ra[:], qpw[:].unsqueeze(1).to_broadcast([128, 8, 32]), ALU.mult, r=[("ra", j % 2), "qpw"], w=[("ra", j % 2)])
                        cb8 = cosb.unsqueeze(1).to_broadcast([128, 8, 16]); sb8 = sinb.unsqueeze(1).to_broadcast([128, 8, 16])
                        pool("tensor_tensor", rb[:, :, 0:16], ra[:, :, 0:16], cb8, ALU.mult, r=[("ra", j % 2), "cosT"], w=[("rb", j % 2)])
                        pool("tensor_tensor", rb[:, :, 16:32], ra[:, :, 16:32], cb8, ALU.mult, r=[("ra", j % 2), "cosT"], w=[("rb", j % 2)])
                        pool("tensor_tensor", rq[:, :, 0:16], ra[:, :, 16:32], sb8, ALU.mult, r=[("ra", j % 2), "sinT"], w=[("rq", j % 2)])
                        pool("tensor_tensor", rq[:, :, 16:32], ra[:, :, 0:16], sb8, ALU.mult, r=[("ra", j % 2), "sinT"], w=[("rq", j % 2)])
                        pool("tensor_tensor", Qf[:, :, 64:80], rb[:, :, 0:16], rq[:, :, 0:16], ALU.subtract, r=[("rb", j % 2), ("rq", j % 2)], w=[("Qf", j % 2)])
                        pool("tensor_tensor", Qf[:, :, 80:96], rb[:, :, 16:32], rq[:, :, 16:32], ALU.add, r=[("rb", j % 2), ("rq", j % 2)], w=[("Qf", j % 2)])
                        dve("tensor_tensor", sq[:, 0:512].rearrange("p (h e) -> p h e", h=8), kvsb[j][:],
                            rs2[:, j, 16:24].unsqueeze(2).to_broadcast([128, 8, 64]), ALU.mult, r=[("kvsb", j), "rs2"], w=[("sq", j % 2)])
                        dve("tensor_tensor", Kf[:, :, 0:64], sq[:, 0:512].rearrange("p (h e) -> p h e", h=8),
                            knw[:].unsqueeze(1).to_broadcast([128, 8, 64]), ALU.mult, r=[("sq", j % 2), "knw"], w=[("Kf", j % 2)])
                        dve("tensor_tensor", ka[:, 0:16], kpn[j][:, 0:16], cosb, ALU.mult, r=[("kpn", j), "cosT"], w=[("ka", j % 2)])
                        dve("tensor_tensor", ka[:, 16:32], kpn[j][:, 16:32], cosb, ALU.mult, r=[("kpn", j), "cosT"], w=[("ka", j % 2)])
                        dve("tensor_tensor", kb[:, 0:16], kpn[j][:, 16:32], sinb, ALU.mult, r=[("kpn", j), "sinT"], w=[("kb", j % 2)])
                        dve("tensor_tensor", kb[:, 16:32], kpn[j][:, 0:16], sinb, ALU.mult, r=[("kpn", j), "sinT"], w=[("kb", j % 2)])
                        dve("tensor_tensor", kr[:, 0:16], ka[:, 0:16], kb[:, 0:16], ALU.subtract, r=[("ka", j % 2), ("kb", j % 2)], w=[("kr", j % 2)])
                        dve("tensor_tensor", kr[:, 16:32], ka[:, 16:32], kb[:, 16:32], ALU.add, r=[("ka", j % 2), ("kb", j % 2)], w=[("kr", j % 2)])
                        dve("tensor_copy", Kf[:, :, 64:96], kr[:].unsqueeze(1).to_broadcast([128, 8, 32]), r=[("kr", j % 2)], w=[("Kf", j % 2)])
                        bqt = nbank()
                        for hh_ in range(8):
                            pe("transpose", PS16(bqt)[0:96, hh_ * 128:(hh_ + 1) * 128], Qf[:, hh_, :], ident[:],
                               r=[("Qf", j % 2), "ident"], w=[pk(bqt)])
                        act("copy", QTb[:, :, j * 128:(j + 1) * 128], PS16(bqt)[0:96, :].rearrange("p (h t) -> p h t", h=8),
                            r=[pk(bqt)], w=["QTb"])
                        bkt = nbank()
                        for hh_ in range(8):
                            pe("transpose", PS16(bkt)[0:96, hh_ * 128:(hh_ + 1) * 128], Kf[:, hh_, :], ident[:],
                               r=[("Kf", j % 2), "ident"], w=[pk(bkt)])
                        act("copy", KTb[:, :, j * 128:(j + 1) * 128], PS16(bkt)[0:96, :].rearrange("p (h t) -> p h t", h=8),
                            r=[pk(bkt)], w=["KTb"])
                        if pg is not None:
                            next(pg, None)
                            if j == 3:
                                for _ in pg:
                                    pass
                    P.dma(QT_d[:, :, blk * 512:(blk + 1) * 512].rearrange("h d s -> d h s"), QTb[:], reads=["QTb"], eng="act")
                    P.dma(KT_d[:, :, blk * 512:(blk + 1) * 512].rearrange("h d s -> d h s"), KTb[:], reads=["KTb"], eng="act")
                P.flush()

        def phase_B(l):
            with contextlib.ExitStack() as st:
                sb = mk_sb(st)
                KT = sb([96, 8, S], BF16)
                Vs = sb([128, NT, 8 * 128], BF16)
                QTs = [sb([96, 8, 512], BF16) for _ in range(2)]
                onesb = sb([128, 512], BF16)
                amask = [sb([128, 512], BF16) for _ in range(4)]
                pT = [sb([128, 2, 512], BF16) for _ in range(4)]
                rden = [sb([128, 512], F32) for _ in range(2)]
                aT = [sb([128, 4, 512], BF16) for _ in range(2)]
                pool("memset", onesb[:], 1.0, w=["onesb"])
                for j in range(4):
                    pool("affine_select", amask[j][:], onesb[:], pattern=[[1, 512]], compare_op=ALU.is_ge, fill=0.0,
                         base=-128 * j, channel_multiplier=-1, r=["onesb"], w=[("amask", j)])
                P.dma(KT[:], KT_d.rearrange("h d s -> d h s"), writes=["KT"])
                for t0 in range(0, NT, 8):
                    t1 = min(NT, t0 + 8)
                    P.dma(Vs[:, t0:t1, :], V_d[t0 * 128:t1 * 128, :].rearrange("(t p) f -> p t f", p=128), writes=[("Vs", t0)])

                def load_q(blk):
                    P.dma(QTs[blk % 2][:], QT_d[:, :, blk * 512:(blk + 1) * 512].rearrange("h d s -> d h s"),
                          writes=[("QTs", blk % 2)])

                its = [(blk, hd, p) for blk in range(NB) for hd in range(NH) for p in range(2 * blk + 2)]

                def emit_s(i):
                    blk, hd, p = its[i]
                    if hd == 0 and p == 0:
                        if blk == 0:
                            load_q(0)
                        if blk + 1 < NB:
                            load_q(blk + 1)
                    qb = QTs[blk % 2]
                    sbp = 2 * (i % 3)
                    pp = pT[i % 4]; pkey = ("pT", i % 4)
                    for a_ in range(2):
                        kt = 2 * p + a_
                        pe("matmul", PS(sbp + a_), KT[:, hd, kt * 128:(kt + 1) * 128], qb[:, hd, :], start=True, stop=True,
                           r=["KT", ("QTs", blk % 2)], w=[pk(sbp + a_)])
                    act("activation", pp[:], psum[:, sbp:sbp + 2, :], AF.Exp, r=[pk(sbp), pk(sbp + 1)], w=[pkey])
                    for a_ in range(2):
                        jd = 2 * p + a_ - 4 * blk
                        if jd >= 0:
                            w_ = 128 * (jd + 1)
                            eng = dve if (jd % 2 == 0 or l + 1 < nlayers) else pool
                            eng("tensor_tensor", pp[:, a_, 0:w_], pp[:, a_, 0:w_], amask[jd][:, 0:w_], ALU.mult,
                                r=[pkey, ("amask", jd)], w=[pkey])

                def emit_pv(i):
                    blk, hd, p = its[i]
                    nk = 4 * blk + 4
                    pp = pT[i % 4]; pkey = ("pT", i % 4)
                    bo = 6 + (hd % 2)
                    for a_ in range(2):
                        kt = 2 * p + a_
                        pe("matmul", PS(bo), Vs[:, kt, hd * 128:(hd + 1) * 128], pp[:, a_, :], start=(kt == 0), stop=(kt == nk - 1),
                           r=[("Vs", (kt // 8) * 8), pkey], w=[pk(bo)])
                    if p == 2 * blk + 1:
                        at = aT[blk % 2]
                        rd = rden[hd % 2]
                        c = hd // 2
                        if hd % 2 == 0:
                            dve("reciprocal", rd[64:128, :], psum[64:128, bo, :], r=[pk(bo)], w=[("rden", 0)])
                            dve("tensor_tensor", at[0:64, c, :], psum[0:64, bo, :], rd[64:128, :], ALU.mult,
                                r=[pk(bo), ("rden", 0)], w=[("aT", blk % 2)])
                        else:
                            dve("reciprocal", rd[0:64, :], psum[0:64, bo, :], r=[pk(bo)], w=[("rden", 1)])
                            dve("tensor_tensor", at[64:128, c, :], psum[64:128, bo, :], rd[0:64, :], ALU.mult,
                                r=[pk(bo), ("rden", 1)], w=[("aT", blk % 2)])
                        if hd == NH - 1:
                            P.dma(mixT_d[0:512, blk * 512:(blk + 1) * 512].rearrange("(c p) s -> p c s", p=128), at[:],
                                  reads=[("aT", blk % 2)])

                n = len(its)
                gen = wprep_gen(l + 1, sb, 1024, "sp") if l + 1 < nlayers else None
                for i in range(n + 2):
                    if i < n:
                        emit_s(i)
                    if i >= 2:
                        emit_pv(i - 2)
                    if gen is not None and i % 3 == 2:
                        next(gen, None)
                if gen is not None:
                    for _ in gen:
                        pass
                P.flush()

        def phase_C(l):
            with contextlib.ExitStack() as st:
                sb = mk_sb(st)
                ab = sb([128, 8], F32); dsk = sb([128, 8], F32); nwb = sb([128, 512], F32)
                P.dma(ab[:], a_log[l:l + 1, :].partition_broadcast(128), writes=["ab"])
                P.dma(dsk[:], d_skip[l:l + 1, :].partition_broadcast(128), writes=["dsk"])
                P.dma(nwb[:], ssd_norm_w[l:l + 1, :].partition_broadcast(128), writes=["nwb"])
                act("activation", ab[:], ab[:], AF.Exp, r=["ab"], w=["ab"])
                dve("tensor_scalar", ab[:], ab[:], -1.0, None, ALU.mult, r=["ab"], w=["ab"])
                BTs = sb([128, 2, S], BF16); CTs = sb([128, 2, S], BF16)
                P.dma(BTs[:], BT_d.rearrange("g n s -> n g s"), writes=["BTs"])
                P.dma(CTs[:], CT_d.rearrange("g n s -> n g s"), writes=["CTs"])
                xs4 = [sb([128, 4, 512], BF16) for _ in range(2)]
                B4 = [sb([128, 4, 256], BF16) for _ in range(2)]
                z4 = [sb([128, 4, 512], BF16) for _ in range(2)]
                sz4 = [sb([128, 4, 512], F32) for _ in range(2)]
                adt_2 = [sb([128, 8], F32) for _ in range(3)]
                rhs_all_2 = [sb([128, 8, 128], F32) for _ in range(3)]
                ET_2 = [sb([128, 8, 128], BF16) for _ in range(3)]
                small_2 = [sb([128, 16], F32) for _ in range(3)]
                dec_2 = [sb([128, 8], F32) for _ in range(3)]
                sm_2 = [sb([128, 2, 128], BF16) for _ in range(3)]
                MT_2 = [sb([128, 8, 128], BF16) for _ in range(3)]
                xdt_2 = [sb([128, 8, 64], BF16) for _ in range(3)]; xdd_2 = [sb([128, 8, 64], BF16) for _ in range(3)]
                state = sb([128, 8, 64], F32); state_bf = sb([128, 8, 64], BF16)
                yo_2 = [sb([128, 8, 64], F32) for _ in range(3)]; y_2 = [sb([128, 512], F32) for _ in range(3)]
                xsd_2 = [sb([128, 512], F32) for _ in range(3)]
                yg = sb([128, 512], F32); junk = sb([128, 256], F32)
                ssg = sb([128, 4, 2], F32); rsg = sb([128, 4, 2], F32)
                yg4 = sb([128, 4, 512], F32)
                yn_2 = [sb([128, 512], BF16) for _ in range(2)]
                ygT = sb([128, 4, 512], BF16)
                dve("memset", state[:], 0.0, w=["state"])
                dve("memset", state_bf[:], 0.0, w=["state_bf"])
                def load_blk(blk):
                    pb = blk % 2
                    rows = slice(blk * 512, (blk + 1) * 512)
                    P.dma(xs4[pb][:], xs_d[rows, :].rearrange("(j p) f -> p j f", p=128), writes=[("xs4", pb)])
                    P.dma(B4[pb][:], B_d[rows, :].rearrange("(j p) f -> p j f", p=128), writes=[("B4", pb)])
                    P.dma(z4[pb][:], z_d[rows, :].rearrange("(j p) f -> p j f", p=128), writes=[("z4", pb)])
                    act("activation", sz4[pb][:], z4[pb][:], AF.Silu, r=[("z4", pb)], w=[("sz4", pb)])

                def s1(blk, j):
                    pb = blk % 2
                    jb = (blk * 4 + j) % 3
                    t = blk * 4 + j
                    cs = slice(t * 128, (t + 1) * 128)
                    dtt = dt_all[:, t, :]
                    adt = adt_2[jb]; rhs_all = rhs_all_2[jb]; ET = ET_2[jb]; small = small_2[jb]; dec = dec_2[jb]
                    sm = sm_2[jb]; MT = MT_2[jb]; xdt = xdt_2[jb]; xdd = xdd_2[jb]
                    yo = yo_2[jb]; y = y_2[jb]; xsd = xsd_2[jb]
                    dve("tensor_tensor", adt[:], dtt, ab[:], ALU.mult, r=["dt_all", "ab"], w=[("adt", jb)])
                    pool("tensor_tensor", rhs_all[:], ltri[:].unsqueeze(1).to_broadcast([128, 8, 128]),
                        adt[:].unsqueeze(2).to_broadcast([128, 8, 128]), ALU.mult, r=["ltri", ("adt", jb)], w=[("rhs_all", jb)])
                    bs = npair()
                    for hf in range(2):
                        pe("matmul", PS(bs + hf), ustr[:], rhs_all[:, hf * 4:(hf + 1) * 4, :].rearrange("p h l -> p (h l)"),
                           start=True, stop=True, r=["ustr", ("rhs_all", jb)], w=[pk(bs + hf)])
                    bsm = nbank()
                    pe("matmul", PS(bsm, 8), ltri[:], adt[:], start=True, stop=True, r=["ltri", ("adt", jb)], w=[pk(bsm)])
                    pe("matmul", psum[:, bsm, 8:16], ones_f[:], adt[:], start=True, stop=True, r=["ones_f", ("adt", jb)], w=[pk(bsm)])
                    segv = PS2(bs).rearrange("p (h l) -> p h l", h=8)
                    act("activation", ET[:], segv, AF.Exp, r=[pk(bs), pk(bs + 1)], w=[("ET", jb)])
                    act("activation", dec[:], segv[:, :, 127], AF.Exp, r=[pk(bs), pk(bs + 1)], w=[("dec", jb)])
                    act("activation", small[:], PS(bsm, 16), AF.Exp, r=[pk(bsm)], w=[("small", jb)])
                    bsc = nbank()
                    for g in range(2):
                        pe("matmul", psum[:, bsc, g * 128:(g + 1) * 128], BTs[:, g, cs], CTs[:, g, cs], start=True, stop=True,
                           r=["BTs", "CTs"], w=[pk(bsc)])
                    dve("tensor_tensor", sm[:], PS(bsc, 256).rearrange("p (g l) -> p g l", g=2),
                        ltri_bf[:].unsqueeze(1).to_broadcast([128, 2, 128]), ALU.mult, r=[pk(bsc), "ltri_bf"], w=[("sm", jb)])
                    for g in range(2):
                        dve("tensor_tensor", MT[:, g * 4:(g + 1) * 4, :], ET[:, g * 4:(g + 1) * 4, :],
                            sm[:, g:g + 1, :].to_broadcast([128, 4, 128]), ALU.mult, r=[("ET", jb), ("sm", jb)], w=[("MT", jb)])
                    xsv = xs4[pb][:, j, :].rearrange("p (h e) -> p h e", h=8)
                    pool("tensor_tensor", xdt[:], xsv, dtt.unsqueeze(2).to_broadcast([128, 8, 64]), ALU.mult,
                         r=[("xs4", pb), "dt_all"], w=[("xdt", jb)])
                    pool("tensor_tensor", xdd[:], xdt[:], dec[:].unsqueeze(2).to_broadcast([128, 8, 64]), ALU.mult,
                         r=[("xdt", jb), ("dec", jb)], w=[("xdd", jb)])

                def s2(blk, j):
                    pb = blk % 2
                    jb = (blk * 4 + j) % 3
                    t = blk * 4 + j
                    cs = slice(t * 128, (t + 1) * 128)
                    adt = adt_2[jb]; rhs_all = rhs_all_2[jb]; ET = ET_2[jb]; small = small_2[jb]; dec = dec_2[jb]
                    sm = sm_2[jb]; MT = MT_2[jb]; xdt = xdt_2[jb]; xdd = xdd_2[jb]
                    yo = yo_2[jb]; y = y_2[jb]; xsd = xsd_2[jb]
                    xsv = xs4[pb][:, j, :].rearrange("p (h e) -> p h e", h=8)
                    byd = nbank()
                    for hd in range(8):
                        pe("matmul", psum[:, byd, hd * 64:(hd + 1) * 64], MT[:, hd, :], xdt[:, hd, :], start=True, stop=True,
                           r=[("MT", jb), ("xdt", jb)], w=[pk(byd)])
                    byo = nbank()
                    for hd in range(8):
                        pe("matmul", psum[:, byo, hd * 64:(hd + 1) * 64], CTs[:, hd // 4, cs], state_bf[:, hd, :], start=True, stop=True,
                           r=["CTs", "state_bf"], w=[pk(byo)])
                    bst = nbank()
                    for hd in range(8):
                        pe("matmul", psum[:, bst, hd * 64:(hd + 1) * 64], B4[pb][:, j, (hd // 4) * 128:(hd // 4 + 1) * 128],
                           xdd[:, hd, :], start=True, stop=True, r=[("B4", pb), ("xdd", jb)], w=[pk(bst)])
                    dve("tensor_tensor", yo[:], PS(byo).rearrange("p (h e) -> p h e", h=8),
                        small[:, 0:8].unsqueeze(2).to_broadcast([128, 8, 64]), ALU.mult, r=[pk(byo), ("small", jb)], w=[("yo", jb)])
                    dve("tensor_tensor", y[:], yo[:].rearrange("p h e -> p (h e)"), PS(byd), ALU.add, r=[("yo", jb), pk(byd)], w=[("y", jb)])
                    pool("tensor_tensor", xsd[:].rearrange("p (h e) -> p h e", h=8), xsv,
                         dsk[:].unsqueeze(2).to_broadcast([128, 8, 64]), ALU.mult, r=[("xs4", pb), "dsk"], w=[("xsd", jb)])
                    pool("tensor_tensor", y[:], y[:], xsd[:], ALU.add, r=[("y", jb), ("xsd", jb)], w=[("y", jb)])
                    dve("tensor_tensor", state[:], state[:], small[:, 8:16].unsqueeze(2).to_broadcast([128, 8, 64]), ALU.mult,
                        r=["state", ("small", jb)], w=["state"])
                    dve("tensor_tensor", state[:], state[:], PS(bst).rearrange("p (h e) -> p h e", h=8), ALU.add,
                        r=["state", pk(bst)], w=["state"])
                    act("copy", state_bf[:], state[:], r=["state"], w=["state_bf"])
                    dve("tensor_tensor", yg4[:, j, :], y[:], sz4[pb][:, j, :], ALU.mult, r=[("y", jb), ("sz4", pb)], w=["yg4"])
                    for g in range(2):
                        act("activation", junk[:], yg4[:, j, g * 256:(g + 1) * 256], AF.Square, accum_out=ssg[:, j, g:g + 1],
                            r=["yg4"], w=["junk", "ssg"])

                def end_blk(blk):
                    act("activation", rsg[:], ssg[:], AF.Ln, scale=1.0 / 256, bias=cst[:, 0:1], r=["ssg", "cst"], w=["rsg"])
                    act("activation", rsg[:], rsg[:], AF.Exp, scale=-0.5, r=["rsg"], w=["rsg"])
                    for j in range(4):
                        yn = yn_2[j % 2]
                        for g in range(2):
                            dve("scalar_tensor_tensor", yn[:, g * 256:(g + 1) * 256], yg4[:, j, g * 256:(g + 1) * 256], rsg[:, j, g:g + 1],
                                nwb[:, g * 256:(g + 1) * 256], ALU.mult, ALU.mult, r=["yg4", "rsg", "nwb"], w=[("yn", j % 2)])
                        b = nbank()
                        for cch in range(4):
                            pe("transpose", PS16(b)[:, cch * 128:(cch + 1) * 128], yn[:, cch * 128:(cch + 1) * 128], ident[:],
                               r=[("yn", j % 2), "ident"], w=[pk(b)])
                        act("copy", ygT[:, :, j * 128:(j + 1) * 128], PS16(b)[:, 0:512].rearrange("p (c t) -> p c t", c=4),
                            r=[pk(b)], w=["ygT"])
                    P.dma(mixT_d[512:1024, blk * 512:(blk + 1) * 512].rearrange("(c p) s -> p c s", p=128), ygT[:],
                          reads=["ygT"], eng="act")

                wgc = wprep_gen(0, sb, 1024, "sp", which="b", plain_eng="act") if l == 0 else None
                for i in range(NT + 2):
                    if wgc is not None:
                        for _k in range(3):
                            next(wgc, None)
                    if i < NT:
                        if i % 4 == 0:
                            load_blk(i // 4)
                        s1(i // 4, i % 4)
                    if i >= 2:
                        s2((i - 2) // 4, (i - 2) % 4)
                        if (i - 2) % 4 == 3:
                            end_blk((i - 2) // 4)
                if wgc is not None:
                    for _ in wgc:
                        pass
                P.flush()

        def phase_D(l, xsrc, xdst):
            with contextlib.ExitStack() as st:
                sb = mk_sb(st)
                wo = sb([128, 8, D], BF16)
                P.dma(wo[:], woutb_d_L[l].rearrange("(k p) c -> p k c", p=128), writes=["wo"])
                mT = [sb([128, 8, 512], BF16) for _ in range(2)]
                xt = [sb([128, D], F32) for _ in range(4)]
                for blk in range(NB):
                    pb = blk % 2
                    P.dma(mT[pb][:], mixT_d[:, blk * 512:(blk + 1) * 512].rearrange("(k p) s -> p k s", p=128), writes=[("mT", pb)])
                    for j in range(4):
                        t = blk * 4 + j
                        i2 = t % 4
                        P.dma(xt[i2][:], xsrc[t * 128:(t + 1) * 128, :], writes=[("xt", i2)])
                        bp = npair()
                        for hf in range(2):
                            for k in range(8):
                                pe("matmul", PS(bp + hf), mT[pb][:, k, j * 128:(j + 1) * 128], wo[:, k, hf * 512:(hf + 1) * 512],
                                   start=(k == 0), stop=(k == 7), r=[("mT", pb), "wo"], w=[pk(bp + hf)])
                        dve("tensor_tensor", xt[i2][:], PS2(bp), xt[i2][:], ALU.add, r=[pk(bp), pk(bp + 1), ("xt", i2)], w=[("xt", i2)])
                        P.dma(xdst[t * 128:(t + 1) * 128, :], xt[i2][:], reads=[("xt", i2)], eng="act")
                P.flush()

        def phase_E(l, xsrc, xdst):
            with contextlib.ExitStack() as st:
                sb = mk_sb(st)
                NF = D_FF // 128
                wgu = sb([128, 8, 2 * D_FF], BF16)
                wd = sb([128, NF, D], BF16)
                weff = sb([128, D], F32); shb = sb([128, D], F32)
                for k in range(8):
                    P.dma(wgu[:, k, :], wgub_d_L[l][k * 128:(k + 1) * 128, :], writes=[("wgu", k)])
                P.dma(wd[:], wdb_d_L[l].rearrange("(k p) c -> p k c", p=128), writes=["wd"])
                P.dma(weff[:], modrow_d_L[l][3:4, :].partition_broadcast(128), writes=["weff"])
                P.dma(shb[:], modrow_d_L[l][4:5, :].partition_broadcast(128), writes=["shb"])
                xt = [sb([128, D], F32) for _ in range(4)]
                ssx = sb([128, 4], F32); rsx = sb([128, 4], F32)
                htmp = [sb([128, D], F32)] * 2
                h = [sb([128, D], BF16) for _ in range(4)]
                hT = sb([128, 8, 512], BF16)
                sg = [sb([128, 512], F32) for _ in range(2)]
                aT = sb([128, NF, 512], BF16)

                def n_stats(blk, j):
                    t = blk * 4 + j
                    P.dma(xt[j][:], xsrc[t * 128:(t + 1) * 128, :], writes=[("xt", j)])
                    act("activation", h[3][:], xt[j][:], AF.Square, accum_out=ssx[:, j:j + 1],
                        r=[("xt", j)], w=[("h", 3), "ssx"])

                def n_rstd():
                    act("activation", rsx[:], ssx[:], AF.Ln, scale=1.0 / D, bias=cst[:, 0:1], r=["ssx", "cst"], w=["rsx"])
                    act("activation", rsx[:], rsx[:], AF.Exp, scale=-0.5, r=["rsx"], w=["rsx"])

                def n_h(j):
                    ht = htmp[j % 2]
                    dve("scalar_tensor_tensor", ht[:], xt[j][:], rsx[:, j:j + 1], weff[:], ALU.mult, ALU.mult,
                        r=[("xt", j), "rsx", "weff"], w=["htmp"])
                    pool("tensor_tensor", h[j][:], ht[:], shb[:], ALU.add, r=["htmp", "shb"], w=[("h", j)])

                def n_T():
                    for j in range(4):
                        b = nbank()
                        for k in range(8):
                            pe("transpose", PS16(b)[:, k * 128:(k + 1) * 128], h[j][:, k * 128:(k + 1) * 128], ident[:],
                               r=[("h", j), "ident"], w=[pk(b)])
                        act("copy", hT[:, :, j * 128:(j + 1) * 128], PS16(b).rearrange("p (k t) -> p k t", k=8),
                            r=[pk(b)], w=["hT"])

                for j in range(4):
                    n_stats(0, j)
                n_rstd()
                for j in range(4):
                    n_h(j)
                n_T()
                for blk in range(NB):
                    nxt = blk + 1 < NB
                    for f in range(NF):
                        bg = nbank()
                        for k in range(8):
                            pe("matmul", PS(bg), wgu[:, k, f * 128:(f + 1) * 128], hT[:, k, :], start=(k == 0), stop=(k == 7),
                               r=[("wgu", k), "hT"], w=[pk(bg)])
                        bu = nbank()
                        for k in range(8):
                            pe("matmul", PS(bu), wgu[:, k, D_FF + f * 128:D_FF + (f + 1) * 128], hT[:, k, :], start=(k == 0), stop=(k == 7),
                               r=[("wgu", k), "hT"], w=[pk(bu)])
                        act("activation", sg[f % 2][:], PS(bg), AF.Silu, r=[pk(bg)], w=[("sg", f % 2)])
                        dve("tensor_tensor", aT[:, f, :], PS(bu), sg[f % 2][:], ALU.mult, r=[pk(bu), ("sg", f % 2)], w=["aT"])
                        if nxt:
                            if f in (1, 3, 5, 7):
                                n_stats(blk + 1, (f - 1) // 2)
                            elif f == 9:
                                n_rstd()
                            elif f in (11, 13, 15, 17):
                                n_h((f - 11) // 2)
                    for j in range(4):
                        t = blk * 4 + j
                        P.dma(xt[j][:], xsrc[t * 128:(t + 1) * 128, :], writes=[("xt", j)])
                        bp = npair()
                        for hf in range(2):
                            for f in range(NF):
                                pe("matmul", PS(bp + hf), aT[:, f, j * 128:(j + 1) * 128], wd[:, f, hf * 512:(hf + 1) * 512],
                                   start=(f == 0), stop=(f == NF - 1), r=["aT", "wd"], w=[pk(bp + hf)])
                        if j == 0 and nxt:
                            n_T()
                        dve("tensor_tensor", xt[j][:], PS2(bp), xt[j][:], ALU.add, r=[pk(bp), pk(bp + 1), ("xt", j)], w=[("xt", j)])
                        P.dma(xdst[t * 128:(t + 1) * 128, :], xt[j][:], reads=[("xt", j)], eng="act")
                P.flush()

        stages = []
        stages.append(("const", phase_const))
        stages.append(("wprep0", phase_start))
        for l in range(nlayers):
            xsrc = x_in if l == 0 else xb_d
            xfin = out if l == nlayers - 1 else xb_d
            stages.append(("A%d" % l, partial(phase_A, l, xsrc)))
            stages.append(("B%d" % l, partial(phase_B, l)))
            stages.append(("C%d" % l, partial(phase_C, l)))
            stages.append(("D%d" % l, partial(phase_D, l, xsrc, xa_d)))
            stages.append(("E%d" % l, partial(phase_E, l, xa_d, xfin)))
        for name, fn in stages:
            fn()
            if upto is not None and name == upto:
                break
    return nc


_INPUT_NAMES = ["norm1_w", "norm2_w", "w_ada", "b_ada", "w_in", "q_a_norm_w", "w_q_up", "kv_a_norm_w", "w_kv_up",
                "q_nope_norm_w", "q_pe_norm_w", "k_nope_norm_w", "k_pe_norm_w", "conv_w", "conv_b", "dt_bias",
                "a_log", "d_skip", "ssd_norm_w", "w_out", "w_gate_up", "w_down"]


def kernel(x, c, positions, **w):
    x = np.asarray(x); c = np.asarray(c); positions = np.asarray(positions)
    Bn, S, _ = x.shape
    nc = build(S)
    shared = {k: np.ascontiguousarray(np.asarray(w[k], dtype=np.float32)) for k in _INPUT_NAMES}
    in_maps = []
    for b in range(Bn):
        m = dict(shared)
        m["x"] = np.ascontiguousarray(x[b], dtype=np.float32)
        m["c"] = np.ascontiguousarray(c[b:b + 1], dtype=np.float32)
        m["positions"] = np.ascontiguousarray(positions[b:b + 1], dtype=np.int32)
        in_maps.append(m)
    res = run_bass_kernel_spmd(nc, in_maps, core_ids=list(range(Bn)))
    return np.stack([np.asarray(r["out"], dtype=np.float32) for r in res.results], axis=0)
```

```python
import contextlib
import math
from functools import partial

import numpy as np
import concourse.bass as bass
import concourse.mybir as mybir
from concourse.bass_utils import run_bass_kernel_spmd

F32 = mybir.dt.float32
BF16 = mybir.dt.bfloat16
I32 = mybir.dt.int32
ALU = mybir.AluOpType
AF = mybir.ActivationFunctionType
AX = mybir.AxisListType

D = 1024
DEPTH = 2
NH = 8
D_IN = 1960
D_FF = 2816
EPS = 1e-6
ENGS = ("pe", "act", "dve", "pool", "sp")
N_DMA_SEMS = 16


class _Op:
    __slots__ = ("eng", "fn", "deps", "dma", "signal", "count", "sem")


class Prog:
    def __init__(self, nc, stack):
        self.nc = nc
        self.esem = {e: stack.enter_context(nc.semaphore("s_" + e)) for e in ENGS}
        self.dsem = [stack.enter_context(nc.semaphore("d%d" % i)) for i in range(N_DMA_SEMS)]
        self.ecnt = {e: 0 for e in ENGS}
        self.dcnt = [0] * N_DMA_SEMS
        self.nd = 0
        self.nops = 0
        self._reset()

    def _reset(self):
        self.ops = []
        self.last_w = {}
        self.readers = {}
        self.eng_ops = {e: [] for e in ENGS}

    def op(self, eng, fn, reads=(), writes=(), dma=False):
        o = _Op()
        o.eng, o.fn, o.dma, o.signal, o.count, o.sem = eng, fn, dma, False, 0, None
        need = {}
        for r in reads:
            w = self.last_w.get(r)
            if w is not None:
                self._dep(need, o, w, "raw")
        for r in writes:
            w = self.last_w.get(r)
            if w is not None:
                self._dep(need, o, w, "waw")
            for rd in self.readers.get(r, ()):
                self._dep(need, o, rd, "war")
        o.deps = list(need.values())
        for r in reads:
            self.readers.setdefault(r, []).append(o)
        for r in writes:
            self.last_w[r] = o
            self.readers[r] = []
        self.ops.append(o)
        self.eng_ops[eng].append(o)
        return o

    @staticmethod
    def _dep(need, o, d, kind):
        if d is o:
            return
        if d.eng == o.eng and not d.dma and not o.dma and o.eng == "pe":
            return
        need[id(d)] = d

    def dma(self, out, in_, reads=(), writes=(), eng="sp", **kw):
        q = {"sp": self.nc.sync, "act": self.nc.scalar, "pool": self.nc.gpsimd}[eng]
        return self.op(eng, partial(q.dma_start, out=out, in_=in_, **kw), reads, writes, dma=True)

    def flush(self):
        nc = self.nc
        self.nops += len(self.ops)
        for o in self.ops:
            for d in o.deps:
                d.signal = True
        for e in ENGS:
            comp = [o for o in self.eng_ops[e] if not o.dma]
            if comp:
                comp[-1].signal = True
        dlast = [None] * N_DMA_SEMS
        for o in self.ops:
            if o.dma:
                o.signal = True
                k = self.nd % N_DMA_SEMS
                self.nd += 1
                self.dcnt[k] += 16
                o.sem, o.count = self.dsem[k], self.dcnt[k]
                if dlast[k] is not None:
                    o.deps.append(dlast[k])
                dlast[k] = o
            elif o.signal:
                self.ecnt[o.eng] += 1
                o.sem, o.count = self.esem[o.eng], self.ecnt[o.eng]
        dcnt = list(self.dcnt)
        ecnt = dict(self.ecnt)
        eng_ops = self.eng_ops
        dsem, esem = self.dsem, self.esem

        def run(engname, e):
            waited = {}
            for o in eng_ops[engname]:
                for d in o.deps:
                    key = id(d.sem)
                    if waited.get(key, 0) >= d.count:
                        continue
                    e.wait_ge(d.sem, d.count)
                    waited[key] = d.count
                ins = o.fn()
                if o.signal:
                    ins.then_inc(o.sem, 16 if o.dma else 1)
            for en in ENGS:
                if ecnt[en] and waited.get(id(esem[en]), 0) < ecnt[en]:
                    e.wait_ge(esem[en], ecnt[en])
            for k in range(N_DMA_SEMS):
                if dcnt[k] and waited.get(id(dsem[k]), 0) < dcnt[k]:
                    e.wait_ge(dsem[k], dcnt[k])

        with nc.Block() as blk:
            @blk.tensor
            def _(e):
                run("pe", e)

            @blk.scalar
            def _(e):
                run("act", e)

            @blk.vector
            def _(e):
                run("dve", e)

            @blk.gpsimd
            def _(e):
                run("pool", e)

            @blk.sync
            def _(e):
                run("sp", e)
        self._reset()


DBG_STOP = [None]


def build(S, nlayers=DEPTH, debug=False, upto=None):
    NT = S // 128
    NB = S // 512
    nc = bass.Bass("TRN2", target_bir_lowering=False)
    okind = "ExternalOutput" if debug else "Internal"

    def din(name, shape, dt=F32):
        return nc.dram_tensor(name, list(shape), dt, kind="ExternalInput").ap()

    def dscr(name, shape, dt):
        return nc.dram_tensor(name, list(shape), dt, kind=okind).ap()

    x_in = din("x", [S, D])
    c_in = din("c", [1, D])
    pos_in = din("positions", [1, S], I32)
    Ld = DEPTH
    norm1_w = din("norm1_w", [Ld, D]); norm2_w = din("norm2_w", [Ld, D])
    w_ada = din("w_ada", [Ld, D, 6 * D]); b_ada = din("b_ada", [Ld, 6 * D])
    w_in = din("w_in", [Ld, D, D_IN])
    q_a_norm_w = din("q_a_norm_w", [Ld, 256]); w_q_up = din("w_q_up", [Ld, 256, 768])
    kv_a_norm_w = din("kv_a_norm_w", [Ld, 128]); w_kv_up = din("w_kv_up", [Ld, 128, 1024])
    q_nope_norm_w = din("q_nope_norm_w", [Ld, 64]); q_pe_norm_w = din("q_pe_norm_w", [Ld, 32])
    k_nope_norm_w = din("k_nope_norm_w", [Ld, 64]); k_pe_norm_w = din("k_pe_norm_w", [Ld, 32])
    conv_w = din("conv_w", [Ld, 4, 1024]); conv_b = din("conv_b", [Ld, 1024])
    dt_bias = din("dt_bias", [Ld, 8]); a_log = din("a_log", [Ld, 8]); d_skip = din("d_skip", [Ld, 8])
    ssd_norm_w = din("ssd_norm_w", [Ld, 512])
    w_out = din("w_out", [Ld, D, D]); w_gate_up = din("w_gate_up", [Ld, D, 2 * D_FF])
    w_down = din("w_down", [Ld, D_FF, D])
    out = nc.dram_tensor("out", [S, D], F32, kind="ExternalOutput").ap()

    modrow_d_L = [dscr("modrow_d%d" % i, [6, D], F32) for i in range(DEPTH)]
    winb_d_L = [dscr("winb_d%d" % i, [D, D_IN], BF16) for i in range(DEPTH)]
    wqb_d_L = [dscr("wqb_d%d" % i, [256, 768], BF16) for i in range(DEPTH)]
    wkvb_d_L = [dscr("wkvb_d%d" % i, [128, 1024], BF16) for i in range(DEPTH)]
    woutb_d_L = [dscr("woutb_d%d" % i, [D, D], BF16) for i in range(DEPTH)]
    wgub_d_L = [dscr("wgub_d%d" % i, [D, 2 * D_FF], BF16) for i in range(DEPTH)]
    wdb_d_L = [dscr("wdb_d%d" % i, [D_FF, D], BF16) for i in range(DEPTH)]
    QT_d = dscr("QT_d", [NH, 96, S], BF16)
    KT_d = dscr("KT_d", [NH, 96, S], BF16)
    V_d = dscr("V_d", [S, NH * 128], BF16)
    z_d = dscr("z_d", [S, 512], BF16)
    xs_d = dscr("xs_d", [S, 512], BF16)
    B_d = dscr("B_d", [S, 256], BF16)
    BT_d = dscr("BT_d", [2, 128, S], BF16)
    CT_d = dscr("CT_d", [2, 128, S], BF16)
    mixT_d = dscr("mixT_d", [D, S], BF16)
    cs_d = dscr("cs_d", [2, 128, NT * 16], F32)
    cwcb_d_L = [dscr("cwcb_d%d" % i, [128, 40], F32) for i in range(DEPTH)]
    xa_d = dscr("xa_d", [S, D], F32)
    xb_d = dscr("xb_d", [S, D], F32)

    with contextlib.ExitStack() as top:
        P = Prog(nc, top)

        gcnt = [0]

        def mk_sb(st):
            def sb(shape, dt=F32, name=None):
                gcnt[0] += 1
                return st.enter_context(nc.sbuf_tensor(name or ("t%d" % gcnt[0]), list(shape), dt))
            return sb

        def E(eng, obj, fname, *a, r=(), w=(), **kw):
            return P.op(eng, partial(getattr(obj, fname), *a, **kw), r, w)

        def dve(fname, *a, r=(), w=(), **kw):
            return E("dve", nc.vector, fname, *a, r=r, w=w, **kw)

        def pool(fname, *a, r=(), w=(), **kw):
            return E("pool", nc.gpsimd, fname, *a, r=r, w=w, **kw)

        def act(fname, *a, r=(), w=(), **kw):
            return E("act", nc.scalar, fname, *a, r=r, w=w, **kw)

        def pe(fname, *a, r=(), w=(), **kw):
            return E("pe", nc.tensor, fname, *a, r=r, w=w, **kw)

        psb = mk_sb(top)
        psum = top.enter_context(nc.psum_tensor("psum", [128, 8, 512], F32))
        ident = psb([128, 128], BF16, "ident")
        ones_f = psb([128, 128], F32, "ones_f")
        ltri = psb([128, 128], F32, "ltri")
        ustr = psb([128, 128], F32, "ustr")
        ltri_bf = psb([128, 128], BF16, "ltri_bf")
        cst = psb([128, 4], F32, "cst")
        dt_all = psb([128, NT, 8], F32, "dt_all")

        bank_rr = [0]

        def nbank():
            b = bank_rr[0] % 4
            bank_rr[0] += 1
            return b

        pair_rr = [0]

        def npair():
            b = 4 + 2 * (pair_rr[0] % 2)
            pair_rr[0] += 1
            return b

        def PS(b, n=512):
            return psum[:, b, 0:n]

        def PS2(b, n=1024):
            return psum[:, b:b + 2, :].rearrange("p a b -> p (a b)")[:, 0:n]

        def PS16(b):
            return psum[:, b, :].bitcast(BF16)

        def pk(b):
            return ("ps", b)

        def cols_from_row(dst, row, n, rkey, wkey):
            b = nbank()
            for i in range(n):
                pe("matmul", psum[:, b, i:i + 1], row[0:1, i * 128:(i + 1) * 128], ones_f[0:1, 0:1], start=True, stop=True,
                   r=[rkey, "ones_f"], w=[pk(b)])
            dve("tensor_copy", dst, psum[:, b, 0:n], r=[pk(b)], w=[wkey])

        def phase_const():
            with contextlib.ExitStack() as st:
                sb = mk_sb(st)
                pool("memset", ones_f[:], 1.0, w=["ones_f"])
                pool("memset", cst[:, 0:1], EPS, w=["cst"])
                pool("memset", cst[:, 1:2], 1.0, w=["cst"])
                pool("affine_select", ltri[:], ones_f[:], pattern=[[1, 128]], compare_op=ALU.is_ge,
                     fill=0.0, base=0, channel_multiplier=-1, r=["ones_f"], w=["ltri"])
                pool("affine_select", ustr[:], ones_f[:], pattern=[[-1, 128]], compare_op=ALU.is_gt,
                     fill=0.0, base=0, channel_multiplier=1, r=["ones_f"], w=["ustr"])
                pool("affine_select", ident[:], ones_f[:], pattern=[[1, 128]], compare_op=ALU.is_equal,
                     fill=0.0, base=0, channel_multiplier=-1, r=["ones_f"], w=["ident"])
                dve("tensor_copy", ltri_bf[:], ltri[:], r=["ltri"], w=["ltri_bf"])
                posf = sb([128, NT], F32)
                invf = sb([128, 16], F32)
                ang = sb([128, NT, 16], F32)
                kq = sb([128, NT, 16], F32)
                ki = sb([128, NT, 16], I32)
                m1 = sb([128, NT, 16], F32)
                rc = sb([128, NT, 16], F32)
                cosT = sb([128, NT, 16], F32)
                sinT = sb([128, NT, 16], F32)
                prow_i = sb([1, S], I32)
                prow_f = sb([1, S], F32)
                P.dma(prow_i[:], pos_in, writes=["prow_i"])
                dve("tensor_copy", prow_f[:], prow_i[:], r=["prow_i"], w=["prow_f"])
                cols_from_row(posf[:], prow_f, NT, "prow_f", "posf")
                inv = (1.0 / (np.float32(10000.0) ** (np.arange(0, 32, 2, dtype=np.float32) / np.float32(32)))).astype(np.float32)
                for j in range(16):
                    pool("memset", invf[:, j:j + 1], float(inv[j]), w=["invf"])
                dve("tensor_tensor", ang[:], posf[:].unsqueeze(2).to_broadcast([128, NT, 16]),
                    invf[:].unsqueeze(1).to_broadcast([128, NT, 16]), ALU.mult, r=["posf", "invf"], w=["ang"])
                TWO_PI = 2.0 * math.pi
                C1 = 6.28125
                C2 = TWO_PI - C1
                PI_LO = 3.1415925

                def reduce_to_pi(src, skey, dst, dkey):
                    dve("tensor_scalar", kq[:], src[:], 1.0 / TWO_PI, None, ALU.mult, r=[skey], w=["kq"])
                    dve("tensor_copy", ki[:], kq[:], r=["kq"], w=["ki"])
                    dve("tensor_copy", kq[:], ki[:], r=["ki"], w=["kq"])
                    dve("scalar_tensor_tensor", dst[:], kq[:], -C1, src[:], ALU.mult, ALU.add, r=["kq", skey], w=[dkey])
                    dve("scalar_tensor_tensor", dst[:], kq[:], -C2, dst[:], ALU.mult, ALU.add, r=["kq", dkey], w=[dkey])
                    dve("tensor_scalar", m1[:], dst[:], math.pi, None, ALU.is_gt, r=[dkey], w=["m1"])
                    dve("scalar_tensor_tensor", dst[:], m1[:], -TWO_PI, dst[:], ALU.mult, ALU.add, r=["m1", dkey], w=[dkey])
                    dve("tensor_scalar", m1[:], dst[:], -math.pi, None, ALU.is_lt, r=[dkey], w=["m1"])
                    dve("scalar_tensor_tensor", dst[:], m1[:], TWO_PI, dst[:], ALU.mult, ALU.add, r=["m1", dkey], w=[dkey])
                    dve("tensor_scalar", dst[:], dst[:], PI_LO, -PI_LO, ALU.min, ALU.max, r=[dkey], w=[dkey])

                reduce_to_pi(ang, "ang", rc, "rc")
                act("activation", sinT[:], rc[:], AF.Sin, r=["rc"], w=["sinT"])
                dve("tensor_scalar", ang[:], rc[:], math.pi / 2, None, ALU.add, r=["rc"], w=["ang"])
                reduce_to_pi(ang, "ang", rc, "rc")
                act("activation", cosT[:], rc[:], AF.Sin, r=["rc"], w=["cosT"])
                P.dma(cs_d[0], cosT[:].rearrange("p t j -> p (t j)"), reads=["cosT"])
                P.dma(cs_d[1], sinT[:].rearrange("p t j -> p (t j)"), reads=["sinT"])
                P.flush()

        def mod_gen(l, T):
            cT, wst, mrow, brow, nrow, orow, crow = T
            P.dma(crow[:], c_in, writes=["crow"])
            act("activation", crow[:], crow[:], AF.Silu, r=["crow"], w=["crow"])
            cols_from_row(cT[:], crow, 8, "crow", "cT")
            P.dma(brow[:], b_ada[l:l + 1, :], writes=["brow"])
            P.dma(nrow[:, 0:D], norm1_w[l:l + 1, :], writes=["nrow"])
            P.dma(nrow[:, D:2 * D], norm2_w[l:l + 1, :], writes=["nrow"])
            for n in range(12):
                ws = wst[n % 2]
                P.dma(ws[:], w_ada[l, :, n * 512:(n + 1) * 512].rearrange("(k p) c -> p k c", p=128),
                      writes=[("wst", n % 2)])
                b = nbank()
                for k in range(8):
                    pe("matmul", psum[0:1, b, :], cT[:, k:k + 1], ws[:, k, :], start=(k == 0), stop=(k == 7),
                       r=["cT", ("wst", n % 2)], w=[pk(b)])
                dve("tensor_tensor", mrow[:, n * 512:(n + 1) * 512], psum[0:1, b, :], brow[:, n * 512:(n + 1) * 512],
                    ALU.add, r=[pk(b), "brow"], w=["mrow"])
                yield
            for half in range(2):
                o = half * 3 * D
                dve("scalar_tensor_tensor", orow[:, o:o + D], mrow[:, o + D:o + 2 * D], 1.0, nrow[:, half * D:(half + 1) * D],
                    ALU.add, ALU.mult, r=["mrow", "nrow"], w=["orow"])
                dve("tensor_copy", orow[:, o + D:o + 2 * D], mrow[:, o:o + D], r=["mrow"], w=["orow"])
                dve("tensor_copy", orow[:, o + 2 * D:o + 3 * D], mrow[:, o + 2 * D:o + 3 * D], r=["mrow"], w=["orow"])
            P.dma(modrow_d_L[l].rearrange("(o a) b -> o (a b)", o=1), orow[:], reads=["orow"], writes=[("modrow", l)], eng="act")

        def phase_start():
            with contextlib.ExitStack() as st:
                sb = mk_sb(st)
                T = (sb([128, 8], F32), [sb([128, 8, 512], F32) for _ in range(2)], sb([1, 6 * D], F32), sb([1, 6 * D], F32),
                     sb([1, 2 * D], F32), sb([1, 6 * D], F32), sb([1, D], F32))
                cwrow = sb([1, 4 * 1024], F32); cbrow = sb([1, 1024], F32); cwcb = sb([128, 40], F32)
                for _ in mod_gen(0, T):
                    pass
                for ll in range(nlayers):
                    P.dma(cwrow[:], conv_w[ll:ll + 1].rearrange("o k c -> o (k c)"), writes=["cwrow"])
                    P.dma(cbrow[:], conv_b[ll:ll + 1, :], writes=["cbrow"])
                    cols_from_row(cwcb[:, 0:32], cwrow, 32, "cwrow", "cwcb")
                    cols_from_row(cwcb[:, 32:40], cbrow, 8, "cbrow", "cwcb")
                    P.dma(cwcb_d_L[ll], cwcb[:], reads=["cwcb"])
                wg = wprep_gen(0, sb, 2048, "act", which="a")
                for l in range(1, nlayers):
                    for _ in mod_gen(l, T):
                        for _k in range(5):
                            next(wg, None)
                for _ in wg:
                    pass
                P.flush()


        def wprep_gen(l, sb, CW, store_eng, which="all", plain_eng="pool"):
            sin_ = [sb([128, CW], F32) for _ in range(3)]
            sout = [sb([128, CW], BF16) for _ in range(3)]
            g1b = sb([128, D], F32); g2b = sb([128, D], F32)
            P.dma(g1b[:], modrow_d_L[l][2:3, :].partition_broadcast(128), reads=[("modrow", l)], writes=["g1b"])
            P.dma(g2b[:], modrow_d_L[l][5:6, :].partition_broadcast(128), reads=[("modrow", l)], writes=["g2b"])
            jobs = []

            def add(src, dst, R, C, gate=None):
                for r0 in range(0, R, 128):
                    for c0 in range(0, C, CW):
                        c1 = min(C, c0 + CW)
                        jobs.append((src[r0:r0 + 128, c0:c1], dst[r0:r0 + 128, c0:c1], c1 - c0, c0, gate))
            wi = w_in[l]
            if which in ("all", "a"):
                add(wi[:, 0:416], winb_d_L[l][:, 0:416], D, 416)
                add(wi[:, 1952:1960], winb_d_L[l][:, 416:424], D, 8)
                add(wi[:, 416:1952], winb_d_L[l][:, 424:1960], D, 1536)
                add(w_q_up[l], wqb_d_L[l], 256, 768)
                add(w_kv_up[l], wkvb_d_L[l], 128, 1024)
            if which in ("all", "b"):
                add(w_out[l], woutb_d_L[l], D, D, gate=(g1b, "g1b"))
                add(w_gate_up[l], wgub_d_L[l], D, 2 * D_FF)
                add(w_down[l], wdb_d_L[l], D_FF, D, gate=(g2b, "g2b"))
            def load(i):
                if i < len(jobs):
                    P.dma(sin_[i % 3][:, 0:jobs[i][2]], jobs[i][0], writes=[("sin", i % 3)])
            load(0)
            load(1)
            for i, (src, dst, cw, c0, gate) in enumerate(jobs):
                k = i % 3
                load(i + 2)
                if gate is not None:
                    gt, gk = gate
                    eng = pool if store_eng == "sp" else (dve if i % 2 == 0 else pool)
                    eng("tensor_tensor", sout[k][:, 0:cw], sin_[k][:, 0:cw], gt[:, c0:c0 + cw], ALU.mult,
                        r=[("sin", k), gk], w=[("sout", k)])
                elif store_eng == "sp" and plain_eng == "act":
                    act("copy", sout[k][:, 0:cw], sin_[k][:, 0:cw], r=[("sin", k)], w=[("sout", k)])
                elif store_eng == "sp":
                    pool("tensor_copy", sout[k][:, 0:cw], sin_[k][:, 0:cw], r=[("sin", k)], w=[("sout", k)])
                elif k == 1:
                    act("copy", sout[k][:, 0:cw], sin_[k][:, 0:cw], r=[("sin", k)], w=[("sout", k)])
                elif k == 0:
                    dve("tensor_copy", sout[k][:, 0:cw], sin_[k][:, 0:cw], r=[("sin", k)], w=[("sout", k)])
                else:
                    pool("tensor_copy", sout[k][:, 0:cw], sin_[k][:, 0:cw], r=[("sin", k)], w=[("sout", k)])
                P.dma(dst, sout[k][:, 0:cw], reads=[("sout", k)], eng=store_eng)
                yield

        def phase_wprep(l):
            with contextlib.ExitStack() as st:
                sb = mk_sb(st)
                cwrow = sb([1, 4 * 1024], F32); cbrow = sb([1, 1024], F32); cwcb = sb([128, 40], F32)
                for ll in range(nlayers):
                    P.dma(cwrow[:], conv_w[ll:ll + 1].rearrange("o k c -> o (k c)"), writes=["cwrow"])
                    P.dma(cbrow[:], conv_b[ll:ll + 1, :], writes=["cbrow"])
                    cols_from_row(cwcb[:, 0:32], cwrow, 32, "cwrow", "cwcb")
                    cols_from_row(cwcb[:, 32:40], cbrow, 8, "cbrow", "cwcb")
                    P.dma(cwcb_d_L[ll], cwcb[:], reads=["cwcb"])
                for _ in wprep_gen(l, sb, 2048, "act"):
                    pass
                P.flush()

        def phase_A(l, xsrc):
            with contextlib.ExitStack() as st:
                sb = mk_sb(st)
                win = sb([128, 8, D_IN], BF16)
                wq = sb([128, 2, 768], BF16)
                wkv = sb([128, 1024], BF16)
                weff = sb([128, D], F32); shb = sb([128, D], F32)
                qaw = sb([128, 256], F32); kvaw = sb([128, 128], F32); kpw = sb([128, 32], F32)
                qnw = sb([128, 64], F32); qpw = sb([128, 32], F32); knw = sb([128, 64], F32)
                cw = sb([128, 4, 8], F32); cb = sb([128, 8], F32); dtb = sb([128, 8], F32)
                invn = sb([128, 27], F32)
                cosT = sb([128, NT, 16], F32)
                sinT = sb([128, NT, 16], F32)
                P.dma(cosT[:].rearrange("p t j -> p (t j)"), cs_d[0], writes=["cosT"])
                P.dma(sinT[:].rearrange("p t j -> p (t j)"), cs_d[1], writes=["sinT"])
                P.dma(win[:], winb_d_L[l].rearrange("(k p) c -> p k c", p=128), writes=["win"])
                P.dma(wq[:], wqb_d_L[l].rearrange("(k p) c -> p k c", p=128), writes=["wq"])
                P.dma(wkv[:], wkvb_d_L[l], writes=["wkv"])
                P.dma(weff[:], modrow_d_L[l][0:1, :].partition_broadcast(128), writes=["weff"])
                P.dma(shb[:], modrow_d_L[l][1:2, :].partition_broadcast(128), writes=["shb"])
                P.dma(qaw[:], q_a_norm_w[l:l + 1, :].partition_broadcast(128), writes=["qaw"])
                P.dma(kvaw[:], kv_a_norm_w[l:l + 1, :].partition_broadcast(128), writes=["kvaw"])
                P.dma(kpw[:], k_pe_norm_w[l:l + 1, :].partition_broadcast(128), writes=["kpw"])
                P.dma(qnw[:], q_nope_norm_w[l:l + 1, :].partition_broadcast(128), writes=["qnw"])
                P.dma(qpw[:], q_pe_norm_w[l:l + 1, :].partition_broadcast(128), writes=["qpw"])
                P.dma(knw[:], k_nope_norm_w[l:l + 1, :].partition_broadcast(128), writes=["knw"])
                P.dma(dtb[:], dt_bias[l:l + 1, :].partition_broadcast(128), writes=["dtb"])
                P.dma(cw[:].rearrange("p k m -> p (k m)"), cwcb_d_L[l][:, 0:32], writes=["cw"])
                P.dma(cb[:], cwcb_d_L[l][:, 32:40], writes=["cb"])
                scale = 96.0 ** -0.5
                dve("tensor_scalar", qnw[:], qnw[:], scale, None, ALU.mult, r=["qnw"], w=["qnw"])
                dve("tensor_scalar", qpw[:], qpw[:], scale, None, ALU.mult, r=["qpw"], w=["qpw"])
                pool("memset", invn[:, 0:1], 1.0 / 256, w=["invn"])
                pool("memset", invn[:, 1:2], 1.0 / 128, w=["invn"])
                pool("memset", invn[:, 2:3], 1.0 / 32, w=["invn"])
                pool("memset", invn[:, 3:11], 1.0 / 64, w=["invn"])
                pool("memset", invn[:, 11:19], 1.0 / 32, w=["invn"])
                pool("memset", invn[:, 19:27], 1.0 / 64, w=["invn"])

                xt = [sb([128, D], F32) for _ in range(4)]
                junk = sb([128, D], BF16)
                ssx = sb([128, 4], F32); rsx = sb([128, 4], F32)
                htmp_2 = [sb([128, D], F32) for _ in range(2)]
                h = [sb([128, D], BF16) for _ in range(2)]
                hT = sb([128, 8, 512], BF16)
                xbcT = [sb([128, 516], BF16) for _ in range(8)]
                dg = sb([128, 8, 4, 128], BF16)
                xsT = sb([128, 8, 512], BF16)
                raw1 = [sb([128, 424], F32) for _ in range(4)]
                zt = [sb([128, 512], BF16) for _ in range(2)]
                dtr = sb([128, 4, 8], F32)
                ss1 = sb([128, 4, 3], F32); rs1 = sb([128, 4, 3], F32)
                latn_2 = [sb([128, 384], BF16) for _ in range(2)]
                latT_2 = [sb([128, 3, 128], BF16) for _ in range(2)]
                qsb = [sb([128, 768], F32) for _ in range(4)]
                kvsb = [sb([128, 8, 64], F32) for _ in range(4)]
                sq_2 = [sb([128, 768], F32) for _ in range(2)]
                ss2 = sb([128, 4, 24], F32); rs2 = sb([128, 4, 24], F32)
                kpn = [sb([128, 32], F32) for _ in range(4)]
                Qf_2 = [sb([128, 8, 96], BF16) for _ in range(2)]; Kf_2 = [sb([128, 8, 96], BF16) for _ in range(2)]
                ra_2 = [sb([128, 8, 32], F32) for _ in range(2)]; rb_2 = [sb([128, 8, 32], F32) for _ in range(2)]
                rq_2 = [sb([128, 8, 32], F32) for _ in range(2)]
                ka_2 = [sb([128, 32], F32) for _ in range(2)]; kb_2 = [sb([128, 32], F32) for _ in range(2)]
                kr_2 = [sb([128, 32], F32) for _ in range(2)]
                vaug = [sb([128, 8, 128], BF16) for _ in range(2)]
                QTb = sb([96, 8, 512], BF16); KTb = sb([96, 8, 512], BF16)
                tok_o = [sb([128, 768], BF16) for _ in range(2)]

                for i in range(2):
                    pool("memset", vaug[i][:], 1.0, w=[("vaug", i)])
                for m in range(8):
                    pool("memset", xbcT[m][:, 512:515], 0.0, w=[("xbcT", m)])
                for m in range(8):
                    for k in range(4):
                        dve("tensor_scalar", dg[:, m, k, :], ident[:], cw[:, k, m:m + 1], None, ALU.mult, r=["ident", "cw"], w=["dg"])

                def n_stats(blk, j):
                    t = blk * 4 + j
                    P.dma(xt[j][:], xsrc[t * 128:(t + 1) * 128, :], writes=[("xt", j)])
                    act("activation", junk[:], xt[j][:], AF.Square, accum_out=ssx[:, j:j + 1],
                        r=[("xt", j)], w=["junk", "ssx"])

                def n_rstd():
                    act("activation", rsx[:], ssx[:], AF.Ln, scale=1.0 / D, bias=cst[:, 0:1], r=["ssx", "cst"], w=["rsx"])
                    act("activation", rsx[:], rsx[:], AF.Exp, scale=-0.5, r=["rsx"], w=["rsx"])

                def n_hT(j, bank=None):
                    hh = h[j % 2]; htmp = htmp_2[j % 2]
                    dve("scalar_tensor_tensor", htmp[:], xt[j][:], rsx[:, j:j + 1], weff[:], ALU.mult, ALU.mult,
                        r=[("xt", j), "rsx", "weff"], w=[("htmp", j % 2)])
                    pool("tensor_tensor", hh[:], htmp[:], shb[:], ALU.add, r=[("htmp", j % 2), "shb"], w=[("h", j % 2)])
                    b = nbank() if bank is None else bank
                    for k in range(8):
                        pe("transpose", PS16(b)[:, k * 128:(k + 1) * 128], hh[:, k * 128:(k + 1) * 128], ident[:],
                           r=[("h", j % 2), "ident"], w=[pk(b)])
                    act("copy", hT[:, :, j * 128:(j + 1) * 128], PS16(b).rearrange("p (k t) -> p k t", k=8),
                        r=[pk(b)], w=["hT"])

                for j in range(4):
                    n_stats(0, j)
                n_rstd()
                for j in range(4):
                    n_hT(j)

                def P_gen(blk):
                    pb = blk % 2
                    for m in range(9):
                        if m == 5:
                            yield
                        if m < 8:
                            b = nbank()
                            c0 = 936 + m * 128
                            for k in range(8):
                                pe("matmul", PS(b), win[:, k, c0:c0 + 128], hT[:, k, :], start=(k == 0), stop=(k == 7),
                                   r=["win", "hT"], w=[pk(b)])
                            cur = xbcT[m]
                            act("copy", cur[:, 0:3], cur[:, 512:515], r=[("xbcT", m)], w=[("xbcT", m)])
                            act("copy", cur[:, 3:515], PS(b), r=[pk(b), ("xbcT", m)], w=[("xbcT", m)])
                        if m >= 1:
                            mm = m - 1
                            b2 = nbank()
                            for kk in range(4):
                                pe("matmul", PS(b2), dg[:, mm, kk, :], xbcT[mm][:, kk:kk + 512], start=(kk == 0), stop=(kk == 3),
                                   r=["dg", ("xbcT", mm)], w=[pk(b2)])
                            act("activation", xsT[:, mm, :], PS(b2), AF.Silu, bias=cb[:, mm:mm + 1], r=[pk(b2), "cb"], w=[("xsT", mm)])
                    yield
                    for g in range(2):
                        P.dma(BT_d[g, :, blk * 512:(blk + 1) * 512], xsT[:, 4 + g, :], reads=[("xsT", 4 + g)], eng="act")
                        P.dma(CT_d[g, :, blk * 512:(blk + 1) * 512], xsT[:, 6 + g, :], reads=[("xsT", 6 + g)], eng="act")
                    for j in range(4):
                        t = blk * 4 + j
                        b = nbank()
                        for m in range(6):
                            pe("transpose", PS16(b)[:, m * 128:(m + 1) * 128], xsT[:, m, j * 128:(j + 1) * 128], ident[:],
                               r=[("xsT", m), "ident"], w=[pk(b)])
                        to = tok_o[j % 2]
                        dve("tensor_copy", to[:], PS16(b)[:, 0:768], r=[pk(b)], w=[("tok_o", j % 2)])
                        P.dma(xs_d[t * 128:(t + 1) * 128, :], to[:, 0:512], reads=[("tok_o", j % 2)], eng="act")
                        P.dma(B_d[t * 128:(t + 1) * 128, :], to[:, 512:768], reads=[("tok_o", j % 2)], eng="act")
                    yield
                    for j in range(4):
                        t = blk * 4 + j
                        b1 = nbank()
                        for k in range(8):
                            pe("matmul", PS(b1, 424), hT[:, k, j * 128:(j + 1) * 128], win[:, k, 0:424], start=(k == 0), stop=(k == 7),
                               r=["hT", "win"], w=[pk(b1)])
                        b2 = nbank()
                        for k in range(8):
                            pe("matmul", PS(b2), hT[:, k, j * 128:(j + 1) * 128], win[:, k, 424:936], start=(k == 0), stop=(k == 7),
                               r=["hT", "win"], w=[pk(b2)])
                        dve("tensor_copy", raw1[j][:], PS(b1, 424), r=[pk(b1)], w=[("raw1", j)])
                        act("copy", zt[j % 2][:], PS(b2), r=[pk(b2)], w=[("zt", j % 2)])
                        P.dma(z_d[t * 128:(t + 1) * 128, :], zt[j % 2][:], reads=[("zt", j % 2)], eng="act")
                        act("activation", junk[:, 0:256], raw1[j][:, 0:256], AF.Square, accum_out=ss1[:, j, 0:1],
                            r=[("raw1", j)], w=["junk", "ss1"])
                        act("activation", junk[:, 0:128], raw1[j][:, 256:384], AF.Square, accum_out=ss1[:, j, 1:2],
                            r=[("raw1", j)], w=["junk", "ss1"])
                        act("activation", junk[:, 0:32], raw1[j][:, 384:416], AF.Square, accum_out=ss1[:, j, 2:3],
                            r=[("raw1", j)], w=["junk", "ss1"])
                        dve("tensor_tensor", dtr[:, j, :], raw1[j][:, 416:424], dtb[:], ALU.add, r=[("raw1", j), "dtb"], w=["dtr"])
                    yield
                    dve("tensor_tensor", ss1[:], ss1[:], invn[:, 0:3].unsqueeze(1).to_broadcast([128, 4, 3]), ALU.mult,
                        r=["ss1", "invn"], w=["ss1"])
                    act("activation", dtr[:], dtr[:], AF.Exp, r=["dtr"], w=["dtr"])
                    act("activation", dt_all[:, blk * 4:(blk + 1) * 4, :], dtr[:], AF.Ln, bias=cst[:, 1:2], r=["dtr", "cst"], w=["dt_all"])
                    act("activation", rs1[:], ss1[:], AF.Ln, bias=cst[:, 0:1], r=["ss1", "cst"], w=["rs1"])
                    act("activation", rs1[:], rs1[:], AF.Exp, scale=-0.5, r=["rs1"], w=["rs1"])

                for _ in P_gen(0):
                    pass
                for blk in range(NB):
                    pb = blk % 2
                    nxt = blk + 1 < NB
                    def La(j):
                        latn = latn_2[j % 2]; latT = latT_2[j % 2]; sq = sq_2[j % 2]
                        dve("scalar_tensor_tensor", latn[:, 0:256], raw1[j][:, 0:256], rs1[:, j, 0:1], qaw[:], ALU.mult, ALU.mult,
                            r=[("raw1", j), "rs1", "qaw"], w=[("latn", j % 2)])
                        dve("scalar_tensor_tensor", latn[:, 256:384], raw1[j][:, 256:384], rs1[:, j, 1:2], kvaw[:], ALU.mult, ALU.mult,
                            r=[("raw1", j), "rs1", "kvaw"], w=[("latn", j % 2)])
                        dve("scalar_tensor_tensor", kpn[j][:], raw1[j][:, 384:416], rs1[:, j, 2:3], kpw[:], ALU.mult, ALU.mult,
                             r=[("raw1", j), "rs1", "kpw"], w=[("kpn", j)])
                        b = 2 if j % 2 == 0 else 6
                        for k in range(3):
                            pe("transpose", PS16(b)[:, k * 128:(k + 1) * 128], latn[:, k * 128:(k + 1) * 128], ident[:],
                               r=[("latn", j % 2), "ident"], w=[pk(b)])
                        act("copy", latT[:], PS16(b)[:, 0:384].rearrange("p (k t) -> p k t", k=3), r=[pk(b)], w=[("latT", j % 2)])
                        bq = 0 if j % 2 == 0 else 4
                        for (n0, n1, bb) in ((0, 512, bq), (512, 768, bq + 1)):
                            for k in range(2):
                                pe("matmul", PS(bb, n1 - n0), latT[:, k, :], wq[:, k, n0:n1], start=(k == 0), stop=(k == 1),
                                   r=[("latT", j % 2), "wq"], w=[pk(bb)])
                        bk = 2 if j % 2 == 0 else 6
                        for hf in range(2):
                            pe("matmul", PS(bk + hf), latT[:, 2, :], wkv[:, hf * 512:(hf + 1) * 512], start=True, stop=True,
                               r=[("latT", j % 2), "wkv"], w=[pk(bk + hf)])
                        return bq, bk

                    def Lb(j, bq, bk):
                        sq = sq_2[j % 2]
                        act("copy", qsb[j][:], PS2(bq, 768), r=[pk(bq), pk(bq + 1)], w=[("qsb", j)])
                        kvv = PS2(bk).rearrange("p (h e) -> p h e", h=8)
                        dve("tensor_copy", kvsb[j][:], kvv[:, :, 0:64], r=[pk(bk), pk(bk + 1)], w=[("kvsb", j)])
                        va = vaug[j % 2]
                        vap = va[:].rearrange("p (c two) e -> p c two e", two=2)
                        for hf in range(2):
                            kvp = PS(bk + hf).rearrange("p (c two e) -> p c two e", two=2, e=128)
                            dve("tensor_copy", vap[:, 2 * hf:2 * hf + 2, 0, 0:64], kvp[:, :, 0, 64:128], r=[pk(bk + hf)], w=[("vaug", j % 2)])
                            dve("tensor_copy", vap[:, 2 * hf:2 * hf + 2, 1, 64:128], kvp[:, :, 1, 64:128], r=[pk(bk + hf)], w=[("vaug", j % 2)])
                        t = blk * 4 + j
                        if True:
                            P.dma(V_d[t * 128:(t + 1) * 128, :], va[:].rearrange("p h e -> p (h e)"), reads=[("vaug", j % 2)], eng="act")
                        act("activation", sq[:], qsb[j][:], AF.Square, r=[("qsb", j)], w=[("sq", j % 2)])
                        sqv = sq[:].rearrange("p (h e) -> p h e", h=8)
                        dve("tensor_reduce", ss2[:, j, 0:8], sqv[:, :, 0:64], AX.X, ALU.add, r=[("sq", j % 2)], w=["ss2"])
                        dve("tensor_reduce", ss2[:, j, 8:16], sqv[:, :, 64:96], AX.X, ALU.add, r=[("sq", j % 2)], w=["ss2"])
                        act("activation", sq[:, 0:512], kvsb[j][:].rearrange("p h e -> p (h e)"), AF.Square, r=[("kvsb", j)], w=[("sq", j % 2)])
                        dve("tensor_reduce", ss2[:, j, 16:24], sq[:, 0:512].rearrange("p (h e) -> p h e", h=8), AX.X, ALU.add,
                            r=[("sq", j % 2)], w=["ss2"])

                    if nxt:
                        for j in range(4):
                            n_stats(blk + 1, j)
                    pend = {}
                    for j in range(5):
                        if j < 4:
                            pend[j] = La(j)
                        if j == 1 and nxt:
                            n_rstd()
                        if j >= 1:
                            Lb(j - 1, *pend[j - 1])
                            if nxt:
                                n_hT(j - 1, bank=(0 if (j - 1) % 2 == 0 else 4))
                    pg = P_gen(blk + 1) if nxt else None
                    dve("tensor_tensor", ss2[:], ss2[:], invn[:, 3:27].unsqueeze(1).to_broadcast([128, 4, 24]), ALU.mult,
                        r=["ss2", "invn"], w=["ss2"])
                    act("activation", rs2[:], ss2[:], AF.Ln, bias=cst[:, 0:1], r=["ss2", "cst"], w=["rs2"])
                    act("activation", rs2[:], rs2[:], AF.Exp, scale=-0.5, r=["rs2"], w=["rs2"])
                    for j in range(4):
                        t = blk * 4 + j
                        sq = sq_2[j % 2]; Qf = Qf_2[j % 2]; Kf = Kf_2[j % 2]; ra = ra_2[j % 2]; rb = rb_2[j % 2]; rq = rq_2[j % 2]
                        ka = ka_2[j % 2]; kb = kb_2[j % 2]; kr = kr_2[j % 2]
                        qv = qsb[j][:].rearrange("p (h e) -> p h e", h=8)
                        cosb = cosT[:, t, :]; sinb = sinT[:, t, :]
                        dve("tensor_tensor", sq[:, 0:512].rearrange("p (h e) -> p h e", h=8), qv[:, :, 0:64],
                            rs2[:, j, 0:8].unsqueeze(2).to_broadcast([128, 8, 64]), ALU.mult, r=[("qsb", j), "rs2"], w=[("sq", j % 2)])
                        dve("tensor_tensor", Qf[:, :, 0:64], sq[:, 0:512].rearrange("p (h e) -> p h e", h=8),
                            qnw[:].unsqueeze(1).to_broadcast([128, 8, 64]), ALU.mult, r=[("sq", j % 2), "qnw"], w=[("Qf", j % 2)])
                        pool("tensor_tensor", ra[:], qv[:, :, 64:96], rs2[:, j, 8:16].unsqueeze(2).to_broadcast([128, 8, 32]), ALU.mult,
                             r=[("qsb", j), "rs2"], w=[("ra", j % 2)])
                        pool("tensor_tensor", ra[:], ra[:], qpw[:].unsqueeze(1).to_broadcast([128, 8, 32]), ALU.mult, r=[("ra", j % 2), "qpw"], w=[("ra", j % 2)])
                        cb8 = cosb.unsqueeze(1).to_broadcast([128, 8, 16]); sb8 = sinb.unsqueeze(1).to_broadcast([128, 8, 16])
                        pool("tensor_tensor", rb[:, :, 0:16], ra[:, :, 0:16], cb8, ALU.mult, r=[("ra", j % 2), "cosT"], w=[("rb", j % 2)])
                        pool("tensor_tensor", rb[:, :, 16:32], ra[:, :, 16:32], cb8, ALU.mult, r=[("ra", j % 2), "cosT"], w=[("rb", j % 2)])
                        pool("tensor_tensor", rq[:, :, 0:16], ra[:, :, 16:32], sb8, ALU.mult, r=[("ra", j % 2), "sinT"], w=[("rq", j % 2)])
                        pool("tensor_tensor", rq[:, :, 16:32], ra[:, :, 0:16], sb8, ALU.mult, r=[("ra", j % 2), "sinT"], w=[("rq", j % 2)])
                        pool("tensor_tensor", Qf[:, :, 64:80], rb[:, :, 0:16], rq[:, :, 0:16], ALU.subtract, r=[("rb", j % 2), ("rq", j % 2)], w=[("Qf", j % 2)])
                        pool("tensor_tensor", Qf[:, :, 80:96], rb[:, :, 16:32], rq[:, :, 16:32], ALU.add, r=[("rb", j % 2), ("rq", j % 2)], w=[("Qf", j % 2)])
                        dve("tensor_tensor", sq[:, 0:512].rearrange("p (h e) -> p h e", h=8), kvsb[j][:],
                            rs2[:, j, 16:24].unsqueeze(2).to_broadcast([128, 8, 64]), ALU.mult, r=[("kvsb", j), "rs2"], w=[("sq", j % 2)])
                        dve("tensor_tensor", Kf[:, :, 0:64], sq[:, 0:512].rearrange("p (h e) -> p h e", h=8),
                            knw[:].unsqueeze(1).to_broadcast([128, 8, 64]), ALU.mult, r=[("sq", j % 2), "knw"], w=[("Kf", j % 2)])
                        dve("tensor_tensor", ka[:, 0:16], kpn[j][:, 0:16], cosb, ALU.mult, r=[("kpn", j), "cosT"], w=[("ka", j % 2)])
                        dve("tensor_tensor", ka[:, 16:32], kpn[j][:, 16:32], cosb, ALU.mult, r=[("kpn", j), "cosT"], w=[("ka", j % 2)])
                        dve("tensor_tensor", kb[:, 0:16], kpn[j][:, 16:32], sinb, ALU.mult, r=[("kpn", j), "sinT"], w=[("kb", j % 2)])
                        dve("tensor_tensor", kb[:, 16:32], kpn[j][:, 0:16], sinb, ALU.mult, r=[("kpn", j), "sinT"], w=[("kb", j % 2)])
                        dve("tensor_tensor", kr[:, 0:16], ka[:, 0:16], kb[:, 0:16], ALU.subtract, r=[("ka", j % 2), ("kb", j % 2)], w=[("kr", j % 2)])
                        dve("tensor_tensor", kr[:, 16:32], ka[:, 16:32], kb[:, 16:32], ALU.add, r=[("ka", j % 2), ("kb", j % 2)], w=[("kr", j % 2)])
                        dve("tensor_copy", Kf[:, :, 64:96], kr[:].unsqueeze(1).to_broadcast([128, 8, 32]), r=[("kr", j % 2)], w=[("Kf", j % 2)])
                        bqt = nbank()
                        for hh_ in range(8):
                            pe("transpose", PS16(bqt)[0:96, hh_ * 128:(hh_ + 1) * 128], Qf[:, hh_, :], ident[:],
                               r=[("Qf", j % 2), "ident"], w=[pk(bqt)])
                        act("copy", QTb[:, :, j * 128:(j + 1) * 128], PS16(bqt)[0:96, :].rearrange("p (h t) -> p h t", h=8),
                            r=[pk(bqt)], w=["QTb"])
                        bkt = nbank()
                        for hh_ in range(8):
                            pe("transpose", PS16(bkt)[0:96, hh_ * 128:(hh_ + 1) * 128], Kf[:, hh_, :], ident[:],
                               r=[("Kf", j % 2), "ident"], w=[pk(bkt)])
                        act("copy", KTb[:, :, j * 128:(j + 1) * 128], PS16(bkt)[0:96, :].rearrange("p (h t) -> p h t", h=8),
                            r=[pk(bkt)], w=["KTb"])
                        if pg is not None:
                            next(pg, None)
                            if j == 3:
                                for _ in pg:
                                    pass
                    P.dma(QT_d[:, :, blk * 512:(blk + 1) * 512].rearrange("h d s -> d h s"), QTb[:], reads=["QTb"], eng="act")
                    P.dma(KT_d[:, :, blk * 512:(blk + 1) * 512].rearrange("h d s -> d h s"), KTb[:], reads=["KTb"], eng="act")
                P.flush()

        def phase_B(l):
            with contextlib.ExitStack() as st:
                sb = mk_sb(st)
                KT = sb([96, 8, S], BF16)
                Vs = sb([128, NT, 8 * 128], BF16)
                QTs = [sb([96, 8, 512], BF16) for _ in range(2)]
                onesb = sb([128, 512], BF16)
                amask = [sb([128, 512], BF16) for _ in range(4)]
                pT = [sb([128, 2, 512], BF16) for _ in range(4)]
                rden = [sb([128, 512], F32) for _ in range(2)]
                aT = [sb([128, 4, 512], BF16) for _ in range(2)]
                pool("memset", onesb[:], 1.0, w=["onesb"])
                for j in range(4):
                    pool("affine_select", amask[j][:], onesb[:], pattern=[[1, 512]], compare_op=ALU.is_ge, fill=0.0,
                         base=-128 * j, channel_multiplier=-1, r=["onesb"], w=[("amask", j)])
                P.dma(KT[:], KT_d.rearrange("h d s -> d h s"), writes=["KT"])
                for t0 in range(0, NT, 8):
                    t1 = min(NT, t0 + 8)
                    P.dma(Vs[:, t0:t1, :], V_d[t0 * 128:t1 * 128, :].rearrange("(t p) f -> p t f", p=128), writes=[("Vs", t0)])

                def load_q(blk):
                    P.dma(QTs[blk % 2][:], QT_d[:, :, blk * 512:(blk + 1) * 512].rearrange("h d s -> d h s"),
                          writes=[("QTs", blk % 2)])

                its = [(blk, hd, p) for blk in range(NB) for hd in range(NH) for p in range(2 * blk + 2)]

                def emit_s(i):
                    blk, hd, p = its[i]
                    if hd == 0 and p == 0:
                        if blk == 0:
                            load_q(0)
                        if blk + 1 < NB:
                            load_q(blk + 1)
                    qb = QTs[blk % 2]
                    sbp = 2 * (i % 3)
                    pp = pT[i % 4]; pkey = ("pT", i % 4)
                    for a_ in range(2):
                        kt = 2 * p + a_
                        pe("matmul", PS(sbp + a_), KT[:, hd, kt * 128:(kt + 1) * 128], qb[:, hd, :], start=True, stop=True,
                           r=["KT", ("QTs", blk % 2)], w=[pk(sbp + a_)])
                    act("activation", pp[:], psum[:, sbp:sbp + 2, :], AF.Exp, r=[pk(sbp), pk(sbp + 1)], w=[pkey])
                    for a_ in range(2):
                        jd = 2 * p + a_ - 4 * blk
                        if jd >= 0:
                            w_ = 128 * (jd + 1)
                            eng = dve if (jd % 2 == 0 or l + 1 < nlayers) else pool
                            eng("tensor_tensor", pp[:, a_, 0:w_], pp[:, a_, 0:w_], amask[jd][:, 0:w_], ALU.mult,
                                r=[pkey, ("amask", jd)], w=[pkey])

                def emit_pv(i):
                    blk, hd, p = its[i]
                    nk = 4 * blk + 4
                    pp = pT[i % 4]; pkey = ("pT", i % 4)
                    bo = 6 + (hd % 2)
                    for a_ in range(2):
                        kt = 2 * p + a_
                        pe("matmul", PS(bo), Vs[:, kt, hd * 128:(hd + 1) * 128], pp[:, a_, :], start=(kt == 0), stop=(kt == nk - 1),
                           r=[("Vs", (kt // 8) * 8), pkey], w=[pk(bo)])
                    if p == 2 * blk + 1:
                        at = aT[blk % 2]
                        rd = rden[hd % 2]
                        c = hd // 2
                        if hd % 2 == 0:
                            dve("reciprocal", rd[64:128, :], psum[64:128, bo, :], r=[pk(bo)], w=[("rden", 0)])
                            dve("tensor_tensor", at[0:64, c, :], psum[0:64, bo, :], rd[64:128, :], ALU.mult,
                                r=[pk(bo), ("rden", 0)], w=[("aT", blk % 2)])
                        else:
                            dve("reciprocal", rd[0:64, :], psum[0:64, bo, :], r=[pk(bo)], w=[("rden", 1)])
                            dve("tensor_tensor", at[64:128, c, :], psum[64:128, bo, :], rd[0:64, :], ALU.mult,
                                r=[pk(bo), ("rden", 1)], w=[("aT", blk % 2)])
                        if hd == NH - 1:
                            P.dma(mixT_d[0:512, blk * 512:(blk + 1) * 512].rearrange("(c p) s -> p c s", p=128), at[:],
                                  reads=[("aT", blk % 2)])

                n = len(its)
                gen = wprep_gen(l + 1, sb, 1024, "sp") if l + 1 < nlayers else None
                for i in range(n + 2):
                    if i < n:
                        emit_s(i)
                    if i >= 2:
                        emit_pv(i - 2)
                    if gen is not None and i % 3 == 2:
                        next(gen, None)
                if gen is not None:
                    for _ in gen:
                        pass
                P.flush()

        def phase_C(l):
            with contextlib.ExitStack() as st:
                sb = mk_sb(st)
                ab = sb([128, 8], F32); dsk = sb([128, 8], F32); nwb = sb([128, 512], F32)
                P.dma(ab[:], a_log[l:l + 1, :].partition_broadcast(128), writes=["ab"])
                P.dma(dsk[:], d_skip[l:l + 1, :].partition_broadcast(128), writes=["dsk"])
                P.dma(nwb[:], ssd_norm_w[l:l + 1, :].partition_broadcast(128), writes=["nwb"])
                act("activation", ab[:], ab[:], AF.Exp, r=["ab"], w=["ab"])
                dve("tensor_scalar", ab[:], ab[:], -1.0, None, ALU.mult, r=["ab"], w=["ab"])
                BTs = sb([128, 2, S], BF16); CTs = sb([128, 2, S], BF16)
                P.dma(BTs[:], BT_d.rearrange("g n s -> n g s"), writes=["BTs"])
                P.dma(CTs[:], CT_d.rearrange("g n s -> n g s"), writes=["CTs"])
                xs4 = [sb([128, 4, 512], BF16) for _ in range(2)]
                B4 = [sb([128, 4, 256], BF16) for _ in range(2)]
                z4 = [sb([128, 4, 512], BF16) for _ in range(2)]
                sz4 = [sb([128, 4, 512], F32) for _ in range(2)]
                adt_2 = [sb([128, 8], F32) for _ in range(3)]
                rhs_all_2 = [sb([128, 2, 8, 128], BF16) for _ in range(3)]
                adt_h = [sb([128, 8], BF16) for _ in range(3)]; adt_hf = [sb([128, 8], F32) for _ in range(3)]
                adt_l = [sb([128, 8], BF16) for _ in range(3)]
                ustr_bf = sb([128, 128], BF16)
                dve("tensor_copy", ustr_bf[:], ustr[:], r=["ustr"], w=["ustr_bf"])
                ET_2 = [sb([128, 8, 128], BF16) for _ in range(3)]
                small_2 = [sb([128, 16], F32) for _ in range(3)]
                dec_2 = [sb([128, 8], F32) for _ in range(3)]
                sm_2 = [sb([128, 2, 128], BF16) for _ in range(3)]
                MT_2 = [sb([128, 8, 128], BF16) for _ in range(3)]
                xdt_2 = [sb([128, 8, 64], BF16) for _ in range(3)]; xdd_2 = [sb([128, 8, 64], BF16) for _ in range(3)]
                state = sb([128, 8, 64], F32); state_bf = sb([128, 8, 64], BF16)
                yo_2 = [sb([128, 8, 64], F32) for _ in range(3)]; y_2 = [sb([128, 512], F32) for _ in range(3)]
                xsd_2 = [sb([128, 512], F32) for _ in range(3)]
                yg = sb([128, 512], F32); junk = sb([128, 256], F32)
                ssg = sb([128, 4, 2], F32); rsg = sb([128, 4, 2], F32)
                yg4 = sb([128, 4, 512], F32)
                yn_2 = [sb([128, 512], BF16) for _ in range(2)]
                ygT = sb([128, 4, 512], BF16)
                dve("memset", state[:], 0.0, w=["state"])
                dve("memset", state_bf[:], 0.0, w=["state_bf"])
                def load_blk(blk):
                    pb = blk % 2
                    rows = slice(blk * 512, (blk + 1) * 512)
                    P.dma(xs4[pb][:], xs_d[rows, :].rearrange("(j p) f -> p j f", p=128), writes=[("xs4", pb)])
                    P.dma(B4[pb][:], B_d[rows, :].rearrange("(j p) f -> p j f", p=128), writes=[("B4", pb)])
                    P.dma(z4[pb][:], z_d[rows, :].rearrange("(j p) f -> p j f", p=128), writes=[("z4", pb)])
                    act("activation", sz4[pb][:], z4[pb][:], AF.Silu, r=[("z4", pb)], w=[("sz4", pb)])

                def s1(blk, j):
                    pb = blk % 2
                    jb = (blk * 4 + j) % 3
                    t = blk * 4 + j
                    cs = slice(t * 128, (t + 1) * 128)
                    dtt = dt_all[:, t, :]
                    adt = adt_2[jb]; rhs_all = rhs_all_2[jb]; ET = ET_2[jb]; small = small_2[jb]; dec = dec_2[jb]
                    sm = sm_2[jb]; MT = MT_2[jb]; xdt = xdt_2[jb]; xdd = xdd_2[jb]
                    yo = yo_2[jb]; y = y_2[jb]; xsd = xsd_2[jb]
                    dve("tensor_tensor", adt[:], dtt, ab[:], ALU.mult, r=["dt_all", "ab"], w=[("adt", jb)])
                    dve("tensor_copy", adt_h[jb][:], adt[:], r=[("adt", jb)], w=[("adt_h", jb)])
                    dve("tensor_copy", adt_hf[jb][:], adt_h[jb][:], r=[("adt_h", jb)], w=[("adt_hf", jb)])
                    dve("tensor_tensor", adt_hf[jb][:], adt[:], adt_hf[jb][:], ALU.subtract, r=[("adt", jb), ("adt_hf", jb)], w=[("adt_hf", jb)])
                    dve("tensor_copy", adt_l[jb][:], adt_hf[jb][:], r=[("adt_hf", jb)], w=[("adt_l", jb)])
                    pool("tensor_tensor", rhs_all[:, 0], ltri_bf[:].unsqueeze(1).to_broadcast([128, 8, 128]),
                         adt_h[jb][:].unsqueeze(2).to_broadcast([128, 8, 128]), ALU.mult, r=["ltri_bf", ("adt_h", jb)], w=[("rhs_all", jb)])
                    pool("tensor_tensor", rhs_all[:, 1], ltri_bf[:].unsqueeze(1).to_broadcast([128, 8, 128]),
                         adt_l[jb][:].unsqueeze(2).to_broadcast([128, 8, 128]), ALU.mult, r=["ltri_bf", ("adt_l", jb)], w=[("rhs_all", jb)])
                    bs = npair()
                    for hf in range(2):
                        for part in range(2):
                            pe("matmul", PS(bs + hf), ustr_bf[:], rhs_all[:, part, hf * 4:(hf + 1) * 4, :].rearrange("p h l -> p (h l)"),
                               start=(part == 0), stop=(part == 1), r=["ustr_bf", ("rhs_all", jb)], w=[pk(bs + hf)])
                    bsm = nbank()
                    pe("matmul", PS(bsm, 8), ltri[:], adt[:], start=True, stop=True, r=["ltri", ("adt", jb)], w=[pk(bsm)])
                    pe("matmul", psum[:, bsm, 8:16], ones_f[:], adt[:], start=True, stop=True, r=["ones_f", ("adt", jb)], w=[pk(bsm)])
                    segv = PS2(bs).rearrange("p (h l) -> p h l", h=8)
                    act("activation", ET[:], segv, AF.Exp, r=[pk(bs), pk(bs + 1)], w=[("ET", jb)])
                    act("activation", dec[:], segv[:, :, 127], AF.Exp, r=[pk(bs), pk(bs + 1)], w=[("dec", jb)])
                    act("activation", small[:], PS(bsm, 16), AF.Exp, r=[pk(bsm)], w=[("small", jb)])
                    bsc = nbank()
                    for g in range(2):
                        pe("matmul", psum[:, bsc, g * 128:(g + 1) * 128], BTs[:, g, cs], CTs[:, g, cs], start=True, stop=True,
                           r=["BTs", "CTs"], w=[pk(bsc)])
                    dve("tensor_tensor", sm[:], PS(bsc, 256).rearrange("p (g l) -> p g l", g=2),
                        ltri_bf[:].unsqueeze(1).to_broadcast([128, 2, 128]), ALU.mult, r=[pk(bsc), "ltri_bf"], w=[("sm", jb)])
                    for g in range(2):
                        dve("tensor_tensor", MT[:, g * 4:(g + 1) * 4, :], ET[:, g * 4:(g + 1) * 4, :],
                            sm[:, g:g + 1, :].to_broadcast([128, 4, 128]), ALU.mult, r=[("ET", jb), ("sm", jb)], w=[("MT", jb)])
                    xsv = xs4[pb][:, j, :].rearrange("p (h e) -> p h e", h=8)
                    pool("tensor_tensor", xdt[:], xsv, dtt.unsqueeze(2).to_broadcast([128, 8, 64]), ALU.mult,
                         r=[("xs4", pb), "dt_all"], w=[("xdt", jb)])
                    pool("tensor_tensor", xdd[:], xdt[:], dec[:].unsqueeze(2).to_broadcast([128, 8, 64]), ALU.mult,
                         r=[("xdt", jb), ("dec", jb)], w=[("xdd", jb)])

                def s2(blk, j):
                    pb = blk % 2
                    jb = (blk * 4 + j) % 3
                    t = blk * 4 + j
                    cs = slice(t * 128, (t + 1) * 128)
                    adt = adt_2[jb]; rhs_all = rhs_all_2[jb]; ET = ET_2[jb]; small = small_2[jb]; dec = dec_2[jb]
                    sm = sm_2[jb]; MT = MT_2[jb]; xdt = xdt_2[jb]; xdd = xdd_2[jb]
                    yo = yo_2[jb]; y = y_2[jb]; xsd = xsd_2[jb]
                    xsv = xs4[pb][:, j, :].rearrange("p (h e) -> p h e", h=8)
                    byd = nbank()
                    for hd in range(8):
                        pe("matmul", psum[:, byd, hd * 64:(hd + 1) * 64], MT[:, hd, :], xdt[:, hd, :], start=True, stop=True,
                           r=[("MT", jb), ("xdt", jb)], w=[pk(byd)])
                    byo = nbank()
                    for hd in range(8):
                        pe("matmul", psum[:, byo, hd * 64:(hd + 1) * 64], CTs[:, hd // 4, cs], state_bf[:, hd, :], start=True, stop=True,
                           r=["CTs", "state_bf"], w=[pk(byo)])
                    bst = nbank()
                    for hd in range(8):
                        pe("matmul", psum[:, bst, hd * 64:(hd + 1) * 64], B4[pb][:, j, (hd // 4) * 128:(hd // 4 + 1) * 128],
                           xdd[:, hd, :], start=True, stop=True, r=[("B4", pb), ("xdd", jb)], w=[pk(bst)])
                    dve("tensor_tensor", yo[:], PS(byo).rearrange("p (h e) -> p h e", h=8),
                        small[:, 0:8].unsqueeze(2).to_broadcast([128, 8, 64]), ALU.mult, r=[pk(byo), ("small", jb)], w=[("yo", jb)])
                    dve("tensor_tensor", y[:], yo[:].rearrange("p h e -> p (h e)"), PS(byd), ALU.add, r=[("yo", jb), pk(byd)], w=[("y", jb)])
                    pool("tensor_tensor", xsd[:].rearrange("p (h e) -> p h e", h=8), xsv,
                         dsk[:].unsqueeze(2).to_broadcast([128, 8, 64]), ALU.mult, r=[("xs4", pb), "dsk"], w=[("xsd", jb)])
                    pool("tensor_tensor", y[:], y[:], xsd[:], ALU.add, r=[("y", jb), ("xsd", jb)], w=[("y", jb)])
                    dve("tensor_tensor", state[:], state[:], small[:, 8:16].unsqueeze(2).to_broadcast([128, 8, 64]), ALU.mult,
                        r=["state", ("small", jb)], w=["state"])
                    dve("tensor_tensor", state[:], state[:], PS(bst).rearrange("p (h e) -> p h e", h=8), ALU.add,
                        r=["state", pk(bst)], w=["state"])
                    act("copy", state_bf[:], state[:], r=["state"], w=["state_bf"])
                    dve("tensor_tensor", yg4[:, j, :], y[:], sz4[pb][:, j, :], ALU.mult, r=[("y", jb), ("sz4", pb)], w=["yg4"])
                    for g in range(2):
                        act("activation", junk[:], yg4[:, j, g * 256:(g + 1) * 256], AF.Square, accum_out=ssg[:, j, g:g + 1],
                            r=["yg4"], w=["junk", "ssg"])

                def end_blk(blk):
                    act("activation", rsg[:], ssg[:], AF.Ln, scale=1.0 / 256, bias=cst[:, 0:1], r=["ssg", "cst"], w=["rsg"])
                    act("activation", rsg[:], rsg[:], AF.Exp, scale=-0.5, r=["rsg"], w=["rsg"])
                    for j in range(4):
                        yn = yn_2[j % 2]
                        for g in range(2):
                            dve("scalar_tensor_tensor", yn[:, g * 256:(g + 1) * 256], yg4[:, j, g * 256:(g + 1) * 256], rsg[:, j, g:g + 1],
                                nwb[:, g * 256:(g + 1) * 256], ALU.mult, ALU.mult, r=["yg4", "rsg", "nwb"], w=[("yn", j % 2)])
                        b = nbank()
                        for cch in range(4):
                            pe("transpose", PS16(b)[:, cch * 128:(cch + 1) * 128], yn[:, cch * 128:(cch + 1) * 128], ident[:],
                               r=[("yn", j % 2), "ident"], w=[pk(b)])
                        act("copy", ygT[:, :, j * 128:(j + 1) * 128], PS16(b)[:, 0:512].rearrange("p (c t) -> p c t", c=4),
                            r=[pk(b)], w=["ygT"])
                    P.dma(mixT_d[512:1024, blk * 512:(blk + 1) * 512].rearrange("(c p) s -> p c s", p=128), ygT[:],
                          reads=["ygT"], eng="act")

                wgc = wprep_gen(0, sb, 1024, "sp", which="b", plain_eng="act") if l == 0 else None
                for i in range(NT + 2):
                    if wgc is not None:
                        for _k in range(3):
                            next(wgc, None)
                    if i < NT:
                        if i % 4 == 0:
                            load_blk(i // 4)
                        s1(i // 4, i % 4)
                    if i >= 2:
                        s2((i - 2) // 4, (i - 2) % 4)
                        if (i - 2) % 4 == 3:
                            end_blk((i - 2) // 4)
                if wgc is not None:
                    for _ in wgc:
                        pass
                P.flush()

        def phase_D(l, xsrc, xdst):
            with contextlib.ExitStack() as st:
                sb = mk_sb(st)
                wo = sb([128, 8, D], BF16)
                P.dma(wo[:], woutb_d_L[l].rearrange("(k p) c -> p k c", p=128), writes=["wo"])
                mT = [sb([128, 8, 512], BF16) for _ in range(2)]
                xt = [sb([128, D], F32) for _ in range(4)]
                for blk in range(NB):
                    pb = blk % 2
                    P.dma(mT[pb][:], mixT_d[:, blk * 512:(blk + 1) * 512].rearrange("(k p) s -> p k s", p=128), writes=[("mT", pb)])
                    for j in range(4):
                        t = blk * 4 + j
                        i2 = t % 4
                        P.dma(xt[i2][:], xsrc[t * 128:(t + 1) * 128, :], writes=[("xt", i2)])
                        bp = npair()
                        for hf in range(2):
                            for k in range(8):
                                pe("matmul", PS(bp + hf), mT[pb][:, k, j * 128:(j + 1) * 128], wo[:, k, hf * 512:(hf + 1) * 512],
                                   start=(k == 0), stop=(k == 7), r=[("mT", pb), "wo"], w=[pk(bp + hf)])
                        dve("tensor_tensor", xt[i2][:], PS2(bp), xt[i2][:], ALU.add, r=[pk(bp), pk(bp + 1), ("xt", i2)], w=[("xt", i2)])
                        P.dma(xdst[t * 128:(t + 1) * 128, :], xt[i2][:], reads=[("xt", i2)], eng="act")
                P.flush()

        def phase_E(l, xsrc, xdst):
            with contextlib.ExitStack() as st:
                sb = mk_sb(st)
                NF = D_FF // 128
                wgu = sb([128, 8, 2 * D_FF], BF16)
                wd = sb([128, NF, D], BF16)
                weff = sb([128, D], F32); shb = sb([128, D], F32)
                for k in range(8):
                    P.dma(wgu[:, k, :], wgub_d_L[l][k * 128:(k + 1) * 128, :], writes=[("wgu", k)])
                P.dma(wd[:], wdb_d_L[l].rearrange("(k p) c -> p k c", p=128), writes=["wd"])
                P.dma(weff[:], modrow_d_L[l][3:4, :].partition_broadcast(128), writes=["weff"])
                P.dma(shb[:], modrow_d_L[l][4:5, :].partition_broadcast(128), writes=["shb"])
                xt = [sb([128, D], F32) for _ in range(4)]
                ssx = sb([128, 4], F32); rsx = sb([128, 4], F32)
                htmp = [sb([128, D], F32)] * 2
                h = [sb([128, D], BF16) for _ in range(4)]
                hT = sb([128, 8, 512], BF16)
                sg = [sb([128, 512], F32) for _ in range(2)]
                aT = sb([128, NF, 512], BF16)

                def n_stats(blk, j):
                    t = blk * 4 + j
                    P.dma(xt[j][:], xsrc[t * 128:(t + 1) * 128, :], writes=[("xt", j)])
                    act("activation", h[3][:], xt[j][:], AF.Square, accum_out=ssx[:, j:j + 1],
                        r=[("xt", j)], w=[("h", 3), "ssx"])

                def n_rstd():
                    act("activation", rsx[:], ssx[:], AF.Ln, scale=1.0 / D, bias=cst[:, 0:1], r=["ssx", "cst"], w=["rsx"])
                    act("activation", rsx[:], rsx[:], AF.Exp, scale=-0.5, r=["rsx"], w=["rsx"])

                def n_h(j):
                    ht = htmp[j % 2]
                    dve("scalar_tensor_tensor", ht[:], xt[j][:], rsx[:, j:j + 1], weff[:], ALU.mult, ALU.mult,
                        r=[("xt", j), "rsx", "weff"], w=["htmp"])
                    pool("tensor_tensor", h[j][:], ht[:], shb[:], ALU.add, r=["htmp", "shb"], w=[("h", j)])

                def n_T():
                    for j in range(4):
                        b = nbank()
                        for k in range(8):
                            pe("transpose", PS16(b)[:, k * 128:(k + 1) * 128], h[j][:, k * 128:(k + 1) * 128], ident[:],
                               r=[("h", j), "ident"], w=[pk(b)])
                        act("copy", hT[:, :, j * 128:(j + 1) * 128], PS16(b).rearrange("p (k t) -> p k t", k=8),
                            r=[pk(b)], w=["hT"])

                for j in range(4):
                    n_stats(0, j)
                n_rstd()
                for j in range(4):
                    n_h(j)
                n_T()
                for blk in range(NB):
                    nxt = blk + 1 < NB
                    for f in range(NF):
                        bg = nbank()
                        for k in range(8):
                            pe("matmul", PS(bg), wgu[:, k, f * 128:(f + 1) * 128], hT[:, k, :], start=(k == 0), stop=(k == 7),
                               r=[("wgu", k), "hT"], w=[pk(bg)])
                        bu = nbank()
                        for k in range(8):
                            pe("matmul", PS(bu), wgu[:, k, D_FF + f * 128:D_FF + (f + 1) * 128], hT[:, k, :], start=(k == 0), stop=(k == 7),
                               r=[("wgu", k), "hT"], w=[pk(bu)])
                        act("activation", sg[f % 2][:], PS(bg), AF.Silu, r=[pk(bg)], w=[("sg", f % 2)])
                        dve("tensor_tensor", aT[:, f, :], PS(bu), sg[f % 2][:], ALU.mult, r=[pk(bu), ("sg", f % 2)], w=["aT"])
                        if nxt:
                            if f in (1, 3, 5, 7):
                                n_stats(blk + 1, (f - 1) // 2)
                            elif f == 9:
                                n_rstd()
                            elif f in (11, 13, 15, 17):
                                n_h((f - 11) // 2)
                    for j in range(4):
                        t = blk * 4 + j
                        P.dma(xt[j][:], xsrc[t * 128:(t + 1) * 128, :], writes=[("xt", j)])
                        bp = npair()
                        for hf in range(2):
                            for f in range(NF):
                                pe("matmul", PS(bp + hf), aT[:, f, j * 128:(j + 1) * 128], wd[:, f, hf * 512:(hf + 1) * 512],
                                   start=(f == 0), stop=(f == NF - 1), r=["aT", "wd"], w=[pk(bp + hf)])
                        if j == 0 and nxt:
                            n_T()
                        dve("tensor_tensor", xt[j][:], PS2(bp), xt[j][:], ALU.add, r=[pk(bp), pk(bp + 1), ("xt", j)], w=[("xt", j)])
                        P.dma(xdst[t * 128:(t + 1) * 128, :], xt[j][:], reads=[("xt", j)], eng="act")
                P.flush()

        stages = []
        stages.append(("const", phase_const))
        stages.append(("wprep0", phase_start))
        for l in range(nlayers):
            xsrc = x_in if l == 0 else xb_d
            xfin = out if l == nlayers - 1 else xb_d
            stages.append(("A%d" % l, partial(phase_A, l, xsrc)))
            stages.append(("B%d" % l, partial(phase_B, l)))
            stages.append(("C%d" % l, partial(phase_C, l)))
            stages.append(("D%d" % l, partial(phase_D, l, xsrc, xa_d)))
            stages.append(("E%d" % l, partial(phase_E, l, xa_d, xfin)))
        for name, fn in stages:
            fn()
            if upto is not None and name == upto:
                break
    return nc


_INPUT_NAMES = ["norm1_w", "norm2_w", "w_ada", "b_ada", "w_in", "q_a_norm_w", "w_q_up", "kv_a_norm_w", "w_kv_up",
                "q_nope_norm_w", "q_pe_norm_w", "k_nope_norm_w", "k_pe_norm_w", "conv_w", "conv_b", "dt_bias",
                "a_log", "d_skip", "ssd_norm_w", "w_out", "w_gate_up", "w_down"]


def kernel(x, c, positions, **w):
    x = np.asarray(x); c = np.asarray(c); positions = np.asarray(positions)
    Bn, S, _ = x.shape
    nc = build(S)
    shared = {k: np.ascontiguousarray(np.asarray(w[k], dtype=np.float32)) for k in _INPUT_NAMES}
    in_maps = []
    for b in range(Bn):
        m = dict(shared)
        m["x"] = np.ascontiguousarray(x[b], dtype=np.float32)
        m["c"] = np.ascontiguousarray(c[b:b + 1], dtype=np.float32)
        m["positions"] = np.ascontiguousarray(positions[b:b + 1], dtype=np.int32)
        in_maps.append(m)
    res = run_bass_kernel_spmd(nc, in_maps, core_ids=list(range(Bn)))
    return np.stack([np.asarray(r["out"], dtype=np.float32) for r in res.results], axis=0)
```

```python
import contextlib
import math
from functools import partial

import numpy as np
import concourse.bass as bass
import concourse.mybir as mybir
from concourse.bass_utils import run_bass_kernel_spmd

F32 = mybir.dt.float32
BF16 = mybir.dt.bfloat16
I32 = mybir.dt.int32
ALU = mybir.AluOpType
AF = mybir.ActivationFunctionType
AX = mybir.AxisListType

D = 1024
DEPTH = 2
NH = 8
D_IN = 1960
D_FF = 2816
EPS = 1e-6
ENGS = ("pe", "act", "dve", "pool", "sp")
N_DMA_SEMS = 16


class _Op:
    __slots__ = ("eng", "fn", "deps", "dma", "signal", "count", "sem")


class Prog:
    def __init__(self, nc, stack):
        self.nc = nc
        self.esem = {e: stack.enter_context(nc.semaphore("s_" + e)) for e in ENGS}
        self.dsem = [stack.enter_context(nc.semaphore("d%d" % i)) for i in range(N_DMA_SEMS)]
        self.ecnt = {e: 0 for e in ENGS}
        self.dcnt = [0] * N_DMA_SEMS
        self.nd = 0
        self.nops = 0
        self._reset()

    def _reset(self):
        self.ops = []
        self.last_w = {}
        self.readers = {}
        self.eng_ops = {e: [] for e in ENGS}

    def op(self, eng, fn, reads=(), writes=(), dma=False):
        o = _Op()
        o.eng, o.fn, o.dma, o.signal, o.count, o.sem = eng, fn, dma, False, 0, None
        need = {}
        for r in reads:
            w = self.last_w.get(r)
            if w is not None:
                self._dep(need, o, w, "raw")
        for r in writes:
            w = self.last_w.get(r)
            if w is not None:
                self._dep(need, o, w, "waw")
            for rd in self.readers.get(r, ()):
                self._dep(need, o, rd, "war")
        o.deps = list(need.values())
        for r in reads:
            self.readers.setdefault(r, []).append(o)
        for r in writes:
            self.last_w[r] = o
            self.readers[r] = []
        self.ops.append(o)
        self.eng_ops[eng].append(o)
        return o

    @staticmethod
    def _dep(need, o, d, kind):
        if d is o:
            return
        if d.eng == o.eng and not d.dma and not o.dma and o.eng == "pe":
            return
        need[id(d)] = d

    def dma(self, out, in_, reads=(), writes=(), eng="sp", **kw):
        q = {"sp": self.nc.sync, "act": self.nc.scalar, "pool": self.nc.gpsimd}[eng]
        return self.op(eng, partial(q.dma_start, out=out, in_=in_, **kw), reads, writes, dma=True)

    def flush(self):
        nc = self.nc
        self.nops += len(self.ops)
        for o in self.ops:
            for d in o.deps:
                d.signal = True
        for e in ENGS:
            comp = [o for o in self.eng_ops[e] if not o.dma]
            if comp:
                comp[-1].signal = True
        dlast = [None] * N_DMA_SEMS
        for o in self.ops:
            if o.dma:
                o.signal = True
                k = self.nd % N_DMA_SEMS
                self.nd += 1
                self.dcnt[k] += 16
                o.sem, o.count = self.dsem[k], self.dcnt[k]
                if dlast[k] is not None:
                    o.deps.append(dlast[k])
                dlast[k] = o
            elif o.signal:
                self.ecnt[o.eng] += 1
                o.sem, o.count = self.esem[o.eng], self.ecnt[o.eng]
        dcnt = list(self.dcnt)
        ecnt = dict(self.ecnt)
        eng_ops = self.eng_ops
        dsem, esem = self.dsem, self.esem

        def run(engname, e):
            waited = {}
            for o in eng_ops[engname]:
                for d in o.deps:
                    key = id(d.sem)
                    if waited.get(key, 0) >= d.count:
                        continue
                    e.wait_ge(d.sem, d.count)
                    waited[key] = d.count
                ins = o.fn()
                if o.signal:
                    ins.then_inc(o.sem, 16 if o.dma else 1)
            for en in ENGS:
                if ecnt[en] and waited.get(id(esem[en]), 0) < ecnt[en]:
                    e.wait_ge(esem[en], ecnt[en])
            for k in range(N_DMA_SEMS):
                if dcnt[k] and waited.get(id(dsem[k]), 0) < dcnt[k]:
                    e.wait_ge(dsem[k], dcnt[k])

        with nc.Block() as blk:
            @blk.tensor
            def _(e):
                run("pe", e)

            @blk.scalar
            def _(e):
                run("act", e)

            @blk.vector
            def _(e):
                run("dve", e)

            @blk.gpsimd
            def _(e):
                run("pool", e)

            @blk.sync
            def _(e):
                run("sp", e)
        self._reset()


DBG_STOP = [None]


def build(S, nlayers=DEPTH, debug=False, upto=None):
    NT = S // 128
    NB = S // 512
    nc = bass.Bass("TRN2", target_bir_lowering=False)
    okind = "ExternalOutput" if debug else "Internal"

    def din(name, shape, dt=F32):
        return nc.dram_tensor(name, list(shape), dt, kind="ExternalInput").ap()

    def dscr(name, shape, dt):
        return nc.dram_tensor(name, list(shape), dt, kind=okind).ap()

    x_in = din("x", [S, D])
    c_in = din("c", [1, D])
    pos_in = din("positions", [1, S], I32)
    Ld = DEPTH
    norm1_w = din("norm1_w", [Ld, D]); norm2_w = din("norm2_w", [Ld, D])
    w_ada = din("w_ada", [Ld, D, 6 * D]); b_ada = din("b_ada", [Ld, 6 * D])
    w_in = din("w_in", [Ld, D, D_IN])
    q_a_norm_w = din("q_a_norm_w", [Ld, 256]); w_q_up = din("w_q_up", [Ld, 256, 768])
    kv_a_norm_w = din("kv_a_norm_w", [Ld, 128]); w_kv_up = din("w_kv_up", [Ld, 128, 1024])
    q_nope_norm_w = din("q_nope_norm_w", [Ld, 64]); q_pe_norm_w = din("q_pe_norm_w", [Ld, 32])
    k_nope_norm_w = din("k_nope_norm_w", [Ld, 64]); k_pe_norm_w = din("k_pe_norm_w", [Ld, 32])
    conv_w = din("conv_w", [Ld, 4, 1024]); conv_b = din("conv_b", [Ld, 1024])
    dt_bias = din("dt_bias", [Ld, 8]); a_log = din("a_log", [Ld, 8]); d_skip = din("d_skip", [Ld, 8])
    ssd_norm_w = din("ssd_norm_w", [Ld, 512])
    w_out = din("w_out", [Ld, D, D]); w_gate_up = din("w_gate_up", [Ld, D, 2 * D_FF])
    w_down = din("w_down", [Ld, D_FF, D])
    out = nc.dram_tensor("out", [S, D], F32, kind="ExternalOutput").ap()

    modrow_d_L = [dscr("modrow_d%d" % i, [6, D], F32) for i in range(DEPTH)]
    winb_d_L = [dscr("winb_d%d" % i, [D, D_IN], BF16) for i in range(DEPTH)]
    wqb_d_L = [dscr("wqb_d%d" % i, [256, 768], BF16) for i in range(DEPTH)]
    wkvb_d_L = [dscr("wkvb_d%d" % i, [128, 1024], BF16) for i in range(DEPTH)]
    woutb_d_L = [dscr("woutb_d%d" % i, [D, D], BF16) for i in range(DEPTH)]
    wgub_d_L = [dscr("wgub_d%d" % i, [D, 2 * D_FF], BF16) for i in range(DEPTH)]
    wdb_d_L = [dscr("wdb_d%d" % i, [D_FF, D], BF16) for i in range(DEPTH)]
    QT_d = dscr("QT_d", [NH, 96, S], BF16)
    KT_d = dscr("KT_d", [NH, 96, S], BF16)
    V_d = dscr("V_d", [S, NH * 128], BF16)
    z_d = dscr("z_d", [S, 512], BF16)
    xs_d = dscr("xs_d", [S, 512], BF16)
    B_d = dscr("B_d", [S, 256], BF16)
    BT_d = dscr("BT_d", [2, 128, S], BF16)
    CT_d = dscr("CT_d", [2, 128, S], BF16)
    mixT_d = dscr("mixT_d", [D, S], BF16)
    cs_d = dscr("cs_d", [2, 128, NT * 16], F32)
    cwcb_d_L = [dscr("cwcb_d%d" % i, [128, 40], F32) for i in range(DEPTH)]
    xa_d = dscr("xa_d", [S, D], F32)
    xb_d = dscr("xb_d", [S, D], F32)

    with contextlib.ExitStack() as top:
        P = Prog(nc, top)

        gcnt = [0]

        def mk_sb(st):
            def sb(shape, dt=F32, name=None):
                gcnt[0] += 1
                return st.enter_context(nc.sbuf_tensor(name or ("t%d" % gcnt[0]), list(shape), dt))
            return sb

        def E(eng, obj, fname, *a, r=(), w=(), **kw):
            return P.op(eng, partial(getattr(obj, fname), *a, **kw), r, w)

        def dve(fname, *a, r=(), w=(), **kw):
            return E("dve", nc.vector, fname, *a, r=r, w=w, **kw)

        def pool(fname, *a, r=(), w=(), **kw):
            return E("pool", nc.gpsimd, fname, *a, r=r, w=w, **kw)

        def act(fname, *a, r=(), w=(), **kw):
            return E("act", nc.scalar, fname, *a, r=r, w=w, **kw)

        def pe(fname, *a, r=(), w=(), **kw):
            return E("pe", nc.tensor, fname, *a, r=r, w=w, **kw)

        psb = mk_sb(top)
        psum = top.enter_context(nc.psum_tensor("psum", [128, 8, 512], F32))
        ident = psb([128, 128], BF16, "ident")
        ones_f = psb([128, 128], F32, "ones_f")
        ltri = psb([128, 128], F32, "ltri")
        ustr = psb([128, 128], F32, "ustr")
        ltri_bf = psb([128, 128], BF16, "ltri_bf")
        cst = psb([128, 4], F32, "cst")
        dt_all = psb([128, NT, 8], F32, "dt_all")

        bank_rr = [0]

        def nbank():
            b = bank_rr[0] % 4
            bank_rr[0] += 1
            return b

        pair_rr = [0]

        def npair():
            b = 4 + 2 * (pair_rr[0] % 2)
            pair_rr[0] += 1
            return b

        def PS(b, n=512):
            return psum[:, b, 0:n]

        def PS2(b, n=1024):
            return psum[:, b:b + 2, :].rearrange("p a b -> p (a b)")[:, 0:n]

        def PS16(b):
            return psum[:, b, :].bitcast(BF16)

        def pk(b):
            return ("ps", b)

        def cols_from_row(dst, row, n, rkey, wkey):
            b = nbank()
            for i in range(n):
                pe("matmul", psum[:, b, i:i + 1], row[0:1, i * 128:(i + 1) * 128], ones_f[0:1, 0:1], start=True, stop=True,
                   r=[rkey, "ones_f"], w=[pk(b)])
            dve("tensor_copy", dst, psum[:, b, 0:n], r=[pk(b)], w=[wkey])

        def phase_const():
            with contextlib.ExitStack() as st:
                sb = mk_sb(st)
                pool("memset", ones_f[:], 1.0, w=["ones_f"])
                pool("memset", cst[:, 0:1], EPS, w=["cst"])
                pool("memset", cst[:, 1:2], 1.0, w=["cst"])
                pool("affine_select", ltri[:], ones_f[:], pattern=[[1, 128]], compare_op=ALU.is_ge,
                     fill=0.0, base=0, channel_multiplier=-1, r=["ones_f"], w=["ltri"])
                pool("affine_select", ustr[:], ones_f[:], pattern=[[-1, 128]], compare_op=ALU.is_gt,
                     fill=0.0, base=0, channel_multiplier=1, r=["ones_f"], w=["ustr"])
                pool("affine_select", ident[:], ones_f[:], pattern=[[1, 128]], compare_op=ALU.is_equal,
                     fill=0.0, base=0, channel_multiplier=-1, r=["ones_f"], w=["ident"])
                dve("tensor_copy", ltri_bf[:], ltri[:], r=["ltri"], w=["ltri_bf"])
                posf = sb([128, NT], F32)
                invf = sb([128, 16], F32)
                ang = sb([128, NT, 16], F32)
                kq = sb([128, NT, 16], F32)
                ki = sb([128, NT, 16], I32)
                m1 = sb([128, NT, 16], F32)
                rc = sb([128, NT, 16], F32)
                cosT = sb([128, NT, 16], F32)
                sinT = sb([128, NT, 16], F32)
                prow_i = sb([1, S], I32)
                prow_f = sb([1, S], F32)
                P.dma(prow_i[:], pos_in, writes=["prow_i"])
                dve("tensor_copy", prow_f[:], prow_i[:], r=["prow_i"], w=["prow_f"])
                cols_from_row(posf[:], prow_f, NT, "prow_f", "posf")
                inv = (1.0 / (np.float32(10000.0) ** (np.arange(0, 32, 2, dtype=np.float32) / np.float32(32)))).astype(np.float32)
                for j in range(16):
                    pool("memset", invf[:, j:j + 1], float(inv[j]), w=["invf"])
                dve("tensor_tensor", ang[:], posf[:].unsqueeze(2).to_broadcast([128, NT, 16]),
                    invf[:].unsqueeze(1).to_broadcast([128, NT, 16]), ALU.mult, r=["posf", "invf"], w=["ang"])
                TWO_PI = 2.0 * math.pi
                C1 = 6.28125
                C2 = TWO_PI - C1
                PI_LO = 3.1415925

                def reduce_to_pi(src, skey, dst, dkey):
                    dve("tensor_scalar", kq[:], src[:], 1.0 / TWO_PI, None, ALU.mult, r=[skey], w=["kq"])
                    dve("tensor_copy", ki[:], kq[:], r=["kq"], w=["ki"])
                    dve("tensor_copy", kq[:], ki[:], r=["ki"], w=["kq"])
                    dve("scalar_tensor_tensor", dst[:], kq[:], -C1, src[:], ALU.mult, ALU.add, r=["kq", skey], w=[dkey])
                    dve("scalar_tensor_tensor", dst[:], kq[:], -C2, dst[:], ALU.mult, ALU.add, r=["kq", dkey], w=[dkey])
                    dve("tensor_scalar", m1[:], dst[:], math.pi, None, ALU.is_gt, r=[dkey], w=["m1"])
                    dve("scalar_tensor_tensor", dst[:], m1[:], -TWO_PI, dst[:], ALU.mult, ALU.add, r=["m1", dkey], w=[dkey])
                    dve("tensor_scalar", m1[:], dst[:], -math.pi, None, ALU.is_lt, r=[dkey], w=["m1"])
                    dve("scalar_tensor_tensor", dst[:], m1[:], TWO_PI, dst[:], ALU.mult, ALU.add, r=["m1", dkey], w=[dkey])
                    dve("tensor_scalar", dst[:], dst[:], PI_LO, -PI_LO, ALU.min, ALU.max, r=[dkey], w=[dkey])

                reduce_to_pi(ang, "ang", rc, "rc")
                act("activation", sinT[:], rc[:], AF.Sin, r=["rc"], w=["sinT"])
                dve("tensor_scalar", ang[:], rc[:], math.pi / 2, None, ALU.add, r=["rc"], w=["ang"])
                reduce_to_pi(ang, "ang", rc, "rc")
                act("activation", cosT[:], rc[:], AF.Sin, r=["rc"], w=["cosT"])
                P.dma(cs_d[0], cosT[:].rearrange("p t j -> p (t j)"), reads=["cosT"])
                P.dma(cs_d[1], sinT[:].rearrange("p t j -> p (t j)"), reads=["sinT"])
                P.flush()

        def mod_gen(l, T):
            cT, wst, mrow, brow, nrow, orow, crow = T
            P.dma(crow[:], c_in, writes=["crow"])
            act("activation", crow[:], crow[:], AF.Silu, r=["crow"], w=["crow"])
            cols_from_row(cT[:], crow, 8, "crow", "cT")
            P.dma(brow[:], b_ada[l:l + 1, :], writes=["brow"])
            P.dma(nrow[:, 0:D], norm1_w[l:l + 1, :], writes=["nrow"])
            P.dma(nrow[:, D:2 * D], norm2_w[l:l + 1, :], writes=["nrow"])
            for n in range(12):
                ws = wst[n % 2]
                P.dma(ws[:], w_ada[l, :, n * 512:(n + 1) * 512].rearrange("(k p) c -> p k c", p=128),
                      writes=[("wst", n % 2)])
                b = nbank()
                for k in range(8):
                    pe("matmul", psum[0:1, b, :], cT[:, k:k + 1], ws[:, k, :], start=(k == 0), stop=(k == 7),
                       r=["cT", ("wst", n % 2)], w=[pk(b)])
                dve("tensor_tensor", mrow[:, n * 512:(n + 1) * 512], psum[0:1, b, :], brow[:, n * 512:(n + 1) * 512],
                    ALU.add, r=[pk(b), "brow"], w=["mrow"])
                yield
            for half in range(2):
                o = half * 3 * D
                dve("scalar_tensor_tensor", orow[:, o:o + D], mrow[:, o + D:o + 2 * D], 1.0, nrow[:, half * D:(half + 1) * D],
                    ALU.add, ALU.mult, r=["mrow", "nrow"], w=["orow"])
                dve("tensor_copy", orow[:, o + D:o + 2 * D], mrow[:, o:o + D], r=["mrow"], w=["orow"])
                dve("tensor_copy", orow[:, o + 2 * D:o + 3 * D], mrow[:, o + 2 * D:o + 3 * D], r=["mrow"], w=["orow"])
            P.dma(modrow_d_L[l].rearrange("(o a) b -> o (a b)", o=1), orow[:], reads=["orow"], writes=[("modrow", l)], eng="act")

        def phase_start():
            with contextlib.ExitStack() as st:
                sb = mk_sb(st)
                T = (sb([128, 8], F32), [sb([128, 8, 512], F32) for _ in range(2)], sb([1, 6 * D], F32), sb([1, 6 * D], F32),
                     sb([1, 2 * D], F32), sb([1, 6 * D], F32), sb([1, D], F32))
                cwrow = sb([1, 4 * 1024], F32); cbrow = sb([1, 1024], F32); cwcb = sb([128, 40], F32)
                for _ in mod_gen(0, T):
                    pass
                for ll in range(nlayers):
                    P.dma(cwrow[:], conv_w[ll:ll + 1].rearrange("o k c -> o (k c)"), writes=["cwrow"])
                    P.dma(cbrow[:], conv_b[ll:ll + 1, :], writes=["cbrow"])
                    cols_from_row(cwcb[:, 0:32], cwrow, 32, "cwrow", "cwcb")
                    cols_from_row(cwcb[:, 32:40], cbrow, 8, "cbrow", "cwcb")
                    P.dma(cwcb_d_L[ll], cwcb[:], reads=["cwcb"])
                wg = wprep_gen(0, sb, 2048, "act", which="a")
                for l in range(1, nlayers):
                    for _ in mod_gen(l, T):
                        for _k in range(5):
                            next(wg, None)
                for _ in wg:
                    pass
                P.flush()


        def wprep_gen(l, sb, CW, store_eng, which="all", plain_eng="pool"):
            specs = list(zip(l, which)) if isinstance(l, (list, tuple)) else [(l, which)]
            sin_ = [sb([128, CW], F32) for _ in range(3)]
            sout = [sb([128, CW], BF16) for _ in range(3)]
            jobs = []

            def add(src, dst, R, C, gate=None):
                for r0 in range(0, R, 128):
                    for c0 in range(0, C, CW):
                        c1 = min(C, c0 + CW)
                        jobs.append((src[r0:r0 + 128, c0:c1], dst[r0:r0 + 128, c0:c1], c1 - c0, c0, gate))
            for (ll, wh) in specs:
                wi = w_in[ll]
                if wh in ("all", "a"):
                    add(wi[:, 0:416], winb_d_L[ll][:, 0:416], D, 416)
                    add(wi[:, 1952:1960], winb_d_L[ll][:, 416:424], D, 8)
                    add(wi[:, 416:1952], winb_d_L[ll][:, 424:1960], D, 1536)
                    add(w_q_up[ll], wqb_d_L[ll], 256, 768)
                    add(w_kv_up[ll], wkvb_d_L[ll], 128, 1024)
                if wh in ("all", "b"):
                    g1b = sb([128, D], F32); g2b = sb([128, D], F32)
                    P.dma(g1b[:], modrow_d_L[ll][2:3, :].partition_broadcast(128), reads=[("modrow", ll)], writes=[("g1b", ll)])
                    P.dma(g2b[:], modrow_d_L[ll][5:6, :].partition_broadcast(128), reads=[("modrow", ll)], writes=[("g2b", ll)])
                    add(w_out[ll], woutb_d_L[ll], D, D, gate=(g1b, ("g1b", ll)))
                    add(w_gate_up[ll], wgub_d_L[ll], D, 2 * D_FF)
                    add(w_down[ll], wdb_d_L[ll], D_FF, D, gate=(g2b, ("g2b", ll)))
            def load(i):
                if i < len(jobs):
                    P.dma(sin_[i % 3][:, 0:jobs[i][2]], jobs[i][0], writes=[("sin", i % 3)])
            load(0)
            load(1)
            for i, (src, dst, cw, c0, gate) in enumerate(jobs):
                k = i % 3
                load(i + 2)
                if gate is not None:
                    gt, gk = gate
                    eng = pool if store_eng == "sp" else (dve if i % 2 == 0 else pool)
                    eng("tensor_tensor", sout[k][:, 0:cw], sin_[k][:, 0:cw], gt[:, c0:c0 + cw], ALU.mult,
                        r=[("sin", k), gk], w=[("sout", k)])
                elif store_eng == "sp" and plain_eng == "act":
                    act("copy", sout[k][:, 0:cw], sin_[k][:, 0:cw], r=[("sin", k)], w=[("sout", k)])
                elif store_eng == "sp":
                    pool("tensor_copy", sout[k][:, 0:cw], sin_[k][:, 0:cw], r=[("sin", k)], w=[("sout", k)])
                elif k == 1:
                    act("copy", sout[k][:, 0:cw], sin_[k][:, 0:cw], r=[("sin", k)], w=[("sout", k)])
                elif k == 0:
                    dve("tensor_copy", sout[k][:, 0:cw], sin_[k][:, 0:cw], r=[("sin", k)], w=[("sout", k)])
                else:
                    pool("tensor_copy", sout[k][:, 0:cw], sin_[k][:, 0:cw], r=[("sin", k)], w=[("sout", k)])
                P.dma(dst, sout[k][:, 0:cw], reads=[("sout", k)], eng=store_eng)
                yield

        def phase_wprep(l):
            with contextlib.ExitStack() as st:
                sb = mk_sb(st)
                cwrow = sb([1, 4 * 1024], F32); cbrow = sb([1, 1024], F32); cwcb = sb([128, 40], F32)
                for ll in range(nlayers):
                    P.dma(cwrow[:], conv_w[ll:ll + 1].rearrange("o k c -> o (k c)"), writes=["cwrow"])
                    P.dma(cbrow[:], conv_b[ll:ll + 1, :], writes=["cbrow"])
                    cols_from_row(cwcb[:, 0:32], cwrow, 32, "cwrow", "cwcb")
                    cols_from_row(cwcb[:, 32:40], cbrow, 8, "cbrow", "cwcb")
                    P.dma(cwcb_d_L[ll], cwcb[:], reads=["cwcb"])
                for _ in wprep_gen(l, sb, 2048, "act"):
                    pass
                P.flush()

        def phase_A(l, xsrc):
            with contextlib.ExitStack() as st:
                sb = mk_sb(st)
                win = sb([128, 8, D_IN], BF16)
                wq = sb([128, 2, 768], BF16)
                wkv = sb([128, 1024], BF16)
                weff = sb([128, D], F32); shb = sb([128, D], F32)
                qaw = sb([128, 256], F32); kvaw = sb([128, 128], F32); kpw = sb([128, 32], F32)
                qnw = sb([128, 64], F32); qpw = sb([128, 32], F32); knw = sb([128, 64], F32)
                cw = sb([128, 4, 8], F32); cb = sb([128, 8], F32); dtb = sb([128, 8], F32)
                invn = sb([128, 27], F32)
                cosT = sb([128, NT, 16], F32)
                sinT = sb([128, NT, 16], F32)
                P.dma(cosT[:].rearrange("p t j -> p (t j)"), cs_d[0], writes=["cosT"])
                P.dma(sinT[:].rearrange("p t j -> p (t j)"), cs_d[1], writes=["sinT"])
                P.dma(win[:], winb_d_L[l].rearrange("(k p) c -> p k c", p=128), writes=["win"])
                P.dma(wq[:], wqb_d_L[l].rearrange("(k p) c -> p k c", p=128), writes=["wq"])
                P.dma(wkv[:], wkvb_d_L[l], writes=["wkv"])
                P.dma(weff[:], modrow_d_L[l][0:1, :].partition_broadcast(128), writes=["weff"])
                P.dma(shb[:], modrow_d_L[l][1:2, :].partition_broadcast(128), writes=["shb"])
                P.dma(qaw[:], q_a_norm_w[l:l + 1, :].partition_broadcast(128), writes=["qaw"])
                P.dma(kvaw[:], kv_a_norm_w[l:l + 1, :].partition_broadcast(128), writes=["kvaw"])
                P.dma(kpw[:], k_pe_norm_w[l:l + 1, :].partition_broadcast(128), writes=["kpw"])
                P.dma(qnw[:], q_nope_norm_w[l:l + 1, :].partition_broadcast(128), writes=["qnw"])
                P.dma(qpw[:], q_pe_norm_w[l:l + 1, :].partition_broadcast(128), writes=["qpw"])
                P.dma(knw[:], k_nope_norm_w[l:l + 1, :].partition_broadcast(128), writes=["knw"])
                P.dma(dtb[:], dt_bias[l:l + 1, :].partition_broadcast(128), writes=["dtb"])
                P.dma(cw[:].rearrange("p k m -> p (k m)"), cwcb_d_L[l][:, 0:32], writes=["cw"])
                P.dma(cb[:], cwcb_d_L[l][:, 32:40], writes=["cb"])
                scale = 96.0 ** -0.5
                dve("tensor_scalar", qnw[:], qnw[:], scale, None, ALU.mult, r=["qnw"], w=["qnw"])
                dve("tensor_scalar", qpw[:], qpw[:], scale, None, ALU.mult, r=["qpw"], w=["qpw"])
                pool("memset", invn[:, 0:1], 1.0 / 256, w=["invn"])
                pool("memset", invn[:, 1:2], 1.0 / 128, w=["invn"])
                pool("memset", invn[:, 2:3], 1.0 / 32, w=["invn"])
                pool("memset", invn[:, 3:11], 1.0 / 64, w=["invn"])
                pool("memset", invn[:, 11:19], 1.0 / 32, w=["invn"])
                pool("memset", invn[:, 19:27], 1.0 / 64, w=["invn"])

                xt = [sb([128, D], F32) for _ in range(4)]
                junk = sb([128, D], BF16)
                ssx = sb([128, 4], F32); rsx = sb([128, 4], F32)
                htmp_2 = [sb([128, D], F32) for _ in range(2)]
                h = [sb([128, D], BF16) for _ in range(2)]
                hT = sb([128, 8, 512], BF16)
                xbcT = [sb([128, 516], BF16) for _ in range(8)]
                dg = sb([128, 8, 4, 128], BF16)
                xsT = sb([128, 8, 512], BF16)
                raw1 = [sb([128, 424], F32) for _ in range(4)]
                zt = [sb([128, 512], BF16) for _ in range(2)]
                dtr = sb([128, 4, 8], F32)
                ss1 = sb([128, 4, 3], F32); rs1 = sb([128, 4, 3], F32)
                latn_2 = [sb([128, 384], BF16) for _ in range(2)]
                latT_2 = [sb([128, 3, 128], BF16) for _ in range(2)]
                qsb = [sb([128, 768], F32) for _ in range(4)]
                kvsb = [sb([128, 8, 64], F32) for _ in range(4)]
                sq_2 = [sb([128, 768], F32) for _ in range(2)]
                ss2 = sb([128, 4, 24], F32); rs2 = sb([128, 4, 24], F32)
                kpn = [sb([128, 32], F32) for _ in range(4)]
                Qf_2 = [sb([128, 8, 96], BF16) for _ in range(2)]; Kf_2 = [sb([128, 8, 96], BF16) for _ in range(2)]
                ra_2 = [sb([128, 8, 32], F32) for _ in range(2)]; rb_2 = [sb([128, 8, 32], F32) for _ in range(2)]
                rq_2 = [sb([128, 8, 32], F32) for _ in range(2)]
                ka_2 = [sb([128, 32], F32) for _ in range(2)]; kb_2 = [sb([128, 32], F32) for _ in range(2)]
                kr_2 = [sb([128, 32], F32) for _ in range(2)]
                vaug = [sb([128, 8, 128], BF16) for _ in range(2)]
                QTb = sb([96, 8, 512], BF16); KTb = sb([96, 8, 512], BF16)
                tok_o = [sb([128, 768], BF16) for _ in range(2)]

                for i in range(2):
                    pool("memset", vaug[i][:], 1.0, w=[("vaug", i)])
                for m in range(8):
                    pool("memset", xbcT[m][:, 512:515], 0.0, w=[("xbcT", m)])
                for m in range(8):
                    for k in range(4):
                        dve("tensor_scalar", dg[:, m, k, :], ident[:], cw[:, k, m:m + 1], None, ALU.mult, r=["ident", "cw"], w=["dg"])

                def n_stats(blk, j):
                    t = blk * 4 + j
                    P.dma(xt[j][:], xsrc[t * 128:(t + 1) * 128, :], writes=[("xt", j)])
                    act("activation", junk[:], xt[j][:], AF.Square, accum_out=ssx[:, j:j + 1],
                        r=[("xt", j)], w=["junk", "ssx"])

                def n_rstd():
                    act("activation", rsx[:], ssx[:], AF.Ln, scale=1.0 / D, bias=cst[:, 0:1], r=["ssx", "cst"], w=["rsx"])
                    act("activation", rsx[:], rsx[:], AF.Exp, scale=-0.5, r=["rsx"], w=["rsx"])

                def n_hT(j, bank=None):
                    hh = h[j % 2]; htmp = htmp_2[j % 2]
                    dve("scalar_tensor_tensor", htmp[:], xt[j][:], rsx[:, j:j + 1], weff[:], ALU.mult, ALU.mult,
                        r=[("xt", j), "rsx", "weff"], w=[("htmp", j % 2)])
                    pool("tensor_tensor", hh[:], htmp[:], shb[:], ALU.add, r=[("htmp", j % 2), "shb"], w=[("h", j % 2)])
                    b = nbank() if bank is None else bank
                    for k in range(8):
                        pe("transpose", PS16(b)[:, k * 128:(k + 1) * 128], hh[:, k * 128:(k + 1) * 128], ident[:],
                           r=[("h", j % 2), "ident"], w=[pk(b)])
                    act("copy", hT[:, :, j * 128:(j + 1) * 128], PS16(b).rearrange("p (k t) -> p k t", k=8),
                        r=[pk(b)], w=["hT"])

                for j in range(4):
                    n_stats(0, j)
                n_rstd()
                for j in range(4):
                    n_hT(j)

                def P_gen(blk):
                    pb = blk % 2
                    for m in range(9):
                        if m == 5:
                            yield
                        if m < 8:
                            b = nbank()
                            c0 = 936 + m * 128
                            for k in range(8):
                                pe("matmul", PS(b), win[:, k, c0:c0 + 128], hT[:, k, :], start=(k == 0), stop=(k == 7),
                                   r=["win", "hT"], w=[pk(b)])
                            cur = xbcT[m]
                            act("copy", cur[:, 0:3], cur[:, 512:515], r=[("xbcT", m)], w=[("xbcT", m)])
                            act("copy", cur[:, 3:515], PS(b), r=[pk(b), ("xbcT", m)], w=[("xbcT", m)])
                        if m >= 1:
                            mm = m - 1
                            b2 = nbank()
                            for kk in range(4):
                                pe("matmul", PS(b2), dg[:, mm, kk, :], xbcT[mm][:, kk:kk + 512], start=(kk == 0), stop=(kk == 3),
                                   r=["dg", ("xbcT", mm)], w=[pk(b2)])
                            act("activation", xsT[:, mm, :], PS(b2), AF.Silu, bias=cb[:, mm:mm + 1], r=[pk(b2), "cb"], w=[("xsT", mm)])
                    yield
                    for g in range(2):
                        P.dma(BT_d[g, :, blk * 512:(blk + 1) * 512], xsT[:, 4 + g, :], reads=[("xsT", 4 + g)], eng="act")
                        P.dma(CT_d[g, :, blk * 512:(blk + 1) * 512], xsT[:, 6 + g, :], reads=[("xsT", 6 + g)], eng="act")
                    for j in range(4):
                        t = blk * 4 + j
                        b = nbank()
                        for m in range(6):
                            pe("transpose", PS16(b)[:, m * 128:(m + 1) * 128], xsT[:, m, j * 128:(j + 1) * 128], ident[:],
                               r=[("xsT", m), "ident"], w=[pk(b)])
                        to = tok_o[j % 2]
                        dve("tensor_copy", to[:], PS16(b)[:, 0:768], r=[pk(b)], w=[("tok_o", j % 2)])
                        P.dma(xs_d[t * 128:(t + 1) * 128, :], to[:, 0:512], reads=[("tok_o", j % 2)], eng="act")
                        P.dma(B_d[t * 128:(t + 1) * 128, :], to[:, 512:768], reads=[("tok_o", j % 2)], eng="act")
                    yield
                    for j in range(4):
                        t = blk * 4 + j
                        b1 = nbank()
                        for k in range(8):
                            pe("matmul", PS(b1, 424), hT[:, k, j * 128:(j + 1) * 128], win[:, k, 0:424], start=(k == 0), stop=(k == 7),
                               r=["hT", "win"], w=[pk(b1)])
                        b2 = nbank()
                        for k in range(8):
                            pe("matmul", PS(b2), hT[:, k, j * 128:(j + 1) * 128], win[:, k, 424:936], start=(k == 0), stop=(k == 7),
                               r=["hT", "win"], w=[pk(b2)])
                        dve("tensor_copy", raw1[j][:], PS(b1, 424), r=[pk(b1)], w=[("raw1", j)])
                        act("copy", zt[j % 2][:], PS(b2), r=[pk(b2)], w=[("zt", j % 2)])
                        P.dma(z_d[t * 128:(t + 1) * 128, :], zt[j % 2][:], reads=[("zt", j % 2)], eng="act")
                        act("activation", junk[:, 0:256], raw1[j][:, 0:256], AF.Square, accum_out=ss1[:, j, 0:1],
                            r=[("raw1", j)], w=["junk", "ss1"])
                        act("activation", junk[:, 0:128], raw1[j][:, 256:384], AF.Square, accum_out=ss1[:, j, 1:2],
                            r=[("raw1", j)], w=["junk", "ss1"])
                        act("activation", junk[:, 0:32], raw1[j][:, 384:416], AF.Square, accum_out=ss1[:, j, 2:3],
                            r=[("raw1", j)], w=["junk", "ss1"])
                        dve("tensor_tensor", dtr[:, j, :], raw1[j][:, 416:424], dtb[:], ALU.add, r=[("raw1", j), "dtb"], w=["dtr"])
                    yield
                    dve("tensor_tensor", ss1[:], ss1[:], invn[:, 0:3].unsqueeze(1).to_broadcast([128, 4, 3]), ALU.mult,
                        r=["ss1", "invn"], w=["ss1"])
                    act("activation", dtr[:], dtr[:], AF.Exp, r=["dtr"], w=["dtr"])
                    act("activation", dt_all[:, blk * 4:(blk + 1) * 4, :], dtr[:], AF.Ln, bias=cst[:, 1:2], r=["dtr", "cst"], w=["dt_all"])
                    act("activation", rs1[:], ss1[:], AF.Ln, bias=cst[:, 0:1], r=["ss1", "cst"], w=["rs1"])
                    act("activation", rs1[:], rs1[:], AF.Exp, scale=-0.5, r=["rs1"], w=["rs1"])

                for _ in P_gen(0):
                    pass
                for blk in range(NB):
                    pb = blk % 2
                    nxt = blk + 1 < NB
                    def La(j):
                        latn = latn_2[j % 2]; latT = latT_2[j % 2]; sq = sq_2[j % 2]
                        dve("scalar_tensor_tensor", latn[:, 0:256], raw1[j][:, 0:256], rs1[:, j, 0:1], qaw[:], ALU.mult, ALU.mult,
                            r=[("raw1", j), "rs1", "qaw"], w=[("latn", j % 2)])
                        dve("scalar_tensor_tensor", latn[:, 256:384], raw1[j][:, 256:384], rs1[:, j, 1:2], kvaw[:], ALU.mult, ALU.mult,
                            r=[("raw1", j), "rs1", "kvaw"], w=[("latn", j % 2)])
                        dve("scalar_tensor_tensor", kpn[j][:], raw1[j][:, 384:416], rs1[:, j, 2:3], kpw[:], ALU.mult, ALU.mult,
                             r=[("raw1", j), "rs1", "kpw"], w=[("kpn", j)])
                        b = 2 if j % 2 == 0 else 6
                        for k in range(3):
                            pe("transpose", PS16(b)[:, k * 128:(k + 1) * 128], latn[:, k * 128:(k + 1) * 128], ident[:],
                               r=[("latn", j % 2), "ident"], w=[pk(b)])
                        act("copy", latT[:], PS16(b)[:, 0:384].rearrange("p (k t) -> p k t", k=3), r=[pk(b)], w=[("latT", j % 2)])
                        bq = 0 if j % 2 == 0 else 4
                        for (n0, n1, bb) in ((0, 512, bq), (512, 768, bq + 1)):
                            for k in range(2):
                                pe("matmul", PS(bb, n1 - n0), latT[:, k, :], wq[:, k, n0:n1], start=(k == 0), stop=(k == 1),
                                   r=[("latT", j % 2), "wq"], w=[pk(bb)])
                        bk = 2 if j % 2 == 0 else 6
                        for hf in range(2):
                            pe("matmul", PS(bk + hf), latT[:, 2, :], wkv[:, hf * 512:(hf + 1) * 512], start=True, stop=True,
                               r=[("latT", j % 2), "wkv"], w=[pk(bk + hf)])
                        return bq, bk

                    def Lb(j, bq, bk):
                        sq = sq_2[j % 2]
                        act("copy", qsb[j][:], PS2(bq, 768), r=[pk(bq), pk(bq + 1)], w=[("qsb", j)])
                        kvv = PS2(bk).rearrange("p (h e) -> p h e", h=8)
                        dve("tensor_copy", kvsb[j][:], kvv[:, :, 0:64], r=[pk(bk), pk(bk + 1)], w=[("kvsb", j)])
                        va = vaug[j % 2]
                        vap = va[:].rearrange("p (c two) e -> p c two e", two=2)
                        for hf in range(2):
                            kvp = PS(bk + hf).rearrange("p (c two e) -> p c two e", two=2, e=128)
                            dve("tensor_copy", vap[:, 2 * hf:2 * hf + 2, 0, 0:64], kvp[:, :, 0, 64:128], r=[pk(bk + hf)], w=[("vaug", j % 2)])
                            dve("tensor_copy", vap[:, 2 * hf:2 * hf + 2, 1, 64:128], kvp[:, :, 1, 64:128], r=[pk(bk + hf)], w=[("vaug", j % 2)])
                        t = blk * 4 + j
                        if True:
                            P.dma(V_d[t * 128:(t + 1) * 128, :], va[:].rearrange("p h e -> p (h e)"), reads=[("vaug", j % 2)], eng="act")
                        act("activation", sq[:], qsb[j][:], AF.Square, r=[("qsb", j)], w=[("sq", j % 2)])
                        sqv = sq[:].rearrange("p (h e) -> p h e", h=8)
                        dve("tensor_reduce", ss2[:, j, 0:8], sqv[:, :, 0:64], AX.X, ALU.add, r=[("sq", j % 2)], w=["ss2"])
                        dve("tensor_reduce", ss2[:, j, 8:16], sqv[:, :, 64:96], AX.X, ALU.add, r=[("sq", j % 2)], w=["ss2"])
                        act("activation", sq[:, 0:512], kvsb[j][:].rearrange("p h e -> p (h e)"), AF.Square, r=[("kvsb", j)], w=[("sq", j % 2)])
                        dve("tensor_reduce", ss2[:, j, 16:24], sq[:, 0:512].rearrange("p (h e) -> p h e", h=8), AX.X, ALU.add,
                            r=[("sq", j % 2)], w=["ss2"])

                    if nxt:
                        for j in range(4):
                            n_stats(blk + 1, j)
                    pend = {}
                    for j in range(5):
                        if j < 4:
                            pend[j] = La(j)
                        if j == 1 and nxt:
                            n_rstd()
                        if j >= 1:
                            Lb(j - 1, *pend[j - 1])
                            if nxt:
                                n_hT(j - 1, bank=(0 if (j - 1) % 2 == 0 else 4))
                    pg = P_gen(blk + 1) if nxt else None
                    dve("tensor_tensor", ss2[:], ss2[:], invn[:, 3:27].unsqueeze(1).to_broadcast([128, 4, 24]), ALU.mult,
                        r=["ss2", "invn"], w=["ss2"])
                    act("activation", rs2[:], ss2[:], AF.Ln, bias=cst[:, 0:1], r=["ss2", "cst"], w=["rs2"])
                    act("activation", rs2[:], rs2[:], AF.Exp, scale=-0.5, r=["rs2"], w=["rs2"])
                    for j in range(4):
                        t = blk * 4 + j
                        sq = sq_2[j % 2]; Qf = Qf_2[j % 2]; Kf = Kf_2[j % 2]; ra = ra_2[j % 2]; rb = rb_2[j % 2]; rq = rq_2[j % 2]
                        ka = ka_2[j % 2]; kb = kb_2[j % 2]; kr = kr_2[j % 2]
                        qv = qsb[j][:].rearrange("p (h e) -> p h e", h=8)
                        cosb = cosT[:, t, :]; sinb = sinT[:, t, :]
                        dve("tensor_tensor", sq[:, 0:512].rearrange("p (h e) -> p h e", h=8), qv[:, :, 0:64],
                            rs2[:, j, 0:8].unsqueeze(2).to_broadcast([128, 8, 64]), ALU.mult, r=[("qsb", j), "rs2"], w=[("sq", j % 2)])
                        dve("tensor_tensor", Qf[:, :, 0:64], sq[:, 0:512].rearrange("p (h e) -> p h e", h=8),
                            qnw[:].unsqueeze(1).to_broadcast([128, 8, 64]), ALU.mult, r=[("sq", j % 2), "qnw"], w=[("Qf", j % 2)])
                        pool("tensor_tensor", ra[:], qv[:, :, 64:96], rs2[:, j, 8:16].unsqueeze(2).to_broadcast([128, 8, 32]), ALU.mult,
                             r=[("qsb", j), "rs2"], w=[("ra", j % 2)])
                        pool("tensor_tensor", ra[:], ra[:], qpw[:].unsqueeze(1).to_broadcast([128, 8, 32]), ALU.mult, r=[("ra", j % 2), "qpw"], w=[("ra", j % 2)])
                        cb8 = cosb.unsqueeze(1).to_broadcast([128, 8, 16]); sb8 = sinb.unsqueeze(1).to_broadcast([128, 8, 16])
                        pool("tensor_tensor", rb[:, :, 0:16], ra[:, :, 0:16], cb8, ALU.mult, r=[("ra", j % 2), "cosT"], w=[("rb", j % 2)])
                        pool("tensor_tensor", rb[:, :, 16:32], ra[:, :, 16:32], cb8, ALU.mult, r=[("ra", j % 2), "cosT"], w=[("rb", j % 2)])
                        pool("tensor_tensor", rq[:, :, 0:16], ra[:, :, 16:32], sb8, ALU.mult, r=[("ra", j % 2), "sinT"], w=[("rq", j % 2)])
                        pool("tensor_tensor", rq[:, :, 16:32], ra[:, :, 0:16], sb8, ALU.mult, r=[("ra", j % 2), "sinT"], w=[("rq", j % 2)])
                        pool("tensor_tensor", Qf[:, :, 64:80], rb[:, :, 0:16], rq[:, :, 0:16], ALU.subtract, r=[("rb", j % 2), ("rq", j % 2)], w=[("Qf", j % 2)])
                        pool("tensor_tensor", Qf[:, :, 80:96], rb[:, :, 16:32], rq[:, :, 16:32], ALU.add, r=[("rb", j % 2), ("rq", j % 2)], w=[("Qf", j % 2)])
                        dve("tensor_tensor", sq[:, 0:512].rearrange("p (h e) -> p h e", h=8), kvsb[j][:],
                            rs2[:, j, 16:24].unsqueeze(2).to_broadcast([128, 8, 64]), ALU.mult, r=[("kvsb", j), "rs2"], w=[("sq", j % 2)])
                        dve("tensor_tensor", Kf[:, :, 0:64], sq[:, 0:512].rearrange("p (h e) -> p h e", h=8),
                            knw[:].unsqueeze(1).to_broadcast([128, 8, 64]), ALU.mult, r=[("sq", j % 2), "knw"], w=[("Kf", j % 2)])
                        dve("tensor_tensor", ka[:, 0:16], kpn[j][:, 0:16], cosb, ALU.mult, r=[("kpn", j), "cosT"], w=[("ka", j % 2)])
                        dve("tensor_tensor", ka[:, 16:32], kpn[j][:, 16:32], cosb, ALU.mult, r=[("kpn", j), "cosT"], w=[("ka", j % 2)])
                        dve("tensor_tensor", kb[:, 0:16], kpn[j][:, 16:32], sinb, ALU.mult, r=[("kpn", j), "sinT"], w=[("kb", j % 2)])
                        dve("tensor_tensor", kb[:, 16:32], kpn[j][:, 0:16], sinb, ALU.mult, r=[("kpn", j), "sinT"], w=[("kb", j % 2)])
                        dve("tensor_tensor", kr[:, 0:16], ka[:, 0:16], kb[:, 0:16], ALU.subtract, r=[("ka", j % 2), ("kb", j % 2)], w=[("kr", j % 2)])
                        dve("tensor_tensor", kr[:, 16:32], ka[:, 16:32], kb[:, 16:32], ALU.add, r=[("ka", j % 2), ("kb", j % 2)], w=[("kr", j % 2)])
                        dve("tensor_copy", Kf[:, :, 64:96], kr[:].unsqueeze(1).to_broadcast([128, 8, 32]), r=[("kr", j % 2)], w=[("Kf", j % 2)])
                        bqt = nbank()
                        for hh_ in range(8):
                            pe("transpose", PS16(bqt)[0:96, hh_ * 128:(hh_ + 1) * 128], Qf[:, hh_, :], ident[:],
                               r=[("Qf", j % 2), "ident"], w=[pk(bqt)])
                        act("copy", QTb[:, :, j * 128:(j + 1) * 128], PS16(bqt)[0:96, :].rearrange("p (h t) -> p h t", h=8),
                            r=[pk(bqt)], w=["QTb"])
                        bkt = nbank()
                        for hh_ in range(8):
                            pe("transpose", PS16(bkt)[0:96, hh_ * 128:(hh_ + 1) * 128], Kf[:, hh_, :], ident[:],
                               r=[("Kf", j % 2), "ident"], w=[pk(bkt)])
                        act("copy", KTb[:, :, j * 128:(j + 1) * 128], PS16(bkt)[0:96, :].rearrange("p (h t) -> p h t", h=8),
                            r=[pk(bkt)], w=["KTb"])
                        if pg is not None:
                            next(pg, None)
                            if j == 3:
                                for _ in pg:
                                    pass
                    P.dma(QT_d[:, :, blk * 512:(blk + 1) * 512].rearrange("h d s -> d h s"), QTb[:], reads=["QTb"], eng="act")
                    P.dma(KT_d[:, :, blk * 512:(blk + 1) * 512].rearrange("h d s -> d h s"), KTb[:], reads=["KTb"], eng="act")
                P.flush()

        def phase_B(l):
            with contextlib.ExitStack() as st:
                sb = mk_sb(st)
                KT = sb([96, 8, S], BF16)
                Vs = sb([128, NT, 8 * 128], BF16)
                QTs = [sb([96, 8, 512], BF16) for _ in range(2)]
                onesb = sb([128, 512], BF16)
                amask = [sb([128, 512], BF16) for _ in range(4)]
                pT = [sb([128, 2, 512], BF16) for _ in range(4)]
                rden = [sb([128, 512], F32) for _ in range(2)]
                aT = [sb([128, 4, 512], BF16) for _ in range(2)]
                pool("memset", onesb[:], 1.0, w=["onesb"])
                for j in range(4):
                    pool("affine_select", amask[j][:], onesb[:], pattern=[[1, 512]], compare_op=ALU.is_ge, fill=0.0,
                         base=-128 * j, channel_multiplier=-1, r=["onesb"], w=[("amask", j)])
                P.dma(KT[:], KT_d.rearrange("h d s -> d h s"), writes=["KT"])
                for t0 in range(0, NT, 8):
                    t1 = min(NT, t0 + 8)
                    P.dma(Vs[:, t0:t1, :], V_d[t0 * 128:t1 * 128, :].rearrange("(t p) f -> p t f", p=128), writes=[("Vs", t0)])

                def load_q(blk):
                    P.dma(QTs[blk % 2][:], QT_d[:, :, blk * 512:(blk + 1) * 512].rearrange("h d s -> d h s"),
                          writes=[("QTs", blk % 2)])

                its = [(blk, hd, p) for blk in range(NB) for hd in range(NH) for p in range(2 * blk + 2)]

                def emit_s(i):
                    blk, hd, p = its[i]
                    if hd == 0 and p == 0:
                        if blk == 0:
                            load_q(0)
                        if blk + 1 < NB:
                            load_q(blk + 1)
                    qb = QTs[blk % 2]
                    sbp = 2 * (i % 3)
                    pp = pT[i % 4]; pkey = ("pT", i % 4)
                    for a_ in range(2):
                        kt = 2 * p + a_
                        pe("matmul", PS(sbp + a_), KT[:, hd, kt * 128:(kt + 1) * 128], qb[:, hd, :], start=True, stop=True,
                           r=["KT", ("QTs", blk % 2)], w=[pk(sbp + a_)])
                    act("activation", pp[:], psum[:, sbp:sbp + 2, :], AF.Exp, r=[pk(sbp), pk(sbp + 1)], w=[pkey])
                    for a_ in range(2):
                        jd = 2 * p + a_ - 4 * blk
                        if jd >= 0:
                            w_ = 128 * (jd + 1)
                            eng = dve if (jd % 2 == 0 or l + 1 < nlayers or l == 0) else pool
                            eng("tensor_tensor", pp[:, a_, 0:w_], pp[:, a_, 0:w_], amask[jd][:, 0:w_], ALU.mult,
                                r=[pkey, ("amask", jd)], w=[pkey])

                def emit_pv(i):
                    blk, hd, p = its[i]
                    nk = 4 * blk + 4
                    pp = pT[i % 4]; pkey = ("pT", i % 4)
                    bo = 6 + (hd % 2)
                    for a_ in range(2):
                        kt = 2 * p + a_
                        pe("matmul", PS(bo), Vs[:, kt, hd * 128:(hd + 1) * 128], pp[:, a_, :], start=(kt == 0), stop=(kt == nk - 1),
                           r=[("Vs", (kt // 8) * 8), pkey], w=[pk(bo)])
                    if p == 2 * blk + 1:
                        at = aT[blk % 2]
                        rd = rden[hd % 2]
                        c = hd // 2
                        if hd % 2 == 0:
                            dve("reciprocal", rd[64:128, :], psum[64:128, bo, :], r=[pk(bo)], w=[("rden", 0)])
                            dve("tensor_tensor", at[0:64, c, :], psum[0:64, bo, :], rd[64:128, :], ALU.mult,
                                r=[pk(bo), ("rden", 0)], w=[("aT", blk % 2)])
                        else:
                            dve("reciprocal", rd[0:64, :], psum[0:64, bo, :], r=[pk(bo)], w=[("rden", 1)])
                            dve("tensor_tensor", at[64:128, c, :], psum[64:128, bo, :], rd[0:64, :], ALU.mult,
                                r=[pk(bo), ("rden", 1)], w=[("aT", blk % 2)])
                        if hd == NH - 1:
                            P.dma(mixT_d[0:512, blk * 512:(blk + 1) * 512].rearrange("(c p) s -> p c s", p=128), at[:],
                                  reads=[("aT", blk % 2)])

                n = len(its)
                gen = wprep_gen([l, l + 1], sb, 1024, "sp", which=["b", "all"]) if (l == 0 and l + 1 < nlayers) else (wprep_gen(l + 1, sb, 1024, "sp") if l + 1 < nlayers else (wprep_gen(l, sb, 1024, "sp", which="b") if l == 0 else None))
                for i in range(n + 2):
                    if i < n:
                        emit_s(i)
                    if i >= 2:
                        emit_pv(i - 2)
                    if gen is not None and i % 2 == 1:
                        next(gen, None)
                if gen is not None:
                    for _ in gen:
                        pass
                P.flush()

        def phase_C(l):
            with contextlib.ExitStack() as st:
                sb = mk_sb(st)
                ab = sb([128, 8], F32); dsk = sb([128, 8], F32); nwb = sb([128, 512], F32)
                P.dma(ab[:], a_log[l:l + 1, :].partition_broadcast(128), writes=["ab"])
                P.dma(dsk[:], d_skip[l:l + 1, :].partition_broadcast(128), writes=["dsk"])
                P.dma(nwb[:], ssd_norm_w[l:l + 1, :].partition_broadcast(128), writes=["nwb"])
                act("activation", ab[:], ab[:], AF.Exp, r=["ab"], w=["ab"])
                dve("tensor_scalar", ab[:], ab[:], -1.0, None, ALU.mult, r=["ab"], w=["ab"])
                BTs = sb([128, 2, S], BF16); CTs = sb([128, 2, S], BF16)
                P.dma(BTs[:], BT_d.rearrange("g n s -> n g s"), writes=["BTs"])
                P.dma(CTs[:], CT_d.rearrange("g n s -> n g s"), writes=["CTs"])
                xs4 = [sb([128, 4, 512], BF16) for _ in range(2)]
                B4 = [sb([128, 4, 256], BF16) for _ in range(2)]
                z4 = [sb([128, 4, 512], BF16) for _ in range(2)]
                sz4 = [sb([128, 4, 512], F32) for _ in range(2)]
                adt_2 = [sb([128, 8], F32) for _ in range(3)]
                rhs_all_2 = [sb([128, 8, 128], F32) for _ in range(3)]
                ET_2 = [sb([128, 8, 128], BF16) for _ in range(3)]
                small_2 = [sb([128, 16], F32) for _ in range(3)]
                dec_2 = [sb([128, 8], F32) for _ in range(3)]
                sm_2 = [sb([128, 2, 128], BF16) for _ in range(3)]
                MT_2 = [sb([128, 8, 128], BF16) for _ in range(3)]
                xdt_2 = [sb([128, 8, 64], BF16) for _ in range(3)]; xdd_2 = [sb([128, 8, 64], BF16) for _ in range(3)]
                state = sb([128, 8, 64], F32); state_bf = sb([128, 8, 64], BF16)
                yo_2 = [sb([128, 8, 64], F32) for _ in range(3)]; y_2 = [sb([128, 512], F32) for _ in range(3)]
                xsd_2 = [sb([128, 512], F32) for _ in range(3)]
                yg = sb([128, 512], F32); junk = sb([128, 256], F32)
                ssg = sb([128, 4, 2], F32); rsg = sb([128, 4, 2], F32)
                yg4 = sb([128, 4, 512], F32)
                yn_2 = [sb([128, 512], BF16) for _ in range(2)]
                ygT = sb([128, 4, 512], BF16)
                dve("memset", state[:], 0.0, w=["state"])
                dve("memset", state_bf[:], 0.0, w=["state_bf"])
                def load_blk(blk):
                    pb = blk % 2
                    rows = slice(blk * 512, (blk + 1) * 512)
                    P.dma(xs4[pb][:], xs_d[rows, :].rearrange("(j p) f -> p j f", p=128), writes=[("xs4", pb)])
                    P.dma(B4[pb][:], B_d[rows, :].rearrange("(j p) f -> p j f", p=128), writes=[("B4", pb)])
                    P.dma(z4[pb][:], z_d[rows, :].rearrange("(j p) f -> p j f", p=128), writes=[("z4", pb)])
                    act("activation", sz4[pb][:], z4[pb][:], AF.Silu, r=[("z4", pb)], w=[("sz4", pb)])

                def s1(blk, j):
                    pb = blk % 2
                    jb = (blk * 4 + j) % 3
                    t = blk * 4 + j
                    cs = slice(t * 128, (t + 1) * 128)
                    dtt = dt_all[:, t, :]
                    adt = adt_2[jb]; rhs_all = rhs_all_2[jb]; ET = ET_2[jb]; small = small_2[jb]; dec = dec_2[jb]
                    sm = sm_2[jb]; MT = MT_2[jb]; xdt = xdt_2[jb]; xdd = xdd_2[jb]
                    yo = yo_2[jb]; y = y_2[jb]; xsd = xsd_2[jb]
                    dve("tensor_tensor", adt[:], dtt, ab[:], ALU.mult, r=["dt_all", "ab"], w=[("adt", jb)])
                    pool("tensor_tensor", rhs_all[:], ltri[:].unsqueeze(1).to_broadcast([128, 8, 128]),
                        adt[:].unsqueeze(2).to_broadcast([128, 8, 128]), ALU.mult, r=["ltri", ("adt", jb)], w=[("rhs_all", jb)])
                    bs = npair()
                    for hf in range(2):
                        pe("matmul", PS(bs + hf), ustr[:], rhs_all[:, hf * 4:(hf + 1) * 4, :].rearrange("p h l -> p (h l)"),
                           start=True, stop=True, r=["ustr", ("rhs_all", jb)], w=[pk(bs + hf)])
                    bsm = nbank()
                    pe("matmul", PS(bsm, 8), ltri[:], adt[:], start=True, stop=True, r=["ltri", ("adt", jb)], w=[pk(bsm)])
                    pe("matmul", psum[:, bsm, 8:16], ones_f[:], adt[:], start=True, stop=True, r=["ones_f", ("adt", jb)], w=[pk(bsm)])
                    segv = PS2(bs).rearrange("p (h l) -> p h l", h=8)
                    act("activation", ET[:], segv, AF.Exp, r=[pk(bs), pk(bs + 1)], w=[("ET", jb)])
                    act("activation", dec[:], segv[:, :, 127], AF.Exp, r=[pk(bs), pk(bs + 1)], w=[("dec", jb)])
                    act("activation", small[:], PS(bsm, 16), AF.Exp, r=[pk(bsm)], w=[("small", jb)])
                    bsc = nbank()
                    for g in range(2):
                        pe("matmul", psum[:, bsc, g * 128:(g + 1) * 128], BTs[:, g, cs], CTs[:, g, cs], start=True, stop=True,
                           r=["BTs", "CTs"], w=[pk(bsc)])
                    dve("tensor_tensor", sm[:], PS(bsc, 256).rearrange("p (g l) -> p g l", g=2),
                        ltri_bf[:].unsqueeze(1).to_broadcast([128, 2, 128]), ALU.mult, r=[pk(bsc), "ltri_bf"], w=[("sm", jb)])
                    for g in range(2):
                        dve("tensor_tensor", MT[:, g * 4:(g + 1) * 4, :], ET[:, g * 4:(g + 1) * 4, :],
                            sm[:, g:g + 1, :].to_broadcast([128, 4, 128]), ALU.mult, r=[("ET", jb), ("sm", jb)], w=[("MT", jb)])
                    xsv = xs4[pb][:, j, :].rearrange("p (h e) -> p h e", h=8)
                    pool("tensor_tensor", xdt[:], xsv, dtt.unsqueeze(2).to_broadcast([128, 8, 64]), ALU.mult,
                         r=[("xs4", pb), "dt_all"], w=[("xdt", jb)])
                    pool("tensor_tensor", xdd[:], xdt[:], dec[:].unsqueeze(2).to_broadcast([128, 8, 64]), ALU.mult,
                         r=[("xdt", jb), ("dec", jb)], w=[("xdd", jb)])

                def s2(blk, j):
                    pb = blk % 2
                    jb = (blk * 4 + j) % 3
                    t = blk * 4 + j
                    cs = slice(t * 128, (t + 1) * 128)
                    adt = adt_2[jb]; rhs_all = rhs_all_2[jb]; ET = ET_2[jb]; small = small_2[jb]; dec = dec_2[jb]
                    sm = sm_2[jb]; MT = MT_2[jb]; xdt = xdt_2[jb]; xdd = xdd_2[jb]
                    yo = yo_2[jb]; y = y_2[jb]; xsd = xsd_2[jb]
                    xsv = xs4[pb][:, j, :].rearrange("p (h e) -> p h e", h=8)
                    byd = nbank()
                    for hd in range(8):
                        pe("matmul", psum[:, byd, hd * 64:(hd + 1) * 64], MT[:, hd, :], xdt[:, hd, :], start=True, stop=True,
                           r=[("MT", jb), ("xdt", jb)], w=[pk(byd)])
                    byo = nbank()
                    for hd in range(8):
                        pe("matmul", psum[:, byo, hd * 64:(hd + 1) * 64], CTs[:, hd // 4, cs], state_bf[:, hd, :], start=True, stop=True,
                           r=["CTs", "state_bf"], w=[pk(byo)])
                    bst = nbank()
                    for hd in range(8):
                        pe("matmul", psum[:, bst, hd * 64:(hd + 1) * 64], B4[pb][:, j, (hd // 4) * 128:(hd // 4 + 1) * 128],
                           xdd[:, hd, :], start=True, stop=True, r=[("B4", pb), ("xdd", jb)], w=[pk(bst)])
                    dve("tensor_tensor", yo[:], PS(byo).rearrange("p (h e) -> p h e", h=8),
                        small[:, 0:8].unsqueeze(2).to_broadcast([128, 8, 64]), ALU.mult, r=[pk(byo), ("small", jb)], w=[("yo", jb)])
                    dve("tensor_tensor", y[:], yo[:].rearrange("p h e -> p (h e)"), PS(byd), ALU.add, r=[("yo", jb), pk(byd)], w=[("y", jb)])
                    pool("tensor_tensor", xsd[:].rearrange("p (h e) -> p h e", h=8), xsv,
                         dsk[:].unsqueeze(2).to_broadcast([128, 8, 64]), ALU.mult, r=[("xs4", pb), "dsk"], w=[("xsd", jb)])
                    pool("tensor_tensor", y[:], y[:], xsd[:], ALU.add, r=[("y", jb), ("xsd", jb)], w=[("y", jb)])
                    dve("tensor_tensor", state[:], state[:], small[:, 8:16].unsqueeze(2).to_broadcast([128, 8, 64]), ALU.mult,
                        r=["state", ("small", jb)], w=["state"])
                    dve("tensor_tensor", state[:], state[:], PS(bst).rearrange("p (h e) -> p h e", h=8), ALU.add,
                        r=["state", pk(bst)], w=["state"])
                    act("copy", state_bf[:], state[:], r=["state"], w=["state_bf"])
                    dve("tensor_tensor", yg4[:, j, :], y[:], sz4[pb][:, j, :], ALU.mult, r=[("y", jb), ("sz4", pb)], w=["yg4"])
                    for g in range(2):
                        act("activation", junk[:], yg4[:, j, g * 256:(g + 1) * 256], AF.Square, accum_out=ssg[:, j, g:g + 1],
                            r=["yg4"], w=["junk", "ssg"])

                def end_blk(blk):
                    act("activation", rsg[:], ssg[:], AF.Ln, scale=1.0 / 256, bias=cst[:, 0:1], r=["ssg", "cst"], w=["rsg"])
                    act("activation", rsg[:], rsg[:], AF.Exp, scale=-0.5, r=["rsg"], w=["rsg"])
                    for j in range(4):
                        yn = yn_2[j % 2]
                        for g in range(2):
                            dve("scalar_tensor_tensor", yn[:, g * 256:(g + 1) * 256], yg4[:, j, g * 256:(g + 1) * 256], rsg[:, j, g:g + 1],
                                nwb[:, g * 256:(g + 1) * 256], ALU.mult, ALU.mult, r=["yg4", "rsg", "nwb"], w=[("yn", j % 2)])
                        b = nbank()
                        for cch in range(4):
                            pe("transpose", PS16(b)[:, cch * 128:(cch + 1) * 128], yn[:, cch * 128:(cch + 1) * 128], ident[:],
                               r=[("yn", j % 2), "ident"], w=[pk(b)])
                        act("copy", ygT[:, :, j * 128:(j + 1) * 128], PS16(b)[:, 0:512].rearrange("p (c t) -> p c t", c=4),
                            r=[pk(b)], w=["ygT"])
                    P.dma(mixT_d[512:1024, blk * 512:(blk + 1) * 512].rearrange("(c p) s -> p c s", p=128), ygT[:],
                          reads=["ygT"], eng="act")

                wgc = None
                for i in range(NT + 2):
                    if wgc is not None:
                        for _k in range(3):
                            next(wgc, None)
                    if i < NT:
                        if i % 4 == 0:
                            load_blk(i // 4)
                        s1(i // 4, i % 4)
                    if i >= 2:
                        s2((i - 2) // 4, (i - 2) % 4)
                        if (i - 2) % 4 == 3:
                            end_blk((i - 2) // 4)
                if wgc is not None:
                    for _ in wgc:
                        pass
                P.flush()

        def phase_D(l, xsrc, xdst):
            with contextlib.ExitStack() as st:
                sb = mk_sb(st)
                wo = sb([128, 8, D], BF16)
                P.dma(wo[:], woutb_d_L[l].rearrange("(k p) c -> p k c", p=128), writes=["wo"])
                mT = [sb([128, 8, 512], BF16) for _ in range(2)]
                xt = [sb([128, D], F32) for _ in range(4)]
                for blk in range(NB):
                    pb = blk % 2
                    P.dma(mT[pb][:], mixT_d[:, blk * 512:(blk + 1) * 512].rearrange("(k p) s -> p k s", p=128), writes=[("mT", pb)])
                    for j in range(4):
                        t = blk * 4 + j
                        i2 = t % 4
                        P.dma(xt[i2][:], xsrc[t * 128:(t + 1) * 128, :], writes=[("xt", i2)])
                        bp = npair()
                        for hf in range(2):
                            for k in range(8):
                                pe("matmul", PS(bp + hf), mT[pb][:, k, j * 128:(j + 1) * 128], wo[:, k, hf * 512:(hf + 1) * 512],
                                   start=(k == 0), stop=(k == 7), r=[("mT", pb), "wo"], w=[pk(bp + hf)])
                        dve("tensor_tensor", xt[i2][:], PS2(bp), xt[i2][:], ALU.add, r=[pk(bp), pk(bp + 1), ("xt", i2)], w=[("xt", i2)])
                        P.dma(xdst[t * 128:(t + 1) * 128, :], xt[i2][:], reads=[("xt", i2)], eng="act")
                P.flush()

        def phase_E(l, xsrc, xdst):
            with contextlib.ExitStack() as st:
                sb = mk_sb(st)
                NF = D_FF // 128
                wgu = sb([128, 8, 2 * D_FF], BF16)
                wd = sb([128, NF, D], BF16)
                weff = sb([128, D], F32); shb = sb([128, D], F32)
                for k in range(8):
                    P.dma(wgu[:, k, :], wgub_d_L[l][k * 128:(k + 1) * 128, :], writes=[("wgu", k)])
                P.dma(wd[:], wdb_d_L[l].rearrange("(k p) c -> p k c", p=128), writes=["wd"])
                P.dma(weff[:], modrow_d_L[l][3:4, :].partition_broadcast(128), writes=["weff"])
                P.dma(shb[:], modrow_d_L[l][4:5, :].partition_broadcast(128), writes=["shb"])
                xt = [sb([128, D], F32) for _ in range(4)]
                ssx = sb([128, 4], F32); rsx = sb([128, 4], F32)
                htmp = [sb([128, D], F32)] * 2
                h = [sb([128, D], BF16) for _ in range(4)]
                hT = sb([128, 8, 512], BF16)
                sg = [sb([128, 512], F32) for _ in range(2)]
                aT = sb([128, NF, 512], BF16)

                def n_stats(blk, j):
                    t = blk * 4 + j
                    P.dma(xt[j][:], xsrc[t * 128:(t + 1) * 128, :], writes=[("xt", j)])
                    act("activation", h[3][:], xt[j][:], AF.Square, accum_out=ssx[:, j:j + 1],
                        r=[("xt", j)], w=[("h", 3), "ssx"])

                def n_rstd():
                    act("activation", rsx[:], ssx[:], AF.Ln, scale=1.0 / D, bias=cst[:, 0:1], r=["ssx", "cst"], w=["rsx"])
                    act("activation", rsx[:], rsx[:], AF.Exp, scale=-0.5, r=["rsx"], w=["rsx"])

                def n_h(j):
                    ht = htmp[j % 2]
                    dve("scalar_tensor_tensor", ht[:], xt[j][:], rsx[:, j:j + 1], weff[:], ALU.mult, ALU.mult,
                        r=[("xt", j), "rsx", "weff"], w=["htmp"])
                    pool("tensor_tensor", h[j][:], ht[:], shb[:], ALU.add, r=["htmp", "shb"], w=[("h", j)])

                def n_T():
                    for j in range(4):
                        b = nbank()
                        for k in range(8):
                            pe("transpose", PS16(b)[:, k * 128:(k + 1) * 128], h[j][:, k * 128:(k + 1) * 128], ident[:],
                               r=[("h", j), "ident"], w=[pk(b)])
                        act("copy", hT[:, :, j * 128:(j + 1) * 128], PS16(b).rearrange("p (k t) -> p k t", k=8),
                            r=[pk(b)], w=["hT"])

                for j in range(4):
                    n_stats(0, j)
                n_rstd()
                for j in range(4):
                    n_h(j)
                n_T()
                for blk in range(NB):
                    nxt = blk + 1 < NB
                    for f in range(NF):
                        bg = nbank()
                        for k in range(8):
                            pe("matmul", PS(bg), wgu[:, k, f * 128:(f + 1) * 128], hT[:, k, :], start=(k == 0), stop=(k == 7),
                               r=[("wgu", k), "hT"], w=[pk(bg)])
                        bu = nbank()
                        for k in range(8):
                            pe("matmul", PS(bu), wgu[:, k, D_FF + f * 128:D_FF + (f + 1) * 128], hT[:, k, :], start=(k == 0), stop=(k == 7),
                               r=[("wgu", k), "hT"], w=[pk(bu)])
                        act("activation", sg[f % 2][:], PS(bg), AF.Silu, r=[pk(bg)], w=[("sg", f % 2)])
                        dve("tensor_tensor", aT[:, f, :], PS(bu), sg[f % 2][:], ALU.mult, r=[pk(bu), ("sg", f % 2)], w=["aT"])
                        if nxt:
                            if f in (1, 3, 5, 7):
                                n_stats(blk + 1, (f - 1) // 2)
                            elif f == 9:
                                n_rstd()
                            elif f in (11, 13, 15, 17):
                                n_h((f - 11) // 2)
                    for j in range(4):
                        t = blk * 4 + j
                        P.dma(xt[j][:], xsrc[t * 128:(t + 1) * 128, :], writes=[("xt", j)])
                        bp = npair()
                        for hf in range(2):
                            for f in range(NF):
                                pe("matmul", PS(bp + hf), aT[:, f, j * 128:(j + 1) * 128], wd[:, f, hf * 512:(hf + 1) * 512],
                                   start=(f == 0), stop=(f == NF - 1), r=["aT", "wd"], w=[pk(bp + hf)])
                        if j == 0 and nxt:
                            n_T()
                        dve("tensor_tensor", xt[j][:], PS2(bp), xt[j][:], ALU.add, r=[pk(bp), pk(bp + 1), ("xt", j)], w=[("xt", j)])
                        P.dma(xdst[t * 128:(t + 1) * 128, :], xt[j][:], reads=[("xt", j)], eng="act")
                P.flush()

        stages = []
        stages.append(("const", phase_const))
        stages.append(("wprep0", phase_start))
        for l in range(nlayers):
            xsrc = x_in if l == 0 else xb_d
            xfin = out if l == nlayers - 1 else xb_d
            stages.append(("A%d" % l, partial(phase_A, l, xsrc)))
            stages.append(("B%d" % l, partial(phase_B, l)))
            stages.append(("C%d" % l, partial(phase_C, l)))
            stages.append(("D%d" % l, partial(phase_D, l, xsrc, xa_d)))
            stages.append(("E%d" % l, partial(phase_E, l, xa_d, xfin)))
        for name, fn in stages:
            fn()
            if upto is not None and name == upto:
                break
    return nc


_INPUT_NAMES = ["norm1_w", "norm2_w", "w_ada", "b_ada", "w_in", "q_a_norm_w", "w_q_up", "kv_a_norm_w", "w_kv_up",
                "q_nope_norm_w", "q_pe_norm_w", "k_nope_norm_w", "k_pe_norm_w", "conv_w", "conv_b", "dt_bias",
                "a_log", "d_skip", "ssd_norm_w", "w_out", "w_gate_up", "w_down"]


def kernel(x, c, positions, **w):
    x = np.asarray(x); c = np.asarray(c); positions = np.asarray(positions)
    Bn, S, _ = x.shape
    nc = build(S)
    shared = {k: np.ascontiguousarray(np.asarray(w[k], dtype=np.float32)) for k in _INPUT_NAMES}
    in_maps = []
    for b in range(Bn):
        m = dict(shared)
        m["x"] = np.ascontiguousarray(x[b], dtype=np.float32)
        m["c"] = np.ascontiguousarray(c[b:b + 1], dtype=np.float32)
        m["positions"] = np.ascontiguousarray(positions[b:b + 1], dtype=np.int32)
        in_maps.append(m)
    res = run_bass_kernel_spmd(nc, in_maps, core_ids=list(range(Bn)))
    return np.stack([np.asarray(r["out"], dtype=np.float32) for r in res.results], axis=0)
```

```python
import contextlib
import math
from functools import partial

import numpy as np
import concourse.bass as bass
import concourse.mybir as mybir
from concourse.bass_utils import run_bass_kernel_spmd

F32 = mybir.dt.float32
BF16 = mybir.dt.bfloat16
I32 = mybir.dt.int32
ALU = mybir.AluOpType
AF = mybir.ActivationFunctionType
AX = mybir.AxisListType

D = 1024
DEPTH = 2
NH = 8
D_IN = 1960
D_FF = 2816
EPS = 1e-6
ENGS = ("pe", "act", "dve", "pool", "sp")
N_DMA_SEMS = 16


class _Op:
    __slots__ = ("eng", "fn", "deps", "dma", "signal", "count", "sem")


class Prog:
    def __init__(self, nc, stack):
        self.nc = nc
        self.esem = {e: stack.enter_context(nc.semaphore("s_" + e)) for e in ENGS}
        self.dsem = [stack.enter_context(nc.semaphore("d%d" % i)) for i in range(N_DMA_SEMS)]
        self.ecnt = {e: 0 for e in ENGS}
        self.dcnt = [0] * N_DMA_SEMS
        self.nd = 0
        self.nops = 0
        self._reset()

    def _reset(self):
        self.ops = []
        self.last_w = {}
        self.readers = {}
        self.eng_ops = {e: [] for e in ENGS}

    def op(self, eng, fn, reads=(), writes=(), dma=False):
        o = _Op()
        o.eng, o.fn, o.dma, o.signal, o.count, o.sem = eng, fn, dma, False, 0, None
        need = {}
        for r in reads:
            w = self.last_w.get(r)
            if w is not None:
                self._dep(need, o, w, "raw")
        for r in writes:
            w = self.last_w.get(r)
            if w is not None:
                self._dep(need, o, w, "waw")
            for rd in self.readers.get(r, ()):
                self._dep(need, o, rd, "war")
        o.deps = list(need.values())
        for r in reads:
            self.readers.setdefault(r, []).append(o)
        for r in writes:
            self.last_w[r] = o
            self.readers[r] = []
        self.ops.append(o)
        self.eng_ops[eng].append(o)
        return o

    @staticmethod
    def _dep(need, o, d, kind):
        if d is o:
            return
        if d.eng == o.eng and not d.dma and not o.dma and o.eng == "pe":
            return
        need[id(d)] = d

    def dma(self, out, in_, reads=(), writes=(), eng="sp", **kw):
        q = {"sp": self.nc.sync, "act": self.nc.scalar, "pool": self.nc.gpsimd}[eng]
        return self.op(eng, partial(q.dma_start, out=out, in_=in_, **kw), reads, writes, dma=True)

    def flush(self):
        nc = self.nc
        self.nops += len(self.ops)
        for o in self.ops:
            for d in o.deps:
                d.signal = True
        for e in ENGS:
            comp = [o for o in self.eng_ops[e] if not o.dma]
            if comp:
                comp[-1].signal = True
        dlast = [None] * N_DMA_SEMS
        for o in self.ops:
            if o.dma:
                o.signal = True
                k = self.nd % N_DMA_SEMS
                self.nd += 1
                self.dcnt[k] += 16
                o.sem, o.count = self.dsem[k], self.dcnt[k]
                if dlast[k] is not None:
                    o.deps.append(dlast[k])
                dlast[k] = o
            elif o.signal:
                self.ecnt[o.eng] += 1
                o.sem, o.count = self.esem[o.eng], self.ecnt[o.eng]
        dcnt = list(self.dcnt)
        ecnt = dict(self.ecnt)
        eng_ops = self.eng_ops
        dsem, esem = self.dsem, self.esem

        def run(engname, e):
            waited = {}
            for o in eng_ops[engname]:
                for d in o.deps:
                    key = id(d.sem)
                    if waited.get(key, 0) >= d.count:
                        continue
                    e.wait_ge(d.sem, d.count)
                    waited[key] = d.count
                ins = o.fn()
                if o.signal:
                    ins.then_inc(o.sem, 16 if o.dma else 1)
            for en in ENGS:
                if ecnt[en] and waited.get(id(esem[en]), 0) < ecnt[en]:
                    e.wait_ge(esem[en], ecnt[en])
            for k in range(N_DMA_SEMS):
                if dcnt[k] and waited.get(id(dsem[k]), 0) < dcnt[k]:
                    e.wait_ge(dsem[k], dcnt[k])

        with nc.Block() as blk:
            @blk.tensor
            def _(e):
                run("pe", e)

            @blk.scalar
            def _(e):
                run("act", e)

            @blk.vector
            def _(e):
                run("dve", e)

            @blk.gpsimd
            def _(e):
                run("pool", e)

            @blk.sync
            def _(e):
                run("sp", e)
        self._reset()


DBG_STOP = [None]


def build(S, nlayers=DEPTH, debug=False, upto=None):
    NT = S // 128
    NB = S // 512
    nc = bass.Bass("TRN2", target_bir_lowering=False)
    okind = "ExternalOutput" if debug else "Internal"

    def din(name, shape, dt=F32):
        return nc.dram_tensor(name, list(shape), dt, kind="ExternalInput").ap()

    def dscr(name, shape, dt):
        return nc.dram_tensor(name, list(shape), dt, kind=okind).ap()

    x_in = din("x", [S, D])
    c_in = din("c", [1, D])
    pos_in = din("positions", [1, S], I32)
    Ld = DEPTH
    norm1_w = din("norm1_w", [Ld, D]); norm2_w = din("norm2_w", [Ld, D])
    w_ada = din("w_ada", [Ld, D, 6 * D]); b_ada = din("b_ada", [Ld, 6 * D])
    w_in = din("w_in", [Ld, D, D_IN])
    q_a_norm_w = din("q_a_norm_w", [Ld, 256]); w_q_up = din("w_q_up", [Ld, 256, 768])
    kv_a_norm_w = din("kv_a_norm_w", [Ld, 128]); w_kv_up = din("w_kv_up", [Ld, 128, 1024])
    q_nope_norm_w = din("q_nope_norm_w", [Ld, 64]); q_pe_norm_w = din("q_pe_norm_w", [Ld, 32])
    k_nope_norm_w = din("k_nope_norm_w", [Ld, 64]); k_pe_norm_w = din("k_pe_norm_w", [Ld, 32])
    conv_w = din("conv_w", [Ld, 4, 1024]); conv_b = din("conv_b", [Ld, 1024])
    dt_bias = din("dt_bias", [Ld, 8]); a_log = din("a_log", [Ld, 8]); d_skip = din("d_skip", [Ld, 8])
    ssd_norm_w = din("ssd_norm_w", [Ld, 512])
    w_out = din("w_out", [Ld, D, D]); w_gate_up = din("w_gate_up", [Ld, D, 2 * D_FF])
    w_down = din("w_down", [Ld, D_FF, D])
    out = nc.dram_tensor("out", [S, D], F32, kind="ExternalOutput").ap()

    modrow_d_L = [dscr("modrow_d%d" % i, [6, D], F32) for i in range(DEPTH)]
    winb_d_L = [dscr("winb_d%d" % i, [D, D_IN], BF16) for i in range(DEPTH)]
    wqb_d_L = [dscr("wqb_d%d" % i, [256, 768], BF16) for i in range(DEPTH)]
    wkvb_d_L = [dscr("wkvb_d%d" % i, [128, 1024], BF16) for i in range(DEPTH)]
    woutb_d_L = [dscr("woutb_d%d" % i, [D, D], BF16) for i in range(DEPTH)]
    wgub_d_L = [dscr("wgub_d%d" % i, [D, 2 * D_FF], BF16) for i in range(DEPTH)]
    wdb_d_L = [dscr("wdb_d%d" % i, [D_FF, D], BF16) for i in range(DEPTH)]
    QT_d = dscr("QT_d", [NH, 96, S], BF16)
    KT_d = dscr("KT_d", [NH, 96, S], BF16)
    V_d = dscr("V_d", [S, NH * 128], BF16)
    z_d = dscr("z_d", [S, 512], BF16)
    xs_d = dscr("xs_d", [S, 512], BF16)
    B_d = dscr("B_d", [S, 256], BF16)
    BT_d = dscr("BT_d", [2, 128, S], BF16)
    CT_d = dscr("CT_d", [2, 128, S], BF16)
    mixT_d = dscr("mixT_d", [D, S], BF16)
    cs_d = dscr("cs_d", [2, 128, NT * 16], F32)
    cwcb_d_L = [dscr("cwcb_d%d" % i, [128, 40], F32) for i in range(DEPTH)]
    xa_d = dscr("xa_d", [S, D], F32)
    xb_d = dscr("xb_d", [S, D], F32)

    with contextlib.ExitStack() as top:
        P = Prog(nc, top)

        gcnt = [0]

        def mk_sb(st):
            def sb(shape, dt=F32, name=None):
                gcnt[0] += 1
                return st.enter_context(nc.sbuf_tensor(name or ("t%d" % gcnt[0]), list(shape), dt))
            return sb

        def E(eng, obj, fname, *a, r=(), w=(), **kw):
            return P.op(eng, partial(getattr(obj, fname), *a, **kw), r, w)

        def dve(fname, *a, r=(), w=(), **kw):
            return E("dve", nc.vector, fname, *a, r=r, w=w, **kw)

        def pool(fname, *a, r=(), w=(), **kw):
            return E("pool", nc.gpsimd, fname, *a, r=r, w=w, **kw)

        def act(fname, *a, r=(), w=(), **kw):
            return E("act", nc.scalar, fname, *a, r=r, w=w, **kw)

        def pe(fname, *a, r=(), w=(), **kw):
            return E("pe", nc.tensor, fname, *a, r=r, w=w, **kw)

        psb = mk_sb(top)
        psum = top.enter_context(nc.psum_tensor("psum", [128, 8, 512], F32))
        ident = psb([128, 128], BF16, "ident")
        ones_f = psb([128, 128], F32, "ones_f")
        ltri = psb([128, 128], F32, "ltri")
        ustr = psb([128, 128], F32, "ustr")
        ltri_bf = psb([128, 128], BF16, "ltri_bf")
        cst = psb([128, 4], F32, "cst")
        dt_all = psb([128, NT, 8], F32, "dt_all")

        bank_rr = [0]

        def nbank():
            b = bank_rr[0] % 4
            bank_rr[0] += 1
            return b

        pair_rr = [0]

        def npair():
            b = 4 + 2 * (pair_rr[0] % 2)
            pair_rr[0] += 1
            return b

        def PS(b, n=512):
            return psum[:, b, 0:n]

        def PS2(b, n=1024):
            return psum[:, b:b + 2, :].rearrange("p a b -> p (a b)")[:, 0:n]

        def PS16(b):
            return psum[:, b, :].bitcast(BF16)

        def pk(b):
            return ("ps", b)

        def cols_from_row(dst, row, n, rkey, wkey):
            b = nbank()
            for i in range(n):
                pe("matmul", psum[:, b, i:i + 1], row[0:1, i * 128:(i + 1) * 128], ones_f[0:1, 0:1], start=True, stop=True,
                   r=[rkey, "ones_f"], w=[pk(b)])
            dve("tensor_copy", dst, psum[:, b, 0:n], r=[pk(b)], w=[wkey])

        def phase_const():
            with contextlib.ExitStack() as st:
                sb = mk_sb(st)
                pool("memset", ones_f[:], 1.0, w=["ones_f"])
                pool("memset", cst[:, 0:1], EPS, w=["cst"])
                pool("memset", cst[:, 1:2], 1.0, w=["cst"])
                pool("affine_select", ltri[:], ones_f[:], pattern=[[1, 128]], compare_op=ALU.is_ge,
                     fill=0.0, base=0, channel_multiplier=-1, r=["ones_f"], w=["ltri"])
                pool("affine_select", ustr[:], ones_f[:], pattern=[[-1, 128]], compare_op=ALU.is_gt,
                     fill=0.0, base=0, channel_multiplier=1, r=["ones_f"], w=["ustr"])
                pool("affine_select", ident[:], ones_f[:], pattern=[[1, 128]], compare_op=ALU.is_equal,
                     fill=0.0, base=0, channel_multiplier=-1, r=["ones_f"], w=["ident"])
                dve("tensor_copy", ltri_bf[:], ltri[:], r=["ltri"], w=["ltri_bf"])
                posf = sb([128, NT], F32)
                invf = sb([128, 16], F32)
                ang = sb([128, NT, 16], F32)
                kq = sb([128, NT, 16], F32)
                ki = sb([128, NT, 16], I32)
                m1 = sb([128, NT, 16], F32)
                rc = sb([128, NT, 16], F32)
                cosT = sb([128, NT, 16], F32)
                sinT = sb([128, NT, 16], F32)
                prow_i = sb([1, S], I32)
                prow_f = sb([1, S], F32)
                P.dma(prow_i[:], pos_in, writes=["prow_i"])
                dve("tensor_copy", prow_f[:], prow_i[:], r=["prow_i"], w=["prow_f"])
                cols_from_row(posf[:], prow_f, NT, "prow_f", "posf")
                inv = (1.0 / (np.float32(10000.0) ** (np.arange(0, 32, 2, dtype=np.float32) / np.float32(32)))).astype(np.float32)
                for j in range(16):
                    pool("memset", invf[:, j:j + 1], float(inv[j]), w=["invf"])
                dve("tensor_tensor", ang[:], posf[:].unsqueeze(2).to_broadcast([128, NT, 16]),
                    invf[:].unsqueeze(1).to_broadcast([128, NT, 16]), ALU.mult, r=["posf", "invf"], w=["ang"])
                TWO_PI = 2.0 * math.pi
                C1 = 6.28125
                C2 = TWO_PI - C1
                PI_LO = 3.1415925

                def reduce_to_pi(src, skey, dst, dkey):
                    dve("tensor_scalar", kq[:], src[:], 1.0 / TWO_PI, None, ALU.mult, r=[skey], w=["kq"])
                    dve("tensor_copy", ki[:], kq[:], r=["kq"], w=["ki"])
                    dve("tensor_copy", kq[:], ki[:], r=["ki"], w=["kq"])
                    dve("scalar_tensor_tensor", dst[:], kq[:], -C1, src[:], ALU.mult, ALU.add, r=["kq", skey], w=[dkey])
                    dve("scalar_tensor_tensor", dst[:], kq[:], -C2, dst[:], ALU.mult, ALU.add, r=["kq", dkey], w=[dkey])
                    dve("tensor_scalar", m1[:], dst[:], math.pi, None, ALU.is_gt, r=[dkey], w=["m1"])
                    dve("scalar_tensor_tensor", dst[:], m1[:], -TWO_PI, dst[:], ALU.mult, ALU.add, r=["m1", dkey], w=[dkey])
                    dve("tensor_scalar", m1[:], dst[:], -math.pi, None, ALU.is_lt, r=[dkey], w=["m1"])
                    dve("scalar_tensor_tensor", dst[:], m1[:], TWO_PI, dst[:], ALU.mult, ALU.add, r=["m1", dkey], w=[dkey])
                    dve("tensor_scalar", dst[:], dst[:], PI_LO, -PI_LO, ALU.min, ALU.max, r=[dkey], w=[dkey])

                reduce_to_pi(ang, "ang", rc, "rc")
                act("activation", sinT[:], rc[:], AF.Sin, r=["rc"], w=["sinT"])
                dve("tensor_scalar", ang[:], rc[:], math.pi / 2, None, ALU.add, r=["rc"], w=["ang"])
                reduce_to_pi(ang, "ang", rc, "rc")
                act("activation", cosT[:], rc[:], AF.Sin, r=["rc"], w=["cosT"])
                P.dma(cs_d[0], cosT[:].rearrange("p t j -> p (t j)"), reads=["cosT"])
                P.dma(cs_d[1], sinT[:].rearrange("p t j -> p (t j)"), reads=["sinT"])
                P.flush()

        def mod_gen(l, T):
            cT, wst, mrow, brow, nrow, orow, crow = T
            P.dma(crow[:], c_in, writes=["crow"])
            act("activation", crow[:], crow[:], AF.Silu, r=["crow"], w=["crow"])
            cols_from_row(cT[:], crow, 8, "crow", "cT")
            P.dma(brow[:], b_ada[l:l + 1, :], writes=["brow"])
            P.dma(nrow[:, 0:D], norm1_w[l:l + 1, :], writes=["nrow"])
            P.dma(nrow[:, D:2 * D], norm2_w[l:l + 1, :], writes=["nrow"])
            for n in range(12):
                ws = wst[n % 2]
                P.dma(ws[:], w_ada[l, :, n * 512:(n + 1) * 512].rearrange("(k p) c -> p k c", p=128),
                      writes=[("wst", n % 2)])
                b = nbank()
                for k in range(8):
                    pe("matmul", psum[0:1, b, :], cT[:, k:k + 1], ws[:, k, :], start=(k == 0), stop=(k == 7),
                       r=["cT", ("wst", n % 2)], w=[pk(b)])
                dve("tensor_tensor", mrow[:, n * 512:(n + 1) * 512], psum[0:1, b, :], brow[:, n * 512:(n + 1) * 512],
                    ALU.add, r=[pk(b), "brow"], w=["mrow"])
                yield
            for half in range(2):
                o = half * 3 * D
                dve("scalar_tensor_tensor", orow[:, o:o + D], mrow[:, o + D:o + 2 * D], 1.0, nrow[:, half * D:(half + 1) * D],
                    ALU.add, ALU.mult, r=["mrow", "nrow"], w=["orow"])
                dve("tensor_copy", orow[:, o + D:o + 2 * D], mrow[:, o:o + D], r=["mrow"], w=["orow"])
                dve("tensor_copy", orow[:, o + 2 * D:o + 3 * D], mrow[:, o + 2 * D:o + 3 * D], r=["mrow"], w=["orow"])
            P.dma(modrow_d_L[l].rearrange("(o a) b -> o (a b)", o=1), orow[:], reads=["orow"], writes=[("modrow", l)], eng="act")

        def phase_start():
            with contextlib.ExitStack() as st:
                sb = mk_sb(st)
                T = (sb([128, 8], F32), [sb([128, 8, 512], F32) for _ in range(2)], sb([1, 6 * D], F32), sb([1, 6 * D], F32),
                     sb([1, 2 * D], F32), sb([1, 6 * D], F32), sb([1, D], F32))
                cwrow = sb([1, 4 * 1024], F32); cbrow = sb([1, 1024], F32); cwcb = sb([128, 40], F32)
                for _ in mod_gen(0, T):
                    pass
                for ll in range(nlayers):
                    P.dma(cwrow[:], conv_w[ll:ll + 1].rearrange("o k c -> o (k c)"), writes=["cwrow"])
                    P.dma(cbrow[:], conv_b[ll:ll + 1, :], writes=["cbrow"])
                    cols_from_row(cwcb[:, 0:32], cwrow, 32, "cwrow", "cwcb")
                    cols_from_row(cwcb[:, 32:40], cbrow, 8, "cbrow", "cwcb")
                    P.dma(cwcb_d_L[ll], cwcb[:], reads=["cwcb"])
                wg = wprep_gen(0, sb, 2048, "act", which="a")
                for l in range(1, nlayers):
                    for _ in mod_gen(l, T):
                        for _k in range(5):
                            next(wg, None)
                for _ in wg:
                    pass
                P.flush()


        def wprep_gen(l, sb, CW, store_eng, which="all", plain_eng="pool"):
            specs = list(zip(l, which)) if isinstance(l, (list, tuple)) else [(l, which)]
            sin_ = [sb([128, CW], F32) for _ in range(3)]
            sout = [sb([128, CW], BF16) for _ in range(3)]
            jobs = []

            def add(src, dst, R, C, gate=None):
                for r0 in range(0, R, 128):
                    for c0 in range(0, C, CW):
                        c1 = min(C, c0 + CW)
                        jobs.append((src[r0:r0 + 128, c0:c1], dst[r0:r0 + 128, c0:c1], c1 - c0, c0, gate))
            for (ll, wh) in specs:
                wi = w_in[ll]
                if wh in ("all", "a"):
                    add(wi[:, 0:416], winb_d_L[ll][:, 0:416], D, 416)
                    add(wi[:, 1952:1960], winb_d_L[ll][:, 416:424], D, 8)
                    add(wi[:, 416:1952], winb_d_L[ll][:, 424:1960], D, 1536)
                    add(w_q_up[ll], wqb_d_L[ll], 256, 768)
                    add(w_kv_up[ll], wkvb_d_L[ll], 128, 1024)
                if wh in ("all", "b"):
                    g1b = sb([128, D], F32); g2b = sb([128, D], F32)
                    P.dma(g1b[:], modrow_d_L[ll][2:3, :].partition_broadcast(128), reads=[("modrow", ll)], writes=[("g1b", ll)])
                    P.dma(g2b[:], modrow_d_L[ll][5:6, :].partition_broadcast(128), reads=[("modrow", ll)], writes=[("g2b", ll)])
                    add(w_out[ll], woutb_d_L[ll], D, D, gate=(g1b, ("g1b", ll)))
                    add(w_gate_up[ll], wgub_d_L[ll], D, 2 * D_FF)
                    add(w_down[ll], wdb_d_L[ll], D_FF, D, gate=(g2b, ("g2b", ll)))
            def load(i):
                if i < len(jobs):
                    P.dma(sin_[i % 3][:, 0:jobs[i][2]], jobs[i][0], writes=[("sin", i % 3)])
            load(0)
            load(1)
            for i, (src, dst, cw, c0, gate) in enumerate(jobs):
                k = i % 3
                load(i + 2)
                if gate is not None:
                    gt, gk = gate
                    eng = pool if store_eng == "sp" else (dve if i % 2 == 0 else pool)
                    eng("tensor_tensor", sout[k][:, 0:cw], sin_[k][:, 0:cw], gt[:, c0:c0 + cw], ALU.mult,
                        r=[("sin", k), gk], w=[("sout", k)])
                elif store_eng == "sp" and plain_eng == "act":
                    act("copy", sout[k][:, 0:cw], sin_[k][:, 0:cw], r=[("sin", k)], w=[("sout", k)])
                elif store_eng == "sp":
                    pool("tensor_copy", sout[k][:, 0:cw], sin_[k][:, 0:cw], r=[("sin", k)], w=[("sout", k)])
                elif k == 1:
                    act("copy", sout[k][:, 0:cw], sin_[k][:, 0:cw], r=[("sin", k)], w=[("sout", k)])
                elif k == 0:
                    dve("tensor_copy", sout[k][:, 0:cw], sin_[k][:, 0:cw], r=[("sin", k)], w=[("sout", k)])
                else:
                    pool("tensor_copy", sout[k][:, 0:cw], sin_[k][:, 0:cw], r=[("sin", k)], w=[("sout", k)])
                P.dma(dst, sout[k][:, 0:cw], reads=[("sout", k)], eng=store_eng)
                yield

        def phase_wprep(l):
            with contextlib.ExitStack() as st:
                sb = mk_sb(st)
                cwrow = sb([1, 4 * 1024], F32); cbrow = sb([1, 1024], F32); cwcb = sb([128, 40], F32)
                for ll in range(nlayers):
                    P.dma(cwrow[:], conv_w[ll:ll + 1].rearrange("o k c -> o (k c)"), writes=["cwrow"])
                    P.dma(cbrow[:], conv_b[ll:ll + 1, :], writes=["cbrow"])
                    cols_from_row(cwcb[:, 0:32], cwrow, 32, "cwrow", "cwcb")
                    cols_from_row(cwcb[:, 32:40], cbrow, 8, "cbrow", "cwcb")
                    P.dma(cwcb_d_L[ll], cwcb[:], reads=["cwcb"])
                for _ in wprep_gen(l, sb, 2048, "act"):
                    pass
                P.flush()

        def phase_A(l, xsrc):
            with contextlib.ExitStack() as st:
                sb = mk_sb(st)
                win = sb([128, 8, D_IN], BF16)
                wq = sb([128, 2, 768], BF16)
                wkv = sb([128, 1024], BF16)
                weff = sb([128, D], F32); shb = sb([128, D], F32)
                qaw = sb([128, 256], F32); kvaw = sb([128, 128], F32); kpw = sb([128, 32], F32)
                qnw = sb([128, 64], F32); qpw = sb([128, 32], F32); knw = sb([128, 64], F32)
                cw = sb([128, 4, 8], F32); cb = sb([128, 8], F32); dtb = sb([128, 8], F32)
                invn = sb([128, 27], F32)
                cosT = sb([128, NT, 16], F32)
                sinT = sb([128, NT, 16], F32)
                P.dma(cosT[:].rearrange("p t j -> p (t j)"), cs_d[0], writes=["cosT"])
                P.dma(sinT[:].rearrange("p t j -> p (t j)"), cs_d[1], writes=["sinT"])
                P.dma(win[:], winb_d_L[l].rearrange("(k p) c -> p k c", p=128), writes=["win"])
                P.dma(wq[:], wqb_d_L[l].rearrange("(k p) c -> p k c", p=128), writes=["wq"])
                P.dma(wkv[:], wkvb_d_L[l], writes=["wkv"])
                P.dma(weff[:], modrow_d_L[l][0:1, :].partition_broadcast(128), writes=["weff"])
                P.dma(shb[:], modrow_d_L[l][1:2, :].partition_broadcast(128), writes=["shb"])
                P.dma(qaw[:], q_a_norm_w[l:l + 1, :].partition_broadcast(128), writes=["qaw"])
                P.dma(kvaw[:], kv_a_norm_w[l:l + 1, :].partition_broadcast(128), writes=["kvaw"])
                P.dma(kpw[:], k_pe_norm_w[l:l + 1, :].partition_broadcast(128), writes=["kpw"])
                P.dma(qnw[:], q_nope_norm_w[l:l + 1, :].partition_broadcast(128), writes=["qnw"])
                P.dma(qpw[:], q_pe_norm_w[l:l + 1, :].partition_broadcast(128), writes=["qpw"])
                P.dma(knw[:], k_nope_norm_w[l:l + 1, :].partition_broadcast(128), writes=["knw"])
                P.dma(dtb[:], dt_bias[l:l + 1, :].partition_broadcast(128), writes=["dtb"])
                P.dma(cw[:].rearrange("p k m -> p (k m)"), cwcb_d_L[l][:, 0:32], writes=["cw"])
                P.dma(cb[:], cwcb_d_L[l][:, 32:40], writes=["cb"])
                scale = 96.0 ** -0.5
                dve("tensor_scalar", qnw[:], qnw[:], scale, None, ALU.mult, r=["qnw"], w=["qnw"])
                dve("tensor_scalar", qpw[:], qpw[:], scale, None, ALU.mult, r=["qpw"], w=["qpw"])
                pool("memset", invn[:, 0:1], 1.0 / 256, w=["invn"])
                pool("memset", invn[:, 1:2], 1.0 / 128, w=["invn"])
                pool("memset", invn[:, 2:3], 1.0 / 32, w=["invn"])
                pool("memset", invn[:, 3:11], 1.0 / 64, w=["invn"])
                pool("memset", invn[:, 11:19], 1.0 / 32, w=["invn"])
                pool("memset", invn[:, 19:27], 1.0 / 64, w=["invn"])

                xt = [sb([128, D], F32) for _ in range(4)]
                junk = sb([128, D], BF16)
                ssx = sb([128, 4], F32); rsx = sb([128, 4], F32)
                htmp_2 = [sb([128, D], F32) for _ in range(2)]
                h = [sb([128, D], BF16) for _ in range(2)]
                hT = sb([128, 8, 512], BF16)
                xbcT = [sb([128, 516], BF16) for _ in range(8)]
                dg = sb([128, 8, 4, 128], BF16)
                xsT = sb([128, 8, 512], BF16)
                raw1 = [sb([128, 424], F32) for _ in range(4)]
                zt = [sb([128, 512], BF16) for _ in range(2)]
                dtr = sb([128, 4, 8], F32)
                ss1 = sb([128, 4, 3], F32); rs1 = sb([128, 4, 3], F32)
                latn_2 = [sb([128, 384], BF16) for _ in range(2)]
                latT_2 = [sb([128, 3, 128], BF16) for _ in range(2)]
                qsb = [sb([128, 768], F32) for _ in range(4)]
                kvsb = [sb([128, 8, 64], F32) for _ in range(4)]
                sq_2 = [sb([128, 768], F32) for _ in range(2)]
                ss2 = sb([128, 4, 24], F32); rs2 = sb([128, 4, 24], F32)
                kpn = [sb([128, 32], F32) for _ in range(4)]
                Qf_2 = [sb([128, 8, 96], BF16) for _ in range(2)]; Kf_2 = [sb([128, 8, 96], BF16) for _ in range(2)]
                ra_2 = [sb([128, 8, 32], F32) for _ in range(2)]; rb_2 = [sb([128, 8, 32], F32) for _ in range(2)]
                rq_2 = [sb([128, 8, 32], F32) for _ in range(2)]
                ka_2 = [sb([128, 32], F32) for _ in range(2)]; kb_2 = [sb([128, 32], F32) for _ in range(2)]
                kr_2 = [sb([128, 32], F32) for _ in range(2)]
                vaug = [sb([128, 8, 128], BF16) for _ in range(2)]
                QTb = sb([96, 8, 512], BF16); KTb = sb([96, 8, 512], BF16)
                tok_o = [sb([128, 768], BF16) for _ in range(2)]

                for i in range(2):
                    pool("memset", vaug[i][:], 1.0, w=[("vaug", i)])
                for m in range(8):
                    pool("memset", xbcT[m][:, 512:515], 0.0, w=[("xbcT", m)])
                for m in range(8):
                    for k in range(4):
                        dve("tensor_scalar", dg[:, m, k, :], ident[:], cw[:, k, m:m + 1], None, ALU.mult, r=["ident", "cw"], w=["dg"])

                def n_stats(blk, j):
                    t = blk * 4 + j
                    P.dma(xt[j][:], xsrc[t * 128:(t + 1) * 128, :], writes=[("xt", j)])
                    act("activation", junk[:], xt[j][:], AF.Square, accum_out=ssx[:, j:j + 1],
                        r=[("xt", j)], w=["junk", "ssx"])

                def n_rstd():
                    act("activation", rsx[:], ssx[:], AF.Ln, scale=1.0 / D, bias=cst[:, 0:1], r=["ssx", "cst"], w=["rsx"])
                    act("activation", rsx[:], rsx[:], AF.Exp, scale=-0.5, r=["rsx"], w=["rsx"])

                def n_hT(j, bank=None):
                    hh = h[j % 2]; htmp = htmp_2[j % 2]
                    dve("scalar_tensor_tensor", htmp[:], xt[j][:], rsx[:, j:j + 1], weff[:], ALU.mult, ALU.mult,
                        r=[("xt", j), "rsx", "weff"], w=[("htmp", j % 2)])
                    pool("tensor_tensor", hh[:], htmp[:], shb[:], ALU.add, r=[("htmp", j % 2), "shb"], w=[("h", j % 2)])
                    b = nbank() if bank is None else bank
                    for k in range(8):
                        pe("transpose", PS16(b)[:, k * 128:(k + 1) * 128], hh[:, k * 128:(k + 1) * 128], ident[:],
                           r=[("h", j % 2), "ident"], w=[pk(b)])
                    act("copy", hT[:, :, j * 128:(j + 1) * 128], PS16(b).rearrange("p (k t) -> p k t", k=8),
                        r=[pk(b)], w=["hT"])

                for j in range(4):
                    n_stats(0, j)
                n_rstd()
                for j in range(4):
                    n_hT(j)

                def P_gen(blk):
                    pb = blk % 2
                    for m in range(9):
                        if m == 5:
                            yield
                        if m < 8:
                            b = nbank()
                            c0 = 936 + m * 128
                            for k in range(8):
                                pe("matmul", PS(b), win[:, k, c0:c0 + 128], hT[:, k, :], start=(k == 0), stop=(k == 7),
                                   r=["win", "hT"], w=[pk(b)])
                            cur = xbcT[m]
                            act("copy", cur[:, 0:3], cur[:, 512:515], r=[("xbcT", m)], w=[("xbcT", m)])
                            act("copy", cur[:, 3:515], PS(b), r=[pk(b), ("xbcT", m)], w=[("xbcT", m)])
                        if m >= 1:
                            mm = m - 1
                            b2 = nbank()
                            for kk in range(4):
                                pe("matmul", PS(b2), dg[:, mm, kk, :], xbcT[mm][:, kk:kk + 512], start=(kk == 0), stop=(kk == 3),
                                   r=["dg", ("xbcT", mm)], w=[pk(b2)])
                            act("activation", xsT[:, mm, :], PS(b2), AF.Silu, bias=cb[:, mm:mm + 1], r=[pk(b2), "cb"], w=[("xsT", mm)])
                    yield
                    for g in range(2):
                        P.dma(BT_d[g, :, blk * 512:(blk + 1) * 512], xsT[:, 4 + g, :], reads=[("xsT", 4 + g)], eng="act")
                        P.dma(CT_d[g, :, blk * 512:(blk + 1) * 512], xsT[:, 6 + g, :], reads=[("xsT", 6 + g)], eng="act")
                    for j in range(4):
                        t = blk * 4 + j
                        b = nbank()
                        for m in range(6):
                            pe("transpose", PS16(b)[:, m * 128:(m + 1) * 128], xsT[:, m, j * 128:(j + 1) * 128], ident[:],
                               r=[("xsT", m), "ident"], w=[pk(b)])
                        to = tok_o[j % 2]
                        dve("tensor_copy", to[:], PS16(b)[:, 0:768], r=[pk(b)], w=[("tok_o", j % 2)])
                        P.dma(xs_d[t * 128:(t + 1) * 128, :], to[:, 0:512], reads=[("tok_o", j % 2)], eng="act")
                        P.dma(B_d[t * 128:(t + 1) * 128, :], to[:, 512:768], reads=[("tok_o", j % 2)], eng="act")
                    yield
                    for j in range(4):
                        t = blk * 4 + j
                        b1 = nbank()
                        for k in range(8):
                            pe("matmul", PS(b1, 424), hT[:, k, j * 128:(j + 1) * 128], win[:, k, 0:424], start=(k == 0), stop=(k == 7),
                               r=["hT", "win"], w=[pk(b1)])
                        b2 = nbank()
                        for k in range(8):
                            pe("matmul", PS(b2), hT[:, k, j * 128:(j + 1) * 128], win[:, k, 424:936], start=(k == 0), stop=(k == 7),
                               r=["hT", "win"], w=[pk(b2)])
                        dve("tensor_copy", raw1[j][:], PS(b1, 424), r=[pk(b1)], w=[("raw1", j)])
                        act("copy", zt[j % 2][:], PS(b2), r=[pk(b2)], w=[("zt", j % 2)])
                        P.dma(z_d[t * 128:(t + 1) * 128, :], zt[j % 2][:], reads=[("zt", j % 2)], eng="act")
                        act("activation", junk[:, 0:256], raw1[j][:, 0:256], AF.Square, accum_out=ss1[:, j, 0:1],
                            r=[("raw1", j)], w=["junk", "ss1"])
                        act("activation", junk[:, 0:128], raw1[j][:, 256:384], AF.Square, accum_out=ss1[:, j, 1:2],
                            r=[("raw1", j)], w=["junk", "ss1"])
                        act("activation", junk[:, 0:32], raw1[j][:, 384:416], AF.Square, accum_out=ss1[:, j, 2:3],
                            r=[("raw1", j)], w=["junk", "ss1"])
                        dve("tensor_tensor", dtr[:, j, :], raw1[j][:, 416:424], dtb[:], ALU.add, r=[("raw1", j), "dtb"], w=["dtr"])
                    yield
                    dve("tensor_tensor", ss1[:], ss1[:], invn[:, 0:3].unsqueeze(1).to_broadcast([128, 4, 3]), ALU.mult,
                        r=["ss1", "invn"], w=["ss1"])
                    act("activation", dtr[:], dtr[:], AF.Exp, r=["dtr"], w=["dtr"])
                    act("activation", dt_all[:, blk * 4:(blk + 1) * 4, :], dtr[:], AF.Ln, bias=cst[:, 1:2], r=["dtr", "cst"], w=["dt_all"])
                    act("activation", rs1[:], ss1[:], AF.Ln, bias=cst[:, 0:1], r=["ss1", "cst"], w=["rs1"])
                    act("activation", rs1[:], rs1[:], AF.Exp, scale=-0.5, r=["rs1"], w=["rs1"])

                for _ in P_gen(0):
                    pass
                for blk in range(NB):
                    pb = blk % 2
                    nxt = blk + 1 < NB
                    def La(j):
                        latn = latn_2[j % 2]; latT = latT_2[j % 2]; sq = sq_2[j % 2]
                        dve("scalar_tensor_tensor", latn[:, 0:256], raw1[j][:, 0:256], rs1[:, j, 0:1], qaw[:], ALU.mult, ALU.mult,
                            r=[("raw1", j), "rs1", "qaw"], w=[("latn", j % 2)])
                        dve("scalar_tensor_tensor", latn[:, 256:384], raw1[j][:, 256:384], rs1[:, j, 1:2], kvaw[:], ALU.mult, ALU.mult,
                            r=[("raw1", j), "rs1", "kvaw"], w=[("latn", j % 2)])
                        dve("scalar_tensor_tensor", kpn[j][:], raw1[j][:, 384:416], rs1[:, j, 2:3], kpw[:], ALU.mult, ALU.mult,
                             r=[("raw1", j), "rs1", "kpw"], w=[("kpn", j)])
                        b = 2 if j % 2 == 0 else 6
                        for k in range(3):
                            pe("transpose", PS16(b)[:, k * 128:(k + 1) * 128], latn[:, k * 128:(k + 1) * 128], ident[:],
                               r=[("latn", j % 2), "ident"], w=[pk(b)])
                        act("copy", latT[:], PS16(b)[:, 0:384].rearrange("p (k t) -> p k t", k=3), r=[pk(b)], w=[("latT", j % 2)])
                        bq = 0 if j % 2 == 0 else 4
                        for (n0, n1, bb) in ((0, 512, bq), (512, 768, bq + 1)):
                            for k in range(2):
                                pe("matmul", PS(bb, n1 - n0), latT[:, k, :], wq[:, k, n0:n1], start=(k == 0), stop=(k == 1),
                                   r=[("latT", j % 2), "wq"], w=[pk(bb)])
                        bk = 2 if j % 2 == 0 else 6
                        for hf in range(2):
                            pe("matmul", PS(bk + hf), latT[:, 2, :], wkv[:, hf * 512:(hf + 1) * 512], start=True, stop=True,
                               r=[("latT", j % 2), "wkv"], w=[pk(bk + hf)])
                        return bq, bk

                    def Lb(j, bq, bk):
                        sq = sq_2[j % 2]
                        act("copy", qsb[j][:], PS2(bq, 768), r=[pk(bq), pk(bq + 1)], w=[("qsb", j)])
                        kvv = PS2(bk).rearrange("p (h e) -> p h e", h=8)
                        dve("tensor_copy", kvsb[j][:], kvv[:, :, 0:64], r=[pk(bk), pk(bk + 1)], w=[("kvsb", j)])
                        va = vaug[j % 2]
                        vap = va[:].rearrange("p (c two) e -> p c two e", two=2)
                        for hf in range(2):
                            kvp = PS(bk + hf).rearrange("p (c two e) -> p c two e", two=2, e=128)
                            dve("tensor_copy", vap[:, 2 * hf:2 * hf + 2, 0, 0:64], kvp[:, :, 0, 64:128], r=[pk(bk + hf)], w=[("vaug", j % 2)])
                            dve("tensor_copy", vap[:, 2 * hf:2 * hf + 2, 1, 64:128], kvp[:, :, 1, 64:128], r=[pk(bk + hf)], w=[("vaug", j % 2)])
                        t = blk * 4 + j
                        if True:
                            P.dma(V_d[t * 128:(t + 1) * 128, :], va[:].rearrange("p h e -> p (h e)"), reads=[("vaug", j % 2)], eng="act")
                        act("activation", sq[:], qsb[j][:], AF.Square, r=[("qsb", j)], w=[("sq", j % 2)])
                        sqv = sq[:].rearrange("p (h e) -> p h e", h=8)
                        dve("tensor_reduce", ss2[:, j, 0:8], sqv[:, :, 0:64], AX.X, ALU.add, r=[("sq", j % 2)], w=["ss2"])
                        dve("tensor_reduce", ss2[:, j, 8:16], sqv[:, :, 64:96], AX.X, ALU.add, r=[("sq", j % 2)], w=["ss2"])
                        act("activation", sq[:, 0:512], kvsb[j][:].rearrange("p h e -> p (h e)"), AF.Square, r=[("kvsb", j)], w=[("sq", j % 2)])
                        dve("tensor_reduce", ss2[:, j, 16:24], sq[:, 0:512].rearrange("p (h e) -> p h e", h=8), AX.X, ALU.add,
                            r=[("sq", j % 2)], w=["ss2"])

                    if nxt:
                        for j in range(4):
                            n_stats(blk + 1, j)
                    pend = {}
                    for j in range(5):
                        if j < 4:
                            pend[j] = La(j)
                        if j == 1 and nxt:
                            n_rstd()
                        if j >= 1:
                            Lb(j - 1, *pend[j - 1])
                            if nxt:
                                n_hT(j - 1, bank=(0 if (j - 1) % 2 == 0 else 4))
                    pg = P_gen(blk + 1) if nxt else None
                    dve("tensor_tensor", ss2[:], ss2[:], invn[:, 3:27].unsqueeze(1).to_broadcast([128, 4, 24]), ALU.mult,
                        r=["ss2", "invn"], w=["ss2"])
                    act("activation", rs2[:], ss2[:], AF.Ln, bias=cst[:, 0:1], r=["ss2", "cst"], w=["rs2"])
                    act("activation", rs2[:], rs2[:], AF.Exp, scale=-0.5, r=["rs2"], w=["rs2"])
                    for j in range(4):
                        t = blk * 4 + j
                        sq = sq_2[j % 2]; Qf = Qf_2[j % 2]; Kf = Kf_2[j % 2]; ra = ra_2[j % 2]; rb = rb_2[j % 2]; rq = rq_2[j % 2]
                        ka = ka_2[j % 2]; kb = kb_2[j % 2]; kr = kr_2[j % 2]
                        qv = qsb[j][:].rearrange("p (h e) -> p h e", h=8)
                        cosb = cosT[:, t, :]; sinb = sinT[:, t, :]
                        dve("tensor_tensor", sq[:, 0:512].rearrange("p (h e) -> p h e", h=8), qv[:, :, 0:64],
                            rs2[:, j, 0:8].unsqueeze(2).to_broadcast([128, 8, 64]), ALU.mult, r=[("qsb", j), "rs2"], w=[("sq", j % 2)])
                        dve("tensor_tensor", Qf[:, :, 0:64], sq[:, 0:512].rearrange("p (h e) -> p h e", h=8),
                            qnw[:].unsqueeze(1).to_broadcast([128, 8, 64]), ALU.mult, r=[("sq", j % 2), "qnw"], w=[("Qf", j % 2)])
                        pool("tensor_tensor", ra[:], qv[:, :, 64:96], rs2[:, j, 8:16].unsqueeze(2).to_broadcast([128, 8, 32]), ALU.mult,
                             r=[("qsb", j), "rs2"], w=[("ra", j % 2)])
                        pool("tensor_tensor", ra[:], ra[:], qpw[:].unsqueeze(1).to_broadcast([128, 8, 32]), ALU.mult, r=[("ra", j % 2), "qpw"], w=[("ra", j % 2)])
                        cb8 = cosb.unsqueeze(1).to_broadcast([128, 8, 16]); sb8 = sinb.unsqueeze(1).to_broadcast([128, 8, 16])
                        pool("tensor_tensor", rb[:, :, 0:16], ra[:, :, 0:16], cb8, ALU.mult, r=[("ra", j % 2), "cosT"], w=[("rb", j % 2)])
                        pool("tensor_tensor", rb[:, :, 16:32], ra[:, :, 16:32], cb8, ALU.mult, r=[("ra", j % 2), "cosT"], w=[("rb", j % 2)])
                        pool("tensor_tensor", rq[:, :, 0:16], ra[:, :, 16:32], sb8, ALU.mult, r=[("ra", j % 2), "sinT"], w=[("rq", j % 2)])
                        pool("tensor_tensor", rq[:, :, 16:32], ra[:, :, 0:16], sb8, ALU.mult, r=[("ra", j % 2), "sinT"], w=[("rq", j % 2)])
                        pool("tensor_tensor", Qf[:, :, 64:80], rb[:, :, 0:16], rq[:, :, 0:16], ALU.subtract, r=[("rb", j % 2), ("rq", j % 2)], w=[("Qf", j % 2)])
                        pool("tensor_tensor", Qf[:, :, 80:96], rb[:, :, 16:32], rq[:, :, 16:32], ALU.add, r=[("rb", j % 2), ("rq", j % 2)], w=[("Qf", j % 2)])
                        dve("tensor_tensor", sq[:, 0:512].rearrange("p (h e) -> p h e", h=8), kvsb[j][:],
                            rs2[:, j, 16:24].unsqueeze(2).to_broadcast([128, 8, 64]), ALU.mult, r=[("kvsb", j), "rs2"], w=[("sq", j % 2)])
                        dve("tensor_tensor", Kf[:, :, 0:64], sq[:, 0:512].rearrange("p (h e) -> p h e", h=8),
                            knw[:].unsqueeze(1).to_broadcast([128, 8, 64]), ALU.mult, r=[("sq", j % 2), "knw"], w=[("Kf", j % 2)])
                        dve("tensor_tensor", ka[:, 0:16], kpn[j][:, 0:16], cosb, ALU.mult, r=[("kpn", j), "cosT"], w=[("ka", j % 2)])
                        dve("tensor_tensor", ka[:, 16:32], kpn[j][:, 16:32], cosb, ALU.mult, r=[("kpn", j), "cosT"], w=[("ka", j % 2)])
                        dve("tensor_tensor", kb[:, 0:16], kpn[j][:, 16:32], sinb, ALU.mult, r=[("kpn", j), "sinT"], w=[("kb", j % 2)])
                        dve("tensor_tensor", kb[:, 16:32], kpn[j][:, 0:16], sinb, ALU.mult, r=[("kpn", j), "sinT"], w=[("kb", j % 2)])
                        dve("tensor_tensor", kr[:, 0:16], ka[:, 0:16], kb[:, 0:16], ALU.subtract, r=[("ka", j % 2), ("kb", j % 2)], w=[("kr", j % 2)])
                        dve("tensor_tensor", kr[:, 16:32], ka[:, 16:32], kb[:, 16:32], ALU.add, r=[("ka", j % 2), ("kb", j % 2)], w=[("kr", j % 2)])
                        dve("tensor_copy", Kf[:, :, 64:96], kr[:].unsqueeze(1).to_broadcast([128, 8, 32]), r=[("kr", j % 2)], w=[("Kf", j % 2)])
                        bqt = nbank()
                        for hh_ in range(8):
                            pe("transpose", PS16(bqt)[0:96, hh_ * 128:(hh_ + 1) * 128], Qf[:, hh_, :], ident[:],
                               r=[("Qf", j % 2), "ident"], w=[pk(bqt)])
                        act("copy", QTb[:, :, j * 128:(j + 1) * 128], PS16(bqt)[0:96, :].rearrange("p (h t) -> p h t", h=8),
                            r=[pk(bqt)], w=["QTb"])
                        bkt = nbank()
                        for hh_ in range(8):
                            pe("transpose", PS16(bkt)[0:96, hh_ * 128:(hh_ + 1) * 128], Kf[:, hh_, :], ident[:],
                               r=[("Kf", j % 2), "ident"], w=[pk(bkt)])
                        act("copy", KTb[:, :, j * 128:(j + 1) * 128], PS16(bkt)[0:96, :].rearrange("p (h t) -> p h t", h=8),
                            r=[pk(bkt)], w=["KTb"])
                        if pg is not None:
                            next(pg, None)
                            if j == 3:
                                for _ in pg:
                                    pass
                    P.dma(QT_d[:, :, blk * 512:(blk + 1) * 512].rearrange("h d s -> d h s"), QTb[:], reads=["QTb"], eng="act")
                    P.dma(KT_d[:, :, blk * 512:(blk + 1) * 512].rearrange("h d s -> d h s"), KTb[:], reads=["KTb"], eng="act")
                P.flush()

        def phase_B(l):
            with contextlib.ExitStack() as st:
                sb = mk_sb(st)
                KT = sb([96, 8, S], BF16)
                Vs = sb([128, NT, 8 * 128], BF16)
                QTs = [sb([96, 8, 512], BF16) for _ in range(2)]
                onesb = sb([128, 512], BF16)
                amask = [sb([128, 512], BF16) for _ in range(4)]
                pT = [sb([128, 2, 512], BF16) for _ in range(4)]
                rden = [sb([128, 512], F32) for _ in range(2)]
                aT = [sb([128, 4, 512], BF16) for _ in range(2)]
                pool("memset", onesb[:], 1.0, w=["onesb"])
                for j in range(4):
                    pool("affine_select", amask[j][:], onesb[:], pattern=[[1, 512]], compare_op=ALU.is_ge, fill=0.0,
                         base=-128 * j, channel_multiplier=-1, r=["onesb"], w=[("amask", j)])
                P.dma(KT[:], KT_d.rearrange("h d s -> d h s"), writes=["KT"])
                for t0 in range(0, NT, 8):
                    t1 = min(NT, t0 + 8)
                    P.dma(Vs[:, t0:t1, :], V_d[t0 * 128:t1 * 128, :].rearrange("(t p) f -> p t f", p=128), writes=[("Vs", t0)])

                def load_q(blk):
                    P.dma(QTs[blk % 2][:], QT_d[:, :, blk * 512:(blk + 1) * 512].rearrange("h d s -> d h s"),
                          writes=[("QTs", blk % 2)])

                its = [(blk, hd, p) for blk in range(NB) for hd in range(NH) for p in range(2 * blk + 2)]

                def emit_s(i):
                    blk, hd, p = its[i]
                    if hd == 0 and p == 0:
                        if blk == 0:
                            load_q(0)
                        if blk + 1 < NB:
                            load_q(blk + 1)
                    qb = QTs[blk % 2]
                    sbp = 2 * (i % 3)
                    pp = pT[i % 4]; pkey = ("pT", i % 4)
                    c0 = 256 if p == 2 * blk + 1 else 0
                    for a_ in range(2):
                        kt = 2 * p + a_
                        pe("matmul", psum[:, sbp + a_, c0:512], KT[:, hd, kt * 128:(kt + 1) * 128], qb[:, hd, c0:512], start=True, stop=True,
                           r=["KT", ("QTs", blk % 2)], w=[pk(sbp + a_)])
                    act("activation", pp[:, :, c0:512], psum[:, sbp:sbp + 2, c0:512], AF.Exp, r=[pk(sbp), pk(sbp + 1)], w=[pkey])
                    for a_ in range(2):
                        jd = 2 * p + a_ - 4 * blk
                        if jd >= 0:
                            w_ = 128 * (jd + 1)
                            eng = dve if (jd % 2 == 0 or l + 1 < nlayers or l == 0) else pool
                            eng("tensor_tensor", pp[:, a_, c0:w_], pp[:, a_, c0:w_], amask[jd][:, c0:w_], ALU.mult,
                                r=[pkey, ("amask", jd)], w=[pkey])

                def emit_pv(i):
                    blk, hd, p = its[i]
                    nk = 4 * blk + 4
                    pp = pT[i % 4]; pkey = ("pT", i % 4)
                    bo = 6 + (hd % 2)
                    c0 = 256 if p == 2 * blk + 1 else 0
                    for a_ in range(2):
                        kt = 2 * p + a_
                        pe("matmul", psum[:, bo, c0:512], Vs[:, kt, hd * 128:(hd + 1) * 128], pp[:, a_, c0:512], start=(kt == 0), stop=(kt == nk - 1),
                           r=[("Vs", (kt // 8) * 8), pkey], w=[pk(bo)])
                    if p == 2 * blk + 1:
                        at = aT[blk % 2]
                        rd = rden[hd % 2]
                        c = hd // 2
                        if hd % 2 == 0:
                            dve("reciprocal", rd[64:128, :], psum[64:128, bo, :], r=[pk(bo)], w=[("rden", 0)])
                            dve("tensor_tensor", at[0:64, c, :], psum[0:64, bo, :], rd[64:128, :], ALU.mult,
                                r=[pk(bo), ("rden", 0)], w=[("aT", blk % 2)])
                        else:
                            dve("reciprocal", rd[0:64, :], psum[0:64, bo, :], r=[pk(bo)], w=[("rden", 1)])
                            dve("tensor_tensor", at[64:128, c, :], psum[64:128, bo, :], rd[0:64, :], ALU.mult,
                                r=[pk(bo), ("rden", 1)], w=[("aT", blk % 2)])
                        if hd == NH - 1:
                            P.dma(mixT_d[0:512, blk * 512:(blk + 1) * 512].rearrange("(c p) s -> p c s", p=128), at[:],
                                  reads=[("aT", blk % 2)])

                n = len(its)
                gen = wprep_gen([l, l + 1], sb, 1024, "sp", which=["b", "all"]) if (l == 0 and l + 1 < nlayers) else (wprep_gen(l + 1, sb, 1024, "sp") if l + 1 < nlayers else (wprep_gen(l, sb, 1024, "sp", which="b") if l == 0 else None))
                for i in range(n + 2):
                    if i < n:
                        emit_s(i)
                    if i >= 2:
                        emit_pv(i - 2)
                    if gen is not None and i % 2 == 1:
                        next(gen, None)
                if gen is not None:
                    for _ in gen:
                        pass
                P.flush()

        def phase_C(l):
            with contextlib.ExitStack() as st:
                sb = mk_sb(st)
                ab = sb([128, 8], F32); dsk = sb([128, 8], F32); nwb = sb([128, 512], F32)
                P.dma(ab[:], a_log[l:l + 1, :].partition_broadcast(128), writes=["ab"])
                P.dma(dsk[:], d_skip[l:l + 1, :].partition_broadcast(128), writes=["dsk"])
                P.dma(nwb[:], ssd_norm_w[l:l + 1, :].partition_broadcast(128), writes=["nwb"])
                act("activation", ab[:], ab[:], AF.Exp, r=["ab"], w=["ab"])
                dve("tensor_scalar", ab[:], ab[:], -1.0, None, ALU.mult, r=["ab"], w=["ab"])
                BTs = sb([128, 2, S], BF16); CTs = sb([128, 2, S], BF16)
                P.dma(BTs[:], BT_d.rearrange("g n s -> n g s"), writes=["BTs"])
                P.dma(CTs[:], CT_d.rearrange("g n s -> n g s"), writes=["CTs"])
                xs4 = [sb([128, 4, 512], BF16) for _ in range(2)]
                B4 = [sb([128, 4, 256], BF16) for _ in range(2)]
                z4 = [sb([128, 4, 512], BF16) for _ in range(2)]
                sz4 = [sb([128, 4, 512], F32) for _ in range(2)]
                adt_2 = [sb([128, 8], F32) for _ in range(3)]
                rhs_all_2 = [sb([128, 8, 128], F32) for _ in range(3)]
                ET_2 = [sb([128, 8, 128], BF16) for _ in range(3)]
                small_2 = [sb([128, 16], F32) for _ in range(3)]
                dec_2 = [sb([128, 8], F32) for _ in range(3)]
                sm_2 = [sb([128, 2, 128], BF16) for _ in range(3)]
                MT_2 = [sb([128, 8, 128], BF16) for _ in range(3)]
                xdt_2 = [sb([128, 8, 64], BF16) for _ in range(3)]; xdd_2 = [sb([128, 8, 64], BF16) for _ in range(3)]
                state = sb([128, 8, 64], F32); state_bf = sb([128, 8, 64], BF16)
                yo_2 = [sb([128, 8, 64], F32) for _ in range(3)]; y_2 = [sb([128, 512], F32) for _ in range(3)]
                xsd_2 = [sb([128, 512], F32) for _ in range(3)]
                yg = sb([128, 512], F32); junk = sb([128, 256], F32)
                ssg = sb([128, 4, 2], F32); rsg = sb([128, 4, 2], F32)
                yg4 = sb([128, 4, 512], F32)
                yn_2 = [sb([128, 512], BF16) for _ in range(2)]
                ygT = sb([128, 4, 512], BF16)
                dve("memset", state[:], 0.0, w=["state"])
                dve("memset", state_bf[:], 0.0, w=["state_bf"])
                def load_blk(blk):
                    pb = blk % 2
                    rows = slice(blk * 512, (blk + 1) * 512)
                    P.dma(xs4[pb][:], xs_d[rows, :].rearrange("(j p) f -> p j f", p=128), writes=[("xs4", pb)])
                    P.dma(B4[pb][:], B_d[rows, :].rearrange("(j p) f -> p j f", p=128), writes=[("B4", pb)])
                    P.dma(z4[pb][:], z_d[rows, :].rearrange("(j p) f -> p j f", p=128), writes=[("z4", pb)])
                    act("activation", sz4[pb][:], z4[pb][:], AF.Silu, r=[("z4", pb)], w=[("sz4", pb)])

                def s1(blk, j):
                    pb = blk % 2
                    jb = (blk * 4 + j) % 3
                    t = blk * 4 + j
                    cs = slice(t * 128, (t + 1) * 128)
                    dtt = dt_all[:, t, :]
                    adt = adt_2[jb]; rhs_all = rhs_all_2[jb]; ET = ET_2[jb]; small = small_2[jb]; dec = dec_2[jb]
                    sm = sm_2[jb]; MT = MT_2[jb]; xdt = xdt_2[jb]; xdd = xdd_2[jb]
                    yo = yo_2[jb]; y = y_2[jb]; xsd = xsd_2[jb]
                    dve("tensor_tensor", adt[:], dtt, ab[:], ALU.mult, r=["dt_all", "ab"], w=[("adt", jb)])
                    pool("tensor_tensor", rhs_all[:], ltri[:].unsqueeze(1).to_broadcast([128, 8, 128]),
                        adt[:].unsqueeze(2).to_broadcast([128, 8, 128]), ALU.mult, r=["ltri", ("adt", jb)], w=[("rhs_all", jb)])
                    bs = npair()
                    for hf in range(2):
                        pe("matmul", PS(bs + hf), ustr[:], rhs_all[:, hf * 4:(hf + 1) * 4, :].rearrange("p h l -> p (h l)"),
                           start=True, stop=True, r=["ustr", ("rhs_all", jb)], w=[pk(bs + hf)])
                    bsm = nbank()
                    pe("matmul", PS(bsm, 8), ltri[:], adt[:], start=True, stop=True, r=["ltri", ("adt", jb)], w=[pk(bsm)])
                    pe("matmul", psum[:, bsm, 8:16], ones_f[:], adt[:], start=True, stop=True, r=["ones_f", ("adt", jb)], w=[pk(bsm)])
                    segv = PS2(bs).rearrange("p (h l) -> p h l", h=8)
                    act("activation", ET[:], segv, AF.Exp, r=[pk(bs), pk(bs + 1)], w=[("ET", jb)])
                    act("activation", dec[:], segv[:, :, 127], AF.Exp, r=[pk(bs), pk(bs + 1)], w=[("dec", jb)])
                    act("activation", small[:], PS(bsm, 16), AF.Exp, r=[pk(bsm)], w=[("small", jb)])
                    bsc = nbank()
                    for g in range(2):
                        pe("matmul", psum[:, bsc, g * 128:(g + 1) * 128], BTs[:, g, cs], CTs[:, g, cs], start=True, stop=True,
                           r=["BTs", "CTs"], w=[pk(bsc)])
                    dve("tensor_tensor", sm[:], PS(bsc, 256).rearrange("p (g l) -> p g l", g=2),
                        ltri_bf[:].unsqueeze(1).to_broadcast([128, 2, 128]), ALU.mult, r=[pk(bsc), "ltri_bf"], w=[("sm", jb)])
                    for g in range(2):
                        dve("tensor_tensor", MT[:, g * 4:(g + 1) * 4, :], ET[:, g * 4:(g + 1) * 4, :],
                            sm[:, g:g + 1, :].to_broadcast([128, 4, 128]), ALU.mult, r=[("ET", jb), ("sm", jb)], w=[("MT", jb)])
                    xsv = xs4[pb][:, j, :].rearrange("p (h e) -> p h e", h=8)
                    pool("tensor_tensor", xdt[:], xsv, dtt.unsqueeze(2).to_broadcast([128, 8, 64]), ALU.mult,
                         r=[("xs4", pb), "dt_all"], w=[("xdt", jb)])
                    pool("tensor_tensor", xdd[:], xdt[:], dec[:].unsqueeze(2).to_broadcast([128, 8, 64]), ALU.mult,
                         r=[("xdt", jb), ("dec", jb)], w=[("xdd", jb)])

                def s2(blk, j):
                    pb = blk % 2
                    jb = (blk * 4 + j) % 3
                    t = blk * 4 + j
                    cs = slice(t * 128, (t + 1) * 128)
                    adt = adt_2[jb]; rhs_all = rhs_all_2[jb]; ET = ET_2[jb]; small = small_2[jb]; dec = dec_2[jb]
                    sm = sm_2[jb]; MT = MT_2[jb]; xdt = xdt_2[jb]; xdd = xdd_2[jb]
                    yo = yo_2[jb]; y = y_2[jb]; xsd = xsd_2[jb]
                    xsv = xs4[pb][:, j, :].rearrange("p (h e) -> p h e", h=8)
                    byd = nbank()
                    for hd in range(8):
                        pe("matmul", psum[:, byd, hd * 64:(hd + 1) * 64], MT[:, hd, :], xdt[:, hd, :], start=True, stop=True,
                           r=[("MT", jb), ("xdt", jb)], w=[pk(byd)])
                    byo = nbank()
                    for hd in range(8):
                        pe("matmul", psum[:, byo, hd * 64:(hd + 1) * 64], CTs[:, hd // 4, cs], state_bf[:, hd, :], start=True, stop=True,
                           r=["CTs", "state_bf"], w=[pk(byo)])
                    bst = nbank()
                    for hd in range(8):
                        pe("matmul", psum[:, bst, hd * 64:(hd + 1) * 64], B4[pb][:, j, (hd // 4) * 128:(hd // 4 + 1) * 128],
                           xdd[:, hd, :], start=True, stop=True, r=[("B4", pb), ("xdd", jb)], w=[pk(bst)])
                    dve("tensor_tensor", yo[:], PS(byo).rearrange("p (h e) -> p h e", h=8),
                        small[:, 0:8].unsqueeze(2).to_broadcast([128, 8, 64]), ALU.mult, r=[pk(byo), ("small", jb)], w=[("yo", jb)])
                    dve("tensor_tensor", y[:], yo[:].rearrange("p h e -> p (h e)"), PS(byd), ALU.add, r=[("yo", jb), pk(byd)], w=[("y", jb)])
                    pool("tensor_tensor", xsd[:].rearrange("p (h e) -> p h e", h=8), xsv,
                         dsk[:].unsqueeze(2).to_broadcast([128, 8, 64]), ALU.mult, r=[("xs4", pb), "dsk"], w=[("xsd", jb)])
                    pool("tensor_tensor", y[:], y[:], xsd[:], ALU.add, r=[("y", jb), ("xsd", jb)], w=[("y", jb)])
                    dve("tensor_tensor", state[:], state[:], small[:, 8:16].unsqueeze(2).to_broadcast([128, 8, 64]), ALU.mult,
                        r=["state", ("small", jb)], w=["state"])
                    dve("tensor_tensor", state[:], state[:], PS(bst).rearrange("p (h e) -> p h e", h=8), ALU.add,
                        r=["state", pk(bst)], w=["state"])
                    act("copy", state_bf[:], state[:], r=["state"], w=["state_bf"])
                    dve("tensor_tensor", yg4[:, j, :], y[:], sz4[pb][:, j, :], ALU.mult, r=[("y", jb), ("sz4", pb)], w=["yg4"])
                    for g in range(2):
                        act("activation", junk[:], yg4[:, j, g * 256:(g + 1) * 256], AF.Square, accum_out=ssg[:, j, g:g + 1],
                            r=["yg4"], w=["junk", "ssg"])

                def end_blk(blk):
                    act("activation", rsg[:], ssg[:], AF.Ln, scale=1.0 / 256, bias=cst[:, 0:1], r=["ssg", "cst"], w=["rsg"])
                    act("activation", rsg[:], rsg[:], AF.Exp, scale=-0.5, r=["rsg"], w=["rsg"])
                    for j in range(4):
                        yn = yn_2[j % 2]
                        for g in range(2):
                            dve("scalar_tensor_tensor", yn[:, g * 256:(g + 1) * 256], yg4[:, j, g * 256:(g + 1) * 256], rsg[:, j, g:g + 1],
                                nwb[:, g * 256:(g + 1) * 256], ALU.mult, ALU.mult, r=["yg4", "rsg", "nwb"], w=[("yn", j % 2)])
                        b = nbank()
                        for cch in range(4):
                            pe("transpose", PS16(b)[:, cch * 128:(cch + 1) * 128], yn[:, cch * 128:(cch + 1) * 128], ident[:],
                               r=[("yn", j % 2), "ident"], w=[pk(b)])
                        act("copy", ygT[:, :, j * 128:(j + 1) * 128], PS16(b)[:, 0:512].rearrange("p (c t) -> p c t", c=4),
                            r=[pk(b)], w=["ygT"])
                    P.dma(mixT_d[512:1024, blk * 512:(blk + 1) * 512].rearrange("(c p) s -> p c s", p=128), ygT[:],
                          reads=["ygT"], eng="act")

                wgc = None
                for i in range(NT + 2):
                    if wgc is not None:
                        for _k in range(3):
                            next(wgc, None)
                    if i < NT:
                        if i % 4 == 0:
                            load_blk(i // 4)
                        s1(i // 4, i % 4)
                    if i >= 2:
                        s2((i - 2) // 4, (i - 2) % 4)
                        if (i - 2) % 4 == 3:
                            end_blk((i - 2) // 4)
                if wgc is not None:
                    for _ in wgc:
                        pass
                P.flush()

        def phase_D(l, xsrc, xdst):
            with contextlib.ExitStack() as st:
                sb = mk_sb(st)
                wo = sb([128, 8, D], BF16)
                P.dma(wo[:], woutb_d_L[l].rearrange("(k p) c -> p k c", p=128), writes=["wo"])
                mT = [sb([128, 8, 512], BF16) for _ in range(2)]
                xt = [sb([128, D], F32) for _ in range(4)]
                for blk in range(NB):
                    pb = blk % 2
                    P.dma(mT[pb][:], mixT_d[:, blk * 512:(blk + 1) * 512].rearrange("(k p) s -> p k s", p=128), writes=[("mT", pb)])
                    for j in range(4):
                        t = blk * 4 + j
                        i2 = t % 4
                        P.dma(xt[i2][:], xsrc[t * 128:(t + 1) * 128, :], writes=[("xt", i2)])
                        bp = npair()
                        for hf in range(2):
                            for k in range(8):
                                pe("matmul", PS(bp + hf), mT[pb][:, k, j * 128:(j + 1) * 128], wo[:, k, hf * 512:(hf + 1) * 512],
                                   start=(k == 0), stop=(k == 7), r=[("mT", pb), "wo"], w=[pk(bp + hf)])
                        dve("tensor_tensor", xt[i2][:], PS2(bp), xt[i2][:], ALU.add, r=[pk(bp), pk(bp + 1), ("xt", i2)], w=[("xt", i2)])
                        P.dma(xdst[t * 128:(t + 1) * 128, :], xt[i2][:], reads=[("xt", i2)], eng="act")
                P.flush()

        def phase_E(l, xsrc, xdst):
            with contextlib.ExitStack() as st:
                sb = mk_sb(st)
                NF = D_FF // 128
                wgu = sb([128, 8, 2 * D_FF], BF16)
                wd = sb([128, NF, D], BF16)
                weff = sb([128, D], F32); shb = sb([128, D], F32)
                for k in range(8):
                    P.dma(wgu[:, k, :], wgub_d_L[l][k * 128:(k + 1) * 128, :], writes=[("wgu", k)])
                P.dma(wd[:], wdb_d_L[l].rearrange("(k p) c -> p k c", p=128), writes=["wd"])
                P.dma(weff[:], modrow_d_L[l][3:4, :].partition_broadcast(128), writes=["weff"])
                P.dma(shb[:], modrow_d_L[l][4:5, :].partition_broadcast(128), writes=["shb"])
                xt = [sb([128, D], F32) for _ in range(4)]
                ssx = sb([128, 4], F32); rsx = sb([128, 4], F32)
                htmp = [sb([128, D], F32)] * 2
                h = [sb([128, D], BF16) for _ in range(4)]
                hT = sb([128, 8, 512], BF16)
                sg = [sb([128, 512], F32) for _ in range(2)]
                aT = sb([128, NF, 512], BF16)

                def n_stats(blk, j):
                    t = blk * 4 + j
                    P.dma(xt[j][:], xsrc[t * 128:(t + 1) * 128, :], writes=[("xt", j)])
                    act("activation", h[3][:], xt[j][:], AF.Square, accum_out=ssx[:, j:j + 1],
                        r=[("xt", j)], w=[("h", 3), "ssx"])

                def n_rstd():
                    act("activation", rsx[:], ssx[:], AF.Ln, scale=1.0 / D, bias=cst[:, 0:1], r=["ssx", "cst"], w=["rsx"])
                    act("activation", rsx[:], rsx[:], AF.Exp, scale=-0.5, r=["rsx"], w=["rsx"])

                def n_h(j):
                    ht = htmp[j % 2]
                    dve("scalar_tensor_tensor", ht[:], xt[j][:], rsx[:, j:j + 1], weff[:], ALU.mult, ALU.mult,
                        r=[("xt", j), "rsx", "weff"], w=["htmp"])
                    pool("tensor_tensor", h[j][:], ht[:], shb[:], ALU.add, r=["htmp", "shb"], w=[("h", j)])

                def n_T():
                    for j in range(4):
                        b = nbank()
                        for k in range(8):
                            pe("transpose", PS16(b)[:, k * 128:(k + 1) * 128], h[j][:, k * 128:(k + 1) * 128], ident[:],
                               r=[("h", j), "ident"], w=[pk(b)])
                        act("copy", hT[:, :, j * 128:(j + 1) * 128], PS16(b).rearrange("p (k t) -> p k t", k=8),
                            r=[pk(b)], w=["hT"])

                for j in range(4):
                    n_stats(0, j)
                n_rstd()
                for j in range(4):
                    n_h(j)
                n_T()
                for blk in range(NB):
                    nxt = blk + 1 < NB
                    for f in range(NF):
                        bg = nbank()
                        for k in range(8):
                            pe("matmul", PS(bg), wgu[:, k, f * 128:(f + 1) * 128], hT[:, k, :], start=(k == 0), stop=(k == 7),
                               r=[("wgu", k), "hT"], w=[pk(bg)])
                        bu = nbank()
                        for k in range(8):
                            pe("matmul", PS(bu), wgu[:, k, D_FF + f * 128:D_FF + (f + 1) * 128], hT[:, k, :], start=(k == 0), stop=(k == 7),
                               r=[("wgu", k), "hT"], w=[pk(bu)])
                        act("activation", sg[f % 2][:], PS(bg), AF.Silu, r=[pk(bg)], w=[("sg", f % 2)])
                        dve("tensor_tensor", aT[:, f, :], PS(bu), sg[f % 2][:], ALU.mult, r=[pk(bu), ("sg", f % 2)], w=["aT"])
                        if nxt:
                            if f in (1, 3, 5, 7):
                                n_stats(blk + 1, (f - 1) // 2)
                            elif f == 9:
                                n_rstd()
                            elif f in (11, 13, 15, 17):
                                n_h((f - 11) // 2)
                    for j in range(4):
                        t = blk * 4 + j
                        P.dma(xt[j][:], xsrc[t * 128:(t + 1) * 128, :], writes=[("xt", j)])
                        bp = npair()
                        for hf in range(2):
                            for f in range(NF):
                                pe("matmul", PS(bp + hf), aT[:, f, j * 128:(j + 1) * 128], wd[:, f, hf * 512:(hf + 1) * 512],
                                   start=(f == 0), stop=(f == NF - 1), r=["aT", "wd"], w=[pk(bp + hf)])
                        if j == 0 and nxt:
                            n_T()
                        dve("tensor_tensor", xt[j][:], PS2(bp), xt[j][:], ALU.add, r=[pk(bp), pk(bp + 1), ("xt", j)], w=[("xt", j)])
                        P.dma(xdst[t * 128:(t + 1) * 128, :], xt[j][:], reads=[("xt", j)], eng="act")
                P.flush()

        stages = []
        stages.append(("const", phase_const))
        stages.append(("wprep0", phase_start))
        for l in range(nlayers):
            xsrc = x_in if l == 0 else xb_d
            xfin = out if l == nlayers - 1 else xb_d
            stages.append(("A%d" % l, partial(phase_A, l, xsrc)))
            stages.append(("B%d" % l, partial(phase_B, l)))
            stages.append(("C%d" % l, partial(phase_C, l)))
            stages.append(("D%d" % l, partial(phase_D, l, xsrc, xa_d)))
            stages.append(("E%d" % l, partial(phase_E, l, xa_d, xfin)))
        for name, fn in stages:
            fn()
            if upto is not None and name == upto:
                break
    return nc


_INPUT_NAMES = ["norm1_w", "norm2_w", "w_ada", "b_ada", "w_in", "q_a_norm_w", "w_q_up", "kv_a_norm_w", "w_kv_up",
                "q_nope_norm_w", "q_pe_norm_w", "k_nope_norm_w", "k_pe_norm_w", "conv_w", "conv_b", "dt_bias",
                "a_log", "d_skip", "ssd_norm_w", "w_out", "w_gate_up", "w_down"]


def kernel(x, c, positions, **w):
    x = np.asarray(x); c = np.asarray(c); positions = np.asarray(positions)
    Bn, S, _ = x.shape
    nc = build(S)
    shared = {k: np.ascontiguousarray(np.asarray(w[k], dtype=np.float32)) for k in _INPUT_NAMES}
    in_maps = []
    for b in range(Bn):
        m = dict(shared)
        m["x"] = np.ascontiguousarray(x[b], dtype=np.float32)
        m["c"] = np.ascontiguousarray(c[b:b + 1], dtype=np.float32)
        m["positions"] = np.ascontiguousarray(positions[b:b + 1], dtype=np.int32)
        in_maps.append(m)
    res = run_bass_kernel_spmd(nc, in_maps, core_ids=list(range(Bn)))
    return np.stack([np.asarray(r["out"], dtype=np.float32) for r in res.results], axis=0)
```

```python
import contextlib
import math
from functools import partial

import numpy as np
import concourse.bass as bass
import concourse.mybir as mybir
from concourse.bass_utils import run_bass_kernel_spmd

F32 = mybir.dt.float32
BF16 = mybir.dt.bfloat16
I32 = mybir.dt.int32
ALU = mybir.AluOpType
AF = mybir.ActivationFunctionType
AX = mybir.AxisListType

D = 1024
DEPTH = 2
NH = 8
D_IN = 1960
D_FF = 2816
EPS = 1e-6
ENGS = ("pe", "act", "dve", "pool", "sp")
N_DMA_SEMS = 16


class _Op:
    __slots__ = ("eng", "fn", "deps", "dma", "signal", "count", "sem")


class Prog:
    def __init__(self, nc, stack):
        self.nc = nc
        self.esem = {e: stack.enter_context(nc.semaphore("s_" + e)) for e in ENGS}
        self.dsem = [stack.enter_context(nc.semaphore("d%d" % i)) for i in range(N_DMA_SEMS)]
        self.ecnt = {e: 0 for e in ENGS}
        self.dcnt = [0] * N_DMA_SEMS
        self.nd = 0
        self.nops = 0
        self._reset()

    def _reset(self):
        self.ops = []
        self.last_w = {}
        self.readers = {}
        self.eng_ops = {e: [] for e in ENGS}

    def op(self, eng, fn, reads=(), writes=(), dma=False):
        o = _Op()
        o.eng, o.fn, o.dma, o.signal, o.count, o.sem = eng, fn, dma, False, 0, None
        need = {}
        for r in reads:
            w = self.last_w.get(r)
            if w is not None:
                self._dep(need, o, w, "raw")
        for r in writes:
            w = self.last_w.get(r)
            if w is not None:
                self._dep(need, o, w, "waw")
            for rd in self.readers.get(r, ()):
                self._dep(need, o, rd, "war")
        o.deps = list(need.values())
        for r in reads:
            self.readers.setdefault(r, []).append(o)
        for r in writes:
            self.last_w[r] = o
            self.readers[r] = []
        self.ops.append(o)
        self.eng_ops[eng].append(o)
        return o

    @staticmethod
    def _dep(need, o, d, kind):
        if d is o:
            return
        if d.eng == o.eng and not d.dma and not o.dma and o.eng == "pe":
            return
        need[id(d)] = d

    def dma(self, out, in_, reads=(), writes=(), eng="sp", **kw):
        q = {"sp": self.nc.sync, "act": self.nc.scalar, "pool": self.nc.gpsimd}[eng]
        return self.op(eng, partial(q.dma_start, out=out, in_=in_, **kw), reads, writes, dma=True)

    def flush(self):
        nc = self.nc
        self.nops += len(self.ops)
        for o in self.ops:
            for d in o.deps:
                d.signal = True
        for e in ENGS:
            comp = [o for o in self.eng_ops[e] if not o.dma]
            if comp:
                comp[-1].signal = True
        dlast = [None] * N_DMA_SEMS
        for o in self.ops:
            if o.dma:
                o.signal = True
                k = self.nd % N_DMA_SEMS
                self.nd += 1
                self.dcnt[k] += 16
                o.sem, o.count = self.dsem[k], self.dcnt[k]
                if dlast[k] is not None:
                    o.deps.append(dlast[k])
                dlast[k] = o
            elif o.signal:
                self.ecnt[o.eng] += 1
                o.sem, o.count = self.esem[o.eng], self.ecnt[o.eng]
        dcnt = list(self.dcnt)
        ecnt = dict(self.ecnt)
        eng_ops = self.eng_ops
        dsem, esem = self.dsem, self.esem

        def run(engname, e):
            waited = {}
            for o in eng_ops[engname]:
                for d in o.deps:
                    key = id(d.sem)
                    if waited.get(key, 0) >= d.count:
                        continue
                    e.wait_ge(d.sem, d.count)
                    waited[key] = d.count
                ins = o.fn()
                if o.signal:
                    ins.then_inc(o.sem, 16 if o.dma else 1)
            for en in ENGS:
                if ecnt[en] and waited.get(id(esem[en]), 0) < ecnt[en]:
                    e.wait_ge(esem[en], ecnt[en])
            for k in range(N_DMA_SEMS):
                if dcnt[k] and waited.get(id(dsem[k]), 0) < dcnt[k]:
                    e.wait_ge(dsem[k], dcnt[k])

        with nc.Block() as blk:
            @blk.tensor
            def _(e):
                run("pe", e)

            @blk.scalar
            def _(e):
                run("act", e)

            @blk.vector
            def _(e):
                run("dve", e)

            @blk.gpsimd
            def _(e):
                run("pool", e)

            @blk.sync
            def _(e):
                run("sp", e)
        self._reset()


DBG_STOP = [None]


def build(S, nlayers=DEPTH, debug=False, upto=None):
    NT = S // 128
    NB = S // 512
    nc = bass.Bass("TRN2", target_bir_lowering=False)
    okind = "ExternalOutput" if debug else "Internal"

    def din(name, shape, dt=F32):
        return nc.dram_tensor(name, list(shape), dt, kind="ExternalInput").ap()

    def dscr(name, shape, dt):
        return nc.dram_tensor(name, list(shape), dt, kind=okind).ap()

    x_in = din("x", [S, D])
    c_in = din("c", [1, D])
    pos_in = din("positions", [1, S], I32)
    Ld = DEPTH
    norm1_w = din("norm1_w", [Ld, D]); norm2_w = din("norm2_w", [Ld, D])
    w_ada = din("w_ada", [Ld, D, 6 * D]); b_ada = din("b_ada", [Ld, 6 * D])
    w_in = din("w_in", [Ld, D, D_IN])
    q_a_norm_w = din("q_a_norm_w", [Ld, 256]); w_q_up = din("w_q_up", [Ld, 256, 768])
    kv_a_norm_w = din("kv_a_norm_w", [Ld, 128]); w_kv_up = din("w_kv_up", [Ld, 128, 1024])
    q_nope_norm_w = din("q_nope_norm_w", [Ld, 64]); q_pe_norm_w = din("q_pe_norm_w", [Ld, 32])
    k_nope_norm_w = din("k_nope_norm_w", [Ld, 64]); k_pe_norm_w = din("k_pe_norm_w", [Ld, 32])
    conv_w = din("conv_w", [Ld, 4, 1024]); conv_b = din("conv_b", [Ld, 1024])
    dt_bias = din("dt_bias", [Ld, 8]); a_log = din("a_log", [Ld, 8]); d_skip = din("d_skip", [Ld, 8])
    ssd_norm_w = din("ssd_norm_w", [Ld, 512])
    w_out = din("w_out", [Ld, D, D]); w_gate_up = din("w_gate_up", [Ld, D, 2 * D_FF])
    w_down = din("w_down", [Ld, D_FF, D])
    out = nc.dram_tensor("out", [S, D], F32, kind="ExternalOutput").ap()

    modrow_d_L = [dscr("modrow_d%d" % i, [6, D], F32) for i in range(DEPTH)]
    winb_d_L = [dscr("winb_d%d" % i, [D, D_IN], BF16) for i in range(DEPTH)]
    wqb_d_L = [dscr("wqb_d%d" % i, [256, 768], BF16) for i in range(DEPTH)]
    wkvb_d_L = [dscr("wkvb_d%d" % i, [128, 1024], BF16) for i in range(DEPTH)]
    woutb_d_L = [dscr("woutb_d%d" % i, [D, D], BF16) for i in range(DEPTH)]
    wgub_d_L = [dscr("wgub_d%d" % i, [D, 2 * D_FF], BF16) for i in range(DEPTH)]
    wdb_d_L = [dscr("wdb_d%d" % i, [D_FF, D], BF16) for i in range(DEPTH)]
    QT_d = dscr("QT_d", [NH, 96, S], BF16)
    KT_d = dscr("KT_d", [NH, 96, S], BF16)
    V_d = dscr("V_d", [S, NH * 128], BF16)
    z_d = dscr("z_d", [S, 512], BF16)
    xs_d = dscr("xs_d", [S, 512], BF16)
    B_d = dscr("B_d", [S, 256], BF16)
    BT_d = dscr("BT_d", [2, 128, S], BF16)
    CT_d = dscr("CT_d", [2, 128, S], BF16)
    mixT_d = dscr("mixT_d", [D, S], BF16)
    cs_d = dscr("cs_d", [2, 128, NT * 16], F32)
    cwcb_d_L = [dscr("cwcb_d%d" % i, [128, 40], F32) for i in range(DEPTH)]
    xa_d = dscr("xa_d", [S, D], F32)
    xb_d = dscr("xb_d", [S, D], F32)

    with contextlib.ExitStack() as top:
        P = Prog(nc, top)

        gcnt = [0]

        def mk_sb(st):
            def sb(shape, dt=F32, name=None):
                gcnt[0] += 1
                return st.enter_context(nc.sbuf_tensor(name or ("t%d" % gcnt[0]), list(shape), dt))
            return sb

        def E(eng, obj, fname, *a, r=(), w=(), **kw):
            return P.op(eng, partial(getattr(obj, fname), *a, **kw), r, w)

        def dve(fname, *a, r=(), w=(), **kw):
            return E("dve", nc.vector, fname, *a, r=r, w=w, **kw)

        def pool(fname, *a, r=(), w=(), **kw):
            return E("pool", nc.gpsimd, fname, *a, r=r, w=w, **kw)

        def act(fname, *a, r=(), w=(), **kw):
            return E("act", nc.scalar, fname, *a, r=r, w=w, **kw)

        def pe(fname, *a, r=(), w=(), **kw):
            return E("pe", nc.tensor, fname, *a, r=r, w=w, **kw)

        psb = mk_sb(top)
        psum = top.enter_context(nc.psum_tensor("psum", [128, 8, 512], F32))
        ident = psb([128, 128], BF16, "ident")
        ones_f = psb([128, 128], F32, "ones_f")
        ltri = psb([128, 128], F32, "ltri")
        ustr = psb([128, 128], F32, "ustr")
        ltri_bf = psb([128, 128], BF16, "ltri_bf")
        cst = psb([128, 4], F32, "cst")
        dt_all = psb([128, NT, 8], F32, "dt_all")

        bank_rr = [0]

        def nbank():
            b = bank_rr[0] % 4
            bank_rr[0] += 1
            return b

        pair_rr = [0]

        def npair():
            b = 4 + 2 * (pair_rr[0] % 2)
            pair_rr[0] += 1
            return b

        def PS(b, n=512):
            return psum[:, b, 0:n]

        def PS2(b, n=1024):
            return psum[:, b:b + 2, :].rearrange("p a b -> p (a b)")[:, 0:n]

        def PS16(b):
            return psum[:, b, :].bitcast(BF16)

        def pk(b):
            return ("ps", b)

        def cols_from_row(dst, row, n, rkey, wkey):
            b = nbank()
            for i in range(n):
                pe("matmul", psum[:, b, i:i + 1], row[0:1, i * 128:(i + 1) * 128], ones_f[0:1, 0:1], start=True, stop=True,
                   r=[rkey, "ones_f"], w=[pk(b)])
            dve("tensor_copy", dst, psum[:, b, 0:n], r=[pk(b)], w=[wkey])

        def phase_const():
            with contextlib.ExitStack() as st:
                sb = mk_sb(st)
                pool("memset", ones_f[:], 1.0, w=["ones_f"])
                pool("memset", cst[:, 0:1], EPS, w=["cst"])
                pool("memset", cst[:, 1:2], 1.0, w=["cst"])
                pool("affine_select", ltri[:], ones_f[:], pattern=[[1, 128]], compare_op=ALU.is_ge,
                     fill=0.0, base=0, channel_multiplier=-1, r=["ones_f"], w=["ltri"])
                pool("affine_select", ustr[:], ones_f[:], pattern=[[-1, 128]], compare_op=ALU.is_gt,
                     fill=0.0, base=0, channel_multiplier=1, r=["ones_f"], w=["ustr"])
                pool("affine_select", ident[:], ones_f[:], pattern=[[1, 128]], compare_op=ALU.is_equal,
                     fill=0.0, base=0, channel_multiplier=-1, r=["ones_f"], w=["ident"])
                dve("tensor_copy", ltri_bf[:], ltri[:], r=["ltri"], w=["ltri_bf"])
                posf = sb([128, NT], F32)
                invf = sb([128, 16], F32)
                ang = sb([128, NT, 16], F32)
                kq = sb([128, NT, 16], F32)
                ki = sb([128, NT, 16], I32)
                m1 = sb([128, NT, 16], F32)
                rc = sb([128, NT, 16], F32)
                cosT = sb([128, NT, 16], F32)
                sinT = sb([128, NT, 16], F32)
                prow_i = sb([1, S], I32)
                prow_f = sb([1, S], F32)
                P.dma(prow_i[:], pos_in, writes=["prow_i"])
                dve("tensor_copy", prow_f[:], prow_i[:], r=["prow_i"], w=["prow_f"])
                cols_from_row(posf[:], prow_f, NT, "prow_f", "posf")
                inv = (1.0 / (np.float32(10000.0) ** (np.arange(0, 32, 2, dtype=np.float32) / np.float32(32)))).astype(np.float32)
                for j in range(16):
                    pool("memset", invf[:, j:j + 1], float(inv[j]), w=["invf"])
                dve("tensor_tensor", ang[:], posf[:].unsqueeze(2).to_broadcast([128, NT, 16]),
                    invf[:].unsqueeze(1).to_broadcast([128, NT, 16]), ALU.mult, r=["posf", "invf"], w=["ang"])
                TWO_PI = 2.0 * math.pi
                C1 = 6.28125
                C2 = TWO_PI - C1
                PI_LO = 3.1415925

                def reduce_to_pi(src, skey, dst, dkey):
                    dve("tensor_scalar", kq[:], src[:], 1.0 / TWO_PI, None, ALU.mult, r=[skey], w=["kq"])
                    dve("tensor_copy", ki[:], kq[:], r=["kq"], w=["ki"])
                    dve("tensor_copy", kq[:], ki[:], r=["ki"], w=["kq"])
                    dve("scalar_tensor_tensor", dst[:], kq[:], -C1, src[:], ALU.mult, ALU.add, r=["kq", skey], w=[dkey])
                    dve("scalar_tensor_tensor", dst[:], kq[:], -C2, dst[:], ALU.mult, ALU.add, r=["kq", dkey], w=[dkey])
                    dve("tensor_scalar", m1[:], dst[:], math.pi, None, ALU.is_gt, r=[dkey], w=["m1"])
                    dve("scalar_tensor_tensor", dst[:], m1[:], -TWO_PI, dst[:], ALU.mult, ALU.add, r=["m1", dkey], w=[dkey])
                    dve("tensor_scalar", m1[:], dst[:], -math.pi, None, ALU.is_lt, r=[dkey], w=["m1"])
                    dve("scalar_tensor_tensor", dst[:], m1[:], TWO_PI, dst[:], ALU.mult, ALU.add, r=["m1", dkey], w=[dkey])
                    dve("tensor_scalar", dst[:], dst[:], PI_LO, -PI_LO, ALU.min, ALU.max, r=[dkey], w=[dkey])

                reduce_to_pi(ang, "ang", rc, "rc")
                act("activation", sinT[:], rc[:], AF.Sin, r=["rc"], w=["sinT"])
                dve("tensor_scalar", ang[:], rc[:], math.pi / 2, None, ALU.add, r=["rc"], w=["ang"])
                reduce_to_pi(ang, "ang", rc, "rc")
                act("activation", cosT[:], rc[:], AF.Sin, r=["rc"], w=["cosT"])
                P.dma(cs_d[0], cosT[:].rearrange("p t j -> p (t j)"), reads=["cosT"])
                P.dma(cs_d[1], sinT[:].rearrange("p t j -> p (t j)"), reads=["sinT"])
                P.flush()

        def mod_gen(l, T):
            cT, wst, mrow, brow, nrow, orow, crow = T
            P.dma(crow[:], c_in, writes=["crow"])
            act("activation", crow[:], crow[:], AF.Silu, r=["crow"], w=["crow"])
            cols_from_row(cT[:], crow, 8, "crow", "cT")
            P.dma(brow[:], b_ada[l:l + 1, :], writes=["brow"])
            P.dma(nrow[:, 0:D], norm1_w[l:l + 1, :], writes=["nrow"])
            P.dma(nrow[:, D:2 * D], norm2_w[l:l + 1, :], writes=["nrow"])
            for n in range(12):
                ws = wst[n % 2]
                P.dma(ws[:], w_ada[l, :, n * 512:(n + 1) * 512].rearrange("(k p) c -> p k c", p=128),
                      writes=[("wst", n % 2)])
                b = nbank()
                for k in range(8):
                    pe("matmul", psum[0:1, b, :], cT[:, k:k + 1], ws[:, k, :], start=(k == 0), stop=(k == 7),
                       r=["cT", ("wst", n % 2)], w=[pk(b)])
                dve("tensor_tensor", mrow[:, n * 512:(n + 1) * 512], psum[0:1, b, :], brow[:, n * 512:(n + 1) * 512],
                    ALU.add, r=[pk(b), "brow"], w=["mrow"])
                yield
            for half in range(2):
                o = half * 3 * D
                dve("scalar_tensor_tensor", orow[:, o:o + D], mrow[:, o + D:o + 2 * D], 1.0, nrow[:, half * D:(half + 1) * D],
                    ALU.add, ALU.mult, r=["mrow", "nrow"], w=["orow"])
                dve("tensor_copy", orow[:, o + D:o + 2 * D], mrow[:, o:o + D], r=["mrow"], w=["orow"])
                dve("tensor_copy", orow[:, o + 2 * D:o + 3 * D], mrow[:, o + 2 * D:o + 3 * D], r=["mrow"], w=["orow"])
            P.dma(modrow_d_L[l].rearrange("(o a) b -> o (a b)", o=1), orow[:], reads=["orow"], writes=[("modrow", l)], eng="act")

        def phase_start():
            with contextlib.ExitStack() as st:
                sb = mk_sb(st)
                T = (sb([128, 8], F32), [sb([128, 8, 512], F32) for _ in range(2)], sb([1, 6 * D], F32), sb([1, 6 * D], F32),
                     sb([1, 2 * D], F32), sb([1, 6 * D], F32), sb([1, D], F32))
                cwrow = sb([1, 4 * 1024], F32); cbrow = sb([1, 1024], F32); cwcb = sb([128, 40], F32)
                for _ in mod_gen(0, T):
                    pass
                for ll in range(nlayers):
                    P.dma(cwrow[:], conv_w[ll:ll + 1].rearrange("o k c -> o (k c)"), writes=["cwrow"])
                    P.dma(cbrow[:], conv_b[ll:ll + 1, :], writes=["cbrow"])
                    cols_from_row(cwcb[:, 0:32], cwrow, 32, "cwrow", "cwcb")
                    cols_from_row(cwcb[:, 32:40], cbrow, 8, "cbrow", "cwcb")
                    P.dma(cwcb_d_L[ll], cwcb[:], reads=["cwcb"])
                wg = wprep_gen(0, sb, 2048, "act", which="a")
                for l in range(1, nlayers):
                    for _ in mod_gen(l, T):
                        for _k in range(5):
                            next(wg, None)
                for _ in wg:
                    pass
                P.flush()


        def wprep_gen(l, sb, CW, store_eng, which="all", plain_eng="pool"):
            specs = list(zip(l, which)) if isinstance(l, (list, tuple)) else [(l, which)]
            sin_ = [sb([128, CW], F32) for _ in range(3)]
            sout = [sb([128, CW], BF16) for _ in range(3)]
            jobs = []

            def add(src, dst, R, C, gate=None):
                for r0 in range(0, R, 128):
                    for c0 in range(0, C, CW):
                        c1 = min(C, c0 + CW)
                        jobs.append((src[r0:r0 + 128, c0:c1], dst[r0:r0 + 128, c0:c1], c1 - c0, c0, gate))
            for (ll, wh) in specs:
                wi = w_in[ll]
                if wh in ("all", "a"):
                    add(wi[:, 0:416], winb_d_L[ll][:, 0:416], D, 416)
                    add(wi[:, 1952:1960], winb_d_L[ll][:, 416:424], D, 8)
                    add(wi[:, 416:1952], winb_d_L[ll][:, 424:1960], D, 1536)
                    add(w_q_up[ll], wqb_d_L[ll], 256, 768)
                    add(w_kv_up[ll], wkvb_d_L[ll], 128, 1024)
                if wh in ("all", "b"):
                    g1b = sb([128, D], F32); g2b = sb([128, D], F32)
                    P.dma(g1b[:], modrow_d_L[ll][2:3, :].partition_broadcast(128), reads=[("modrow", ll)], writes=[("g1b", ll)])
                    P.dma(g2b[:], modrow_d_L[ll][5:6, :].partition_broadcast(128), reads=[("modrow", ll)], writes=[("g2b", ll)])
                    add(w_out[ll], woutb_d_L[ll], D, D, gate=(g1b, ("g1b", ll)))
                    add(w_gate_up[ll], wgub_d_L[ll], D, 2 * D_FF)
                    add(w_down[ll], wdb_d_L[ll], D_FF, D, gate=(g2b, ("g2b", ll)))
            def load(i):
                if i < len(jobs):
                    P.dma(sin_[i % 3][:, 0:jobs[i][2]], jobs[i][0], writes=[("sin", i % 3)])
            load(0)
            load(1)
            for i, (src, dst, cw, c0, gate) in enumerate(jobs):
                k = i % 3
                load(i + 2)
                if gate is not None:
                    gt, gk = gate
                    eng = pool if store_eng == "sp" else (dve if i % 2 == 0 else pool)
                    eng("tensor_tensor", sout[k][:, 0:cw], sin_[k][:, 0:cw], gt[:, c0:c0 + cw], ALU.mult,
                        r=[("sin", k), gk], w=[("sout", k)])
                elif store_eng == "sp" and plain_eng == "act":
                    act("copy", sout[k][:, 0:cw], sin_[k][:, 0:cw], r=[("sin", k)], w=[("sout", k)])
                elif store_eng == "sp":
                    pool("tensor_copy", sout[k][:, 0:cw], sin_[k][:, 0:cw], r=[("sin", k)], w=[("sout", k)])
                elif k == 1:
                    act("copy", sout[k][:, 0:cw], sin_[k][:, 0:cw], r=[("sin", k)], w=[("sout", k)])
                elif k == 0:
                    dve("tensor_copy", sout[k][:, 0:cw], sin_[k][:, 0:cw], r=[("sin", k)], w=[("sout", k)])
                else:
                    pool("tensor_copy", sout[k][:, 0:cw], sin_[k][:, 0:cw], r=[("sin", k)], w=[("sout", k)])
                P.dma(dst, sout[k][:, 0:cw], reads=[("sout", k)], eng=store_eng)
                yield

        def phase_wprep(l):
            with contextlib.ExitStack() as st:
                sb = mk_sb(st)
                cwrow = sb([1, 4 * 1024], F32); cbrow = sb([1, 1024], F32); cwcb = sb([128, 40], F32)
                for ll in range(nlayers):
                    P.dma(cwrow[:], conv_w[ll:ll + 1].rearrange("o k c -> o (k c)"), writes=["cwrow"])
                    P.dma(cbrow[:], conv_b[ll:ll + 1, :], writes=["cbrow"])
                    cols_from_row(cwcb[:, 0:32], cwrow, 32, "cwrow", "cwcb")
                    cols_from_row(cwcb[:, 32:40], cbrow, 8, "cbrow", "cwcb")
                    P.dma(cwcb_d_L[ll], cwcb[:], reads=["cwcb"])
                for _ in wprep_gen(l, sb, 2048, "act"):
                    pass
                P.flush()

        def phase_A(l, xsrc):
            with contextlib.ExitStack() as st:
                sb = mk_sb(st)
                win = sb([128, 8, D_IN], BF16)
                wq = sb([128, 2, 768], BF16)
                wkv = sb([128, 1024], BF16)
                weff = sb([128, D], F32); shb = sb([128, D], F32)
                qaw = sb([128, 256], F32); kvaw = sb([128, 128], F32); kpw = sb([128, 32], F32)
                qnw = sb([128, 64], F32); qpw = sb([128, 32], F32); knw = sb([128, 64], F32)
                cw = sb([128, 4, 8], F32); cb = sb([128, 8], F32); dtb = sb([128, 8], F32)
                invn = sb([128, 27], F32)
                cosT = sb([128, NT, 16], F32)
                sinT = sb([128, NT, 16], F32)
                P.dma(cosT[:].rearrange("p t j -> p (t j)"), cs_d[0], writes=["cosT"])
                P.dma(sinT[:].rearrange("p t j -> p (t j)"), cs_d[1], writes=["sinT"])
                P.dma(win[:], winb_d_L[l].rearrange("(k p) c -> p k c", p=128), writes=["win"])
                P.dma(wq[:], wqb_d_L[l].rearrange("(k p) c -> p k c", p=128), writes=["wq"])
                P.dma(wkv[:], wkvb_d_L[l], writes=["wkv"])
                P.dma(weff[:], modrow_d_L[l][0:1, :].partition_broadcast(128), writes=["weff"])
                P.dma(shb[:], modrow_d_L[l][1:2, :].partition_broadcast(128), writes=["shb"])
                P.dma(qaw[:], q_a_norm_w[l:l + 1, :].partition_broadcast(128), writes=["qaw"])
                P.dma(kvaw[:], kv_a_norm_w[l:l + 1, :].partition_broadcast(128), writes=["kvaw"])
                P.dma(kpw[:], k_pe_norm_w[l:l + 1, :].partition_broadcast(128), writes=["kpw"])
                P.dma(qnw[:], q_nope_norm_w[l:l + 1, :].partition_broadcast(128), writes=["qnw"])
                P.dma(qpw[:], q_pe_norm_w[l:l + 1, :].partition_broadcast(128), writes=["qpw"])
                P.dma(knw[:], k_nope_norm_w[l:l + 1, :].partition_broadcast(128), writes=["knw"])
                P.dma(dtb[:], dt_bias[l:l + 1, :].partition_broadcast(128), writes=["dtb"])
                P.dma(cw[:].rearrange("p k m -> p (k m)"), cwcb_d_L[l][:, 0:32], writes=["cw"])
                P.dma(cb[:], cwcb_d_L[l][:, 32:40], writes=["cb"])
                scale = 96.0 ** -0.5
                dve("tensor_scalar", qnw[:], qnw[:], scale, None, ALU.mult, r=["qnw"], w=["qnw"])
                dve("tensor_scalar", qpw[:], qpw[:], scale, None, ALU.mult, r=["qpw"], w=["qpw"])
                pool("memset", invn[:, 0:1], 1.0 / 256, w=["invn"])
                pool("memset", invn[:, 1:2], 1.0 / 128, w=["invn"])
                pool("memset", invn[:, 2:3], 1.0 / 32, w=["invn"])
                pool("memset", invn[:, 3:11], 1.0 / 64, w=["invn"])
                pool("memset", invn[:, 11:19], 1.0 / 32, w=["invn"])
                pool("memset", invn[:, 19:27], 1.0 / 64, w=["invn"])

                xt = [sb([128, D], F32) for _ in range(4)]
                junk = sb([128, D], BF16)
                ssx = sb([128, 4], F32); rsx = sb([128, 4], F32)
                htmp_2 = [sb([128, D], F32) for _ in range(2)]
                h = [sb([128, D], BF16) for _ in range(2)]
                hT = sb([128, 8, 512], BF16)
                xbcT = [sb([128, 516], BF16) for _ in range(8)]
                dg = sb([128, 8, 4, 128], BF16)
                xsT = sb([128, 8, 512], BF16)
                raw1 = [sb([128, 424], F32) for _ in range(4)]
                zt = [sb([128, 512], BF16) for _ in range(2)]
                dtr = sb([128, 4, 8], F32)
                ss1 = sb([128, 4, 3], F32); rs1 = sb([128, 4, 3], F32)
                latn_2 = [sb([128, 384], BF16) for _ in range(2)]
                latT_2 = [sb([128, 3, 128], BF16) for _ in range(2)]
                qsb = [sb([128, 768], F32) for _ in range(4)]
                kvsb = [sb([128, 8, 64], F32) for _ in range(4)]
                sq_2 = [sb([128, 768], F32) for _ in range(2)]
                ss2 = sb([128, 4, 24], F32); rs2 = sb([128, 4, 24], F32)
                kpn = [sb([128, 32], F32) for _ in range(4)]
                Qf_2 = [sb([128, 8, 96], BF16) for _ in range(2)]; Kf_2 = [sb([128, 8, 96], BF16) for _ in range(2)]
                ra_2 = [sb([128, 8, 32], F32) for _ in range(2)]; rb_2 = [sb([128, 8, 32], F32) for _ in range(2)]
                rq_2 = [sb([128, 8, 32], F32) for _ in range(2)]
                ka_2 = [sb([128, 32], F32) for _ in range(2)]; kb_2 = [sb([128, 32], F32) for _ in range(2)]
                kr_2 = [sb([128, 32], F32) for _ in range(2)]
                vaug = [sb([128, 8, 128], BF16) for _ in range(2)]
                QTb = sb([96, 8, 512], BF16); KTb = sb([96, 8, 512], BF16)
                tok_o = [sb([128, 768], BF16) for _ in range(2)]

                for i in range(2):
                    pool("memset", vaug[i][:], 1.0, w=[("vaug", i)])
                for m in range(8):
                    pool("memset", xbcT[m][:, 512:515], 0.0, w=[("xbcT", m)])
                for m in range(8):
                    for k in range(4):
                        dve("tensor_scalar", dg[:, m, k, :], ident[:], cw[:, k, m:m + 1], None, ALU.mult, r=["ident", "cw"], w=["dg"])

                def n_stats(blk, j):
                    t = blk * 4 + j
                    P.dma(xt[j][:], xsrc[t * 128:(t + 1) * 128, :], writes=[("xt", j)])
                    act("activation", junk[:], xt[j][:], AF.Square, accum_out=ssx[:, j:j + 1],
                        r=[("xt", j)], w=["junk", "ssx"])

                def n_rstd():
                    act("activation", rsx[:], ssx[:], AF.Ln, scale=1.0 / D, bias=cst[:, 0:1], r=["ssx", "cst"], w=["rsx"])
                    act("activation", rsx[:], rsx[:], AF.Exp, scale=-0.5, r=["rsx"], w=["rsx"])

                def n_hT(j, bank=None):
                    hh = h[j % 2]; htmp = htmp_2[j % 2]
                    dve("scalar_tensor_tensor", htmp[:], xt[j][:], rsx[:, j:j + 1], weff[:], ALU.mult, ALU.mult,
                        r=[("xt", j), "rsx", "weff"], w=[("htmp", j % 2)])
                    pool("tensor_tensor", hh[:], htmp[:], shb[:], ALU.add, r=[("htmp", j % 2), "shb"], w=[("h", j % 2)])
                    b = nbank() if bank is None else bank
                    for k in range(8):
                        pe("transpose", PS16(b)[:, k * 128:(k + 1) * 128], hh[:, k * 128:(k + 1) * 128], ident[:],
                           r=[("h", j % 2), "ident"], w=[pk(b)])
                    act("copy", hT[:, :, j * 128:(j + 1) * 128], PS16(b).rearrange("p (k t) -> p k t", k=8),
                        r=[pk(b)], w=["hT"])

                for j in range(4):
                    n_stats(0, j)
                n_rstd()
                for j in range(4):
                    n_hT(j)

                def P_gen(blk):
                    pb = blk % 2
                    for m in range(9):
                        if m == 5:
                            yield
                        if m < 8:
                            b = nbank()
                            c0 = 936 + m * 128
                            for k in range(8):
                                pe("matmul", PS(b), win[:, k, c0:c0 + 128], hT[:, k, :], start=(k == 0), stop=(k == 7),
                                   r=["win", "hT"], w=[pk(b)])
                            cur = xbcT[m]
                            act("copy", cur[:, 0:3], cur[:, 512:515], r=[("xbcT", m)], w=[("xbcT", m)])
                            act("copy", cur[:, 3:515], PS(b), r=[pk(b), ("xbcT", m)], w=[("xbcT", m)])
                        if m >= 1:
                            mm = m - 1
                            b2 = nbank()
                            for kk in range(4):
                                pe("matmul", PS(b2), dg[:, mm, kk, :], xbcT[mm][:, kk:kk + 512], start=(kk == 0), stop=(kk == 3),
                                   r=["dg", ("xbcT", mm)], w=[pk(b2)])
                            act("activation", xsT[:, mm, :], PS(b2), AF.Silu, bias=cb[:, mm:mm + 1], r=[pk(b2), "cb"], w=[("xsT", mm)])
                    yield
                    for g in range(2):
                        P.dma(BT_d[g, :, blk * 512:(blk + 1) * 512], xsT[:, 4 + g, :], reads=[("xsT", 4 + g)], eng="act")
                        P.dma(CT_d[g, :, blk * 512:(blk + 1) * 512], xsT[:, 6 + g, :], reads=[("xsT", 6 + g)], eng="act")
                    for j in range(4):
                        t = blk * 4 + j
                        b = nbank()
                        for m in range(6):
                            pe("transpose", PS16(b)[:, m * 128:(m + 1) * 128], xsT[:, m, j * 128:(j + 1) * 128], ident[:],
                               r=[("xsT", m), "ident"], w=[pk(b)])
                        to = tok_o[j % 2]
                        dve("tensor_copy", to[:], PS16(b)[:, 0:768], r=[pk(b)], w=[("tok_o", j % 2)])
                        P.dma(xs_d[t * 128:(t + 1) * 128, :], to[:, 0:512], reads=[("tok_o", j % 2)], eng="act")
                        P.dma(B_d[t * 128:(t + 1) * 128, :], to[:, 512:768], reads=[("tok_o", j % 2)], eng="act")
                    yield
                    for j in range(4):
                        t = blk * 4 + j
                        b1 = nbank()
                        for k in range(8):
                            pe("matmul", PS(b1, 424), hT[:, k, j * 128:(j + 1) * 128], win[:, k, 0:424], start=(k == 0), stop=(k == 7),
                               r=["hT", "win"], w=[pk(b1)])
                        b2 = nbank()
                        for k in range(8):
                            pe("matmul", PS(b2), hT[:, k, j * 128:(j + 1) * 128], win[:, k, 424:936], start=(k == 0), stop=(k == 7),
                               r=["hT", "win"], w=[pk(b2)])
                        dve("tensor_copy", raw1[j][:], PS(b1, 424), r=[pk(b1)], w=[("raw1", j)])
                        act("copy", zt[j % 2][:], PS(b2), r=[pk(b2)], w=[("zt", j % 2)])
                        P.dma(z_d[t * 128:(t + 1) * 128, :], zt[j % 2][:], reads=[("zt", j % 2)], eng="act")
                        act("activation", junk[:, 0:256], raw1[j][:, 0:256], AF.Square, accum_out=ss1[:, j, 0:1],
                            r=[("raw1", j)], w=["junk", "ss1"])
                        act("activation", junk[:, 0:128], raw1[j][:, 256:384], AF.Square, accum_out=ss1[:, j, 1:2],
                            r=[("raw1", j)], w=["junk", "ss1"])
                        act("activation", junk[:, 0:32], raw1[j][:, 384:416], AF.Square, accum_out=ss1[:, j, 2:3],
                            r=[("raw1", j)], w=["junk", "ss1"])
                        dve("tensor_tensor", dtr[:, j, :], raw1[j][:, 416:424], dtb[:], ALU.add, r=[("raw1", j), "dtb"], w=["dtr"])
                    yield
                    dve("tensor_tensor", ss1[:], ss1[:], invn[:, 0:3].unsqueeze(1).to_broadcast([128, 4, 3]), ALU.mult,
                        r=["ss1", "invn"], w=["ss1"])
                    act("activation", dtr[:], dtr[:], AF.Exp, r=["dtr"], w=["dtr"])
                    act("activation", dt_all[:, blk * 4:(blk + 1) * 4, :], dtr[:], AF.Ln, bias=cst[:, 1:2], r=["dtr", "cst"], w=["dt_all"])
                    act("activation", rs1[:], ss1[:], AF.Ln, bias=cst[:, 0:1], r=["ss1", "cst"], w=["rs1"])
                    act("activation", rs1[:], rs1[:], AF.Exp, scale=-0.5, r=["rs1"], w=["rs1"])

                for _ in P_gen(0):
                    pass
                for blk in range(NB):
                    pb = blk % 2
                    nxt = blk + 1 < NB
                    def La(j):
                        latn = latn_2[j % 2]; latT = latT_2[j % 2]; sq = sq_2[j % 2]
                        dve("scalar_tensor_tensor", latn[:, 0:256], raw1[j][:, 0:256], rs1[:, j, 0:1], qaw[:], ALU.mult, ALU.mult,
                            r=[("raw1", j), "rs1", "qaw"], w=[("latn", j % 2)])
                        dve("scalar_tensor_tensor", latn[:, 256:384], raw1[j][:, 256:384], rs1[:, j, 1:2], kvaw[:], ALU.mult, ALU.mult,
                            r=[("raw1", j), "rs1", "kvaw"], w=[("latn", j % 2)])
                        dve("scalar_tensor_tensor", kpn[j][:], raw1[j][:, 384:416], rs1[:, j, 2:3], kpw[:], ALU.mult, ALU.mult,
                             r=[("raw1", j), "rs1", "kpw"], w=[("kpn", j)])
                        b = 2 if j % 2 == 0 else 6
                        for k in range(3):
                            pe("transpose", PS16(b)[:, k * 128:(k + 1) * 128], latn[:, k * 128:(k + 1) * 128], ident[:],
                               r=[("latn", j % 2), "ident"], w=[pk(b)])
                        act("copy", latT[:], PS16(b)[:, 0:384].rearrange("p (k t) -> p k t", k=3), r=[pk(b)], w=[("latT", j % 2)])
                        bq = 0 if j % 2 == 0 else 4
                        for (n0, n1, bb) in ((0, 512, bq), (512, 768, bq + 1)):
                            for k in range(2):
                                pe("matmul", PS(bb, n1 - n0), latT[:, k, :], wq[:, k, n0:n1], start=(k == 0), stop=(k == 1),
                                   r=[("latT", j % 2), "wq"], w=[pk(bb)])
                        bk = 2 if j % 2 == 0 else 6
                        for hf in range(2):
                            pe("matmul", PS(bk + hf), latT[:, 2, :], wkv[:, hf * 512:(hf + 1) * 512], start=True, stop=True,
                               r=[("latT", j % 2), "wkv"], w=[pk(bk + hf)])
                        return bq, bk

                    def Lb(j, bq, bk):
                        sq = sq_2[j % 2]
                        act("copy", qsb[j][:], PS2(bq, 768), r=[pk(bq), pk(bq + 1)], w=[("qsb", j)])
                        kvv = PS2(bk).rearrange("p (h e) -> p h e", h=8)
                        dve("tensor_copy", kvsb[j][:], kvv[:, :, 0:64], r=[pk(bk), pk(bk + 1)], w=[("kvsb", j)])
                        va = vaug[j % 2]
                        vap = va[:].rearrange("p (c two) e -> p c two e", two=2)
                        for hf in range(2):
                            kvp = PS(bk + hf).rearrange("p (c two e) -> p c two e", two=2, e=128)
                            dve("tensor_copy", vap[:, 2 * hf:2 * hf + 2, 0, 0:64], kvp[:, :, 0, 64:128], r=[pk(bk + hf)], w=[("vaug", j % 2)])
                            dve("tensor_copy", vap[:, 2 * hf:2 * hf + 2, 1, 64:128], kvp[:, :, 1, 64:128], r=[pk(bk + hf)], w=[("vaug", j % 2)])
                        t = blk * 4 + j
                        if True:
                            P.dma(V_d[t * 128:(t + 1) * 128, :], va[:].rearrange("p h e -> p (h e)"), reads=[("vaug", j % 2)], eng="act")
                        act("activation", sq[:], qsb[j][:], AF.Square, r=[("qsb", j)], w=[("sq", j % 2)])
                        sqv = sq[:].rearrange("p (h e) -> p h e", h=8)
                        dve("tensor_reduce", ss2[:, j, 0:8], sqv[:, :, 0:64], AX.X, ALU.add, r=[("sq", j % 2)], w=["ss2"])
                        dve("tensor_reduce", ss2[:, j, 8:16], sqv[:, :, 64:96], AX.X, ALU.add, r=[("sq", j % 2)], w=["ss2"])
                        act("activation", sq[:, 0:512], kvsb[j][:].rearrange("p h e -> p (h e)"), AF.Square, r=[("kvsb", j)], w=[("sq", j % 2)])
                        dve("tensor_reduce", ss2[:, j, 16:24], sq[:, 0:512].rearrange("p (h e) -> p h e", h=8), AX.X, ALU.add,
                            r=[("sq", j % 2)], w=["ss2"])

                    if nxt:
                        for j in range(4):
                            n_stats(blk + 1, j)
                    pend = {}
                    for j in range(5):
                        if j < 4:
                            pend[j] = La(j)
                        if j == 1 and nxt:
                            n_rstd()
                        if j >= 1:
                            Lb(j - 1, *pend[j - 1])
                            if nxt:
                                n_hT(j - 1, bank=(0 if (j - 1) % 2 == 0 else 4))
                    pg = P_gen(blk + 1) if nxt else None
                    dve("tensor_tensor", ss2[:], ss2[:], invn[:, 3:27].unsqueeze(1).to_broadcast([128, 4, 24]), ALU.mult,
                        r=["ss2", "invn"], w=["ss2"])
                    act("activation", rs2[:], ss2[:], AF.Ln, bias=cst[:, 0:1], r=["ss2", "cst"], w=["rs2"])
                    act("activation", rs2[:], rs2[:], AF.Exp, scale=-0.5, r=["rs2"], w=["rs2"])
                    for j in range(4):
                        t = blk * 4 + j
                        sq = sq_2[j % 2]; Qf = Qf_2[j % 2]; Kf = Kf_2[j % 2]; ra = ra_2[j % 2]; rb = rb_2[j % 2]; rq = rq_2[j % 2]
                        ka = ka_2[j % 2]; kb = kb_2[j % 2]; kr = kr_2[j % 2]
                        qv = qsb[j][:].rearrange("p (h e) -> p h e", h=8)
                        cosb = cosT[:, t, :]; sinb = sinT[:, t, :]
                        dve("tensor_tensor", sq[:, 0:512].rearrange("p (h e) -> p h e", h=8), qv[:, :, 0:64],
                            rs2[:, j, 0:8].unsqueeze(2).to_broadcast([128, 8, 64]), ALU.mult, r=[("qsb", j), "rs2"], w=[("sq", j % 2)])
                        dve("tensor_tensor", Qf[:, :, 0:64], sq[:, 0:512].rearrange("p (h e) -> p h e", h=8),
                            qnw[:].unsqueeze(1).to_broadcast([128, 8, 64]), ALU.mult, r=[("sq", j % 2), "qnw"], w=[("Qf", j % 2)])
                        pool("tensor_tensor", ra[:], qv[:, :, 64:96], rs2[:, j, 8:16].unsqueeze(2).to_broadcast([128, 8, 32]), ALU.mult,
                             r=[("qsb", j), "rs2"], w=[("ra", j % 2)])
                        pool("tensor_tensor", ra[:], ra[:], qpw[:].unsqueeze(1).to_broadcast([128, 8, 32]), ALU.mult, r=[("ra", j % 2), "qpw"], w=[("ra", j % 2)])
                        cb8 = cosb.unsqueeze(1).to_broadcast([128, 8, 16]); sb8 = sinb.unsqueeze(1).to_broadcast([128, 8, 16])
                        pool("tensor_tensor", rb[:, :, 0:16], ra[:, :, 0:16], cb8, ALU.mult, r=[("ra", j % 2), "cosT"], w=[("rb", j % 2)])
                        pool("tensor_tensor", rb[:, :, 16:32], ra[:, :, 16:32], cb8, ALU.mult, r=[("ra", j % 2), "cosT"], w=[("rb", j % 2)])
                        pool("tensor_tensor", rq[:, :, 0:16], ra[:, :, 16:32], sb8, ALU.mult, r=[("ra", j % 2), "sinT"], w=[("rq", j % 2)])
                        pool("tensor_tensor", rq[:, :, 16:32], ra[:, :, 0:16], sb8, ALU.mult, r=[("ra", j % 2), "sinT"], w=[("rq", j % 2)])
                        pool("tensor_tensor", Qf[:, :, 64:80], rb[:, :, 0:16], rq[:, :, 0:16], ALU.subtract, r=[("rb", j % 2), ("rq", j % 2)], w=[("Qf", j % 2)])
                        pool("tensor_tensor", Qf[:, :, 80:96], rb[:, :, 16:32], rq[:, :, 16:32], ALU.add, r=[("rb", j % 2), ("rq", j % 2)], w=[("Qf", j % 2)])
                        dve("tensor_tensor", sq[:, 0:512].rearrange("p (h e) -> p h e", h=8), kvsb[j][:],
                            rs2[:, j, 16:24].unsqueeze(2).to_broadcast([128, 8, 64]), ALU.mult, r=[("kvsb", j), "rs2"], w=[("sq", j % 2)])
                        dve("tensor_tensor", Kf[:, :, 0:64], sq[:, 0:512].rearrange("p (h e) -> p h e", h=8),
                            knw[:].unsqueeze(1).to_broadcast([128, 8, 64]), ALU.mult, r=[("sq", j % 2), "knw"], w=[("Kf", j % 2)])
                        dve("tensor_tensor", ka[:, 0:16], kpn[j][:, 0:16], cosb, ALU.mult, r=[("kpn", j), "cosT"], w=[("ka", j % 2)])
                        dve("tensor_tensor", ka[:, 16:32], kpn[j][:, 16:32], cosb, ALU.mult, r=[("kpn", j), "cosT"], w=[("ka", j % 2)])
                        dve("tensor_tensor", kb[:, 0:16], kpn[j][:, 16:32], sinb, ALU.mult, r=[("kpn", j), "sinT"], w=[("kb", j % 2)])
                        dve("tensor_tensor", kb[:, 16:32], kpn[j][:, 0:16], sinb, ALU.mult, r=[("kpn", j), "sinT"], w=[("kb", j % 2)])
                        dve("tensor_tensor", kr[:, 0:16], ka[:, 0:16], kb[:, 0:16], ALU.subtract, r=[("ka", j % 2), ("kb", j % 2)], w=[("kr", j % 2)])
                        dve("tensor_tensor", kr[:, 16:32], ka[:, 16:32], kb[:, 16:32], ALU.add, r=[("ka", j % 2), ("kb", j % 2)], w=[("kr", j % 2)])
                        dve("tensor_copy", Kf[:, :, 64:96], kr[:].unsqueeze(1).to_broadcast([128, 8, 32]), r=[("kr", j % 2)], w=[("Kf", j % 2)])
                        bqt = nbank()
                        for hh_ in range(8):
                            pe("transpose", PS16(bqt)[0:96, hh_ * 128:(hh_ + 1) * 128], Qf[:, hh_, :], ident[:],
                               r=[("Qf", j % 2), "ident"], w=[pk(bqt)])
                        act("copy", QTb[:, :, j * 128:(j + 1) * 128], PS16(bqt)[0:96, :].rearrange("p (h t) -> p h t", h=8),
                            r=[pk(bqt)], w=["QTb"])
                        bkt = nbank()
                        for hh_ in range(8):
                            pe("transpose", PS16(bkt)[0:96, hh_ * 128:(hh_ + 1) * 128], Kf[:, hh_, :], ident[:],
                               r=[("Kf", j % 2), "ident"], w=[pk(bkt)])
                        act("copy", KTb[:, :, j * 128:(j + 1) * 128], PS16(bkt)[0:96, :].rearrange("p (h t) -> p h t", h=8),
                            r=[pk(bkt)], w=["KTb"])
                        if pg is not None:
                            next(pg, None)
                            if j == 3:
                                for _ in pg:
                                    pass
                    P.dma(QT_d[:, :, blk * 512:(blk + 1) * 512].rearrange("h d s -> d h s"), QTb[:], reads=["QTb"], eng="act")
                    P.dma(KT_d[:, :, blk * 512:(blk + 1) * 512].rearrange("h d s -> d h s"), KTb[:], reads=["KTb"], eng="act")
                P.flush()

        def phase_B(l):
            with contextlib.ExitStack() as st:
                sb = mk_sb(st)
                KT = sb([96, 8, S], BF16)
                Vs = sb([128, NT, 8 * 128], BF16)
                QTs = [sb([96, 8, 512], BF16) for _ in range(2)]
                onesb = sb([128, 512], BF16)
                amask = [sb([128, 512], BF16) for _ in range(4)]
                pT = [sb([128, 2, 512], BF16) for _ in range(4)]
                rden = [sb([128, 512], F32) for _ in range(2)]
                aT = [sb([128, 4, 512], BF16) for _ in range(2)]
                pool("memset", onesb[:], 1.0, w=["onesb"])
                for j in range(4):
                    pool("affine_select", amask[j][:], onesb[:], pattern=[[1, 512]], compare_op=ALU.is_ge, fill=0.0,
                         base=-128 * j, channel_multiplier=-1, r=["onesb"], w=[("amask", j)])
                P.dma(KT[:], KT_d.rearrange("h d s -> d h s"), writes=["KT"])
                P.dma(QTs[0][:], QT_d[:, :, 0:512].rearrange("h d s -> d h s"), writes=[("QTs", 0)])
                for t0 in range(0, NT, 8):
                    t1 = min(NT, t0 + 8)
                    P.dma(Vs[:, t0:t1, :], V_d[t0 * 128:t1 * 128, :].rearrange("(t p) f -> p t f", p=128), writes=[("Vs", t0)])

                def load_q(blk):
                    P.dma(QTs[blk % 2][:], QT_d[:, :, blk * 512:(blk + 1) * 512].rearrange("h d s -> d h s"),
                          writes=[("QTs", blk % 2)])

                its = [(blk, hd, p) for blk in range(NB) for hd in range(NH) for p in range(2 * blk + 2)]

                def emit_s(i):
                    blk, hd, p = its[i]
                    if hd == 0 and p == 0:
                        if blk + 1 < NB:
                            load_q(blk + 1)
                    qb = QTs[blk % 2]
                    sbp = 2 * (i % 3)
                    pp = pT[i % 4]; pkey = ("pT", i % 4)
                    c0 = 256 if p == 2 * blk + 1 else 0
                    for a_ in range(2):
                        kt = 2 * p + a_
                        pe("matmul", psum[:, sbp + a_, c0:512], KT[:, hd, kt * 128:(kt + 1) * 128], qb[:, hd, c0:512], start=True, stop=True,
                           r=["KT", ("QTs", blk % 2)], w=[pk(sbp + a_)])
                    act("activation", pp[:, :, c0:512], psum[:, sbp:sbp + 2, c0:512], AF.Exp, r=[pk(sbp), pk(sbp + 1)], w=[pkey])
                    for a_ in range(2):
                        jd = 2 * p + a_ - 4 * blk
                        if jd >= 0:
                            w_ = 128 * (jd + 1)
                            eng = dve if (jd % 2 == 0 or l + 1 < nlayers or l == 0) else pool
                            eng("tensor_tensor", pp[:, a_, c0:w_], pp[:, a_, c0:w_], amask[jd][:, c0:w_], ALU.mult,
                                r=[pkey, ("amask", jd)], w=[pkey])

                def emit_pv(i):
                    blk, hd, p = its[i]
                    nk = 4 * blk + 4
                    pp = pT[i % 4]; pkey = ("pT", i % 4)
                    bo = 6 + (hd % 2)
                    c0 = 256 if p == 2 * blk + 1 else 0
                    for a_ in range(2):
                        kt = 2 * p + a_
                        pe("matmul", psum[:, bo, c0:512], Vs[:, kt, hd * 128:(hd + 1) * 128], pp[:, a_, c0:512], start=(kt == 0), stop=(kt == nk - 1),
                           r=[("Vs", (kt // 8) * 8), pkey], w=[pk(bo)])
                    if p == 2 * blk + 1:
                        at = aT[blk % 2]
                        rd = rden[hd % 2]
                        c = hd // 2
                        if hd % 2 == 0:
                            dve("reciprocal", rd[64:128, :], psum[64:128, bo, :], r=[pk(bo)], w=[("rden", 0)])
                            dve("tensor_tensor", at[0:64, c, :], psum[0:64, bo, :], rd[64:128, :], ALU.mult,
                                r=[pk(bo), ("rden", 0)], w=[("aT", blk % 2)])
                        else:
                            dve("reciprocal", rd[0:64, :], psum[0:64, bo, :], r=[pk(bo)], w=[("rden", 1)])
                            dve("tensor_tensor", at[64:128, c, :], psum[64:128, bo, :], rd[0:64, :], ALU.mult,
                                r=[pk(bo), ("rden", 1)], w=[("aT", blk % 2)])
                        if hd == NH - 1:
                            P.dma(mixT_d[0:512, blk * 512:(blk + 1) * 512].rearrange("(c p) s -> p c s", p=128), at[:],
                                  reads=[("aT", blk % 2)])

                n = len(its)
                gen = wprep_gen([l, l + 1], sb, 1024, "sp", which=["b", "all"]) if (l == 0 and l + 1 < nlayers) else (wprep_gen(l + 1, sb, 1024, "sp") if l + 1 < nlayers else (wprep_gen(l, sb, 1024, "sp", which="b") if l == 0 else None))
                for i in range(n + 2):
                    if i < n:
                        emit_s(i)
                    if i >= 2:
                        emit_pv(i - 2)
                    if gen is not None and i % 2 == 1:
                        next(gen, None)
                if gen is not None:
                    for _ in gen:
                        pass
                P.flush()

        def phase_C(l):
            with contextlib.ExitStack() as st:
                sb = mk_sb(st)
                ab = sb([128, 8], F32); dsk = sb([128, 8], F32); nwb = sb([128, 512], F32)
                P.dma(ab[:], a_log[l:l + 1, :].partition_broadcast(128), writes=["ab"])
                P.dma(dsk[:], d_skip[l:l + 1, :].partition_broadcast(128), writes=["dsk"])
                P.dma(nwb[:], ssd_norm_w[l:l + 1, :].partition_broadcast(128), writes=["nwb"])
                act("activation", ab[:], ab[:], AF.Exp, r=["ab"], w=["ab"])
                dve("tensor_scalar", ab[:], ab[:], -1.0, None, ALU.mult, r=["ab"], w=["ab"])
                BTs = sb([128, 2, S], BF16); CTs = sb([128, 2, S], BF16)
                P.dma(BTs[:], BT_d.rearrange("g n s -> n g s"), writes=["BTs"])
                P.dma(CTs[:], CT_d.rearrange("g n s -> n g s"), writes=["CTs"])
                xs4 = [sb([128, 4, 512], BF16) for _ in range(2)]
                B4 = [sb([128, 4, 256], BF16) for _ in range(2)]
                z4 = [sb([128, 4, 512], BF16) for _ in range(2)]
                sz4 = [sb([128, 4, 512], F32) for _ in range(2)]
                adt_2 = [sb([128, 8], F32) for _ in range(3)]
                rhs_all_2 = [sb([128, 8, 128], F32) for _ in range(3)]
                ET_2 = [sb([128, 8, 128], BF16) for _ in range(3)]
                small_2 = [sb([128, 16], F32) for _ in range(3)]
                dec_2 = [sb([128, 8], F32) for _ in range(3)]
                sm_2 = [sb([128, 2, 128], BF16) for _ in range(3)]
                MT_2 = [sb([128, 8, 128], BF16) for _ in range(3)]
                xdt_2 = [sb([128, 8, 64], BF16) for _ in range(3)]; xdd_2 = [sb([128, 8, 64], BF16) for _ in range(3)]
                state = sb([128, 8, 64], F32); state_bf = sb([128, 8, 64], BF16)
                yo_2 = [sb([128, 8, 64], F32) for _ in range(3)]; y_2 = [sb([128, 512], F32) for _ in range(3)]
                xsd_2 = [sb([128, 512], F32) for _ in range(3)]
                yg = sb([128, 512], F32); junk = sb([128, 256], F32)
                ssg = sb([128, 4, 2], F32); rsg = sb([128, 4, 2], F32)
                yg4 = sb([128, 4, 512], F32)
                yn_2 = [sb([128, 512], BF16) for _ in range(2)]
                ygT = sb([128, 4, 512], BF16)
                dve("memset", state[:], 0.0, w=["state"])
                dve("memset", state_bf[:], 0.0, w=["state_bf"])
                def load_blk(blk):
                    pb = blk % 2
                    rows = slice(blk * 512, (blk + 1) * 512)
                    P.dma(xs4[pb][:], xs_d[rows, :].rearrange("(j p) f -> p j f", p=128), writes=[("xs4", pb)])
                    P.dma(B4[pb][:], B_d[rows, :].rearrange("(j p) f -> p j f", p=128), writes=[("B4", pb)])
                    P.dma(z4[pb][:], z_d[rows, :].rearrange("(j p) f -> p j f", p=128), writes=[("z4", pb)])
                    act("activation", sz4[pb][:], z4[pb][:], AF.Silu, r=[("z4", pb)], w=[("sz4", pb)])

                def s1(blk, j):
                    pb = blk % 2
                    jb = (blk * 4 + j) % 3
                    t = blk * 4 + j
                    cs = slice(t * 128, (t + 1) * 128)
                    dtt = dt_all[:, t, :]
                    adt = adt_2[jb]; rhs_all = rhs_all_2[jb]; ET = ET_2[jb]; small = small_2[jb]; dec = dec_2[jb]
                    sm = sm_2[jb]; MT = MT_2[jb]; xdt = xdt_2[jb]; xdd = xdd_2[jb]
                    yo = yo_2[jb]; y = y_2[jb]; xsd = xsd_2[jb]
                    dve("tensor_tensor", adt[:], dtt, ab[:], ALU.mult, r=["dt_all", "ab"], w=[("adt", jb)])
                    pool("tensor_tensor", rhs_all[:], ltri[:].unsqueeze(1).to_broadcast([128, 8, 128]),
                        adt[:].unsqueeze(2).to_broadcast([128, 8, 128]), ALU.mult, r=["ltri", ("adt", jb)], w=[("rhs_all", jb)])
                    bs = npair()
                    for hf in range(2):
                        pe("matmul", PS(bs + hf), ustr[:], rhs_all[:, hf * 4:(hf + 1) * 4, :].rearrange("p h l -> p (h l)"),
                           start=True, stop=True, r=["ustr", ("rhs_all", jb)], w=[pk(bs + hf)])
                    bsm = nbank()
                    pe("matmul", PS(bsm, 8), ltri[:], adt[:], start=True, stop=True, r=["ltri", ("adt", jb)], w=[pk(bsm)])
                    pe("matmul", psum[:, bsm, 8:16], ones_f[:], adt[:], start=True, stop=True, r=["ones_f", ("adt", jb)], w=[pk(bsm)])
                    segv = PS2(bs).rearrange("p (h l) -> p h l", h=8)
                    act("activation", ET[:], segv, AF.Exp, r=[pk(bs), pk(bs + 1)], w=[("ET", jb)])
                    act("activation", dec[:], segv[:, :, 127], AF.Exp, r=[pk(bs), pk(bs + 1)], w=[("dec", jb)])
                    act("activation", small[:], PS(bsm, 16), AF.Exp, r=[pk(bsm)], w=[("small", jb)])
                    bsc = nbank()
                    for g in range(2):
                        pe("matmul", psum[:, bsc, g * 128:(g + 1) * 128], BTs[:, g, cs], CTs[:, g, cs], start=True, stop=True,
                           r=["BTs", "CTs"], w=[pk(bsc)])
                    dve("tensor_tensor", sm[:], PS(bsc, 256).rearrange("p (g l) -> p g l", g=2),
                        ltri_bf[:].unsqueeze(1).to_broadcast([128, 2, 128]), ALU.mult, r=[pk(bsc), "ltri_bf"], w=[("sm", jb)])
                    for g in range(2):
                        dve("tensor_tensor", MT[:, g * 4:(g + 1) * 4, :], ET[:, g * 4:(g + 1) * 4, :],
                            sm[:, g:g + 1, :].to_broadcast([128, 4, 128]), ALU.mult, r=[("ET", jb), ("sm", jb)], w=[("MT", jb)])
                    xsv = xs4[pb][:, j, :].rearrange("p (h e) -> p h e", h=8)
                    pool("tensor_tensor", xdt[:], xsv, dtt.unsqueeze(2).to_broadcast([128, 8, 64]), ALU.mult,
                         r=[("xs4", pb), "dt_all"], w=[("xdt", jb)])
                    pool("tensor_tensor", xdd[:], xdt[:], dec[:].unsqueeze(2).to_broadcast([128, 8, 64]), ALU.mult,
                         r=[("xdt", jb), ("dec", jb)], w=[("xdd", jb)])

                def s2(blk, j):
                    pb = blk % 2
                    jb = (blk * 4 + j) % 3
                    t = blk * 4 + j
                    cs = slice(t * 128, (t + 1) * 128)
                    adt = adt_2[jb]; rhs_all = rhs_all_2[jb]; ET = ET_2[jb]; small = small_2[jb]; dec = dec_2[jb]
                    sm = sm_2[jb]; MT = MT_2[jb]; xdt = xdt_2[jb]; xdd = xdd_2[jb]
                    yo = yo_2[jb]; y = y_2[jb]; xsd = xsd_2[jb]
                    xsv = xs4[pb][:, j, :].rearrange("p (h e) -> p h e", h=8)
                    byd = nbank()
                    for hd in range(8):
                        pe("matmul", psum[:, byd, hd * 64:(hd + 1) * 64], MT[:, hd, :], xdt[:, hd, :], start=True, stop=True,
                           r=[("MT", jb), ("xdt", jb)], w=[pk(byd)])
                    byo = nbank()
                    for hd in range(8):
                        pe("matmul", psum[:, byo, hd * 64:(hd + 1) * 64], CTs[:, hd // 4, cs], state_bf[:, hd, :], start=True, stop=True,
                           r=["CTs", "state_bf"], w=[pk(byo)])
                    bst = nbank()
                    for hd in range(8):
                        pe("matmul", psum[:, bst, hd * 64:(hd + 1) * 64], B4[pb][:, j, (hd // 4) * 128:(hd // 4 + 1) * 128],
                           xdd[:, hd, :], start=True, stop=True, r=[("B4", pb), ("xdd", jb)], w=[pk(bst)])
                    dve("tensor_tensor", yo[:], PS(byo).rearrange("p (h e) -> p h e", h=8),
                        small[:, 0:8].unsqueeze(2).to_broadcast([128, 8, 64]), ALU.mult, r=[pk(byo), ("small", jb)], w=[("yo", jb)])
                    dve("tensor_tensor", y[:], yo[:].rearrange("p h e -> p (h e)"), PS(byd), ALU.add, r=[("yo", jb), pk(byd)], w=[("y", jb)])
                    pool("tensor_tensor", xsd[:].rearrange("p (h e) -> p h e", h=8), xsv,
                         dsk[:].unsqueeze(2).to_broadcast([128, 8, 64]), ALU.mult, r=[("xs4", pb), "dsk"], w=[("xsd", jb)])
                    pool("tensor_tensor", y[:], y[:], xsd[:], ALU.add, r=[("y", jb), ("xsd", jb)], w=[("y", jb)])
                    dve("tensor_tensor", state[:], state[:], small[:, 8:16].unsqueeze(2).to_broadcast([128, 8, 64]), ALU.mult,
                        r=["state", ("small", jb)], w=["state"])
                    dve("tensor_tensor", state[:], state[:], PS(bst).rearrange("p (h e) -> p h e", h=8), ALU.add,
                        r=["state", pk(bst)], w=["state"])
                    act("copy", state_bf[:], state[:], r=["state"], w=["state_bf"])
                    dve("tensor_tensor", yg4[:, j, :], y[:], sz4[pb][:, j, :], ALU.mult, r=[("y", jb), ("sz4", pb)], w=["yg4"])
                    for g in range(2):
                        act("activation", junk[:], yg4[:, j, g * 256:(g + 1) * 256], AF.Square, accum_out=ssg[:, j, g:g + 1],
                            r=["yg4"], w=["junk", "ssg"])

                def end_blk(blk):
                    act("activation", rsg[:], ssg[:], AF.Ln, scale=1.0 / 256, bias=cst[:, 0:1], r=["ssg", "cst"], w=["rsg"])
                    act("activation", rsg[:], rsg[:], AF.Exp, scale=-0.5, r=["rsg"], w=["rsg"])
                    for j in range(4):
                        yn = yn_2[j % 2]
                        for g in range(2):
                            dve("scalar_tensor_tensor", yn[:, g * 256:(g + 1) * 256], yg4[:, j, g * 256:(g + 1) * 256], rsg[:, j, g:g + 1],
                                nwb[:, g * 256:(g + 1) * 256], ALU.mult, ALU.mult, r=["yg4", "rsg", "nwb"], w=[("yn", j % 2)])
                        b = nbank()
                        for cch in range(4):
                            pe("transpose", PS16(b)[:, cch * 128:(cch + 1) * 128], yn[:, cch * 128:(cch + 1) * 128], ident[:],
                               r=[("yn", j % 2), "ident"], w=[pk(b)])
                        act("copy", ygT[:, :, j * 128:(j + 1) * 128], PS16(b)[:, 0:512].rearrange("p (c t) -> p c t", c=4),
                            r=[pk(b)], w=["ygT"])
                    P.dma(mixT_d[512:1024, blk * 512:(blk + 1) * 512].rearrange("(c p) s -> p c s", p=128), ygT[:],
                          reads=["ygT"], eng="act")

                wgc = None
                for i in range(NT + 2):
                    if wgc is not None:
                        for _k in range(3):
                            next(wgc, None)
                    if i < NT:
                        if i % 4 == 0:
                            load_blk(i // 4)
                        s1(i // 4, i % 4)
                    if i >= 2:
                        s2((i - 2) // 4, (i - 2) % 4)
                        if (i - 2) % 4 == 3:
                            end_blk((i - 2) // 4)
                if wgc is not None:
                    for _ in wgc:
                        pass
                P.flush()

        def phase_D(l, xsrc, xdst):
            with contextlib.ExitStack() as st:
                sb = mk_sb(st)
                wo = sb([128, 8, D], BF16)
                P.dma(wo[:], woutb_d_L[l].rearrange("(k p) c -> p k c", p=128), writes=["wo"])
                mT = [sb([128, 8, 512], BF16) for _ in range(2)]
                xt = [sb([128, D], F32) for _ in range(4)]
                for blk in range(NB):
                    pb = blk % 2
                    P.dma(mT[pb][:], mixT_d[:, blk * 512:(blk + 1) * 512].rearrange("(k p) s -> p k s", p=128), writes=[("mT", pb)])
                    for j in range(4):
                        t = blk * 4 + j
                        i2 = t % 4
                        P.dma(xt[i2][:], xsrc[t * 128:(t + 1) * 128, :], writes=[("xt", i2)])
                        bp = npair()
                        for hf in range(2):
                            for k in range(8):
                                pe("matmul", PS(bp + hf), mT[pb][:, k, j * 128:(j + 1) * 128], wo[:, k, hf * 512:(hf + 1) * 512],
                                   start=(k == 0), stop=(k == 7), r=[("mT", pb), "wo"], w=[pk(bp + hf)])
                        dve("tensor_tensor", xt[i2][:], PS2(bp), xt[i2][:], ALU.add, r=[pk(bp), pk(bp + 1), ("xt", i2)], w=[("xt", i2)])
                        P.dma(xdst[t * 128:(t + 1) * 128, :], xt[i2][:], reads=[("xt", i2)], eng="act")
                P.flush()

        def phase_E(l, xsrc, xdst):
            with contextlib.ExitStack() as st:
                sb = mk_sb(st)
                NF = D_FF // 128
                wgu = sb([128, 8, 2 * D_FF], BF16)
                wd = sb([128, NF, D], BF16)
                weff = sb([128, D], F32); shb = sb([128, D], F32)
                P.dma(weff[:], modrow_d_L[l][3:4, :].partition_broadcast(128), writes=["weff"])
                P.dma(shb[:], modrow_d_L[l][4:5, :].partition_broadcast(128), writes=["shb"])
                xt = [sb([128, D], F32) for _ in range(4)]
                ssx = sb([128, 4], F32); rsx = sb([128, 4], F32)
                htmp = [sb([128, D], F32)] * 2
                h = [sb([128, D], BF16) for _ in range(4)]
                hT = sb([128, 8, 512], BF16)
                sg = [sb([128, 512], F32) for _ in range(2)]
                aT = sb([128, NF, 512], BF16)

                def n_stats(blk, j):
                    t = blk * 4 + j
                    P.dma(xt[j][:], xsrc[t * 128:(t + 1) * 128, :], writes=[("xt", j)])
                    act("activation", h[3][:], xt[j][:], AF.Square, accum_out=ssx[:, j:j + 1],
                        r=[("xt", j)], w=[("h", 3), "ssx"])

                def n_rstd():
                    act("activation", rsx[:], ssx[:], AF.Ln, scale=1.0 / D, bias=cst[:, 0:1], r=["ssx", "cst"], w=["rsx"])
                    act("activation", rsx[:], rsx[:], AF.Exp, scale=-0.5, r=["rsx"], w=["rsx"])

                def n_h(j):
                    ht = htmp[j % 2]
                    dve("scalar_tensor_tensor", ht[:], xt[j][:], rsx[:, j:j + 1], weff[:], ALU.mult, ALU.mult,
                        r=[("xt", j), "rsx", "weff"], w=["htmp"])
                    pool("tensor_tensor", h[j][:], ht[:], shb[:], ALU.add, r=["htmp", "shb"], w=[("h", j)])

                def n_T():
                    for j in range(4):
                        b = nbank()
                        for k in range(8):
                            pe("transpose", PS16(b)[:, k * 128:(k + 1) * 128], h[j][:, k * 128:(k + 1) * 128], ident[:],
                               r=[("h", j), "ident"], w=[pk(b)])
                        act("copy", hT[:, :, j * 128:(j + 1) * 128], PS16(b).rearrange("p (k t) -> p k t", k=8),
                            r=[pk(b)], w=["hT"])

                for j in range(4):
                    n_stats(0, j)
                for k in range(8):
                    P.dma(wgu[:, k, :], wgub_d_L[l][k * 128:(k + 1) * 128, :], writes=[("wgu", k)])
                P.dma(wd[:], wdb_d_L[l].rearrange("(k p) c -> p k c", p=128), writes=["wd"])
                n_rstd()
                for j in range(4):
                    n_h(j)
                n_T()
                for blk in range(NB):
                    nxt = blk + 1 < NB
                    for f in range(NF):
                        bg = nbank()
                        for k in range(8):
                            pe("matmul", PS(bg), wgu[:, k, f * 128:(f + 1) * 128], hT[:, k, :], start=(k == 0), stop=(k == 7),
                               r=[("wgu", k), "hT"], w=[pk(bg)])
                        bu = nbank()
                        for k in range(8):
                            pe("matmul", PS(bu), wgu[:, k, D_FF + f * 128:D_FF + (f + 1) * 128], hT[:, k, :], start=(k == 0), stop=(k == 7),
                               r=[("wgu", k), "hT"], w=[pk(bu)])
                        act("activation", sg[f % 2][:], PS(bg), AF.Silu, r=[pk(bg)], w=[("sg", f % 2)])
                        dve("tensor_tensor", aT[:, f, :], PS(bu), sg[f % 2][:], ALU.mult, r=[pk(bu), ("sg", f % 2)], w=["aT"])
                        if nxt:
                            if f in (1, 3, 5, 7):
                                n_stats(blk + 1, (f - 1) // 2)
                            elif f == 9:
                                n_rstd()
                            elif f in (11, 13, 15, 17):
                                n_h((f - 11) // 2)
                    for j in range(4):
                        t = blk * 4 + j
                        P.dma(xt[j][:], xsrc[t * 128:(t + 1) * 128, :], writes=[("xt", j)])
                        bp = npair()
                        for hf in range(2):
                            for f in range(NF):
                                pe("matmul", PS(bp + hf), aT[:, f, j * 128:(j + 1) * 128], wd[:, f, hf * 512:(hf + 1) * 512],
                                   start=(f == 0), stop=(f == NF - 1), r=["aT", "wd"], w=[pk(bp + hf)])
                        if j == 0 and nxt:
                            n_T()
                        dve("tensor_tensor", xt[j][:], PS2(bp), xt[j][:], ALU.add, r=[pk(bp), pk(bp + 1), ("xt", j)], w=[("xt", j)])
                        P.dma(xdst[t * 128:(t + 1) * 128, :], xt[j][:], reads=[("xt", j)], eng="act")
                P.flush()

        stages = []
        stages.append(("const", phase_const))
        stages.append(("wprep0", phase_start))
        for l in range(nlayers):
            xsrc = x_in if l == 0 else xb_d
            xfin = out if l == nlayers - 1 else xb_d
            stages.append(("A%d" % l, partial(phase_A, l, xsrc)))
            stages.append(("B%d" % l, partial(phase_B, l)))
            stages.append(("C%d" % l, partial(phase_C, l)))
            stages.append(("D%d" % l, partial(phase_D, l, xsrc, xa_d)))
            stages.append(("E%d" % l, partial(phase_E, l, xa_d, xfin)))
        for name, fn in stages:
            fn()
            if upto is not None and name == upto:
                break
    return nc


_INPUT_NAMES = ["norm1_w", "norm2_w", "w_ada", "b_ada", "w_in", "q_a_norm_w", "w_q_up", "kv_a_norm_w", "w_kv_up",
                "q_nope_norm_w", "q_pe_norm_w", "k_nope_norm_w", "k_pe_norm_w", "conv_w", "conv_b", "dt_bias",
                "a_log", "d_skip", "ssd_norm_w", "w_out", "w_gate_up", "w_down"]


def kernel(x, c, positions, **w):
    x = np.asarray(x); c = np.asarray(c); positions = np.asarray(positions)
    Bn, S, _ = x.shape
    nc = build(S)
    shared = {k: np.ascontiguousarray(np.asarray(w[k], dtype=np.float32)) for k in _INPUT_NAMES}
    in_maps = []
    for b in range(Bn):
        m = dict(shared)
        m["x"] = np.ascontiguousarray(x[b], dtype=np.float32)
        m["c"] = np.ascontiguousarray(c[b:b + 1], dtype=np.float32)
        m["positions"] = np.ascontiguousarray(positions[b:b + 1], dtype=np.int32)
        in_maps.append(m)
    res = run_bass_kernel_spmd(nc, in_maps, core_ids=list(range(Bn)))
    return np.stack([np.asarray(r["out"], dtype=np.float32) for r in res.results], axis=0)
```
